# Optimizing a Trainium2 kernel written in Bass

```python
import jax, jax.numpy as jnp
from jax import lax
import numpy as np

D_MODEL = 2048
BATCH = 16
SEQ = 256
DEPTH = 2
DEC_BATCH = 8
DEC_SEQ = 2048
PAST_LEN = 512

GRID_W = 64
HEAD_DIM = 128
ATTN_HEADS = (D_MODEL // 2) // HEAD_DIM
ATTN_KV_HEADS = 2
ATTN_WIDTH = ATTN_HEADS * HEAD_DIM
KV_WIDTH = ATTN_KV_HEADS * HEAD_DIM
FOURIER_WIDTH = D_MODEL // 4
FOURIER_GROUPS = 4
RWKV_WIDTH = D_MODEL // 4
RWKV_N = 64
RWKV_HEADS = RWKV_WIDTH // RWKV_N
DECAY_LORA = 64
ICLR_LORA = 64
GATE_LORA = 128
RWKV_IN = 3 * RWKV_WIDTH + DECAY_LORA + ICLR_LORA + GATE_LORA
MIX_WIDTH = FOURIER_WIDTH + ATTN_WIDTH + RWKV_WIDTH
IN_WIDTH = FOURIER_WIDTH + ATTN_WIDTH + 2 * KV_WIDTH + RWKV_IN
D_FF = 5504
Q_BLOCK = 128
ROPE_THETA = 10000.0
NORM_EPS = 1e-6
GN_EPS = 64e-5

kernel_name = "hybrid_fourier_gqa_rwkv7_diffusion_step"


def rmsnorm(x, g):
    x32 = x.astype(jnp.float32)
    y = x32 * lax.rsqrt(jnp.mean(x32 * x32, axis=-1, keepdims=True) + NORM_EPS)
    return (y * g.astype(jnp.float32)).astype(x.dtype)


def dwconv3(x, w):
    xp = jnp.pad(x, ((0, 0), (1, 1), (0, 0)))
    return w[0] * xp[:, :-2] + w[1] * xp[:, 1:-1] + w[2] * xp[:, 2:]


def grid_positions(T):
    n_rows = T // GRID_W
    rows = jnp.repeat(jnp.arange(n_rows, dtype=jnp.int32), GRID_W)
    cols = jnp.tile(jnp.arange(GRID_W, dtype=jnp.int32), n_rows)
    return rows, cols


def axial_rope(x, rows, cols):
    def rot(xh, pos):
        d = xh.shape[-1]
        inv = 1.0 / (ROPE_THETA ** (jnp.arange(0, d, 2, dtype=jnp.float32) / d))
        ang = pos.astype(jnp.float32)[:, None] * inv[None, :]
        cos = jnp.concatenate([jnp.cos(ang), jnp.cos(ang)], -1)[None, :, None, :]
        sin = jnp.concatenate([jnp.sin(ang), jnp.sin(ang)], -1)[None, :, None, :]
        x1, x2 = jnp.split(xh, 2, axis=-1)
        return xh * cos + jnp.concatenate([-x2, x1], -1) * sin
    half = x.shape[-1] // 2
    x32 = x.astype(jnp.float32)
    out = jnp.concatenate([rot(x32[..., :half], rows), rot(x32[..., half:], cols)], -1)
    return out.astype(x.dtype)


def block_attention(q, k, v):
    B, T, H, Dh = q.shape
    KV = k.shape[2]
    G = H // KV
    nb = T // Q_BLOCK
    qb = q.reshape(B, nb, Q_BLOCK, KV, G, Dh).transpose(1, 0, 2, 3, 4, 5)
    scale = Dh ** -0.5

    def one_block(qblk):
        s = jnp.einsum('bqkgd,bskd->bkgqs', qblk, k).astype(jnp.float32) * scale
        p = jax.nn.softmax(s, axis=-1).astype(v.dtype)
        return jnp.einsum('bkgqs,bskd->bqkgd', p, v)

    o = lax.map(one_block, qb)
    return o.transpose(1, 0, 2, 3, 4, 5).reshape(B, T, H * Dh)


def fourier_mix(u):
    B, T, C = u.shape
    ug = u.astype(jnp.float32).reshape(B, T, FOURIER_GROUPS, C // FOURIER_GROUPS)
    f = jnp.fft.fft2(ug, axes=(1, 3), norm="ortho")
    return jnp.real(f).reshape(B, T, C).astype(u.dtype)


def rwkv_mix(rw, lp, init_state):
    B, T, _ = rw.shape
    C, H, N = RWKV_WIDTH, RWKV_HEADS, RWKV_N
    z = rw.astype(jnp.float32)
    r, k, v, w_in, a_in, g_in = jnp.split(
        z, [C, 2 * C, 3 * C, 3 * C + DECAY_LORA, 3 * C + DECAY_LORA + ICLR_LORA], axis=-1)
    w = -jax.nn.softplus(-(lp['rw_w0'] + jnp.einsum('btl,dlc->btdc', jnp.tanh(w_in), lp['rw_w2']))) - 0.5
    decay = jnp.exp(-jnp.exp(w))
    a = jax.nn.sigmoid(lp['rw_a0'] + jnp.einsum('btl,dlc->btdc', a_in, lp['rw_a2']))
    g = jax.nn.sigmoid(g_in) @ lp['rw_g2']
    kk = (k * lp['rw_kk']).reshape(B, T, H, N)
    kk = kk * lax.rsqrt(jnp.sum(kk * kk, axis=-1, keepdims=True) + 1e-12)
    kmod = k[:, :, None, :] * (1.0 + (a - 1.0) * lp['rw_ka'])

    def heads(t):
        return t.reshape(t.shape[:-1] + (H, N))

    def both(t):
        return jnp.stack([t, t[:, ::-1]], axis=2)

    def orient(t):
        return jnp.stack([t[:, :, 0], t[:, ::-1, 1]], axis=2)

    xs = (both(heads(r)), orient(heads(decay)), orient(heads(kmod)), both(heads(v)), both(kk), orient(heads(a)))
    xs = tuple(jnp.moveaxis(t, 1, 0) for t in xs)

    def step(S, inp):
        r_t, w_t, k_t, v_t, kk_t, a_t = inp
        sa = jnp.einsum('bdhvk,bdhk->bdhv', S, kk_t)
        S = (S * w_t[..., None, :] - sa[..., :, None] * (kk_t * a_t)[..., None, :]
             + v_t[..., :, None] * k_t[..., None, :])
        return S, jnp.einsum('bdhvk,bdhk->bdhv', S, r_t)

    S_fin, y = lax.scan(step, init_state.astype(jnp.float32), xs)
    y = jnp.moveaxis(y, 0, 1)
    y = y[:, :, 0] + y[:, ::-1, 1]
    mu = jnp.mean(y, axis=-1, keepdims=True)
    var = jnp.mean(jnp.square(y - mu), axis=-1, keepdims=True)
    y = ((y - mu) * lax.rsqrt(var + GN_EPS)).reshape(B, T, C) * lp['rw_lnx_g'] + lp['rw_lnx_b']
    bonus = jnp.einsum('bthn,btdhn,hn->bth', heads(r), heads(kmod), lp['rw_rk'])[..., None] * heads(v)
    out = (y + bonus.reshape(B, T, C)) * g
    return out.astype(rw.dtype), S_fin.astype(rw.dtype)


def trunk_layer(x, mod, pos, ctx_k, ctx_v, init_state, lp):
    B, T, _ = x.shape
    shift1, scale1, gate1, shift2, scale2, gate2 = jnp.split(mod, 6, axis=-1)
    h = rmsnorm(x, lp['norm1_g']) * (1.0 + scale1) + shift1
    proj = h @ lp['w_in']
    u_f, q, k, v, rw = jnp.split(
        proj, [FOURIER_WIDTH, FOURIER_WIDTH + ATTN_WIDTH, FOURIER_WIDTH + ATTN_WIDTH + KV_WIDTH,
               FOURIER_WIDTH + ATTN_WIDTH + 2 * KV_WIDTH], axis=-1)
    f_out = fourier_mix(u_f)
    q = rmsnorm(q.reshape(B, T, ATTN_HEADS, HEAD_DIM), lp['q_norm_g'])
    k = rmsnorm(k.reshape(B, T, ATTN_KV_HEADS, HEAD_DIM), lp['k_norm_g'])
    v = v.reshape(B, T, ATTN_KV_HEADS, HEAD_DIM)
    keys, vals = k, v
    if pos is not None:
        q = axial_rope(q, pos[0], pos[1])
        keys = axial_rope(k, pos[0], pos[1])
    if ctx_k is not None:
        keys = jnp.concatenate([keys, ctx_k.astype(keys.dtype)], axis=1)
        vals = jnp.concatenate([vals, ctx_v.astype(vals.dtype)], axis=1)
    a_out = block_attention(q, keys, vals)
    r_out, final_state = rwkv_mix(dwconv3(rw, lp['rw_conv']), lp, init_state)
    mix = jnp.concatenate([f_out, a_out, r_out], axis=-1) @ lp['w_out']
    x = x + gate1 * mix
    h2 = rmsnorm(x, lp['norm2_g']) * (1.0 + scale2) + shift2
    u = dwconv3(h2 @ lp['ffn_up'], lp['ffn_conv_w']) + lp['ffn_conv_b']
    ua, ug = jnp.split(u, 2, axis=-1)
    x = x + gate2 * ((jax.nn.silu(ug) * ua) @ lp['ffn_down'])
    return x, k, v, final_state


def setup_inputs(seed: int = 0) -> dict:
    key = jax.random.key(seed)
    ks = iter(jax.random.split(key, 40))

    def nrm(shape, scale):
        return jax.random.normal(next(ks), shape, jnp.float32) * scale

    conv_base = jnp.array([0.25, 0.5, 0.25], jnp.float32)[None, :, None]
    return {
        "x_prompt": nrm((BATCH, SEQ, D_MODEL), 1.0),
        "x_sample": nrm((DEC_BATCH, DEC_SEQ, D_MODEL), 1.0),
        "cache_attn_k": nrm((DEC_BATCH, DEPTH, PAST_LEN, ATTN_KV_HEADS, HEAD_DIM), 1.0),
        "cache_attn_v": nrm((DEC_BATCH, DEPTH, PAST_LEN, ATTN_KV_HEADS, HEAD_DIM), 1.0),
        "state_rwkv": nrm((DEC_BATCH, DEPTH, 2, RWKV_HEADS, RWKV_N, RWKV_N), 0.3),
        "c": nrm((DEC_BATCH, D_MODEL), 1.0),
        "c_ctx": nrm((D_MODEL,), 1.0),
        "w_ada": nrm((DEPTH, D_MODEL, 6 * D_MODEL), 0.5 * D_MODEL ** -0.5),
        "b_ada": nrm((DEPTH, 6 * D_MODEL), 0.01),
        "norm1_g": 1.0 + nrm((DEPTH, D_MODEL), 0.02),
        "norm2_g": 1.0 + nrm((DEPTH, D_MODEL), 0.02),
        "w_in": nrm((DEPTH, D_MODEL, IN_WIDTH), D_MODEL ** -0.5),
        "w_out": nrm((DEPTH, MIX_WIDTH, D_MODEL), MIX_WIDTH ** -0.5),
        "q_norm_g": 1.0 + nrm((DEPTH, HEAD_DIM), 0.02),
        "k_norm_g": 1.0 + nrm((DEPTH, HEAD_DIM), 0.02),
        "rw_conv": conv_base + nrm((DEPTH, 3, RWKV_IN), 0.05),
        "rw_w0": nrm((DEPTH, 2, RWKV_WIDTH), 0.5),
        "rw_w2": nrm((DEPTH, 2, DECAY_LORA, RWKV_WIDTH), 0.5 * DECAY_LORA ** -0.5),
        "rw_a0": nrm((DEPTH, 2, RWKV_WIDTH), 0.5),
        "rw_a2": nrm((DEPTH, 2, ICLR_LORA, RWKV_WIDTH), 0.5 * ICLR_LORA ** -0.5),
        "rw_g2": nrm((DEPTH, GATE_LORA, RWKV_WIDTH), GATE_LORA ** -0.5),
        "rw_kk": 0.85 + nrm((DEPTH, RWKV_WIDTH), 0.05),
        "rw_ka": 1.0 + nrm((DEPTH, RWKV_WIDTH), 0.05),
        "rw_rk": nrm((DEPTH, RWKV_HEADS, RWKV_N), 0.1),
        "rw_lnx_g": 1.0 + nrm((DEPTH, RWKV_WIDTH), 0.02),
        "rw_lnx_b": nrm((DEPTH, RWKV_WIDTH), 0.01),
        "ffn_up": nrm((DEPTH, D_MODEL, 2 * D_FF), D_MODEL ** -0.5),
        "ffn_conv_w": conv_base + nrm((DEPTH, 3, 2 * D_FF), 0.05),
        "ffn_conv_b": nrm((DEPTH, 2 * D_FF), 0.01),
        "ffn_down": nrm((DEPTH, D_FF, D_MODEL), D_FF ** -0.5),
        "final_norm_g": 1.0 + nrm((D_MODEL,), 0.02),
    }


def reference(x_prompt, x_sample, cache_attn_k, cache_attn_v, state_rwkv, c, c_ctx,
              w_ada, b_ada, norm1_g, norm2_g, w_in, w_out, q_norm_g, k_norm_g,
              rw_conv, rw_w0, rw_w2, rw_a0, rw_a2, rw_g2, rw_kk, rw_ka, rw_rk, rw_lnx_g, rw_lnx_b,
              ffn_up, ffn_conv_w, ffn_conv_b, ffn_down, final_norm_g):
    B_p = x_prompt.shape[0]
    pos_lat = grid_positions(x_sample.shape[1])
    xp, xs = x_prompt, x_sample
    new_k, new_v, new_s = [], [], []
    for l in range(DEPTH):
        lp = dict(norm1_g=norm1_g[l], norm2_g=norm2_g[l], w_in=w_in[l], w_out=w_out[l],
                  q_norm_g=q_norm_g[l], k_norm_g=k_norm_g[l], rw_conv=rw_conv[l],
                  rw_w0=rw_w0[l], rw_w2=rw_w2[l], rw_a0=rw_a0[l], rw_a2=rw_a2[l], rw_g2=rw_g2[l],
                  rw_kk=rw_kk[l], rw_ka=rw_ka[l], rw_rk=rw_rk[l], rw_lnx_g=rw_lnx_g[l],
                  rw_lnx_b=rw_lnx_b[l], ffn_up=ffn_up[l], ffn_conv_w=ffn_conv_w[l],
                  ffn_conv_b=ffn_conv_b[l], ffn_down=ffn_down[l])
        mod_ctx = (jax.nn.silu(c_ctx) @ w_ada[l] + b_ada[l])[None, None, :]
        zero_state = jnp.zeros((B_p, 2, RWKV_HEADS, RWKV_N, RWKV_N), jnp.float32)
        xp, k_ctx, v_ctx, s_ctx = trunk_layer(xp, mod_ctx, None, None, None, zero_state, lp)
        new_k.append(k_ctx)
        new_v.append(v_ctx)
        new_s.append(s_ctx)
        mod_lat = (jax.nn.silu(c) @ w_ada[l] + b_ada[l])[:, None, :]
        xs, _, _, _ = trunk_layer(xs, mod_lat, pos_lat, cache_attn_k[:, l], cache_attn_v[:, l],
                                  state_rwkv[:, l], lp)
    y_prompt = rmsnorm(xp, final_norm_g)
    y_sample = rmsnorm(xs, final_norm_g)
    new_attn_k = jnp.stack(new_k, axis=1)
    new_attn_v = jnp.stack(new_v, axis=1)
    new_state_rwkv = jnp.stack(new_s, axis=1)
    return (y_prompt, y_sample, new_attn_k, new_attn_v, new_state_rwkv)
```

```python
import contextlib
import math
import numpy as np
import ml_dtypes
import concourse.bass as bass
import concourse.mybir as mybir
from concourse.bass_utils import run_bass_kernel_spmd

F32 = mybir.dt.float32
BF16 = mybir.dt.bfloat16
ALU = mybir.AluOpType
AF = mybir.ActivationFunctionType
AX = mybir.AxisListType

EPOCH = 8000
RING = 8


class Buf:
    __slots__ = ("wc", "wd", "rc", "rd", "excl")

    def __init__(self, excl=False):
        self.excl = excl
        self.wc = {}
        self.wd = {}
        self.rc = {}
        self.rd = {}


class Op:
    __slots__ = ("eng", "fn", "waits", "done", "is_dma", "stage")

    def __init__(self, eng, fn, is_dma):
        self.stage = None
        self.eng = eng
        self.fn = fn
        self.waits = {}
        self.done = None
        self.is_dma = is_dma


class Prog:
    ENGS = ("pe", "act", "dve", "pool", "sp")

    def __init__(self, nc):
        self.nc = nc
        self.q = {e: [] for e in self.ENGS}
        self.cnt = {e: 0 for e in self.ENGS}
        self.dcnt = {e: 0 for e in self.ENGS}
        self.sems = {}
        self.semkeys = []
        self.last_dma = {}
        self.pending = {}

    def _semkey(self, key):
        if key not in self.sems:
            self.sems[key] = None
            self.semkeys.append(key)
        return key

    def _add_dep(self, op, dep, raw):
        if dep is None or dep is op:
            return
        if dep.eng == op.eng and not dep.is_dma and not op.is_dma:
            if op.eng == "pe":
                return
        key, val = dep.done
        if op.waits.get(key, 0) < val:
            op.waits[key] = val

    def op(self, eng, fn, reads=(), writes=(), dma=False):
        o = Op(eng, fn, dma)
        o.stage = getattr(self, "stage", None)
        if any(b.excl for b in reads):
            writes = list(writes) + [b for b in reads if b.excl and b not in writes]
            reads = [b for b in reads if not b.excl]
        self.nops = getattr(self, "nops", 0) + 1
        if self.nops > getattr(self, "limit", 1 << 60):
            return o
        pend = self.pending.pop(eng, None)
        if pend:
            for key, val in pend.items():
                if o.waits.get(key, 0) < val:
                    o.waits[key] = val
        for b in reads:
            for d in b.wc.values():
                self._add_dep(o, d, True)
            for lst in b.wd.values():
                for d in lst:
                    self._add_dep(o, d, True)
        for b in writes:
            for d in b.wc.values():
                self._add_dep(o, d, False)
            for lst in b.wd.values():
                for d in lst:
                    self._add_dep(o, d, False)
            for d in b.rc.values():
                self._add_dep(o, d, False)
            for lst in b.rd.values():
                for d in lst:
                    self._add_dep(o, d, False)
        if dma:
            k = self.dcnt[eng]
            self.dcnt[eng] += 1
            slot = k % RING
            key = self._semkey(("d", eng, slot))
            o.done = (key, 16 * (k // RING + 1))
            prev = self.last_dma.get((eng, slot))
            if prev is not None:
                self._add_dep(o, prev, True)
            self.last_dma[(eng, slot)] = o
        else:
            k = self.cnt[eng]
            self.cnt[eng] += 1
            key = self._semkey(("c", eng, k // EPOCH))
            o.done = (key, k % EPOCH + 1)
        for b in reads:
            if dma:
                lst = b.rd.setdefault(eng, [])
                lst.append(o)
                if len(lst) > RING:
                    del lst[0]
            else:
                b.rc[eng] = o
        for b in writes:
            b.rc = {}
            b.rd = {}
            if dma:
                lst = b.wd.setdefault(eng, [])
                lst.append(o)
                if len(lst) > RING:
                    del lst[0]
            else:
                b.wc[eng] = o
        self.q[eng].append(o)
        return o

    def emit(self):
        nc = self.nc
        with contextlib.ExitStack() as st:
            for key in self.semkeys:
                self.sems[key] = st.enter_context(nc.semaphore("s_" + "_".join(str(x) for x in key)))
            block = st.enter_context(nc.Block())
            engmap = {"pe": block.tensor, "act": block.scalar, "dve": block.vector,
                      "pool": block.gpsimd, "sp": block.sync}
            all_ops = self.q

            def make(ename):
                ops = all_ops[ename]

                def body(e):
                    seen = {}
                    for o in ops:
                        for key, val in o.waits.items():
                            if seen.get(key, 0) >= val:
                                continue
                            seen[key] = val
                            e.wait_ge(self.sems[key], val)
                        ins = o.fn(e)
                        if self.annotate and o.stage:
                            ins.annotate(o.stage)
                        key, val = o.done
                        ins.then_inc(self.sems[key], 16 if o.is_dma else 1)
                    if ename == "sp":
                        fin = {}
                        for en in self.ENGS:
                            for o in all_ops[en][-1:]:
                                key, val = o.done
                                fin[key] = max(fin.get(key, 0), val)
                        for o in self.last_dma.values():
                            key, val = o.done
                            fin[key] = max(fin.get(key, 0), val)
                        for key, val in fin.items():
                            if seen.get(key, 0) < val:
                                e.wait_ge(self.sems[key], val)
                return body

            for ename in self.ENGS:
                if all_ops[ename] or ename == "sp":
                    engmap[ename](make(ename))


class TB:
    def __init__(self, h, excl=False):
        self.h = h
        self.bufs = {}
        self.excl = excl

    def __getitem__(self, idx):
        return self.h[idx]

    def b(self, key=None):
        if key not in self.bufs:
            self.bufs[key] = Buf(self.excl)
        return self.bufs[key]


class TBV:
    def __init__(self, ap):
        self.ap = ap
        self.buf = Buf(True)

    def __getitem__(self, idx):
        return self.ap[idx]

    def b(self, key=None):
        return self.buf


class DT:
    def __init__(self, ap):
        self.ap = ap
        self.bufs = {}

    def __getitem__(self, idx):
        return self.ap[idx]

    def b(self, key=None):
        if key not in self.bufs:
            self.bufs[key] = Buf()
        return self.bufs[key]


D = 2048
KT = 16
NH = 8
NKV = 2
PAST = 512
IN_W = 3840
RW_IN = 1792
GRID_W = 64
NORM_EPS = 1e-6
GN_EPS = 64e-5
LWC = -math.exp(-0.5)


def host_consts(TS, TP):
    c = {}
    bf = ml_dtypes.bfloat16
    c["ident_f"] = np.eye(128, dtype=np.float32)
    c["ident_b"] = np.eye(128, dtype=np.float32).astype(bf)
    c["ones_b"] = np.ones((128, 128), np.float32).astype(bf)
    bo = np.zeros((128, 128), np.float32)
    bo[:64, :64] = 1.0
    bo[64:, 64:] = 1.0
    c["bones_b"] = bo.astype(bf)
    c["bones_f"] = bo
    prot = np.zeros((128, 128), np.float32)
    for m in range(128):
        j = m % 64
        if j < 32:
            prot[m + 32, m] = -1.0
        else:
            prot[m - 32, m] = 1.0
    c["prot_f"] = prot
    t = np.arange(TS)
    rows = (t // GRID_W).astype(np.float64)
    cols = (t % GRID_W).astype(np.float64)
    inv = 1.0 / (10000.0 ** (np.arange(0, 64, 2, dtype=np.float64) / 64.0))
    cosT = np.zeros((128, TS), np.float64)
    sinT = np.zeros((128, TS), np.float64)
    for d in range(128):
        pos = rows if d < 64 else cols
        f = inv[(d % 64) % 32]
        ang = np.float32(pos).astype(np.float32) * np.float32(f)
        cosT[d] = np.cos(ang.astype(np.float64))
        sinT[d] = np.sin(ang.astype(np.float64))
    c["cosT"] = cosT.astype(np.float32)
    c["sinT"] = sinT.astype(np.float32)
    i = np.arange(128)
    ang = 2 * np.pi * np.outer(i, i) / 128.0
    c["csC"] = (np.concatenate([np.cos(ang), np.sin(ang)], 1) / np.sqrt(128.0)).astype(np.float32).astype(bf)
    for nm, T in (("S", TS), ("P", TP)):
        i = np.arange(T)
        ang = 2 * np.pi * ((np.outer(i, i)) % T) / float(T)
        c["ct" + nm] = (np.cos(ang) / np.sqrt(T)).astype(np.float32).astype(bf)
        c["nst" + nm] = (-np.sin(ang) / np.sqrt(T)).astype(np.float32).astype(bf)
    idx = np.arange(128)
    su = (idx[:, None] < idx[None, :]).astype(np.float32)
    iu = (idx[:, None] <= idx[None, :]).astype(np.float32)
    sl = su.T.copy()
    il = iu.T.copy()
    c["masks"] = np.stack([su, sl, iu, il, -su, -sl], 0).astype(np.float32)
    c["masks4"] = np.repeat(c["masks"][:, None, :, :], 4, axis=1).transpose(2, 0, 1, 3).copy().astype(np.float32)
    c["ident4"] = np.repeat(np.eye(128, dtype=np.float32)[:, None, :], 4, axis=1).copy()
    c["tri2"] = np.stack([np.concatenate([iu, su], 1), np.concatenate([il, sl], 1)], 0).astype(np.float32) * np.float32(LWC)
    return c


class KB:
    def __init__(self, cfg):
        self.cfg = cfg
        self.TS = cfg["TS"]
        self.TP = cfg["TP"]
        self.NPS = cfg["NPS"]
        self.DFF = cfg["DFF"]
        self.FT = self.DFF // 128
        self.DEPTH = cfg["DEPTH"]
        self.dbg = set(cfg.get("dbg", ()))
        self.nc = bass.Bass("TRN2", target_bir_lowering=False)
        self.P = Prog(self.nc)
        self.P.limit = cfg.get("limit", 1 << 60)
        self.P.annotate = bool(cfg.get("annotate"))
        self.P.stage = "init"
        self.st = contextlib.ExitStack()
        self.dram = {}
        self.uid = 0
        self.bar_uid = 0
        self.groups = {"s": (self.TS, 1, self.TS), "p": (self.NPS * self.TP, self.NPS, self.TP)}

    def din(self, name, shape, dt=F32):
        t = DT(self.nc.dram_tensor(name, list(shape), dt, kind="ExternalInput").ap())
        self.dram[name] = t
        return t

    def dout(self, name, shape, dt=F32):
        t = DT(self.nc.dram_tensor(name, list(shape), dt, kind="ExternalOutput").ap())
        self.dram[name] = t
        return t

    def dscr(self, name, shape, dt):
        kind = "ExternalOutput" if name in self.dbg else "Internal"
        t = DT(self.nc.dram_tensor(name, list(shape), dt, kind=kind).ap())
        self.dram[name] = t
        return t

    def sb(self, name, shape, dt, stack=None):
        self.uid += 1
        h = (stack or self.st).enter_context(self.nc.sbuf_tensor(f"{name}_{self.uid}", list(shape), dt))
        return TB(h)

    def ps(self, name, shape, dt=F32, stack=None):
        self.uid += 1
        h = (stack or self.st).enter_context(self.nc.psum_tensor(f"{name}_{self.uid}", list(shape), dt))
        return TB(h, excl=True)

    def dma(self, out, in_, reads, writes, eng="sp", **kw):
        return self.P.op(eng, lambda e: e.dma_start(out=out, in_=in_, **kw), reads, writes, dma=True)

    def mm(self, out, lhsT, rhs, start, stop, reads, writes):
        return self.P.op("pe", lambda e: e.matmul(out, lhsT=lhsT, rhs=rhs, start=start, stop=stop), reads, writes)

    def tr(self, out, in_, ident, reads, writes):
        return self.P.op("pe", lambda e: e.transpose(out=out, in_=in_, identity=ident), reads, writes)

    def act(self, out, in_, func, reads, writes, **kw):
        return self.P.op("act", lambda e: e.activation(out=out, in_=in_, func=func, **kw), reads, writes)

    def tt(self, eng, out, in0, in1, op, reads, writes):
        return self.P.op(eng, lambda e: e.tensor_tensor(out=out, in0=in0, in1=in1, op=op), reads, writes)

    def ts(self, eng, out, in0, s1, s2, op0, op1, reads, writes):
        if s2 is None:
            return self.P.op(eng, lambda e: e.tensor_scalar(out=out, in0=in0, scalar1=s1, scalar2=None, op0=op0), reads, writes)
        return self.P.op(eng, lambda e: e.tensor_scalar(out=out, in0=in0, scalar1=s1, scalar2=s2, op0=op0, op1=op1), reads, writes)

    def stt(self, eng, out, in0, scalar, in1, op0, op1, reads, writes):
        return self.P.op(eng, lambda e: e.scalar_tensor_tensor(out=out, in0=in0, scalar=scalar, in1=in1, op0=op0, op1=op1), reads, writes)

    def cp(self, eng, out, in_, reads, writes):
        if eng == "act":
            return self.P.op("act", lambda e: e.copy(out=out, in_=in_), reads, writes)
        return self.P.op(eng, lambda e: e.tensor_copy(out=out, in_=in_), reads, writes)

    def memset(self, eng, ap, val, writes):
        return self.P.op(eng, lambda e: e.memset(ap, val), [], writes)

    def recip(self, out, in_, reads, writes):
        return self.P.op("dve", lambda e: e.reciprocal(out=out, in_=in_), reads, writes)

    def barrier(self):
        P = self.P
        fin = {}
        for en in P.ENGS:
            for o in P.q[en][-1:]:
                key, val = o.done
                fin[key] = max(fin.get(key, 0), val)
        for o in P.last_dma.values():
            key, val = o.done
            fin[key] = max(fin.get(key, 0), val)
        for en in P.ENGS:
            d = P.pending.setdefault(en, {})
            for key, val in fin.items():
                d[key] = max(d.get(key, 0), val)

    def declare(self):
        TS, TP, NPS, DEPTH, DFF = self.TS, self.TP, self.NPS, self.DEPTH, self.DFF
        TPG = NPS * TP
        di = self.din
        self.xs = di("xs", [TS, D])
        self.xp = di("xp", [TPG, D])
        self.ck = di("ck", [DEPTH, PAST, 256])
        self.cv = di("cv", [DEPTH, PAST, 256])
        self.st0 = di("st0", [DEPTH, 2, 8, 64, 64])
        self.cc = di("cc", [2, D])
        self.w_ada = di("w_ada", [DEPTH, D, 6 * D])
        self.b_ada = di("b_ada", [DEPTH, 6 * D])
        self.norm1_g = di("norm1_g", [DEPTH, D])
        self.norm2_g = di("norm2_g", [DEPTH, D])
        self.w_in = di("w_in", [DEPTH, D, IN_W])
        self.w_out = di("w_out", [DEPTH, D, D])
        self.q_norm_g = di("q_norm_g", [DEPTH, 128])
        self.k_norm_g = di("k_norm_g", [DEPTH, 128])
        self.rw_conv = di("rw_conv", [DEPTH, 3, RW_IN])
        self.rw_w0 = di("rw_w0", [DEPTH, 2, 512])
        self.rw_w2 = di("rw_w2", [DEPTH, 2, 64, 512])
        self.rw_a0 = di("rw_a0", [DEPTH, 2, 512])
        self.rw_a2 = di("rw_a2", [DEPTH, 2, 64, 512])
        self.rw_g2 = di("rw_g2", [DEPTH, 128, 512])
        self.rw_kk = di("rw_kk", [DEPTH, 512])
        self.rw_ka = di("rw_ka", [DEPTH, 512])
        self.rw_rk = di("rw_rk", [DEPTH, 512])
        self.rw_lnx_g = di("rw_lnx_g", [DEPTH, 512])
        self.rw_lnx_b = di("rw_lnx_b", [DEPTH, 512])
        self.ffn_up = di("ffn_up", [DEPTH, D, 2 * DFF])
        self.ffn_conv_w = di("ffn_conv_w", [DEPTH, 3, 2 * DFF])
        self.ffn_conv_b = di("ffn_conv_b", [DEPTH, 2 * DFF])
        self.ffn_down = di("ffn_down", [DEPTH, DFF, D])
        self.final_norm_g = di("final_norm_g", [D])
        self.c_ident_f = di("ident_f", [128, 128])
        self.c_ident_b = di("ident_b", [128, 128], BF16)
        self.c_ones_b = di("ones_b", [128, 128], BF16)
        self.c_bones_b = di("bones_b", [128, 128], BF16)
        self.c_bones_f = di("bones_f", [128, 128])
        self.c_prot_f = di("prot_f", [128, 128])
        self.c_cosT = di("cosT", [128, TS])
        self.c_sinT = di("sinT", [128, TS])
        self.c_csC = di("csC", [128, 256], BF16)
        self.c_ct = {"s": di("ctS", [TS, TS], BF16), "p": di("ctP", [TP, TP], BF16)}
        self.c_nst = {"s": di("nstS", [TS, TS], BF16), "p": di("nstP", [TP, TP], BF16)}
        self.c_masks = di("masks", [6, 128, 128])
        self.c_tri2 = di("tri2", [2, 128, 256])
        self.c_masks4 = di("masks4", [128, 6, 4, 128])
        self.c_ident4 = di("ident4", [128, 4, 128])
        self.ys = self.dout("ys", [TS, D])
        self.yp = self.dout("yp", [TPG, D])
        self.nk = self.dout("nk", [NPS, DEPTH, TP, 256])
        self.nv = self.dout("nv", [NPS, DEPTH, TP, 256])
        self.ns = self.dout("ns", [NPS, DEPTH, 2, 8, 64, 64])
        ds = self.dscr
        FT = self.FT
        self.Win_t = ds("Win_t", [DEPTH, 30, 128, KT, 128], BF16)
        self.Wv_t = ds("Wv_t", [DEPTH, 128, KT, 256], BF16)
        self.Wout_t = ds("Wout_t", [DEPTH, 16, 128, KT, 128], BF16)
        self.Wup_t = ds("Wup_t", [DEPTH, 2 * FT, 128, KT, 128], BF16)
        self.Wdn_t = ds("Wdn_t", [DEPTH, 16, 128, FT, 128], BF16)
        self.XT = {}
        self.XTB = {}
        self.QT = {}
        self.KTs = {}
        self.Vs = {}
        self.RW = {}
        self.AB = {}
        self.MIXT = {}
        for g, (Tg, nseq, Tseq) in self.groups.items():
            self.XT[g] = ds("XT_" + g, [KT, 128, Tg], F32)
            self.XTB[g] = ds("XTB_" + g, [KT, 128, Tg], F32)
            self.QT[g] = ds("QT_" + g, [NH, 128, Tg], BF16)
            self.KTs[g] = ds("KT_" + g, [NKV, 128, Tg], BF16)
            self.Vs[g] = ds("V_" + g, [Tg, 256], BF16)
            self.RW[g] = ds("RW_" + g, [RW_IN, Tg], F32)
            self.AB[g] = ds("AB_" + g, [Tg, 1024], BF16)
            self.MIXT[g] = ds("MIXT_" + g, [KT, 128, Tg], BF16)

    def load_const(self, name, src, shape, dt):
        t = self.sb(name, shape, dt)
        self.dma(t[:], src.ap, [src.b()], [t.b()])
        return t

    def setup_consts(self):
        self.P.stage = "consts"
        TS = self.TS
        self.ident_f = self.load_const("ident_f", self.c_ident_f, [128, 128], F32)
        self.ident_b = self.load_const("ident_b", self.c_ident_b, [128, 128], BF16)
        self.ones_b = self.load_const("ones_b", self.c_ones_b, [128, 128], BF16)
        self.bones_b = self.load_const("bones_b", self.c_bones_b, [128, 128], BF16)
        self.bones_f = self.load_const("bones_f", self.c_bones_f, [128, 128], F32)
        self.prot_f = self.load_const("prot_f", self.c_prot_f, [128, 128], F32)
        self.csC = self.load_const("csC", self.c_csC, [128, 256], BF16)
        self.masks = self.sb("masks", [128, 6, 128], F32)
        self.dma(self.masks[:], self.c_masks.ap.rearrange("m p c -> p m c"), [self.c_masks.b()], [self.masks.b()])
        self.tri2 = self.sb("tri2", [128, 2, 256], F32)
        self.dma(self.tri2[:], self.c_tri2.ap.rearrange("m p c -> p m c"), [self.c_tri2.b()], [self.tri2.b()])
        self.masks4 = self.load_const("masks4", self.c_masks4, [128, 6, 4, 128], F32)
        self.ident4 = self.load_const("ident4", self.c_ident4, [128, 4, 128], F32)
        self.eps6 = self.sb("eps6", [128, 1], F32)
        self.memset("pool", self.eps6[:], NORM_EPS, [self.eps6.b()])
        self.eps12 = self.sb("eps12", [128, 1], F32)
        self.memset("pool", self.eps12[:], 1e-12, [self.eps12.b()])
        self.epsgn = self.sb("epsgn", [128, 1], F32)
        self.memset("pool", self.epsgn[:], GN_EPS, [self.epsgn.b()])
        self.pw = [self.ps(f"pw{i}", [128, 1024], F32) for i in range(4)]
        self.pb = [TBV(self.pw[i // 2].h[:, (i % 2) * 512:(i % 2) * 512 + 512]) for i in range(8)]

    def cast_jobs(self, l, which):
        FT = self.FT
        jobs = []
        if which == "in":
            for c0 in range(0, IN_W, 512):
                wd = min(512, IN_W - c0)
                dsts = []
                for j in range(wd // 128):
                    ot = c0 // 128 + j
                    if ot == 14:
                        dsts.append((self.Wv_t[l], self.Wv_t.b(l), j * 128, 256))
                    elif ot != 15:
                        dsts.append((self.Win_t[l, ot], self.Win_t.b((l, ot)), j * 128, 128))
                jobs.append((self.w_in, l, c0, wd, KT, dsts))
        else:
            for c0 in range(0, D, 512):
                dsts = [(self.Wout_t[l, c0 // 128 + j], self.Wout_t.b((l, c0 // 128 + j)), j * 128, 128) for j in range(4)]
                jobs.append((self.w_out, l, c0, 512, KT, dsts))
            for c0 in range(0, 2 * self.DFF, 512):
                wd = min(512, 2 * self.DFF - c0)
                dsts = [(self.Wup_t[l, c0 // 128 + j], self.Wup_t.b((l, c0 // 128 + j)), j * 128, 128) for j in range(wd // 128)]
                jobs.append((self.ffn_up, l, c0, wd, KT, dsts))
            cw = 128 if FT > 16 else 512
            for c0 in range(0, D, cw):
                dsts = [(self.Wdn_t[l, c0 // 128 + j], self.Wdn_t.b((l, c0 // 128 + j)), j * 128, 128) for j in range(cw // 128)]
                jobs.append((self.ffn_down, l, c0, cw, FT, dsts))
        return jobs

    def bg_begin(self, stk, jobs, engs=("pool",)):
        nel = max(KT * 512, self.FT * (128 if self.FT > 16 else 512))
        self.bg = dict(jobs=list(jobs), i=0, pend=None, engs=engs, n=0,
                       wf=[self.sb("wcf", [128, nel], F32, stk) for _ in range(2)],
                       wb=[self.sb("wcb", [128, nel], BF16, stk) for _ in range(2)])

    def bg_step(self):
        bg = getattr(self, "bg", None)
        if bg is None:
            return
        if bg["pend"] is not None:
            (src, l, c0, wd, ktn, dsts), f, b = bg["pend"]
            fv = f[:, 0:ktn * wd].rearrange("p (kt c) -> p kt c", c=wd)
            off = 0
            for (dap, dbuf, co, w) in dsts:
                view = b[:, off:off + ktn * w].rearrange("p (kt c) -> p kt c", c=w)
                self.cp(bg["engs"][bg["n"] % len(bg["engs"])], view, fv[:, :, co:co + w], [f.b()], [b.b()])
                bg["n"] += 1
                self.dma(dap, view, [b.b()], [dbuf])
                off += ktn * w
            bg["pend"] = None
        if bg["i"] < len(bg["jobs"]):
            job = bg["jobs"][bg["i"]]
            f = bg["wf"][bg["i"] % 2]
            b = bg["wb"][bg["i"] % 2]
            (src, l, c0, wd, ktn, dsts) = job
            fv = f[:, 0:ktn * wd].rearrange("p (kt c) -> p kt c", c=wd)
            self.dma(fv, src[l, :, c0:c0 + wd].rearrange("(kt p) c -> p kt c", p=128), [src.b()], [f.b()])
            bg["pend"] = (job, f, b)
            bg["i"] += 1

    def bg_end(self):
        bg = getattr(self, "bg", None)
        if bg is None:
            return
        while bg["pend"] is not None or bg["i"] < len(bg["jobs"]):
            self.bg_step()
        self.bg = None

    def cast_weights(self):
        self.P.stage = "cast"
        sched = self.cfg.get("bg_cast", True)
        jobs = self.cast_jobs(0, "in")
        self.bg_sched = {}
        if sched:
            self.bg_sched[("attn", 0, "s")] = self.cast_jobs(0, "rest")
            later = []
            for l in range(1, self.DEPTH):
                later += self.cast_jobs(l, "in") + self.cast_jobs(l, "rest")
            self.bg_sched[("rwkv", 0, "p")] = later
        else:
            jobs += self.cast_jobs(0, "rest")
            for l in range(1, self.DEPTH):
                jobs += self.cast_jobs(l, "in") + self.cast_jobs(l, "rest")
        with contextlib.ExitStack() as stk:
            self.bg_begin(stk, jobs, engs=("act", "dve", "pool", "dve"))
            self.bg_end()
            self.barrier()

    def load_pp(self, dst, dst_b, src_rows, n, src_b):
        k = self._pp_i = getattr(self, "_pp_i", 0) + 1
        stg = self.pp_stage[k % 2]
        bank = self.pb[6 + (k % 2)]
        self.dma(stg[0:n, :], src_rows, [src_b], [stg.b()])
        self.tr(bank[:, 0:n], stg[0:n, :], self.ident_f[0:n, 0:n], [stg.b(), self.ident_f.b()], [bank.b()])
        self.cp("dve", dst, bank[:, 0:n], [bank.b()], [dst_b])

    def to_feature_major(self):
        self.P.stage = "tofm"
        with contextlib.ExitStack() as stk:
            xin = [self.sb("xin", [128, D], F32, stk) for _ in range(2)]
            xo = [self.sb("xo", [128, KT, 128], F32, stk) for _ in range(2)]
            i = 0
            for g, src in (("s", self.xs), ("p", self.xp)):
                Tg = self.groups[g][0]
                for blk in range(Tg // 128):
                    a = xin[i % 2]
                    o = xo[i % 2]
                    self.dma(a[:], src[blk * 128:(blk + 1) * 128, :], [src.b()], [a.b()])
                    for q in range(4):
                        bank = self.pb[(i * 4 + q) % 4]
                        for j in range(4):
                            kt = q * 4 + j
                            self.tr(bank[:, j * 128:(j + 1) * 128], a[:, kt * 128:(kt + 1) * 128], self.ident_f[:],
                                    [a.b(), self.ident_f.b()], [bank.b()])
                        eng = "act" if q % 2 else "dve"
                        self.cp(eng, o[:, q * 4:(q + 1) * 4, :], bank[:].rearrange("p (j t) -> p j t", j=4), [bank.b()], [o.b()])
                    self.dma(self.XT[g][:, :, blk * 128:(blk + 1) * 128].rearrange("kt p t -> p kt t"), o[:],
                             [o.b()], [self.XT[g].b(blk // 4)])
                    i += 1
            self.barrier()

    def setup_params(self):
        self.P.stage = "params"
        DEPTH, FT = self.DEPTH, self.FT
        self.pp_stage = [self.sb("ppstg", [128, 128], F32) for _ in range(2)]
        self.prm = []
        ccT = self.sb("ccT", [128, 32], F32)
        self.load_pp(ccT[:], ccT.b(), self.cc.ap.rearrange("v (kt p) -> (v kt) p", p=128), 32, self.cc.b())
        sc = self.sb("sc", [128, 32], F32)
        self.act(sc[:], ccT[:], AF.Silu, [ccT.b()], [sc.b()])
        fng = self.sb("fng", [128, KT], F32)
        self.load_pp(fng[:], fng.b(), self.final_norm_g.ap.rearrange("(kt p) -> kt p", p=128), KT, self.final_norm_g.b())
        self.fng = fng
        for l in range(DEPTH):
            pr = {}
            def vec(name, src, n, rows):
                t = self.sb(name, [128, n], F32)
                self.load_pp(t[:], t.b(), rows, n, src.b())
                return t
            pr["n1g"] = vec("n1g", self.norm1_g, KT, self.norm1_g[l].rearrange("(kt p) -> kt p", p=128))
            pr["n2g"] = vec("n2g", self.norm2_g, KT, self.norm2_g[l].rearrange("(kt p) -> kt p", p=128))
            pr["qng"] = vec("qng", self.q_norm_g, 1, self.q_norm_g[l:l + 1, :])
            pr["kng"] = vec("kng", self.k_norm_g, 1, self.k_norm_g[l:l + 1, :])
            pr["rwconv"] = vec("rwconv", self.rw_conv, 42, self.rw_conv[l].rearrange("j (t p) -> (j t) p", p=128))
            for nm, src in (("kk", self.rw_kk), ("ka", self.rw_ka), ("rk", self.rw_rk), ("lng", self.rw_lnx_g), ("lnb", self.rw_lnx_b)):
                pr[nm] = vec(nm, src, 4, src[l].rearrange("(t p) -> t p", p=128))
            pr["a0"] = vec("a0", self.rw_a0, 8, self.rw_a0[l].rearrange("d (t p) -> (d t) p", p=128))
            c1 = self.sb("c1", [128, 4], F32)
            self.ts("dve", c1[:], pr["ka"][:], -1.0, 1.0, ALU.mult, ALU.add, [pr["ka"].b()], [c1.b()])
            pr["c1"] = c1
            nft = 2 * FT
            fcw = self.sb("fcw", [128, 3, nft], F32)
            for j in range(3):
                self.load_pp(fcw[:, j, :], fcw.b(), self.ffn_conv_w[l, j].rearrange("(t p) -> t p", p=128), nft, self.ffn_conv_w.b())
            pr["fcw"] = fcw
            nfcw = self.sb("nfcw", [128, 3, nft], F32)
            self.ts("dve", nfcw[:], fcw[:], -1.0, None, ALU.mult, None, [fcw.b()], [nfcw.b()])
            pr["nfcw"] = nfcw
            pr["fcb"] = vec("fcb", self.ffn_conv_b, nft, self.ffn_conv_b[l].rearrange("(t p) -> t p", p=128))
            bada = vec("bada", self.b_ada, 96, self.b_ada[l].rearrange("(t p) -> t p", p=128))
            acc = self.pb[5]
            mods = [self.sb(f"mod{v}", [128, 96], F32) for v in range(2)]
            with contextlib.ExitStack() as stk:
                wa = [self.sb("wada", [128, KT, 512], F32, stk) for _ in range(2)]
                for blk in range(24):
                    w = wa[blk % 2]
                    self.dma(w[:], self.w_ada[l, :, blk * 512:(blk + 1) * 512].rearrange("(kt p) c -> p kt c", p=128),
                             [self.w_ada.b()], [w.b()])
                    for j in range(4):
                        ot = blk * 4 + j
                        for kt in range(KT):
                            self.mm(acc[:, ot * 2:ot * 2 + 2], w[:, kt, j * 128:(j + 1) * 128], sc[:, kt:32:16],
                                    kt == 0, kt == KT - 1, [w.b(), sc.b()], [acc.b()])
                for v in range(2):
                    m = mods[v]
                    self.tt("dve", m[:], acc[:, v:192:2], bada[:], ALU.add, [acc.b(), bada.b()], [m.b()])
                self.barrier()
            pr["mod"] = mods
            for v in range(2):
                for nm, gname, j in (("gs1", "n1g", 1), ("gs2", "n2g", 4)):
                    t = self.sb(f"{nm}_{v}", [128, KT], F32)
                    self.stt("dve", t[:], mods[v][:, j * 16:(j + 1) * 16], 1.0, pr[gname][:], ALU.add, ALU.mult,
                             [mods[v].b(), pr[gname].b()], [t.b()])
                    pr[f"{nm}_{v}"] = t
            self.prm.append(pr)

    def norm_mod(self, xt, chunks, gs, sh_ap_fn, hT, stk, ss_banks):
        Wtot = chunks[-1][1]
        sq = [self.sb("nsq", [128, Wtot], BF16, stk) for _ in range(2)]
        rstd = self.sb("nrstd", [128, Wtot], F32, stk)
        tmp = [self.sb("ntmp", [128, Wtot], F32, stk) for _ in range(2)]
        for kt in range(KT):
            s = sq[kt % 2]
            self.act(s[:], xt[:, kt, :], AF.Square, [xt.b()], [s.b()])
            for ci, (c0, c1) in enumerate(chunks):
                bk = ss_banks[ci]
                self.mm(bk[:, 0:c1 - c0], self.ones_b[:], s[:, c0:c1], kt == 0, kt == KT - 1,
                        [s.b(), self.ones_b.b()], [bk.b()])
        for ci, (c0, c1) in enumerate(chunks):
            bk = ss_banks[ci]
            self.act(rstd[:, c0:c1], bk[:, 0:c1 - c0], AF.Sqrt, [bk.b(), self.eps6.b()], [rstd.b()],
                     scale=1.0 / D, bias=self.eps6[:, 0:1])
        self.recip(rstd[:], rstd[:], [rstd.b()], [rstd.b()])
        for kt in range(KT):
            t = tmp[kt % 2]
            self.tt("pool" if kt % 2 else "dve", t[:], xt[:, kt, :], rstd[:], ALU.mult, [xt.b(), rstd.b()], [t.b()])
            self.act(hT[:, kt, :], t[:], AF.Identity, [t.b(), gs.b()], [hT.b()],
                     scale=gs[:, kt:kt + 1], bias=sh_ap_fn(kt))

    def zero_rwkv(self, g):
        Tg = self.groups[g][0]
        with contextlib.ExitStack() as stk:
            z = self.sb("zz", [128, Tg], BF16, stk)
            self.memset("pool", z[:], 0.0, [z.b()])
            for r in range(12, 16):
                self.dma(self.MIXT[g][r], z[:], [z.b()], [self.MIXT[g].b(i) for i in range((Tg + 511) // 512)])
            self.barrier()

    def stage1_x(self, l, g, c0, X):
        self._X1 = X
        return self.stage1(l, g, c0)

    def stage1(self, l, g, c0):
        self.P.stage = f"s1_{l}_{g}"
        W = 512
        v = 0 if g == "s" else 1
        pr = self.prm[l]
        mod = pr["mod"][v]
        ti = c0 // 512
        pb = self.pb
        is_s = (g == "s")
        with contextlib.ExitStack() as stk:
            xt = self.sb("xt", [128, KT, W], F32, stk)
            hT = self.sb("hT", [128, KT, W], BF16, stk)
            self.dma(xt[:], self._X1[g][:, :, c0:c0 + W].rearrange("kt p t -> p kt t"), [self._X1[g].b(ti)], [xt.b()])
            self.norm_mod(xt, [(0, W)], pr[f"gs1_{v}"], lambda kt: mod[:, kt:kt + 1], hT, stk, [pb[7]])
            wts = [self.sb("wt", [128, KT, 128], BF16, stk) for _ in range(3)]
            wv = self.sb("wv", [128, KT, 256], BF16, stk)
            uT = self.sb("uT", [128, W], BF16, stk)
            abt = self.sb("abt", [128, 4, 256], BF16, stk)
            sqh = self.sb("sqh", [128, W], BF16, stk)
            rq = self.sb("rq", [128, W], F32, stk)
            qn = [self.sb("qn", [128, W], F32, stk) for _ in range(2)]
            t1 = self.sb("t1", [128, W], F32, stk)
            t2 = self.sb("t2", [128, W], F32, stk)
            qr = [self.sb("qr", [128, W], BF16, stk) for _ in range(2)]
            ktok = self.sb("ktok", [128, 4, 128], F32, stk)
            vt = self.sb("vt", [128, 4, 256], BF16, stk)
            vtf = self.sb("vtf", [128, 4, 256], F32, stk)
            rwt = [self.sb("rwt", [128, W], F32, stk) for _ in range(2)]
            if is_s:
                cosb = self.sb("cosb", [128, W], F32, stk)
                sinb = self.sb("sinb", [128, W], F32, stk)
                self.dma(cosb[:], self.c_cosT[:, c0:c0 + W], [self.c_cosT.b()], [cosb.b()])
                self.dma(sinb[:], self.c_sinT[:, c0:c0 + W], [self.c_sinT.b()], [sinb.b()])

            order = list(range(14)) + list(range(16, 30))

            def load_w(oi):
                ot = order[oi]
                w = wts[oi % 3]
                self.dma(w[:], self.Win_t[l, ot], [self.Win_t.b((l, ot))], [w.b()])
            load_w(0)
            load_w(1)
            self.dma(wv[:], self.Wv_t[l], [self.Wv_t.b(l)], [wv.b()])
            for oi, ot in enumerate(order):
                if oi + 2 < len(order):
                    load_w(oi + 2)
                w = wts[oi % 3]
                acc = pb[oi % 3]
                for kt in range(KT):
                    self.mm(acc[:], w[:, kt, :], hT[:, kt, :], kt == 0, kt == KT - 1, [w.b(), hT.b()], [acc.b()])
                skip = self.cfg.get("s1_skip", "")
                if ("f" in skip and ot < 4) or ("q" in skip and 4 <= ot < 14) or ("r" in skip and ot >= 16):
                    continue
                if ot < 4:
                    self.cp("act", uT[:], acc[:], [acc.b()], [uT.b()])
                    for sub in range(4):
                        reg = pb[5][:, (sub % 2) * 256:(sub % 2) * 256 + 256]
                        self.mm(reg, uT[:, sub * 128:(sub + 1) * 128], self.csC[:], True, True,
                                [uT.b(), self.csC.b()], [pb[5].b()])
                        self.cp("dve", abt[:, sub, :], reg, [pb[5].b()], [abt.b()])
                    self.dma(self.AB[g][c0:c0 + W, ot * 256:(ot + 1) * 256].rearrange("(s p) c -> p s c", p=128), abt[:],
                             [abt.b()], [self.AB[g].b(ti)])
                elif ot < 14:
                    isk = ot >= 12
                    hd = ot - 12 if isk else ot - 4
                    gq = pr["kng"] if isk else pr["qng"]
                    q_n = qn[oi % 2]
                    q_r = qr[oi % 2]
                    self.act(sqh[:], acc[:], AF.Square, [acc.b()], [sqh.b()])
                    self.mm(pb[3][:], self.ones_b[:], sqh[:], True, True, [sqh.b(), self.ones_b.b()], [pb[3].b()])
                    self.act(rq[:], pb[3][:], AF.Sqrt, [pb[3].b(), self.eps6.b()], [rq.b()], scale=1.0 / 128, bias=self.eps6[:, 0:1])
                    self.recip(rq[:], rq[:], [rq.b()], [rq.b()])
                    self.stt("dve", q_n[:], acc[:], gq[:, 0:1], rq[:], ALU.mult, ALU.mult, [acc.b(), gq.b(), rq.b()], [q_n.b()])
                    if isk and not is_s:
                        for sub in range(4):
                            self.tr(pb[5][:, sub * 128:(sub + 1) * 128], q_n[:, sub * 128:(sub + 1) * 128], self.ident_f[:],
                                    [q_n.b(), self.ident_f.b()], [pb[5].b()])
                        self.cp("dve", ktok[:], pb[5][:].rearrange("p (s c) -> p s c", s=4), [pb[5].b()], [ktok.b()])
                        for sub in range(4):
                            tok = sub * 128
                            sq_, off = tok // self.TP, tok % self.TP
                            self.dma(self.nk[sq_, l, off:off + 128, hd * 128:(hd + 1) * 128], ktok[:, sub, :],
                                     [ktok.b()], [self.nk.b()])
                    if is_s:
                        self.mm(pb[4][:], self.prot_f[:], q_n[:], True, True, [self.prot_f.b(), q_n.b()], [pb[4].b()])
                        self.tt("pool", t1[:], q_n[:], cosb[:], ALU.mult, [q_n.b(), cosb.b()], [t1.b()])
                        self.tt("dve", t2[:], pb[4][:], sinb[:], ALU.mult, [pb[4].b(), sinb.b()], [t2.b()])
                        self.tt("pool", q_r[:], t1[:], t2[:], ALU.add, [t1.b(), t2.b()], [q_r.b()])
                    else:
                        self.cp("pool", q_r[:], q_n[:], [q_n.b()], [q_r.b()])
                    dst = self.KTs[g] if isk else self.QT[g]
                    self.dma(dst[hd, :, c0:c0 + W], q_r[:], [q_r.b()], [dst.b(ti)])
                else:
                    r = rwt[oi % 2]
                    self.cp("act" if oi % 2 else "dve", r[:], acc[:], [acc.b()], [r.b()])
                    self.dma(self.RW[g][(ot - 16) * 128:(ot - 15) * 128, c0:c0 + W], r[:], [r.b()], [self.RW[g].b(ti)])
                if oi == self.cfg.get("v_at", 13) and "v" not in skip:
                    for sub in range(4):
                        reg = pb[6][:, (sub % 2) * 256:(sub % 2) * 256 + 256]
                        for kt in range(KT):
                            self.mm(reg, hT[:, kt, sub * 128:(sub + 1) * 128], wv[:, kt, :], kt == 0, kt == KT - 1,
                                    [hT.b(), wv.b()], [pb[6].b()])
                        self.cp("act", vt[:, sub, :], reg, [pb[6].b()], [vt.b()])
                        if not is_s:
                            self.cp(self.cfg.get("vtf_eng", "dve"), vtf[:, sub, :], reg, [pb[6].b()], [vtf.b()])
                    self.dma(self.Vs[g][c0:c0 + W, :].rearrange("(s p) c -> p s c", p=128), vt[:], [vt.b()], [self.Vs[g].b(ti)])
                    if not is_s:
                        for sub in range(4):
                            tok = sub * 128
                            sq_, off = tok // self.TP, tok % self.TP
                            self.dma(self.nv[sq_, l, off:off + 128, :], vtf[:, sub, :], [vtf.b()], [self.nv.b()])
            self.barrier()

    def stage_fourier(self, l, g):
        self.P.stage = f"four_{l}_{g}"
        Tg, nseq, Tseq = self.groups[g]
        nch = Tseq // 128
        TW = min(512, Tseq)
        pb = self.pb
        allab = [self.AB[g].b(i) for i in range((Tg + 511) // 512)]
        with contextlib.ExitStack() as stk:
            ab = self.sb("fab", [128, nch, 1024], BF16, stk)
            ctb = [self.sb("fct", [128, nch, TW], BF16, stk) for _ in range(2)]
            nsb = [self.sb("fns", [128, nch, TW], BF16, stk) for _ in range(2)]
            fo = [self.sb("ffo", [128, TW], BF16, stk) for _ in range(2)]
            n = 0
            for s in range(nseq):
                self.dma(ab[:], self.AB[g][s * Tseq:(s + 1) * Tseq, :].rearrange("(c p) x -> p c x", p=128), allab, [ab.b()])
                for ti, t0 in enumerate(range(0, Tseq, TW)):
                    cb = ctb[ti % 2]
                    sbb = nsb[ti % 2]
                    self.dma(cb[:], self.c_ct[g][:, t0:t0 + TW].rearrange("(c p) t -> p c t", p=128), [self.c_ct[g].b()], [cb.b()])
                    self.dma(sbb[:], self.c_nst[g][:, t0:t0 + TW].rearrange("(c p) t -> p c t", p=128), [self.c_nst[g].b()], [sbb.b()])
                    for grp in range(4):
                        acc = pb[n % 4]
                        for c in range(nch):
                            self.mm(acc[:, 0:TW], ab[:, c, grp * 256:grp * 256 + 128], cb[:, c, :], c == 0, False,
                                    [ab.b(), cb.b()], [acc.b()])
                            self.mm(acc[:, 0:TW], ab[:, c, grp * 256 + 128:grp * 256 + 256], sbb[:, c, :], False, c == nch - 1,
                                    [ab.b(), sbb.b()], [acc.b()])
                        f = fo[n % 2]
                        self.cp("act" if n % 2 else "dve", f[:], acc[:, 0:TW], [acc.b()], [f.b()])
                        c0 = s * Tseq + t0
                        self.dma(self.MIXT[g][grp, :, c0:c0 + TW], f[:], [f.b()], [self.MIXT[g].b(c0 // 512)])
                        n += 1
            self.barrier()

    def stage_attn(self, l, g):
        self.P.stage = f"attn_{l}_{g}"
        Tg, nseq, Tseq = self.groups[g]
        is_s = (g == "s")
        Stot = Tseq + (PAST if is_s else 0)
        nck = Stot // 128
        QW = min(512, Tseq)
        pb = self.pb
        ntile = (Tg + 511) // 512
        scale = 128.0 ** -0.5
        with contextlib.ExitStack() as stk:
            kT = self.sb("akT", [128, NKV, Stot], BF16, stk)
            vv = self.sb("avv", [128, nck, 256], BF16, stk)
            qT = [self.sb("aqT", [128, Tseq], BF16, stk) for _ in range(2)]
            pT = [self.sb("apT", [128, QW], BF16, stk) for _ in range(3)]
            rden = self.sb("arden", [128, QW], F32, stk)
            ob = [self.sb("aob", [128, QW], BF16, stk) for _ in range(2)]
            if is_s:
                ckf = self.sb("ackf", [128, PAST // 128, 256], F32, stk)
                cvf = self.sb("acvf", [128, PAST // 128, 256], F32, stk)
            nq = 0
            bgj = self.bg_sched.pop(("attn", l, g), None)
            if bgj:
                self.bg_begin(stk, bgj)
            for s in range(nseq):
                c0s = s * Tseq
                for kv in range(NKV):
                    self.dma(kT[:, kv, 0:Tseq], self.KTs[g][kv, :, c0s:c0s + Tseq], [self.KTs[g].b(i) for i in range(ntile)], [kT.b()])
                self.dma(vv[:, 0:Tseq // 128, :], self.Vs[g][c0s:c0s + Tseq, :].rearrange("(c p) x -> p c x", p=128),
                         [self.Vs[g].b(i) for i in range(ntile)], [vv.b()])
                if is_s:
                    self.dma(ckf[:], self.ck[l].rearrange("(c p) x -> p c x", p=128), [self.ck.b()], [ckf.b()])
                    self.dma(cvf[:], self.cv[l].rearrange("(c p) x -> p c x", p=128), [self.cv.b()], [cvf.b()])
                    self.cp("pool", vv[:, Tseq // 128:nck, :], cvf[:], [cvf.b()], [vv.b()])
                    for kv in range(NKV):
                        bank = pb[kv]
                        for c in range(PAST // 128):
                            self.tr(bank[:, c * 128:(c + 1) * 128], ckf[:, c, kv * 128:(kv + 1) * 128], self.ident_f[:],
                                    [ckf.b(), self.ident_f.b()], [bank.b()])
                        self.cp("act", kT[:, kv, Tseq:Stot], bank[:, 0:PAST], [bank.b()], [kT.b()])
                for h in range(NH):
                    kv = h // (NH // NKV)
                    q = qT[h % 2]
                    self.dma(q[:], self.QT[g][h, :, c0s:c0s + Tseq], [self.QT[g].b(i) for i in range(ntile)], [q.b()])
                    for q0 in range(0, Tseq, QW):
                        self.bg_step()
                        oacc = pb[4 + nq % 2]
                        dacc = pb[6 + nq % 2]
                        def score(c):
                            sT = pb[c % 3]
                            p = pT[c % 3]
                            self.mm(sT[:, 0:QW], kT[:, kv, c * 128:(c + 1) * 128], q[:, q0:q0 + QW], True, True,
                                    [kT.b(), q.b()], [sT.b()])
                            self.act(p[:], sT[:, 0:QW], AF.Exp, [sT.b()], [p.b()], scale=scale)
                        pipe = self.cfg.get("attn_pipe", True)
                        if pipe:
                            score(0)
                        for c in range(nck):
                            if not pipe:
                                score(c)
                            elif c + 1 < nck:
                                score(c + 1)
                            p = pT[c % 3]
                            self.mm(oacc[:, 0:QW], vv[:, c, kv * 128:(kv + 1) * 128], p[:], c == 0, c == nck - 1,
                                    [vv.b(), p.b()], [oacc.b()])
                            self.mm(dacc[:, 0:QW], self.ones_b[:], p[:], c == 0, c == nck - 1,
                                    [self.ones_b.b(), p.b()], [dacc.b()])
                        self.recip(rden[:], dacc[:, 0:QW], [dacc.b()], [rden.b()])
                        o = ob[nq % 2]
                        self.tt("dve", o[:], oacc[:, 0:QW], rden[:], ALU.mult, [oacc.b(), rden.b()], [o.b()])
                        cc0 = c0s + q0
                        self.dma(self.MIXT[g][4 + h, :, cc0:cc0 + QW], o[:], [o.b()], [self.MIXT[g].b(cc0 // 512)])
                        nq += 1
            self.bg_end()
            self.barrier()

    def stage3(self, l, g, c0, X):
        self.P.stage = f"s3_{l}_{g}"
        W = 512
        v = 0 if g == "s" else 1
        pr = self.prm[l]
        mod = pr["mod"][v]
        ti = c0 // 512
        pb = self.pb
        with contextlib.ExitStack() as stk:
            xt = self.sb("xt3", [128, KT, W], F32, stk)
            mx = self.sb("mx3", [128, KT, W], BF16, stk)
            wts = [self.sb("wt3", [128, KT, 128], BF16, stk) for _ in range(3)]
            self.dma(xt[:], X[g][:, :, c0:c0 + W].rearrange("kt p t -> p kt t"), [X[g].b(ti)], [xt.b()])
            self.dma(mx[:], self.MIXT[g][:, :, c0:c0 + W].rearrange("kt p t -> p kt t"), [self.MIXT[g].b(ti)], [mx.b()])

            def load_w(ot):
                w = wts[ot % 3]
                self.dma(w[:], self.Wout_t[l, ot], [self.Wout_t.b((l, ot))], [w.b()])
            load_w(0)
            load_w(1)
            for ot in range(KT):
                if ot + 2 < KT:
                    load_w(ot + 2)
                w = wts[ot % 3]
                acc = pb[ot % 4]
                for kt in range(KT):
                    self.mm(acc[:], w[:, kt, :], mx[:, kt, :], kt == 0, kt == KT - 1, [w.b(), mx.b()], [acc.b()])
                self.stt("dve", xt[:, ot, :], acc[:], mod[:, 32 + ot:33 + ot], xt[:, ot, :], ALU.mult, ALU.add,
                         [acc.b(), xt.b(), xt.b(ot)], [xt.b(ot)])
            self.dma(X[g][:, :, c0:c0 + W].rearrange("kt p t -> p kt t"), xt[:], [xt.b(ot) for ot in range(KT)] + [xt.b()], [X[g].b(ti)])
            self.barrier()

    def stage4(self, l, g, c0, X, Y):
        self.P.stage = f"s4_{l}_{g}"
        W = 512
        Tg, nseq, Tseq = self.groups[g]
        v = 0 if g == "s" else 1
        pr = self.prm[l]
        mod = pr["mod"][v]
        ti = c0 // 512
        nti = (Tg + 511) // 512
        pb, pw = self.pb, self.pw
        FT = self.FT
        fcw, fcb, nfcw = pr["fcw"], pr["fcb"], pr["nfcw"]
        with contextlib.ExitStack() as stk:
            xt = self.sb("xt4", [128, KT, W + 2], F32, stk)
            hT = self.sb("hT4", [128, KT, W + 2], BF16, stk)
            actT = self.sb("act4", [128, FT, W], BF16, stk)
            wts = [self.sb("wt4", [128, KT, 128], BF16, stk) for _ in range(3)]
            wdn = [self.sb("wd4", [128, FT, 128], BF16, stk) for _ in range(2)]
            ua = self.sb("ua4", [128, W], F32, stk)
            ug = self.sb("ug4", [128, W], F32, stk)
            sg = self.sb("sg4", [128, W], F32, stk)
            self.dma(xt[:, :, 0:W], X[g][:, :, c0:c0 + W].rearrange("kt p t -> p kt t"), [X[g].b(ti)], [xt.b()])
            lzero = (c0 % Tseq == 0)
            rzero = ((c0 + W) % Tseq == 0)
            if lzero:
                self.memset("pool", xt[:, :, W:W + 1], 0.0, [xt.b()])
            else:
                self.dma(xt[:, :, W:W + 1], X[g][:, :, c0 - 1:c0].rearrange("kt p t -> p kt t"), [X[g].b(ti - 1)], [xt.b()],
                         allow_slow_non_contiguous=True)
            if rzero:
                self.memset("pool", xt[:, :, W + 1:W + 2], 0.0, [xt.b()])
            else:
                self.dma(xt[:, :, W + 1:W + 2], X[g][:, :, c0 + W:c0 + W + 1].rearrange("kt p t -> p kt t"), [X[g].b(ti + 1)], [xt.b()],
                         allow_slow_non_contiguous=True)
            self.norm_mod(xt, [(0, W), (W, W + 2)], pr[f"gs2_{v}"], lambda kt: mod[:, 48 + kt:49 + kt], hT, stk, [pb[7], pb[6]])
            if lzero:
                self.memset("pool", hT[:, :, W:W + 1], 0.0, [hT.b()])
            if rzero:
                self.memset("pool", hT[:, :, W + 1:W + 2], 0.0, [hT.b()])
            inner = [b for b in range(Tseq, W, Tseq)] if Tseq < W else []

            def load_w(j):
                i, half = j // 2, j % 2
                w = wts[j % 3]
                ot = i + half * FT
                self.dma(w[:], self.Wup_t[l, ot], [self.Wup_t.b((l, ot))], [w.b()])
            load_w(0)
            load_w(1)
            for j in range(2 * FT):
                if j + 2 < 2 * FT:
                    load_w(j + 2)
                i, half = j // 2, j % 2
                ot = i + half * FT
                w = wts[j % 3]
                wide = j % 3
                A, Bk = pb[2 * wide], pb[2 * wide + 1]
                for kt in range(KT):
                    self.mm(A[:], w[:, kt, :], hT[:, kt, 0:W], kt == 0, kt == KT - 1, [w.b(), hT.b()], [A.b()])
                for kt in range(KT):
                    self.mm(Bk[:, 0:2], w[:, kt, :], hT[:, kt, W:W + 2], kt == 0, kt == KT - 1, [w.b(), hT.b()], [Bk.b()])
                u = ug if half else ua
                w0 = fcw[:, 0, ot:ot + 1]
                w1 = fcw[:, 1, ot:ot + 1]
                w2 = fcw[:, 2, ot:ot + 1]
                self.act(u[:], A[:], AF.Identity, [A.b(), fcw.b(), fcb.b()], [u.b()], scale=w1, bias=fcb[:, ot:ot + 1])
                self.stt("dve", u[:, 1:W], A[:, 0:W - 1], w0, u[:, 1:W], ALU.mult, ALU.add, [A.b(), u.b()], [u.b()])
                self.stt("dve", u[:, 0:W - 1], A[:, 1:W], w2, u[:, 0:W - 1], ALU.mult, ALU.add, [A.b(), u.b()], [u.b()])
                self.stt("dve", u[:, 0:1], Bk[:, 0:1], w0, u[:, 0:1], ALU.mult, ALU.add, [Bk.b(), u.b()], [u.b()])
                self.stt("dve", u[:, W - 1:W], Bk[:, 1:2], w2, u[:, W - 1:W], ALU.mult, ALU.add, [Bk.b(), u.b()], [u.b()])
                for bnd in inner:
                    self.stt("dve", u[:, bnd - 1:bnd], A[:, bnd:bnd + 1], nfcw[:, 2, ot:ot + 1], u[:, bnd - 1:bnd], ALU.mult, ALU.add,
                             [A.b(), u.b(), nfcw.b()], [u.b()])
                    self.stt("dve", u[:, bnd:bnd + 1], A[:, bnd - 1:bnd], nfcw[:, 0, ot:ot + 1], u[:, bnd:bnd + 1], ALU.mult, ALU.add,
                             [A.b(), u.b(), nfcw.b()], [u.b()])
                if half:
                    self.act(sg[:], ug[:], AF.Silu, [ug.b()], [sg.b()])
                    self.tt("pool", actT[:, i, :], sg[:], ua[:], ALU.mult, [sg.b(), ua.b()], [actT.b()])
            self.dma(wdn[0][:], self.Wdn_t[l, 0], [self.Wdn_t.b((l, 0))], [wdn[0].b()])
            for ot in range(KT):
                if ot + 1 < KT:
                    self.dma(wdn[(ot + 1) % 2][:], self.Wdn_t[l, ot + 1], [self.Wdn_t.b((l, ot + 1))], [wdn[(ot + 1) % 2].b()])
                w = wdn[ot % 2]
                acc = pb[6 + ot % 2]
                for kt in range(FT):
                    self.mm(acc[:], w[:, kt, :], actT[:, kt, :], kt == 0, kt == FT - 1, [w.b(), actT.b()], [acc.b()])
                self.stt("dve", xt[:, ot, 0:W], acc[:], mod[:, 80 + ot:81 + ot], xt[:, ot, 0:W], ALU.mult, ALU.add,
                         [acc.b(), xt.b(), xt.b(ot)], [xt.b(ot)])
            self.dma(Y[g][:, :, c0:c0 + W].rearrange("kt p t -> p kt t"), xt[:, :, 0:W], [xt.b(ot) for ot in range(KT)] + [xt.b()], [Y[g].b(ti)])
            self.barrier()

    def stage5(self, g, c0, X):
        self.P.stage = f"s5_{g}"
        W = 512
        ti = c0 // 512
        pb = self.pb
        out = self.ys if g == "s" else self.yp
        with contextlib.ExitStack() as stk:
            xt = self.sb("xt5", [128, KT, W], F32, stk)
            xn = self.sb("xn5", [128, KT, W], F32, stk)
            sq = [self.sb("sq5", [128, W], BF16, stk) for _ in range(2)]
            rstd = self.sb("rstd5", [128, W], F32, stk)
            yt = [self.sb("yt5", [128, D], F32, stk) for _ in range(2)]
            self.dma(xt[:], X[g][:, :, c0:c0 + W].rearrange("kt p t -> p kt t"), [X[g].b(ti)], [xt.b()])
            for kt in range(KT):
                s = sq[kt % 2]
                self.act(s[:], xt[:, kt, :], AF.Square, [xt.b()], [s.b()])
                self.mm(pb[7][:], self.ones_b[:], s[:], kt == 0, kt == KT - 1, [s.b(), self.ones_b.b()], [pb[7].b()])
            self.act(rstd[:], pb[7][:], AF.Sqrt, [pb[7].b(), self.eps6.b()], [rstd.b()], scale=1.0 / D, bias=self.eps6[:, 0:1])
            self.recip(rstd[:], rstd[:], [rstd.b()], [rstd.b()])
            for kt in range(KT):
                self.stt("dve", xn[:, kt, :], xt[:, kt, :], self.fng[:, kt:kt + 1], rstd[:], ALU.mult, ALU.mult,
                         [xt.b(), rstd.b(), self.fng.b()], [xn.b()])
            for sub in range(4):
                y = yt[sub % 2]
                for q in range(4):
                    bank = pb[q]
                    for j in range(4):
                        kt = q * 4 + j
                        self.tr(bank[:, j * 128:(j + 1) * 128], xn[:, kt, sub * 128:(sub + 1) * 128], self.ident_f[:],
                                [xn.b(), self.ident_f.b()], [bank.b()])
                    self.cp("act" if q % 2 else "dve", y[:, q * 512:(q + 1) * 512], bank[:], [bank.b()], [y.b()])
                self.dma(out[c0 + sub * 128:c0 + (sub + 1) * 128, :], y[:], [y.b()], [out.b()])
            self.barrier()

    def stage_rwkv(self, l, g):
        self.P.stage = f"rwkv_{l}_{g}"
        Tg, nseq, Tseq = self.groups[g]
        is_s = (g == "s")
        T = Tseq
        nch = T // 128
        NB = 4
        pr = self.prm[l]
        pb = self.pb
        ntile = (Tg + 511) // 512
        rwall = [self.RW[g].b(i) for i in range(ntile)]
        conv = pr["rwconv"]
        CW = min(512, T)
        with contextlib.ExitStack() as stk:
            sbt = lambda name, shape, dt: self.sb(name, shape, dt, stk)
            w2e = sbt("w2e", [65, 2, 512], F32)
            a2f = sbt("a2f", [128, 2, 512], F32)
            g2f = sbt("g2f", [128, 512], F32)
            self.dma(w2e[0:64, :, :], self.rw_w2[l].rearrange("d k c -> k d c"), [self.rw_w2.b()], [w2e.b()])
            self.dma(w2e[64:65, :, :], self.rw_w0[l:l + 1], [self.rw_w0.b()], [w2e.b()])
            self.dma(a2f[64:128, :, :], self.rw_a2[l].rearrange("d k c -> k d c"), [self.rw_a2.b()], [a2f.b()])
            self.dma(g2f[:], self.rw_g2[l], [self.rw_g2.b()], [g2f.b()])
            t12 = sbt("t12", [128, T], F32)
            tw = sbt("tw", [65, T], F32)
            sg = sbt("sg", [128, T], F32)
            rc = sbt("rc", [128, T], F32)
            kkn = sbt("kkn", [128, T], F32)
            bA = [sbt("bA", [128, T], BF16) for _ in range(2)]
            KM = [sbt("KM", [128, T], BF16) for _ in range(2)]
            bon = sbt("bon", [128, T], F32)
            yacc = sbt("yacc", [128, T], F32)
            s1 = sbt("scr1", [128, T + 2], F32)
            s2 = sbt("scr2", [128, T], F32)
            s3 = sbt("scr3", [128, T], F32)
            al = sbt("al", [128, T], BF16)
            be = sbt("be", [128, T], BF16)
            ka_ = sbt("kap", [128, T], BF16)
            rt = sbt("rt", [128, T], BF16)
            VtmP = sbt("VtmP", [128, nch, 2, 128], BF16)
            BTt = sbt("BTt", [128, nch, 128], BF16)
            KTt = sbt("KTt", [128, nch, 128], BF16)
            PL = sbt("PL", [128, nch], F32)
            sigT = [sbt("sigT", [128, 128], F32) for _ in range(2)]
            Ptmp = [sbt("Ptmp", [128, 3, 256], F32) for _ in range(2)]
            NI = NB * 2
            Xa = sbt("Xa", [128, NI, 128], BF16)
            XTa = sbt("XTa", [128, NI, 128], BF16)
            Xb = sbt("Xb", [128, NI, 128], BF16)
            XTb = sbt("XTb", [128, NI, 128], BF16)
            Pf = sbt("Pf", [128, NI, 128], F32)
            Pfin = sbt("Pfin", [128, NI, 128], BF16)
            MakT = sbt("MakT", [128, NI, 128], BF16)
            Wbr = sbt("Wbr", [128, NI, 128], BF16)
            Wkr = sbt("Wkr", [128, NI, 128], BF16)
            Sf = sbt("Sf", [128, 128], F32)
            Sb = sbt("Sb", [128, 128], BF16)
            Sld = sbt("Sld", [128, 128], F32)
            RHSs = sbt("RHSs", [128, 128], BF16)
            Upad = sbt("Upad", [128, 2, 128], BF16)
            Ufull = sbt("Ufull", [128, 128], BF16)
            tS = sbt("tS", [128, 128], F32)
            sq = sbt("sqr", [128, CW], BF16)
            rstd = sbt("rstdr", [128, CW], F32)
            ob = [sbt("obr", [128, CW], BF16) for _ in range(2)]

            bgj = self.bg_sched.pop(("rwkv", l, g), None)
            if bgj:
                self.bg_begin(stk, bgj)
            self.memset("pool", VtmP[:], 0.0, [VtmP.b()])
            self.memset("pool", Upad[:], 0.0, [Upad.b()])
            self.memset("pool", tw[64:65, :], 1.0, [tw.b()])
            self.memset("pool", s1[:, 0:1], 0.0, [s1.b()])
            self.memset("pool", s1[:, T + 1:T + 2], 0.0, [s1.b()])

            def conv_tile(tile_idx, c0s, dst):
                self.dma(s1[:, 1:T + 1], self.RW[g][tile_idx * 128:(tile_idx + 1) * 128, c0s:c0s + T], rwall, [s1.b()])
                w = lambda j: conv[:, j * 14 + tile_idx:j * 14 + tile_idx + 1]
                self.act(dst[:], s1[:, 1:T + 1], AF.Copy, [s1.b(), conv.b()], [dst.b()], scale=w(1))
                self.stt("dve", dst[:], s1[:, 0:T], w(0), dst[:], ALU.mult, ALU.add, [s1.b(), dst.b()], [dst.b()])
                self.stt("dve", dst[:], s1[:, 2:T + 2], w(2), dst[:], ALU.mult, ALU.add, [s1.b(), dst.b()], [dst.b()])

            for s in range(nseq):
                c0s = s * T
                conv_tile(12, c0s, t12)
                self.act(tw[0:64, :], t12[0:64, :], AF.Tanh, [t12.b()], [tw.b()])
                conv_tile(13, c0s, sg)
                self.act(sg[:], sg[:], AF.Sigmoid, [sg.b()], [sg.b()])
                for ct in range(4):
                    self.bg_step()
                    conv_tile(ct, c0s, rc)
                    conv_tile(4 + ct, c0s, s2)
                    self.ts("pool", s3[:], s2[:], pr["kk"][:, ct:ct + 1], None, ALU.mult, None, [s2.b(), pr["kk"].b()], [s3.b()])
                    for x0 in range(0, T, CW):
                        bank = pb[(x0 // CW) % 2]
                        self.act(sq[:], s3[:, x0:x0 + CW], AF.Square, [s3.b()], [sq.b()])
                        self.mm(bank[:, 0:CW], self.bones_b[:], sq[:], True, True, [self.bones_b.b(), sq.b()], [bank.b()])
                        self.act(rstd[:], bank[:, 0:CW], AF.Sqrt, [bank.b(), self.eps12.b()], [rstd.b()], bias=self.eps12[:, 0:1])
                        self.recip(rstd[:], rstd[:], [rstd.b()], [rstd.b()])
                        self.tt("dve", kkn[:, x0:x0 + CW], s3[:, x0:x0 + CW], rstd[:], ALU.mult, [s3.b(), rstd.b()], [kkn.b()])
                    for d in range(2):
                        for x0 in range(0, T, CW):
                            bank = pb[2 + (x0 // CW) % 2]
                            self.mm(bank[:, 0:CW], a2f[64:128, d, ct * 128:(ct + 1) * 128], t12[64:128, x0:x0 + CW], True, True,
                                    [a2f.b(), t12.b()], [bank.b()])
                            self.act(s1[:, 1 + x0:1 + x0 + CW], bank[:, 0:CW], AF.Sigmoid, [bank.b(), pr["a0"].b()], [s1.b()],
                                     bias=pr["a0"][:, d * 4 + ct:d * 4 + ct + 1])
                        self.tt("pool", bA[d][:], kkn[:], s1[:, 1:T + 1], ALU.mult, [kkn.b(), s1.b()], [bA[d].b()])
                        self.ts("dve", s1[:, 1:T + 1], s1[:, 1:T + 1], pr["ka"][:, ct:ct + 1], pr["c1"][:, ct:ct + 1], ALU.mult, ALU.add,
                                [s1.b(), pr["ka"].b(), pr["c1"].b()], [s1.b()])
                        self.tt("dve", KM[d][:], s1[:, 1:T + 1], s2[:], ALU.mult, [s1.b(), s2.b()], [KM[d].b()])
                    conv_tile(8 + ct, c0s, s3)
                    self.tt("pool", s2[:], KM[0][:], KM[1][:], ALU.add, [KM[0].b(), KM[1].b()], [s2.b()])
                    self.stt("dve", s2[:], rc[:], pr["rk"][:, ct:ct + 1], s2[:], ALU.mult, ALU.mult, [rc.b(), s2.b(), pr["rk"].b()], [s2.b()])
                    for x0 in range(0, T, CW):
                        bank = pb[(x0 // CW) % 2]
                        self.mm(bank[:, 0:CW], self.bones_f[:], s2[:, x0:x0 + CW], True, True, [self.bones_f.b(), s2.b()], [bank.b()])
                        self.tt("dve", bon[:, x0:x0 + CW], bank[:, 0:CW], s3[:, x0:x0 + CW], ALU.mult, [bank.b(), s3.b()], [bon.b()])
                    for c in range(nch):
                        bank = pb[2 + c % 2]
                        self.tr(bank[:, 0:128], s3[:, c * 128:(c + 1) * 128], self.ident_f[:], [s3.b(), self.ident_f.b()], [bank.b()])
                        for hh in range(2):
                            self.cp("act" if hh else "dve", VtmP[:, c, hh, hh * 64:(hh + 1) * 64], bank[:, hh * 64:(hh + 1) * 64],
                                    [bank.b()], [VtmP.b()])
                    for d in range(2):
                        self.rwkv_dir(l, g, s, ct, d, T, nch, NB, stk, locals())
                    for x0 in range(0, T, CW):
                        bank = pb[(x0 // CW) % 2]
                        bank2 = pb[2 + (x0 // CW) % 2]
                        bank3 = pb[4 + (x0 // CW) % 2]
                        ys_ = yacc[:, x0:x0 + CW]
                        self.mm(bank[:, 0:CW], self.bones_f[:], ys_, True, True, [self.bones_f.b(), yacc.b()], [bank.b()])
                        self.stt("dve", s2[:, x0:x0 + CW], bank[:, 0:CW], -1.0 / 64, ys_, ALU.mult, ALU.add, [bank.b(), yacc.b()], [s2.b()])
                        self.act(sq[:], s2[:, x0:x0 + CW], AF.Square, [s2.b()], [sq.b()])
                        self.mm(bank2[:, 0:CW], self.bones_b[:], sq[:], True, True, [self.bones_b.b(), sq.b()], [bank2.b()])
                        self.act(rstd[:], bank2[:, 0:CW], AF.Sqrt, [bank2.b(), self.epsgn.b()], [rstd.b()], scale=1.0 / 64, bias=self.epsgn[:, 0:1])
                        self.recip(rstd[:], rstd[:], [rstd.b()], [rstd.b()])
                        self.tt("dve", s2[:, x0:x0 + CW], s2[:, x0:x0 + CW], rstd[:], ALU.mult, [s2.b(), rstd.b()], [s2.b()])
                        self.ts("dve", s2[:, x0:x0 + CW], s2[:, x0:x0 + CW], pr["lng"][:, ct:ct + 1], pr["lnb"][:, ct:ct + 1], ALU.mult, ALU.add,
                                [s2.b(), pr["lng"].b(), pr["lnb"].b()], [s2.b()])
                        self.tt("pool", s2[:, x0:x0 + CW], s2[:, x0:x0 + CW], bon[:, x0:x0 + CW], ALU.add, [s2.b(), bon.b()], [s2.b()])
                        self.mm(bank3[:, 0:CW], g2f[:, ct * 128:(ct + 1) * 128], sg[:, x0:x0 + CW], True, True, [g2f.b(), sg.b()], [bank3.b()])
                        o = ob[(x0 // CW) % 2]
                        self.tt("dve", o[:], bank3[:, 0:CW], s2[:, x0:x0 + CW], ALU.mult, [bank3.b(), s2.b()], [o.b()])
                        cc0 = c0s + x0
                        self.dma(self.MIXT[g][12 + ct, :, cc0:cc0 + CW], o[:], [o.b()], [self.MIXT[g].b(cc0 // 512)])
            self.bg_end()
            self.barrier()

    def rwkv_dir(self, l, g, s, ct, d, T, nch, NB, stk, L):
        pr = self.prm[l]
        pb = self.pb
        is_s = (g == "s")
        tw, w2e, sigT, Ptmp, kkn, bA, KM, rc = L["tw"], L["w2e"], L["sigT"], L["Ptmp"], L["kkn"], L["bA"], L["KM"], L["rc"]
        al, be, ka_, rt, PL = L["al"], L["be"], L["ka_"], L["rt"], L["PL"]
        BTt, KTt, VtmP = L["BTt"], L["KTt"], L["VtmP"]
        Xa, XTa, Xb, XTb, Pf, Pfin, MakT, Wbr, Wkr = (L[k] for k in ("Xa", "XTa", "Xb", "XTb", "Pf", "Pfin", "MakT", "Wbr", "Wkr"))
        Sf, Sb, Sld, RHSs, Upad, Ufull, tS, yacc = (L[k] for k in ("Sf", "Sb", "Sld", "RHSs", "Upad", "Ufull", "tS", "yacc"))
        masks = self.masks
        m_n, m_nt, m_s, m_i = ((4, 5, 0, 2) if d == 0 else (5, 4, 1, 3))
        for cp_ in range(0, nch, 2):
            cs = [c for c in (cp_, cp_ + 1) if c < nch]
            bank = pb[(cp_ // 2) % 2]
            for j, c in enumerate(cs):
                sgt = sigT[c % 2]
                bk2 = pb[2 + c % 2]
                self.mm(bk2[:, 0:128], tw[0:65, c * 128:(c + 1) * 128], w2e[0:65, d, ct * 128:(ct + 1) * 128], True, True,
                        [tw.b(), w2e.b()], [bk2.b()])
                self.act(sgt[:], bk2[:, 0:128], AF.Sigmoid, [bk2.b()], [sgt.b()])
                self.mm(bank[:, j * 256:(j + 1) * 256], sgt[:], self.tri2[:, d, :], True, True, [sgt.b(), self.tri2.b()], [bank.b()])
            n = len(cs)
            pt = Ptmp[(cp_ // 2) % 2]
            bv = bank[:, 0:n * 256].rearrange("p (j x) -> p j x", j=n)
            cols = slice(cp_ * 128, (cp_ + n) * 128)
            v3 = lambda tb_: tb_[:, cols].rearrange("p (j x) -> p j x", j=n)
            self.act(pt[:, 0, 0:n * 128].rearrange("p (j x) -> p j x", j=n), bv[:, :, 0:128], AF.Exp, [bank.b()], [pt.b()])
            self.act(pt[:, 1, 0:n * 128].rearrange("p (j x) -> p j x", j=n), bv[:, :, 0:128], AF.Exp, [bank.b()], [pt.b()], scale=-1.0)
            self.act(pt[:, 2, 0:n * 128].rearrange("p (j x) -> p j x", j=n), bv[:, :, 128:256], AF.Exp, [bank.b()], [pt.b()])
            for j, c in enumerate(cs):
                col = (127 if d == 0 else 0)
                self.cp("pool", PL[:, c:c + 1], pt[:, 0, j * 128 + col:j * 128 + col + 1], [pt.b()], [PL.b()])
            w_ = n * 128
            self.tt("dve", al[:, cols], pt[:, 2, 0:w_], kkn[:, cols], ALU.mult, [pt.b(), kkn.b()], [al.b()])
            self.tt("pool", be[:, cols], pt[:, 1, 0:w_], bA[d][:, cols], ALU.mult, [pt.b(), bA[d].b()], [be.b()])
            self.tt("dve", ka_[:, cols], pt[:, 1, 0:w_], KM[d][:, cols], ALU.mult, [pt.b(), KM[d].b()], [ka_.b()])
            self.tt("pool", rt[:, cols], pt[:, 0, 0:w_], rc[:, cols], ALU.mult, [pt.b(), rc.b()], [rt.b()])
        self.bg_step()
        for c in range(nch):
            bank = pb[c % 2]
            self.mm(bank[:, 0:128], be[:, c * 128:(c + 1) * 128], self.ident_b[:], True, True, [be.b(), self.ident_b.b()], [bank.b()])
            self.mm(bank[:, 128:256], ka_[:, c * 128:(c + 1) * 128], self.ident_b[:], True, True, [ka_.b(), self.ident_b.b()], [bank.b()])
            self.cp("act", BTt[:, c, :], bank[:, 0:128], [bank.b()], [BTt.b()])
            self.cp("dve", KTt[:, c, :], bank[:, 128:256], [bank.b()], [KTt.b()])
        self.bg_step()
        if is_s:
            self.memset("pool", Sld[:], 0.0, [Sld.b()])
            for hh in range(2):
                self.dma(Sld[hh * 64:(hh + 1) * 64, hh * 64:(hh + 1) * 64], self.st0[l, d, ct * 2 + hh], [self.st0.b()], [Sld.b()])
            self.tr(pb[7][:, 0:128], Sld[:], self.ident_f[:], [Sld.b(), self.ident_f.b()], [pb[7].b()])
            self.cp("dve", Sf[:], pb[7][:, 0:128], [pb[7].b()], [Sf.b()])
        else:
            self.memset("pool", Sf[:], 0.0, [Sf.b()])
        self.cp("act", Sb[:], Sf[:], [Sf.b()], [Sb.b()])
        order = list(range(nch)) if d == 0 else list(range(nch - 1, -1, -1))
        for b0 in range(0, nch, NB):
            batch = order[b0:b0 + NB]
            nb_ = len(batch)
            grps = [[(c, hh) for c in batch] for hh in range(2)]
            ngrp = 2
            bk = [0]

            def nbank():
                bk[0] += 1
                return pb[bk[0] % 4]
            m4 = self.masks4
            for gi in range(ngrp):
                g4 = slice(gi * 4, gi * 4 + nb_)
                gin = grps[gi]

                def ops(c, hh):
                    rows = slice(hh * 64, (hh + 1) * 64)
                    cc = slice(c * 128, (c + 1) * 128)
                    return al[rows, cc], be[rows, cc], ka_[rows, cc], rt[rows, cc]
                rb = [al.b(), be.b(), ka_.b(), rt.b()]
                for (dst, mi, sel) in ((Xa, m_n, (1, 0)), (XTa, m_nt, (0, 1)), (MakT, m_s, (2, 0)), (Wbr, m_i, (1, 3)), (Wkr, m_i, (2, 3))):
                    bank = nbank()
                    for j, (c, hh) in enumerate(gin):
                        o_ = ops(c, hh)
                        self.mm(bank[:, j * 128:(j + 1) * 128], o_[sel[0]], o_[sel[1]], True, True, rb, [bank.b()])
                    self.tt("dve", dst[:, g4, :], bank[:, 0:nb_ * 128].rearrange("p (j x) -> p j x", j=nb_), m4[:, mi, 0:nb_, :], ALU.mult,
                            [bank.b(), m4.b()], [dst.b(gi)])
                self.tt("pool", Pf[:, g4, :], Xa[:, g4, :], self.ident4[:, 0:nb_, :], ALU.add, [Xa.b(gi), self.ident4.b()], [Pf.b(gi)])
                self.cp("act", Pfin[:, g4, :], Pf[:, g4, :], [Pf.b(gi)], [Pfin.b(gi)])
            X, XT, Xn, XTn = Xa, XTa, Xb, XTb
            for lev in range(6):
                for gi in range(ngrp):
                    g4 = slice(gi * 4, gi * 4 + nb_)
                    if lev < 5:
                        bA_ = nbank()
                        for j in range(nb_):
                            ii = gi * 4 + j
                            self.mm(bA_[:, j * 128:(j + 1) * 128], XT[:, ii, :], X[:, ii, :], True, True, [X.b(gi), XT.b(gi)], [bA_.b()])
                    bB_ = nbank()
                    for j in range(nb_):
                        ii = gi * 4 + j
                        self.mm(bB_[:, j * 128:(j + 1) * 128], X[:, ii, :], XT[:, ii, :], True, True, [X.b(gi), XT.b(gi)], [bB_.b()])
                    if lev < 5:
                        self.cp("act", Xn[:, g4, :], bA_[:, 0:nb_ * 128].rearrange("p (j x) -> p j x", j=nb_), [bA_.b()], [Xn.b(gi)])
                    self.cp("dve", XTn[:, g4, :], bB_[:, 0:nb_ * 128].rearrange("p (j x) -> p j x", j=nb_), [bB_.b()], [XTn.b(gi)])
                    bC_ = nbank()
                    for j in range(nb_):
                        ii = gi * 4 + j
                        self.mm(bC_[:, j * 128:(j + 1) * 128], XTn[:, ii, :], Pfin[:, ii, :], True, True, [XTn.b(gi), Pfin.b(gi)], [bC_.b()])
                    self.tt("dve", Pf[:, g4, :], Pf[:, g4, :], bC_[:, 0:nb_ * 128].rearrange("p (j x) -> p j x", j=nb_), ALU.add, [Pf.b(gi), bC_.b()], [Pf.b(gi)])
                    self.cp("act", Pfin[:, g4, :], Pf[:, g4, :], [Pf.b(gi)], [Pfin.b(gi)])
                X, XT, Xn, XTn = Xn, XTn, X, XT
            self.bg_step()
            for bi, c in enumerate(batch):
                cc = slice(c * 128, (c + 1) * 128)
                p_rhs, p_u, p_y, p_s = pb[4], pb[5], pb[6], pb[7]
                for hh in range(2):
                    ii = hh * 4 + bi
                    rows = slice(hh * 64, (hh + 1) * 64)
                    vcols = slice(hh * 64, (hh + 1) * 64)
                    self.mm(p_rhs[:, vcols], al[rows, cc], Sb[rows, vcols], True, False, [al.b(), Sb.b()], [p_rhs.b()])
                    self.mm(p_rhs[:, vcols], MakT[:, ii, :], VtmP[:, c, hh, vcols], False, True, [MakT.b(ii // 4), VtmP.b()], [p_rhs.b()])
                self.cp("act", RHSs[:], p_rhs[:, 0:128], [p_rhs.b()], [RHSs.b()])
                for hh in range(2):
                    ii = hh * 4 + bi
                    vcols = slice(hh * 64, (hh + 1) * 64)
                    self.mm(p_u[:, vcols], Pfin[:, ii, :], RHSs[:, vcols], True, True, [Pfin.b(ii // 4), RHSs.b()], [p_u.b()])
                self.ts("dve", Ufull[:], p_u[:, 0:128], -1.0, None, ALU.mult, None, [p_u.b()], [Ufull.b()])
                for hh in range(2):
                    vcols = slice(hh * 64, (hh + 1) * 64)
                    self.cp("act" if hh else "dve", Upad[:, hh, vcols], Ufull[:, vcols], [Ufull.b()], [Upad.b()])
                self.mm(p_y[:, 0:128], Sb[:], rt[:, cc], True, False, [Sb.b(), rt.b()], [p_y.b()])
                for hh in range(2):
                    ii = hh * 4 + bi
                    self.mm(p_y[:, 0:128], Upad[:, hh, :], Wbr[:, ii, :], False, False, [Upad.b(), Wbr.b(ii // 4)], [p_y.b()])
                    self.mm(p_y[:, 0:128], VtmP[:, c, hh, :], Wkr[:, ii, :], False, hh == 1, [VtmP.b(), Wkr.b(ii // 4)], [p_y.b()])
                if d == 0:
                    self.cp("act", yacc[:, cc], p_y[:, 0:128], [p_y.b()], [yacc.b()])
                else:
                    self.tt("dve", yacc[:, cc], yacc[:, cc], p_y[:, 0:128], ALU.add, [yacc.b(), p_y.b()], [yacc.b()])
                self.mm(p_s[:, 0:128], BTt[:, c, :], Ufull[:], True, False, [BTt.b(), Ufull.b()], [p_s.b()])
                for hh in range(2):
                    self.mm(p_s[:, 0:128], KTt[:, c, :], VtmP[:, c, hh, :], False, hh == 1, [KTt.b(), VtmP.b()], [p_s.b()])
                self.stt("dve", tS[:], p_s[:, 0:128], PL[:, c:c + 1], self.bones_f[:], ALU.mult, ALU.mult,
                         [p_s.b(), PL.b(), self.bones_f.b()], [tS.b()])
                self.stt("dve", Sf[:], Sf[:], PL[:, c:c + 1], tS[:], ALU.mult, ALU.add, [Sf.b(), PL.b(), tS.b()], [Sf.b()])
                self.cp("act", Sb[:], Sf[:], [Sf.b()], [Sb.b()])
        self.bg_step()
        if not is_s:
            self.tr(pb[7][:, 0:128], Sf[:], self.ident_f[:], [Sf.b(), self.ident_f.b()], [pb[7].b()])
            self.cp("dve", Sld[:], pb[7][:, 0:128], [pb[7].b()], [Sld.b()])
            for hh in range(2):
                self.dma(self.ns[s, l, d, ct * 2 + hh], Sld[hh * 64:(hh + 1) * 64, hh * 64:(hh + 1) * 64], [Sld.b()], [self.ns.b()])

    def tiles(self):
        out = []
        for g, (Tg, nseq, Tseq) in self.groups.items():
            for c0 in range(0, Tg, 512):
                out.append((g, c0))
        return out

    def build(self):
        stages = self.cfg.get("stages", "all")
        self.declare()
        self.setup_consts()
        if stages != "nocast":
            self.cast_weights()
        if stages == "s0a":
            self.P.emit()
            return self.nc
        self.setup_params()
        if stages == "s0b":
            self.P.emit()
            return self.nc
        self.to_feature_major()
        if stages in ("s0c", "nocast"):
            self.P.emit()
            return self.nc
        X, Y = self.XT, self.XTB
        for l in range(self.DEPTH):
            for (g, c0) in self.tiles():
                if g in self.cfg.get("s1_groups", "sp"):
                    self.stage1_x(l, g, c0, X)
            if stages == "s1":
                break
            for g in self.groups:
                self.stage_fourier(l, g)
                self.stage_attn(l, g)
                if stages == "s2fa":
                    self.zero_rwkv(g)
                else:
                    self.stage_rwkv(l, g)
            if stages == "s2":
                break
            for (g, c0) in self.tiles():
                self.stage3(l, g, c0, X)
            for (g, c0) in self.tiles():
                self.stage4(l, g, c0, X, Y)
            X, Y = Y, X
        if stages in ("all", "s2fa"):
            for (g, c0) in self.tiles():
                self.stage5(g, c0, X)
        self.P.emit()
        return self.nc


def make_in_maps(inputs, cfg, ncores):
    TS, TP, NPS, DEPTH = cfg["TS"], cfg["TP"], cfg["NPS"], cfg["DEPTH"]
    consts = host_consts(TS, TP)
    f = lambda a: np.ascontiguousarray(np.asarray(a, dtype=np.float32))
    shared = {}
    for k in ("w_ada", "b_ada", "norm1_g", "norm2_g", "w_in", "w_out", "q_norm_g", "k_norm_g", "rw_conv",
              "rw_w0", "rw_w2", "rw_a0", "rw_a2", "rw_g2", "rw_kk", "rw_ka", "rw_lnx_g", "rw_lnx_b",
              "ffn_up", "ffn_conv_w", "ffn_conv_b", "ffn_down", "final_norm_g"):
        shared[k] = f(inputs[k])
    shared["rw_rk"] = f(inputs["rw_rk"]).reshape(DEPTH, 512)
    shared.update(consts)
    maps = []
    for b in range(ncores):
        m = dict(shared)
        m["xs"] = f(inputs["x_sample"][b])
        m["xp"] = f(inputs["x_prompt"][NPS * b:NPS * (b + 1)]).reshape(NPS * TP, D)
        m["ck"] = f(inputs["cache_attn_k"][b]).reshape(DEPTH, PAST, 256)
        m["cv"] = f(inputs["cache_attn_v"][b]).reshape(DEPTH, PAST, 256)
        m["st0"] = f(inputs["state_rwkv"][b])
        m["cc"] = np.stack([f(inputs["c"][b]), f(inputs["c_ctx"])], 0)
        maps.append(m)
    return maps


_CACHE = {}


def kernel(**inputs):
    xs = np.asarray(inputs["x_sample"])
    xp = np.asarray(inputs["x_prompt"])
    ncores = xs.shape[0]
    TS, TP = xs.shape[1], xp.shape[1]
    NPS = xp.shape[0] // ncores
    DEPTH = np.asarray(inputs["w_in"]).shape[0]
    DFF = np.asarray(inputs["ffn_down"]).shape[1]
    cfg = dict(TS=TS, TP=TP, NPS=NPS, DEPTH=DEPTH, DFF=DFF, stages="all")
    key = (TS, TP, NPS, DEPTH, DFF)
    if key not in _CACHE:
        kb = KB(cfg)
        _CACHE[key] = kb.build()
    nc = _CACHE[key]
    maps = make_in_maps(inputs, cfg, ncores)
    decl = set()
    for alloc in nc.allocations:
        if isinstance(alloc, mybir.MemoryLocationSet) and alloc.kind == "ExternalInput":
            decl.add(alloc.memorylocations[0].name)
    maps = [{k: v for k, v in m.items() if k in decl} for m in maps]
    res = run_bass_kernel_spmd(nc, maps, core_ids=list(range(ncores)))
    r = res.results
    y_sample = np.stack([np.asarray(r[b]["ys"], np.float32) for b in range(ncores)], 0)
    y_prompt = np.concatenate([np.asarray(r[b]["yp"], np.float32).reshape(NPS, TP, D) for b in range(ncores)], 0)
    nk = np.concatenate([np.asarray(r[b]["nk"], np.float32).reshape(NPS, DEPTH, TP, NKV, 128) for b in range(ncores)], 0)
    nv = np.concatenate([np.asarray(r[b]["nv"], np.float32).reshape(NPS, DEPTH, TP, NKV, 128) for b in range(ncores)], 0)
    ns = np.concatenate([np.asarray(r[b]["ns"], np.float32) for b in range(ncores)], 0)
    return (y_prompt, y_sample, nk, nv, ns)
```

```python
import contextlib
import math
import numpy as np
import ml_dtypes
import concourse.bass as bass
import concourse.mybir as mybir
from concourse.bass_utils import run_bass_kernel_spmd

F32 = mybir.dt.float32
BF16 = mybir.dt.bfloat16
ALU = mybir.AluOpType
AF = mybir.ActivationFunctionType
AX = mybir.AxisListType

EPOCH = 8000
RING = 8


class Buf:
    __slots__ = ("wc", "wd", "rc", "rd", "excl")

    def __init__(self, excl=False):
        self.excl = excl
        self.wc = {}
        self.wd = {}
        self.rc = {}
        self.rd = {}


class Op:
    __slots__ = ("eng", "fn", "waits", "done", "is_dma", "stage")

    def __init__(self, eng, fn, is_dma):
        self.stage = None
        self.eng = eng
        self.fn = fn
        self.waits = {}
        self.done = None
        self.is_dma = is_dma


class Prog:
    ENGS = ("pe", "act", "dve", "pool", "sp")

    def __init__(self, nc):
        self.nc = nc
        self.q = {e: [] for e in self.ENGS}
        self.cnt = {e: 0 for e in self.ENGS}
        self.dcnt = {e: 0 for e in self.ENGS}
        self.sems = {}
        self.semkeys = []
        self.last_dma = {}
        self.pending = {}

    def _semkey(self, key):
        if key not in self.sems:
            self.sems[key] = None
            self.semkeys.append(key)
        return key

    def _add_dep(self, op, dep, raw):
        if dep is None or dep is op:
            return
        if dep.eng == op.eng and not dep.is_dma and not op.is_dma:
            if op.eng == "pe":
                return
        key, val = dep.done
        if op.waits.get(key, 0) < val:
            op.waits[key] = val

    def op(self, eng, fn, reads=(), writes=(), dma=False):
        o = Op(eng, fn, dma)
        o.stage = getattr(self, "stage", None)
        if any(b.excl for b in reads):
            writes = list(writes) + [b for b in reads if b.excl and b not in writes]
            reads = [b for b in reads if not b.excl]
        self.nops = getattr(self, "nops", 0) + 1
        if self.nops > getattr(self, "limit", 1 << 60):
            return o
        pend = self.pending.pop(eng, None)
        if pend:
            for key, val in pend.items():
                if o.waits.get(key, 0) < val:
                    o.waits[key] = val
        for b in reads:
            for d in b.wc.values():
                self._add_dep(o, d, True)
            for lst in b.wd.values():
                for d in lst:
                    self._add_dep(o, d, True)
        for b in writes:
            for d in b.wc.values():
                self._add_dep(o, d, False)
            for lst in b.wd.values():
                for d in lst:
                    self._add_dep(o, d, False)
            for d in b.rc.values():
                self._add_dep(o, d, False)
            for lst in b.rd.values():
                for d in lst:
                    self._add_dep(o, d, False)
        if dma:
            k = self.dcnt[eng]
            self.dcnt[eng] += 1
            slot = k % RING
            key = self._semkey(("d", eng, slot))
            o.done = (key, 16 * (k // RING + 1))
            prev = self.last_dma.get((eng, slot))
            if prev is not None:
                self._add_dep(o, prev, True)
            self.last_dma[(eng, slot)] = o
        else:
            k = self.cnt[eng]
            self.cnt[eng] += 1
            key = self._semkey(("c", eng, k // EPOCH))
            o.done = (key, k % EPOCH + 1)
        for b in reads:
            if dma:
                lst = b.rd.setdefault(eng, [])
                lst.append(o)
                if len(lst) > RING:
                    del lst[0]
            else:
                b.rc[eng] = o
        for b in writes:
            b.rc = {}
            b.rd = {}
            if dma:
                lst = b.wd.setdefault(eng, [])
                lst.append(o)
                if len(lst) > RING:
                    del lst[0]
            else:
                b.wc[eng] = o
        self.q[eng].append(o)
        return o

    def emit(self):
        nc = self.nc
        with contextlib.ExitStack() as st:
            for key in self.semkeys:
                self.sems[key] = st.enter_context(nc.semaphore("s_" + "_".join(str(x) for x in key)))
            block = st.enter_context(nc.Block())
            engmap = {"pe": block.tensor, "act": block.scalar, "dve": block.vector,
                      "pool": block.gpsimd, "sp": block.sync}
            all_ops = self.q

            def make(ename):
                ops = all_ops[ename]

                def body(e):
                    seen = {}
                    for o in ops:
                        for key, val in o.waits.items():
                            if seen.get(key, 0) >= val:
                                continue
                            seen[key] = val
                            e.wait_ge(self.sems[key], val)
                        ins = o.fn(e)
                        if self.annotate and o.stage:
                            ins.annotate(o.stage)
                        key, val = o.done
                        ins.then_inc(self.sems[key], 16 if o.is_dma else 1)
                    if ename == "sp":
                        fin = {}
                        for en in self.ENGS:
                            for o in all_ops[en][-1:]:
                                key, val = o.done
                                fin[key] = max(fin.get(key, 0), val)
                        for o in self.last_dma.values():
                            key, val = o.done
                            fin[key] = max(fin.get(key, 0), val)
                        for key, val in fin.items():
                            if seen.get(key, 0) < val:
                                e.wait_ge(self.sems[key], val)
                return body

            for ename in self.ENGS:
                if all_ops[ename] or ename == "sp":
                    engmap[ename](make(ename))


class TB:
    def __init__(self, h, excl=False):
        self.h = h
        self.bufs = {}
        self.excl = excl

    def __getitem__(self, idx):
        return self.h[idx]

    def b(self, key=None):
        if key not in self.bufs:
            self.bufs[key] = Buf(self.excl)
        return self.bufs[key]


class TBV:
    def __init__(self, ap):
        self.ap = ap
        self.buf = Buf(True)

    def __getitem__(self, idx):
        return self.ap[idx]

    def b(self, key=None):
        return self.buf


class DT:
    def __init__(self, ap):
        self.ap = ap
        self.bufs = {}

    def __getitem__(self, idx):
        return self.ap[idx]

    def b(self, key=None):
        if key not in self.bufs:
            self.bufs[key] = Buf()
        return self.bufs[key]


D = 2048
KT = 16
NH = 8
NKV = 2
PAST = 512
IN_W = 3840
RW_IN = 1792
GRID_W = 64
NORM_EPS = 1e-6
GN_EPS = 64e-5
LWC = -math.exp(-0.5)


def host_consts(TS, TP):
    c = {}
    bf = ml_dtypes.bfloat16
    c["ident_f"] = np.eye(128, dtype=np.float32)
    c["ident_b"] = np.eye(128, dtype=np.float32).astype(bf)
    c["ones_b"] = np.ones((128, 128), np.float32).astype(bf)
    bo = np.zeros((128, 128), np.float32)
    bo[:64, :64] = 1.0
    bo[64:, 64:] = 1.0
    c["bones_b"] = bo.astype(bf)
    c["bones_f"] = bo
    prot = np.zeros((128, 128), np.float32)
    for m in range(128):
        j = m % 64
        if j < 32:
            prot[m + 32, m] = -1.0
        else:
            prot[m - 32, m] = 1.0
    c["prot_f"] = prot
    t = np.arange(TS)
    rows = (t // GRID_W).astype(np.float64)
    cols = (t % GRID_W).astype(np.float64)
    inv = 1.0 / (10000.0 ** (np.arange(0, 64, 2, dtype=np.float64) / 64.0))
    cosT = np.zeros((128, TS), np.float64)
    sinT = np.zeros((128, TS), np.float64)
    for d in range(128):
        pos = rows if d < 64 else cols
        f = inv[(d % 64) % 32]
        ang = np.float32(pos).astype(np.float32) * np.float32(f)
        cosT[d] = np.cos(ang.astype(np.float64))
        sinT[d] = np.sin(ang.astype(np.float64))
    c["cosT"] = cosT.astype(np.float32)
    c["sinT"] = sinT.astype(np.float32)
    i = np.arange(128)
    ang = 2 * np.pi * np.outer(i, i) / 128.0
    c["csC"] = (np.concatenate([np.cos(ang), np.sin(ang)], 1) / np.sqrt(128.0)).astype(np.float32).astype(bf)
    for nm, T in (("S", TS), ("P", TP)):
        i = np.arange(T)
        ang = 2 * np.pi * ((np.outer(i, i)) % T) / float(T)
        c["ct" + nm] = (np.cos(ang) / np.sqrt(T)).astype(np.float32).astype(bf)
        c["nst" + nm] = (-np.sin(ang) / np.sqrt(T)).astype(np.float32).astype(bf)
    idx = np.arange(128)
    su = (idx[:, None] < idx[None, :]).astype(np.float32)
    iu = (idx[:, None] <= idx[None, :]).astype(np.float32)
    sl = su.T.copy()
    il = iu.T.copy()
    c["masks"] = np.stack([su, sl, iu, il, -su, -sl], 0).astype(np.float32)
    c["masks4"] = np.repeat(c["masks"][:, None, :, :], 4, axis=1).transpose(2, 0, 1, 3).copy().astype(np.float32)
    c["ident4"] = np.repeat(np.eye(128, dtype=np.float32)[:, None, :], 4, axis=1).copy()
    c["tri2"] = np.stack([np.concatenate([iu, su], 1), np.concatenate([il, sl], 1)], 0).astype(np.float32) * np.float32(LWC)
    return c


class KB:
    def __init__(self, cfg):
        self.cfg = cfg
        self.TS = cfg["TS"]
        self.TP = cfg["TP"]
        self.NPS = cfg["NPS"]
        self.DFF = cfg["DFF"]
        self.FT = self.DFF // 128
        self.DEPTH = cfg["DEPTH"]
        self.dbg = set(cfg.get("dbg", ()))
        self.nc = bass.Bass("TRN2", target_bir_lowering=False)
        self.P = Prog(self.nc)
        self.P.limit = cfg.get("limit", 1 << 60)
        self.P.annotate = bool(cfg.get("annotate"))
        self.P.stage = "init"
        self.st = contextlib.ExitStack()
        self.dram = {}
        self.uid = 0
        self.bar_uid = 0
        self.groups = {"s": (self.TS, 1, self.TS), "p": (self.NPS * self.TP, self.NPS, self.TP)}

    def din(self, name, shape, dt=F32):
        t = DT(self.nc.dram_tensor(name, list(shape), dt, kind="ExternalInput").ap())
        self.dram[name] = t
        return t

    def dout(self, name, shape, dt=F32):
        t = DT(self.nc.dram_tensor(name, list(shape), dt, kind="ExternalOutput").ap())
        self.dram[name] = t
        return t

    def dscr(self, name, shape, dt):
        kind = "ExternalOutput" if name in self.dbg else "Internal"
        t = DT(self.nc.dram_tensor(name, list(shape), dt, kind=kind).ap())
        self.dram[name] = t
        return t

    def sb(self, name, shape, dt, stack=None):
        self.uid += 1
        h = (stack or self.st).enter_context(self.nc.sbuf_tensor(f"{name}_{self.uid}", list(shape), dt))
        return TB(h)

    def ps(self, name, shape, dt=F32, stack=None):
        self.uid += 1
        h = (stack or self.st).enter_context(self.nc.psum_tensor(f"{name}_{self.uid}", list(shape), dt))
        return TB(h, excl=True)

    def dma(self, out, in_, reads, writes, eng="sp", **kw):
        return self.P.op(eng, lambda e: e.dma_start(out=out, in_=in_, **kw), reads, writes, dma=True)

    def mm(self, out, lhsT, rhs, start, stop, reads, writes):
        return self.P.op("pe", lambda e: e.matmul(out, lhsT=lhsT, rhs=rhs, start=start, stop=stop), reads, writes)

    def tr(self, out, in_, ident, reads, writes):
        return self.P.op("pe", lambda e: e.transpose(out=out, in_=in_, identity=ident), reads, writes)

    def act(self, out, in_, func, reads, writes, **kw):
        return self.P.op("act", lambda e: e.activation(out=out, in_=in_, func=func, **kw), reads, writes)

    def tt(self, eng, out, in0, in1, op, reads, writes):
        return self.P.op(eng, lambda e: e.tensor_tensor(out=out, in0=in0, in1=in1, op=op), reads, writes)

    def ts(self, eng, out, in0, s1, s2, op0, op1, reads, writes):
        if s2 is None:
            return self.P.op(eng, lambda e: e.tensor_scalar(out=out, in0=in0, scalar1=s1, scalar2=None, op0=op0), reads, writes)
        return self.P.op(eng, lambda e: e.tensor_scalar(out=out, in0=in0, scalar1=s1, scalar2=s2, op0=op0, op1=op1), reads, writes)

    def stt(self, eng, out, in0, scalar, in1, op0, op1, reads, writes):
        return self.P.op(eng, lambda e: e.scalar_tensor_tensor(out=out, in0=in0, scalar=scalar, in1=in1, op0=op0, op1=op1), reads, writes)

    def cp(self, eng, out, in_, reads, writes):
        if eng == "act":
            return self.P.op("act", lambda e: e.copy(out=out, in_=in_), reads, writes)
        return self.P.op(eng, lambda e: e.tensor_copy(out=out, in_=in_), reads, writes)

    def memset(self, eng, ap, val, writes):
        return self.P.op(eng, lambda e: e.memset(ap, val), [], writes)

    def recip(self, out, in_, reads, writes):
        return self.P.op("dve", lambda e: e.reciprocal(out=out, in_=in_), reads, writes)

    def barrier(self):
        P = self.P
        fin = {}
        for en in P.ENGS:
            for o in P.q[en][-1:]:
                key, val = o.done
                fin[key] = max(fin.get(key, 0), val)
        for o in P.last_dma.values():
            key, val = o.done
            fin[key] = max(fin.get(key, 0), val)
        for en in P.ENGS:
            d = P.pending.setdefault(en, {})
            for key, val in fin.items():
                d[key] = max(d.get(key, 0), val)

    def declare(self):
        TS, TP, NPS, DEPTH, DFF = self.TS, self.TP, self.NPS, self.DEPTH, self.DFF
        TPG = NPS * TP
        di = self.din
        self.xs = di("xs", [TS, D])
        self.xp = di("xp", [TPG, D])
        self.ck = di("ck", [DEPTH, PAST, 256])
        self.cv = di("cv", [DEPTH, PAST, 256])
        self.st0 = di("st0", [DEPTH, 2, 8, 64, 64])
        self.cc = di("cc", [2, D])
        self.w_ada = di("w_ada", [DEPTH, D, 6 * D])
        self.b_ada = di("b_ada", [DEPTH, 6 * D])
        self.norm1_g = di("norm1_g", [DEPTH, D])
        self.norm2_g = di("norm2_g", [DEPTH, D])
        self.w_in = di("w_in", [DEPTH, D, IN_W])
        self.w_out = di("w_out", [DEPTH, D, D])
        self.q_norm_g = di("q_norm_g", [DEPTH, 128])
        self.k_norm_g = di("k_norm_g", [DEPTH, 128])
        self.rw_conv = di("rw_conv", [DEPTH, 3, RW_IN])
        self.rw_w0 = di("rw_w0", [DEPTH, 2, 512])
        self.rw_w2 = di("rw_w2", [DEPTH, 2, 64, 512])
        self.rw_a0 = di("rw_a0", [DEPTH, 2, 512])
        self.rw_a2 = di("rw_a2", [DEPTH, 2, 64, 512])
        self.rw_g2 = di("rw_g2", [DEPTH, 128, 512])
        self.rw_kk = di("rw_kk", [DEPTH, 512])
        self.rw_ka = di("rw_ka", [DEPTH, 512])
        self.rw_rk = di("rw_rk", [DEPTH, 512])
        self.rw_lnx_g = di("rw_lnx_g", [DEPTH, 512])
        self.rw_lnx_b = di("rw_lnx_b", [DEPTH, 512])
        self.ffn_up = di("ffn_up", [DEPTH, D, 2 * DFF])
        self.ffn_conv_w = di("ffn_conv_w", [DEPTH, 3, 2 * DFF])
        self.ffn_conv_b = di("ffn_conv_b", [DEPTH, 2 * DFF])
        self.ffn_down = di("ffn_down", [DEPTH, DFF, D])
        self.final_norm_g = di("final_norm_g", [D])
        self.c_ident_f = di("ident_f", [128, 128])
        self.c_ident_b = di("ident_b", [128, 128], BF16)
        self.c_ones_b = di("ones_b", [128, 128], BF16)
        self.c_bones_b = di("bones_b", [128, 128], BF16)
        self.c_bones_f = di("bones_f", [128, 128])
        self.c_prot_f = di("prot_f", [128, 128])
        self.c_cosT = di("cosT", [128, TS])
        self.c_sinT = di("sinT", [128, TS])
        self.c_csC = di("csC", [128, 256], BF16)
        self.c_ct = {"s": di("ctS", [TS, TS], BF16), "p": di("ctP", [TP, TP], BF16)}
        self.c_nst = {"s": di("nstS", [TS, TS], BF16), "p": di("nstP", [TP, TP], BF16)}
        self.c_masks = di("masks", [6, 128, 128])
        self.c_tri2 = di("tri2", [2, 128, 256])
        self.c_masks4 = di("masks4", [128, 6, 4, 128])
        self.c_ident4 = di("ident4", [128, 4, 128])
        self.ys = self.dout("ys", [TS, D])
        self.yp = self.dout("yp", [TPG, D])
        self.nk = self.dout("nk", [NPS, DEPTH, TP, 256])
        self.nv = self.dout("nv", [NPS, DEPTH, TP, 256])
        self.ns = self.dout("ns", [NPS, DEPTH, 2, 8, 64, 64])
        ds = self.dscr
        FT = self.FT
        self.Win_t = ds("Win_t", [DEPTH, 30, 128, KT, 128], BF16)
        self.Wv_t = ds("Wv_t", [DEPTH, 128, KT, 256], BF16)
        self.Wout_t = ds("Wout_t", [DEPTH, 16, 128, KT, 128], BF16)
        self.Wup_t = ds("Wup_t", [DEPTH, 2 * FT, 128, KT, 128], BF16)
        self.Wdn_t = ds("Wdn_t", [DEPTH, 16, 128, FT, 128], BF16)
        self.mod_rows = [ds(f"modrows{l}", [2, 6 * D], F32) for l in range(DEPTH)]
        self.XT = {}
        self.XTB = {}
        self.QT = {}
        self.KTs = {}
        self.Vs = {}
        self.RW = {}
        self.AB = {}
        self.MIXT = {}
        for g, (Tg, nseq, Tseq) in self.groups.items():
            self.XT[g] = ds("XT_" + g, [KT, 128, Tg], F32)
            self.XTB[g] = ds("XTB_" + g, [KT, 128, Tg], F32)
            self.QT[g] = ds("QT_" + g, [NH, 128, Tg], BF16)
            self.KTs[g] = ds("KT_" + g, [NKV, 128, Tg], BF16)
            self.Vs[g] = ds("V_" + g, [Tg, 256], BF16)
            self.RW[g] = ds("RW_" + g, [RW_IN, Tg], F32)
            self.AB[g] = ds("AB_" + g, [Tg, 1024], BF16)
            self.MIXT[g] = ds("MIXT_" + g, [KT, 128, Tg], BF16)

    def load_const(self, name, src, shape, dt):
        t = self.sb(name, shape, dt)
        self.dma(t[:], src.ap, [src.b()], [t.b()])
        return t

    def setup_consts(self):
        self.P.stage = "consts"
        TS = self.TS
        self.ident_f = self.load_const("ident_f", self.c_ident_f, [128, 128], F32)
        self.ident_b = self.load_const("ident_b", self.c_ident_b, [128, 128], BF16)
        self.ones_b = self.load_const("ones_b", self.c_ones_b, [128, 128], BF16)
        self.bones_b = self.load_const("bones_b", self.c_bones_b, [128, 128], BF16)
        self.bones_f = self.load_const("bones_f", self.c_bones_f, [128, 128], F32)
        self.prot_f = self.load_const("prot_f", self.c_prot_f, [128, 128], F32)
        self.csC = self.load_const("csC", self.c_csC, [128, 256], BF16)
        self.masks = self.sb("masks", [128, 6, 128], F32)
        self.dma(self.masks[:], self.c_masks.ap.rearrange("m p c -> p m c"), [self.c_masks.b()], [self.masks.b()])
        self.tri2 = self.sb("tri2", [128, 2, 256], F32)
        self.dma(self.tri2[:], self.c_tri2.ap.rearrange("m p c -> p m c"), [self.c_tri2.b()], [self.tri2.b()])
        self.masks4 = self.load_const("masks4", self.c_masks4, [128, 6, 4, 128], F32)
        self.ident4 = self.load_const("ident4", self.c_ident4, [128, 4, 128], F32)
        self.eps6 = self.sb("eps6", [128, 1], F32)
        self.memset("pool", self.eps6[:], NORM_EPS, [self.eps6.b()])
        self.eps12 = self.sb("eps12", [128, 1], F32)
        self.memset("pool", self.eps12[:], 1e-12, [self.eps12.b()])
        self.epsgn = self.sb("epsgn", [128, 1], F32)
        self.memset("pool", self.epsgn[:], GN_EPS, [self.epsgn.b()])
        self.pw = [self.ps(f"pw{i}", [128, 1024], F32) for i in range(4)]
        self.pb = [TBV(self.pw[i // 2].h[:, (i % 2) * 512:(i % 2) * 512 + 512]) for i in range(8)]

    def cast_jobs(self, l, which):
        FT = self.FT
        jobs = []
        if which == "in":
            for c0 in range(0, IN_W, 512):
                wd = min(512, IN_W - c0)
                dsts = []
                for j in range(wd // 128):
                    ot = c0 // 128 + j
                    if ot == 14:
                        dsts.append((self.Wv_t[l], self.Wv_t.b(l), j * 128, 256))
                    elif ot != 15:
                        dsts.append((self.Win_t[l, ot], self.Win_t.b((l, ot)), j * 128, 128))
                jobs.append((self.w_in, l, c0, wd, KT, dsts))
        else:
            for c0 in range(0, D, 512):
                dsts = [(self.Wout_t[l, c0 // 128 + j], self.Wout_t.b((l, c0 // 128 + j)), j * 128, 128) for j in range(4)]
                jobs.append((self.w_out, l, c0, 512, KT, dsts))
            for c0 in range(0, 2 * self.DFF, 512):
                wd = min(512, 2 * self.DFF - c0)
                dsts = [(self.Wup_t[l, c0 // 128 + j], self.Wup_t.b((l, c0 // 128 + j)), j * 128, 128) for j in range(wd // 128)]
                jobs.append((self.ffn_up, l, c0, wd, KT, dsts))
            cw = 128 if FT > 16 else 512
            for c0 in range(0, D, cw):
                dsts = [(self.Wdn_t[l, c0 // 128 + j], self.Wdn_t.b((l, c0 // 128 + j)), j * 128, 128) for j in range(cw // 128)]
                jobs.append((self.ffn_down, l, c0, cw, FT, dsts))
        return jobs

    def bg_begin(self, stk, jobs, engs=("pool",)):
        nel = max(KT * 512, self.FT * (128 if self.FT > 16 else 512))
        self.bg = dict(jobs=list(jobs), i=0, pend=None, engs=engs, n=0,
                       wf=[self.sb("wcf", [128, nel], F32, stk) for _ in range(2)],
                       wb=[self.sb("wcb", [128, nel], BF16, stk) for _ in range(2)])

    def bg_step(self):
        bg = getattr(self, "bg", None)
        if bg is None:
            return
        if bg["pend"] is not None:
            (src, l, c0, wd, ktn, dsts), f, b = bg["pend"]
            fv = f[:, 0:ktn * wd].rearrange("p (kt c) -> p kt c", c=wd)
            off = 0
            for (dap, dbuf, co, w) in dsts:
                view = b[:, off:off + ktn * w].rearrange("p (kt c) -> p kt c", c=w)
                self.cp(bg["engs"][bg["n"] % len(bg["engs"])], view, fv[:, :, co:co + w], [f.b()], [b.b()])
                bg["n"] += 1
                self.dma(dap, view, [b.b()], [dbuf])
                off += ktn * w
            bg["pend"] = None
        if bg["i"] < len(bg["jobs"]):
            job = bg["jobs"][bg["i"]]
            f = bg["wf"][bg["i"] % 2]
            b = bg["wb"][bg["i"] % 2]
            (src, l, c0, wd, ktn, dsts) = job
            fv = f[:, 0:ktn * wd].rearrange("p (kt c) -> p kt c", c=wd)
            self.dma(fv, src[l, :, c0:c0 + wd].rearrange("(kt p) c -> p kt c", p=128), [src.b()], [f.b()])
            bg["pend"] = (job, f, b)
            bg["i"] += 1

    def bg_end(self):
        bg = getattr(self, "bg", None)
        if bg is None:
            return
        while bg["pend"] is not None or bg["i"] < len(bg["jobs"]):
            self.bg_step()
        self.bg = None

    def cast_weights(self):
        self.P.stage = "cast"
        sched = self.cfg.get("bg_cast", True)
        jobs = self.cast_jobs(0, "in")
        self.bg_sched = {}
        if sched:
            self.bg_sched[("attn", 0, "s")] = self.cast_jobs(0, "rest")
            later = []
            for l in range(1, self.DEPTH):
                later += self.cast_jobs(l, "in") + self.cast_jobs(l, "rest")
            self.bg_sched[("rwkv", 0, "p")] = later
        else:
            jobs += self.cast_jobs(0, "rest")
            for l in range(1, self.DEPTH):
                jobs += self.cast_jobs(l, "in") + self.cast_jobs(l, "rest")
        with contextlib.ExitStack() as stk:
            self.bg_begin(stk, jobs, engs=("act", "dve", "pool", "dve"))
            self.bg_end()
            self.barrier()

    def load_pp(self, dst, dst_b, src_rows, n, src_b):
        k = self._pp_i = getattr(self, "_pp_i", 0) + 1
        stg = self.pp_stage[k % 2]
        bank = self.pb[6 + (k % 2)]
        self.dma(stg[0:n, :], src_rows, [src_b], [stg.b()])
        self.tr(bank[:, 0:n], stg[0:n, :], self.ident_f[0:n, 0:n], [stg.b(), self.ident_f.b()], [bank.b()])
        self.cp("dve", dst, bank[:, 0:n], [bank.b()], [dst_b])

    def to_feature_major(self):
        self.P.stage = "tofm"
        with contextlib.ExitStack() as stk:
            xin = [self.sb("xin", [128, D], F32, stk) for _ in range(2)]
            xo = [self.sb("xo", [128, KT, 128], F32, stk) for _ in range(2)]
            i = 0
            for g, src in (("s", self.xs), ("p", self.xp)):
                Tg = self.groups[g][0]
                for blk in range(Tg // 128):
                    a = xin[i % 2]
                    o = xo[i % 2]
                    self.dma(a[:], src[blk * 128:(blk + 1) * 128, :], [src.b()], [a.b()])
                    for q in range(4):
                        bank = self.pb[(i * 4 + q) % 4]
                        for j in range(4):
                            kt = q * 4 + j
                            self.tr(bank[:, j * 128:(j + 1) * 128], a[:, kt * 128:(kt + 1) * 128], self.ident_f[:],
                                    [a.b(), self.ident_f.b()], [bank.b()])
                        eng = "act" if q % 2 else "dve"
                        self.cp(eng, o[:, q * 4:(q + 1) * 4, :], bank[:].rearrange("p (j t) -> p j t", j=4), [bank.b()], [o.b()])
                    self.dma(self.XT[g][:, :, blk * 128:(blk + 1) * 128].rearrange("kt p t -> p kt t"), o[:],
                             [o.b()], [self.XT[g].b(blk // 4)])
                    i += 1
            self.barrier()

    def setup_params(self):
        self.P.stage = "params"
        DEPTH, FT = self.DEPTH, self.FT
        self.pp_stage = [self.sb("ppstg", [128, 128], F32) for _ in range(2)]
        self.prm = []
        ccT = self.sb("ccT", [128, 32], F32)
        self.load_pp(ccT[:], ccT.b(), self.cc.ap.rearrange("v (kt p) -> (v kt) p", p=128), 32, self.cc.b())
        sc = self.sb("sc", [128, 32], F32)
        self.act(sc[:], ccT[:], AF.Silu, [ccT.b()], [sc.b()])
        fng = self.sb("fng", [128, KT], F32)
        self.load_pp(fng[:], fng.b(), self.final_norm_g.ap.rearrange("(kt p) -> kt p", p=128), KT, self.final_norm_g.b())
        self.fng = fng
        for l in range(DEPTH):
            pr = {}
            def vec(name, src, n, rows):
                t = self.sb(name, [128, n], F32)
                self.load_pp(t[:], t.b(), rows, n, src.b())
                return t
            pr["n1g"] = vec("n1g", self.norm1_g, KT, self.norm1_g[l].rearrange("(kt p) -> kt p", p=128))
            pr["n2g"] = vec("n2g", self.norm2_g, KT, self.norm2_g[l].rearrange("(kt p) -> kt p", p=128))
            pr["qng"] = vec("qng", self.q_norm_g, 1, self.q_norm_g[l:l + 1, :])
            pr["kng"] = vec("kng", self.k_norm_g, 1, self.k_norm_g[l:l + 1, :])
            pr["rwconv"] = vec("rwconv", self.rw_conv, 42, self.rw_conv[l].rearrange("j (t p) -> (j t) p", p=128))
            for nm, src in (("kk", self.rw_kk), ("ka", self.rw_ka), ("rk", self.rw_rk), ("lng", self.rw_lnx_g), ("lnb", self.rw_lnx_b)):
                pr[nm] = vec(nm, src, 4, src[l].rearrange("(t p) -> t p", p=128))
            pr["a0"] = vec("a0", self.rw_a0, 8, self.rw_a0[l].rearrange("d (t p) -> (d t) p", p=128))
            c1 = self.sb("c1", [128, 4], F32)
            self.ts("dve", c1[:], pr["ka"][:], -1.0, 1.0, ALU.mult, ALU.add, [pr["ka"].b()], [c1.b()])
            pr["c1"] = c1
            nft = 2 * FT
            fcw = self.sb("fcw", [128, 3, nft], F32)
            for j in range(3):
                self.load_pp(fcw[:, j, :], fcw.b(), self.ffn_conv_w[l, j].rearrange("(t p) -> t p", p=128), nft, self.ffn_conv_w.b())
            pr["fcw"] = fcw
            nfcw = self.sb("nfcw", [128, 3, nft], F32)
            self.ts("dve", nfcw[:], fcw[:], -1.0, None, ALU.mult, None, [fcw.b()], [nfcw.b()])
            pr["nfcw"] = nfcw
            pr["fcb"] = vec("fcb", self.ffn_conv_b, nft, self.ffn_conv_b[l].rearrange("(t p) -> t p", p=128))
            bada = vec("bada", self.b_ada, 96, self.b_ada[l].rearrange("(t p) -> t p", p=128))
            mods = [self.sb(f"mod{v}", [128, 96], F32) for v in range(2)]
            modr = self.mod_rows[l]
            with contextlib.ExitStack() as stk:
                wa = [self.sb("wada", [128, KT, 512], F32, stk) for _ in range(2)]
                rowt = [self.sb("modrow", [2, 512], F32, stk) for _ in range(2)]
                for blk in range(24):
                    w = wa[blk % 2]
                    acc = self.pb[4 + blk % 2]
                    self.dma(w[:], self.w_ada[l, :, blk * 512:(blk + 1) * 512].rearrange("(kt p) c -> p kt c", p=128),
                             [self.w_ada.b()], [w.b()])
                    for kt in range(KT):
                        self.mm(acc[0:2, :], sc[:, kt:32:16], w[:, kt, :], kt == 0, kt == KT - 1, [w.b(), sc.b()], [acc.b()])
                    rt_ = rowt[blk % 2]
                    self.cp("dve" if blk % 2 else "act", rt_[:], acc[0:2, :], [acc.b()], [rt_.b()])
                    self.dma(modr[:, blk * 512:(blk + 1) * 512], rt_[:], [rt_.b()], [modr.b()])
                for v in range(2):
                    m = mods[v]
                    self.load_pp(m[:], m.b(), modr[v].rearrange("(t p) -> t p", p=128), 96, modr.b())
                    self.tt("dve", m[:], m[:], bada[:], ALU.add, [m.b(), bada.b()], [m.b()])
                self.barrier()
            pr["mod"] = mods
            for v in range(2):
                for nm, gname, j in (("gs1", "n1g", 1), ("gs2", "n2g", 4)):
                    t = self.sb(f"{nm}_{v}", [128, KT], F32)
                    self.stt("dve", t[:], mods[v][:, j * 16:(j + 1) * 16], 1.0, pr[gname][:], ALU.add, ALU.mult,
                             [mods[v].b(), pr[gname].b()], [t.b()])
                    pr[f"{nm}_{v}"] = t
            self.prm.append(pr)

    def norm_mod(self, xt, chunks, gs, sh_ap_fn, hT, stk, ss_banks):
        Wtot = chunks[-1][1]
        sq = [self.sb("nsq", [128, Wtot], BF16, stk) for _ in range(2)]
        rstd = self.sb("nrstd", [128, Wtot], F32, stk)
        tmp = [self.sb("ntmp", [128, Wtot], F32, stk) for _ in range(2)]
        for kt in range(KT):
            s = sq[kt % 2]
            self.act(s[:], xt[:, kt, :], AF.Square, [xt.b()], [s.b()])
            for ci, (c0, c1) in enumerate(chunks):
                bk = ss_banks[ci]
                self.mm(bk[:, 0:c1 - c0], self.ones_b[:], s[:, c0:c1], kt == 0, kt == KT - 1,
                        [s.b(), self.ones_b.b()], [bk.b()])
        for ci, (c0, c1) in enumerate(chunks):
            bk = ss_banks[ci]
            self.act(rstd[:, c0:c1], bk[:, 0:c1 - c0], AF.Sqrt, [bk.b(), self.eps6.b()], [rstd.b()],
                     scale=1.0 / D, bias=self.eps6[:, 0:1])
        self.recip(rstd[:], rstd[:], [rstd.b()], [rstd.b()])
        for kt in range(KT):
            t = tmp[kt % 2]
            self.tt("pool" if kt % 2 else "dve", t[:], xt[:, kt, :], rstd[:], ALU.mult, [xt.b(), rstd.b()], [t.b()])
            self.act(hT[:, kt, :], t[:], AF.Identity, [t.b(), gs.b()], [hT.b()],
                     scale=gs[:, kt:kt + 1], bias=sh_ap_fn(kt))

    def zero_rwkv(self, g):
        Tg = self.groups[g][0]
        with contextlib.ExitStack() as stk:
            z = self.sb("zz", [128, Tg], BF16, stk)
            self.memset("pool", z[:], 0.0, [z.b()])
            for r in range(12, 16):
                self.dma(self.MIXT[g][r], z[:], [z.b()], [self.MIXT[g].b(i) for i in range((Tg + 511) // 512)])
            self.barrier()

    def stage1_x(self, l, g, c0, X):
        self._X1 = X
        return self.stage1(l, g, c0)

    def stage1(self, l, g, c0):
        self.P.stage = f"s1_{l}_{g}"
        W = 512
        v = 0 if g == "s" else 1
        pr = self.prm[l]
        mod = pr["mod"][v]
        ti = c0 // 512
        pb = self.pb
        is_s = (g == "s")
        with contextlib.ExitStack() as stk:
            xt = self.sb("xt", [128, KT, W], F32, stk)
            hT = self.sb("hT", [128, KT, W], BF16, stk)
            self.dma(xt[:], self._X1[g][:, :, c0:c0 + W].rearrange("kt p t -> p kt t"), [self._X1[g].b(ti)], [xt.b()])
            self.norm_mod(xt, [(0, W)], pr[f"gs1_{v}"], lambda kt: mod[:, kt:kt + 1], hT, stk, [pb[7]])
            wts = [self.sb("wt", [128, KT, 128], BF16, stk) for _ in range(3)]
            wv = self.sb("wv", [128, KT, 256], BF16, stk)
            uT = self.sb("uT", [128, W], BF16, stk)
            abt = self.sb("abt", [128, 4, 256], BF16, stk)
            sqh = self.sb("sqh", [128, W], BF16, stk)
            rq = self.sb("rq", [128, W], F32, stk)
            qn = [self.sb("qn", [128, W], F32, stk) for _ in range(2)]
            t1 = self.sb("t1", [128, W], F32, stk)
            t2 = self.sb("t2", [128, W], F32, stk)
            qr = [self.sb("qr", [128, W], BF16, stk) for _ in range(2)]
            ktok = self.sb("ktok", [128, 4, 128], F32, stk)
            vt = self.sb("vt", [128, 4, 256], BF16, stk)
            vtf = self.sb("vtf", [128, 4, 256], F32, stk)
            rwt = [self.sb("rwt", [128, W], F32, stk) for _ in range(2)]
            if is_s:
                cosb = self.sb("cosb", [128, W], F32, stk)
                sinb = self.sb("sinb", [128, W], F32, stk)
                self.dma(cosb[:], self.c_cosT[:, c0:c0 + W], [self.c_cosT.b()], [cosb.b()])
                self.dma(sinb[:], self.c_sinT[:, c0:c0 + W], [self.c_sinT.b()], [sinb.b()])

            order = list(range(14)) + list(range(16, 30))

            def load_w(oi):
                ot = order[oi]
                w = wts[oi % 3]
                self.dma(w[:], self.Win_t[l, ot], [self.Win_t.b((l, ot))], [w.b()])
            load_w(0)
            load_w(1)
            self.dma(wv[:], self.Wv_t[l], [self.Wv_t.b(l)], [wv.b()])
            for oi, ot in enumerate(order):
                if oi + 2 < len(order):
                    load_w(oi + 2)
                w = wts[oi % 3]
                acc = pb[oi % 3]
                for kt in range(KT):
                    self.mm(acc[:], w[:, kt, :], hT[:, kt, :], kt == 0, kt == KT - 1, [w.b(), hT.b()], [acc.b()])
                skip = self.cfg.get("s1_skip", "")
                if ("f" in skip and ot < 4) or ("q" in skip and 4 <= ot < 14) or ("r" in skip and ot >= 16):
                    continue
                if ot < 4:
                    self.cp("act", uT[:], acc[:], [acc.b()], [uT.b()])
                    for sub in range(4):
                        reg = pb[5][:, (sub % 2) * 256:(sub % 2) * 256 + 256]
                        self.mm(reg, uT[:, sub * 128:(sub + 1) * 128], self.csC[:], True, True,
                                [uT.b(), self.csC.b()], [pb[5].b()])
                        self.cp("dve", abt[:, sub, :], reg, [pb[5].b()], [abt.b()])
                    self.dma(self.AB[g][c0:c0 + W, ot * 256:(ot + 1) * 256].rearrange("(s p) c -> p s c", p=128), abt[:],
                             [abt.b()], [self.AB[g].b(ti)])
                elif ot < 14:
                    isk = ot >= 12
                    hd = ot - 12 if isk else ot - 4
                    gq = pr["kng"] if isk else pr["qng"]
                    q_n = qn[oi % 2]
                    q_r = qr[oi % 2]
                    self.act(sqh[:], acc[:], AF.Square, [acc.b()], [sqh.b()])
                    self.mm(pb[3][:], self.ones_b[:], sqh[:], True, True, [sqh.b(), self.ones_b.b()], [pb[3].b()])
                    self.act(rq[:], pb[3][:], AF.Sqrt, [pb[3].b(), self.eps6.b()], [rq.b()], scale=1.0 / 128, bias=self.eps6[:, 0:1])
                    self.recip(rq[:], rq[:], [rq.b()], [rq.b()])
                    self.stt("dve", q_n[:], acc[:], gq[:, 0:1], rq[:], ALU.mult, ALU.mult, [acc.b(), gq.b(), rq.b()], [q_n.b()])
                    if isk and not is_s:
                        for sub in range(4):
                            self.tr(pb[5][:, sub * 128:(sub + 1) * 128], q_n[:, sub * 128:(sub + 1) * 128], self.ident_f[:],
                                    [q_n.b(), self.ident_f.b()], [pb[5].b()])
                        self.cp("dve", ktok[:], pb[5][:].rearrange("p (s c) -> p s c", s=4), [pb[5].b()], [ktok.b()])
                        for sub in range(4):
                            tok = sub * 128
                            sq_, off = tok // self.TP, tok % self.TP
                            self.dma(self.nk[sq_, l, off:off + 128, hd * 128:(hd + 1) * 128], ktok[:, sub, :],
                                     [ktok.b()], [self.nk.b()])
                    if is_s:
                        self.mm(pb[4][:], self.prot_f[:], q_n[:], True, True, [self.prot_f.b(), q_n.b()], [pb[4].b()])
                        self.tt("pool", t1[:], q_n[:], cosb[:], ALU.mult, [q_n.b(), cosb.b()], [t1.b()])
                        self.tt("dve", t2[:], pb[4][:], sinb[:], ALU.mult, [pb[4].b(), sinb.b()], [t2.b()])
                        self.tt("pool", q_r[:], t1[:], t2[:], ALU.add, [t1.b(), t2.b()], [q_r.b()])
                    else:
                        self.cp("pool", q_r[:], q_n[:], [q_n.b()], [q_r.b()])
                    dst = self.KTs[g] if isk else self.QT[g]
                    self.dma(dst[hd, :, c0:c0 + W], q_r[:], [q_r.b()], [dst.b(ti)])
                else:
                    r = rwt[oi % 2]
                    self.cp("act" if oi % 2 else "dve", r[:], acc[:], [acc.b()], [r.b()])
                    self.dma(self.RW[g][(ot - 16) * 128:(ot - 15) * 128, c0:c0 + W], r[:], [r.b()], [self.RW[g].b(ti)])
                if oi == self.cfg.get("v_at", 13) and "v" not in skip:
                    for sub in range(4):
                        reg = pb[6][:, (sub % 2) * 256:(sub % 2) * 256 + 256]
                        for kt in range(KT):
                            self.mm(reg, hT[:, kt, sub * 128:(sub + 1) * 128], wv[:, kt, :], kt == 0, kt == KT - 1,
                                    [hT.b(), wv.b()], [pb[6].b()])
                        self.cp("act", vt[:, sub, :], reg, [pb[6].b()], [vt.b()])
                        if not is_s:
                            self.cp(self.cfg.get("vtf_eng", "dve"), vtf[:, sub, :], reg, [pb[6].b()], [vtf.b()])
                    self.dma(self.Vs[g][c0:c0 + W, :].rearrange("(s p) c -> p s c", p=128), vt[:], [vt.b()], [self.Vs[g].b(ti)])
                    if not is_s:
                        for sub in range(4):
                            tok = sub * 128
                            sq_, off = tok // self.TP, tok % self.TP
                            self.dma(self.nv[sq_, l, off:off + 128, :], vtf[:, sub, :], [vtf.b()], [self.nv.b()])
            self.barrier()

    def stage_fourier(self, l, g):
        self.P.stage = f"four_{l}_{g}"
        Tg, nseq, Tseq = self.groups[g]
        nch = Tseq // 128
        TW = min(512, Tseq)
        pb = self.pb
        allab = [self.AB[g].b(i) for i in range((Tg + 511) // 512)]
        with contextlib.ExitStack() as stk:
            ab = self.sb("fab", [128, nch, 1024], BF16, stk)
            ctb = [self.sb("fct", [128, nch, TW], BF16, stk) for _ in range(2)]
            nsb = [self.sb("fns", [128, nch, TW], BF16, stk) for _ in range(2)]
            fo = [self.sb("ffo", [128, TW], BF16, stk) for _ in range(2)]
            n = 0
            for s in range(nseq):
                self.dma(ab[:], self.AB[g][s * Tseq:(s + 1) * Tseq, :].rearrange("(c p) x -> p c x", p=128), allab, [ab.b()])
                for ti, t0 in enumerate(range(0, Tseq, TW)):
                    cb = ctb[ti % 2]
                    sbb = nsb[ti % 2]
                    self.dma(cb[:], self.c_ct[g][:, t0:t0 + TW].rearrange("(c p) t -> p c t", p=128), [self.c_ct[g].b()], [cb.b()])
                    self.dma(sbb[:], self.c_nst[g][:, t0:t0 + TW].rearrange("(c p) t -> p c t", p=128), [self.c_nst[g].b()], [sbb.b()])
                    for grp in range(4):
                        acc = pb[n % 4]
                        for c in range(nch):
                            self.mm(acc[:, 0:TW], ab[:, c, grp * 256:grp * 256 + 128], cb[:, c, :], c == 0, False,
                                    [ab.b(), cb.b()], [acc.b()])
                            self.mm(acc[:, 0:TW], ab[:, c, grp * 256 + 128:grp * 256 + 256], sbb[:, c, :], False, c == nch - 1,
                                    [ab.b(), sbb.b()], [acc.b()])
                        f = fo[n % 2]
                        self.cp("act" if n % 2 else "dve", f[:], acc[:, 0:TW], [acc.b()], [f.b()])
                        c0 = s * Tseq + t0
                        self.dma(self.MIXT[g][grp, :, c0:c0 + TW], f[:], [f.b()], [self.MIXT[g].b(c0 // 512)])
                        n += 1
            self.barrier()

    def stage_attn(self, l, g):
        self.P.stage = f"attn_{l}_{g}"
        Tg, nseq, Tseq = self.groups[g]
        is_s = (g == "s")
        Stot = Tseq + (PAST if is_s else 0)
        nck = Stot // 128
        QW = min(512, Tseq)
        pb = self.pb
        ntile = (Tg + 511) // 512
        scale = 128.0 ** -0.5
        with contextlib.ExitStack() as stk:
            kT = self.sb("akT", [128, NKV, Stot], BF16, stk)
            vv = self.sb("avv", [128, nck, 256], BF16, stk)
            qT = [self.sb("aqT", [128, Tseq], BF16, stk) for _ in range(2)]
            pT = [self.sb("apT", [128, QW], BF16, stk) for _ in range(3)]
            rden = self.sb("arden", [128, QW], F32, stk)
            ob = [self.sb("aob", [128, QW], BF16, stk) for _ in range(2)]
            if is_s:
                ckf = self.sb("ackf", [128, PAST // 128, 256], F32, stk)
                cvf = self.sb("acvf", [128, PAST // 128, 256], F32, stk)
            nq = 0
            bgj = self.bg_sched.pop(("attn", l, g), None)
            if bgj:
                self.bg_begin(stk, bgj, engs=("dve", "pool", "dve"))
            for s in range(nseq):
                c0s = s * Tseq
                for kv in range(NKV):
                    self.dma(kT[:, kv, 0:Tseq], self.KTs[g][kv, :, c0s:c0s + Tseq], [self.KTs[g].b(i) for i in range(ntile)], [kT.b()])
                self.dma(vv[:, 0:Tseq // 128, :], self.Vs[g][c0s:c0s + Tseq, :].rearrange("(c p) x -> p c x", p=128),
                         [self.Vs[g].b(i) for i in range(ntile)], [vv.b()])
                if is_s:
                    self.dma(ckf[:], self.ck[l].rearrange("(c p) x -> p c x", p=128), [self.ck.b()], [ckf.b()])
                    self.dma(cvf[:], self.cv[l].rearrange("(c p) x -> p c x", p=128), [self.cv.b()], [cvf.b()])
                    self.cp("pool", vv[:, Tseq // 128:nck, :], cvf[:], [cvf.b()], [vv.b()])
                    for kv in range(NKV):
                        bank = pb[kv]
                        for c in range(PAST // 128):
                            self.tr(bank[:, c * 128:(c + 1) * 128], ckf[:, c, kv * 128:(kv + 1) * 128], self.ident_f[:],
                                    [ckf.b(), self.ident_f.b()], [bank.b()])
                        self.cp("act", kT[:, kv, Tseq:Stot], bank[:, 0:PAST], [bank.b()], [kT.b()])
                for h in range(NH):
                    kv = h // (NH // NKV)
                    q = qT[h % 2]
                    self.dma(q[:], self.QT[g][h, :, c0s:c0s + Tseq], [self.QT[g].b(i) for i in range(ntile)], [q.b()])
                    for q0 in range(0, Tseq, QW):
                        self.bg_step()
                        oacc = pb[4 + nq % 2]
                        dacc = pb[6 + nq % 2]
                        def score(c):
                            sT = pb[c % 3]
                            p = pT[c % 3]
                            self.mm(sT[:, 0:QW], kT[:, kv, c * 128:(c + 1) * 128], q[:, q0:q0 + QW], True, True,
                                    [kT.b(), q.b()], [sT.b()])
                            self.act(p[:], sT[:, 0:QW], AF.Exp, [sT.b()], [p.b()], scale=scale)
                        pipe = self.cfg.get("attn_pipe", True)
                        if pipe:
                            score(0)
                        for c in range(nck):
                            if not pipe:
                                score(c)
                            elif c + 1 < nck:
                                score(c + 1)
                            p = pT[c % 3]
                            self.mm(oacc[:, 0:QW], vv[:, c, kv * 128:(kv + 1) * 128], p[:], c == 0, c == nck - 1,
                                    [vv.b(), p.b()], [oacc.b()])
                            self.mm(dacc[:, 0:QW], self.ones_b[:], p[:], c == 0, c == nck - 1,
                                    [self.ones_b.b(), p.b()], [dacc.b()])
                        self.recip(rden[:], dacc[:, 0:QW], [dacc.b()], [rden.b()])
                        o = ob[nq % 2]
                        self.tt("dve", o[:], oacc[:, 0:QW], rden[:], ALU.mult, [oacc.b(), rden.b()], [o.b()])
                        cc0 = c0s + q0
                        self.dma(self.MIXT[g][4 + h, :, cc0:cc0 + QW], o[:], [o.b()], [self.MIXT[g].b(cc0 // 512)])
                        nq += 1
            self.bg_end()
            self.barrier()

    def stage3(self, l, g, c0, X):
        self.P.stage = f"s3_{l}_{g}"
        W = 512
        v = 0 if g == "s" else 1
        pr = self.prm[l]
        mod = pr["mod"][v]
        ti = c0 // 512
        pb = self.pb
        with contextlib.ExitStack() as stk:
            xt = self.sb("xt3", [128, KT, W], F32, stk)
            mx = self.sb("mx3", [128, KT, W], BF16, stk)
            wts = [self.sb("wt3", [128, KT, 128], BF16, stk) for _ in range(3)]
            self.dma(xt[:], X[g][:, :, c0:c0 + W].rearrange("kt p t -> p kt t"), [X[g].b(ti)], [xt.b()])
            self.dma(mx[:], self.MIXT[g][:, :, c0:c0 + W].rearrange("kt p t -> p kt t"), [self.MIXT[g].b(ti)], [mx.b()])

            def load_w(ot):
                w = wts[ot % 3]
                self.dma(w[:], self.Wout_t[l, ot], [self.Wout_t.b((l, ot))], [w.b()])
            load_w(0)
            load_w(1)
            for ot in range(KT):
                if ot + 2 < KT:
                    load_w(ot + 2)
                w = wts[ot % 3]
                acc = pb[ot % 4]
                for kt in range(KT):
                    self.mm(acc[:], w[:, kt, :], mx[:, kt, :], kt == 0, kt == KT - 1, [w.b(), mx.b()], [acc.b()])
                self.stt("dve", xt[:, ot, :], acc[:], mod[:, 32 + ot:33 + ot], xt[:, ot, :], ALU.mult, ALU.add,
                         [acc.b(), xt.b(), xt.b(ot)], [xt.b(ot)])
            self.dma(X[g][:, :, c0:c0 + W].rearrange("kt p t -> p kt t"), xt[:], [xt.b(ot) for ot in range(KT)] + [xt.b()], [X[g].b(ti)])
            self.barrier()

    def stage4(self, l, g, c0, X, Y):
        self.P.stage = f"s4_{l}_{g}"
        W = 512
        Tg, nseq, Tseq = self.groups[g]
        v = 0 if g == "s" else 1
        pr = self.prm[l]
        mod = pr["mod"][v]
        ti = c0 // 512
        nti = (Tg + 511) // 512
        pb, pw = self.pb, self.pw
        FT = self.FT
        fcw, fcb, nfcw = pr["fcw"], pr["fcb"], pr["nfcw"]
        with contextlib.ExitStack() as stk:
            xt = self.sb("xt4", [128, KT, W + 2], F32, stk)
            hT = self.sb("hT4", [128, KT, W + 2], BF16, stk)
            actT = self.sb("act4", [128, FT, W], BF16, stk)
            wts = [self.sb("wt4", [128, KT, 128], BF16, stk) for _ in range(3)]
            wdn = [self.sb("wd4", [128, FT, 128], BF16, stk) for _ in range(2)]
            ua = self.sb("ua4", [128, W], F32, stk)
            ug = self.sb("ug4", [128, W], F32, stk)
            sg = self.sb("sg4", [128, W], F32, stk)
            self.dma(xt[:, :, 0:W], X[g][:, :, c0:c0 + W].rearrange("kt p t -> p kt t"), [X[g].b(ti)], [xt.b()])
            lzero = (c0 % Tseq == 0)
            rzero = ((c0 + W) % Tseq == 0)
            if lzero:
                self.memset("pool", xt[:, :, W:W + 1], 0.0, [xt.b()])
            else:
                self.dma(xt[:, :, W:W + 1], X[g][:, :, c0 - 1:c0].rearrange("kt p t -> p kt t"), [X[g].b(ti - 1)], [xt.b()],
                         allow_slow_non_contiguous=True)
            if rzero:
                self.memset("pool", xt[:, :, W + 1:W + 2], 0.0, [xt.b()])
            else:
                self.dma(xt[:, :, W + 1:W + 2], X[g][:, :, c0 + W:c0 + W + 1].rearrange("kt p t -> p kt t"), [X[g].b(ti + 1)], [xt.b()],
                         allow_slow_non_contiguous=True)
            self.norm_mod(xt, [(0, W), (W, W + 2)], pr[f"gs2_{v}"], lambda kt: mod[:, 48 + kt:49 + kt], hT, stk, [pb[7], pb[6]])
            if lzero:
                self.memset("pool", hT[:, :, W:W + 1], 0.0, [hT.b()])
            if rzero:
                self.memset("pool", hT[:, :, W + 1:W + 2], 0.0, [hT.b()])
            inner = [b for b in range(Tseq, W, Tseq)] if Tseq < W else []

            def load_w(j):
                i, half = j // 2, j % 2
                w = wts[j % 3]
                ot = i + half * FT
                self.dma(w[:], self.Wup_t[l, ot], [self.Wup_t.b((l, ot))], [w.b()])
            load_w(0)
            load_w(1)
            for j in range(2 * FT):
                if j + 2 < 2 * FT:
                    load_w(j + 2)
                i, half = j // 2, j % 2
                ot = i + half * FT
                w = wts[j % 3]
                wide = j % 3
                A, Bk = pb[2 * wide], pb[2 * wide + 1]
                for kt in range(KT):
                    self.mm(A[:], w[:, kt, :], hT[:, kt, 0:W], kt == 0, kt == KT - 1, [w.b(), hT.b()], [A.b()])
                for kt in range(KT):
                    self.mm(Bk[:, 0:2], w[:, kt, :], hT[:, kt, W:W + 2], kt == 0, kt == KT - 1, [w.b(), hT.b()], [Bk.b()])
                u = ug if half else ua
                w0 = fcw[:, 0, ot:ot + 1]
                w1 = fcw[:, 1, ot:ot + 1]
                w2 = fcw[:, 2, ot:ot + 1]
                self.act(u[:], A[:], AF.Identity, [A.b(), fcw.b(), fcb.b()], [u.b()], scale=w1, bias=fcb[:, ot:ot + 1])
                self.stt("dve", u[:, 1:W], A[:, 0:W - 1], w0, u[:, 1:W], ALU.mult, ALU.add, [A.b(), u.b()], [u.b()])
                self.stt("dve", u[:, 0:W - 1], A[:, 1:W], w2, u[:, 0:W - 1], ALU.mult, ALU.add, [A.b(), u.b()], [u.b()])
                self.stt("dve", u[:, 0:1], Bk[:, 0:1], w0, u[:, 0:1], ALU.mult, ALU.add, [Bk.b(), u.b()], [u.b()])
                self.stt("dve", u[:, W - 1:W], Bk[:, 1:2], w2, u[:, W - 1:W], ALU.mult, ALU.add, [Bk.b(), u.b()], [u.b()])
                for bnd in inner:
                    self.stt("dve", u[:, bnd - 1:bnd], A[:, bnd:bnd + 1], nfcw[:, 2, ot:ot + 1], u[:, bnd - 1:bnd], ALU.mult, ALU.add,
                             [A.b(), u.b(), nfcw.b()], [u.b()])
                    self.stt("dve", u[:, bnd:bnd + 1], A[:, bnd - 1:bnd], nfcw[:, 0, ot:ot + 1], u[:, bnd:bnd + 1], ALU.mult, ALU.add,
                             [A.b(), u.b(), nfcw.b()], [u.b()])
                if half:
                    self.act(sg[:], ug[:], AF.Silu, [ug.b()], [sg.b()])
                    self.tt("pool", actT[:, i, :], sg[:], ua[:], ALU.mult, [sg.b(), ua.b()], [actT.b()])
            self.dma(wdn[0][:], self.Wdn_t[l, 0], [self.Wdn_t.b((l, 0))], [wdn[0].b()])
            for ot in range(KT):
                if ot + 1 < KT:
                    self.dma(wdn[(ot + 1) % 2][:], self.Wdn_t[l, ot + 1], [self.Wdn_t.b((l, ot + 1))], [wdn[(ot + 1) % 2].b()])
                w = wdn[ot % 2]
                acc = pb[6 + ot % 2]
                for kt in range(FT):
                    self.mm(acc[:], w[:, kt, :], actT[:, kt, :], kt == 0, kt == FT - 1, [w.b(), actT.b()], [acc.b()])
                self.stt("dve", xt[:, ot, 0:W], acc[:], mod[:, 80 + ot:81 + ot], xt[:, ot, 0:W], ALU.mult, ALU.add,
                         [acc.b(), xt.b(), xt.b(ot)], [xt.b(ot)])
            self.dma(Y[g][:, :, c0:c0 + W].rearrange("kt p t -> p kt t"), xt[:, :, 0:W], [xt.b(ot) for ot in range(KT)] + [xt.b()], [Y[g].b(ti)])
            self.barrier()

    def stage5(self, g, c0, X):
        self.P.stage = f"s5_{g}"
        W = 512
        ti = c0 // 512
        pb = self.pb
        out = self.ys if g == "s" else self.yp
        with contextlib.ExitStack() as stk:
            xt = self.sb("xt5", [128, KT, W], F32, stk)
            xn = self.sb("xn5", [128, KT, W], F32, stk)
            sq = [self.sb("sq5", [128, W], BF16, stk) for _ in range(2)]
            rstd = self.sb("rstd5", [128, W], F32, stk)
            yt = [self.sb("yt5", [128, D], F32, stk) for _ in range(2)]
            self.dma(xt[:], X[g][:, :, c0:c0 + W].rearrange("kt p t -> p kt t"), [X[g].b(ti)], [xt.b()])
            for kt in range(KT):
                s = sq[kt % 2]
                self.act(s[:], xt[:, kt, :], AF.Square, [xt.b()], [s.b()])
                self.mm(pb[7][:], self.ones_b[:], s[:], kt == 0, kt == KT - 1, [s.b(), self.ones_b.b()], [pb[7].b()])
            self.act(rstd[:], pb[7][:], AF.Sqrt, [pb[7].b(), self.eps6.b()], [rstd.b()], scale=1.0 / D, bias=self.eps6[:, 0:1])
            self.recip(rstd[:], rstd[:], [rstd.b()], [rstd.b()])
            for kt in range(KT):
                self.stt("dve", xn[:, kt, :], xt[:, kt, :], self.fng[:, kt:kt + 1], rstd[:], ALU.mult, ALU.mult,
                         [xt.b(), rstd.b(), self.fng.b()], [xn.b()])
            for sub in range(4):
                y = yt[sub % 2]
                for q in range(4):
                    bank = pb[q]
                    for j in range(4):
                        kt = q * 4 + j
                        self.tr(bank[:, j * 128:(j + 1) * 128], xn[:, kt, sub * 128:(sub + 1) * 128], self.ident_f[:],
                                [xn.b(), self.ident_f.b()], [bank.b()])
                    self.cp("act" if q % 2 else "dve", y[:, q * 512:(q + 1) * 512], bank[:], [bank.b()], [y.b()])
                self.dma(out[c0 + sub * 128:c0 + (sub + 1) * 128, :], y[:], [y.b()], [out.b()])
            self.barrier()

    def stage_rwkv(self, l, g):
        self.P.stage = f"rwkv_{l}_{g}"
        Tg, nseq, Tseq = self.groups[g]
        is_s = (g == "s")
        T = Tseq
        nch = T // 128
        NB = 4
        pr = self.prm[l]
        pb = self.pb
        ntile = (Tg + 511) // 512
        rwall = [self.RW[g].b(i) for i in range(ntile)]
        conv = pr["rwconv"]
        CW = min(512, T)
        with contextlib.ExitStack() as stk:
            sbt = lambda name, shape, dt: self.sb(name, shape, dt, stk)
            w2e = sbt("w2e", [65, 2, 512], F32)
            a2f = sbt("a2f", [128, 2, 512], F32)
            g2f = sbt("g2f", [128, 512], F32)
            self.dma(w2e[0:64, :, :], self.rw_w2[l].rearrange("d k c -> k d c"), [self.rw_w2.b()], [w2e.b()])
            self.dma(w2e[64:65, :, :], self.rw_w0[l:l + 1], [self.rw_w0.b()], [w2e.b()])
            self.dma(a2f[64:128, :, :], self.rw_a2[l].rearrange("d k c -> k d c"), [self.rw_a2.b()], [a2f.b()])
            self.dma(g2f[:], self.rw_g2[l], [self.rw_g2.b()], [g2f.b()])
            t12 = sbt("t12", [128, T], F32)
            tw = sbt("tw", [65, T], F32)
            sg = sbt("sg", [128, T], F32)
            rc = sbt("rc", [128, T], F32)
            kkn = sbt("kkn", [128, T], F32)
            bA = [sbt("bA", [128, T], BF16) for _ in range(2)]
            KM = [sbt("KM", [128, T], BF16) for _ in range(2)]
            bon = sbt("bon", [128, T], F32)
            yacc = sbt("yacc", [128, T], F32)
            s1 = sbt("scr1", [128, T + 2], F32)
            s2 = sbt("scr2", [128, T], F32)
            s3 = sbt("scr3", [128, T], F32)
            al = sbt("al", [128, T], BF16)
            be = sbt("be", [128, T], BF16)
            ka_ = sbt("kap", [128, T], BF16)
            rt = sbt("rt", [128, T], BF16)
            VtmP = sbt("VtmP", [128, nch, 2, 128], BF16)
            BTt = sbt("BTt", [128, nch, 128], BF16)
            KTt = sbt("KTt", [128, nch, 128], BF16)
            PL = sbt("PL", [128, nch], F32)
            sigT = [sbt("sigT", [128, 128], F32) for _ in range(2)]
            Ptmp = [sbt("Ptmp", [128, 3, 256], F32) for _ in range(2)]
            NI = NB * 2
            Xa = sbt("Xa", [128, NI, 128], BF16)
            XTa = sbt("XTa", [128, NI, 128], BF16)
            Xb = sbt("Xb", [128, NI, 128], BF16)
            XTb = sbt("XTb", [128, NI, 128], BF16)
            Pf = sbt("Pf", [128, NI, 128], F32)
            Pfin = sbt("Pfin", [128, NI, 128], BF16)
            MakT = sbt("MakT", [128, NI, 128], BF16)
            Wbr = sbt("Wbr", [128, NI, 128], BF16)
            Wkr = sbt("Wkr", [128, NI, 128], BF16)
            Sf = sbt("Sf", [128, 128], F32)
            Sb = sbt("Sb", [128, 128], BF16)
            Sld = sbt("Sld", [128, 128], F32)
            RHSs = sbt("RHSs", [128, 128], BF16)
            Upad = sbt("Upad", [128, 2, 128], BF16)
            Ufull = sbt("Ufull", [128, 128], BF16)
            tS = sbt("tS", [128, 128], F32)
            sq = sbt("sqr", [128, CW], BF16)
            rstd = sbt("rstdr", [128, CW], F32)
            ob = [sbt("obr", [128, CW], BF16) for _ in range(2)]

            bgj = self.bg_sched.pop(("rwkv", l, g), None)
            if bgj:
                self.bg_begin(stk, bgj, engs=("act", "dve", "pool", "act", "dve"))
            self.memset("pool", VtmP[:], 0.0, [VtmP.b()])
            self.memset("pool", Upad[:], 0.0, [Upad.b()])
            self.memset("pool", tw[64:65, :], 1.0, [tw.b()])
            self.memset("pool", s1[:, 0:1], 0.0, [s1.b()])
            self.memset("pool", s1[:, T + 1:T + 2], 0.0, [s1.b()])

            def conv_tile(tile_idx, c0s, dst):
                self.dma(s1[:, 1:T + 1], self.RW[g][tile_idx * 128:(tile_idx + 1) * 128, c0s:c0s + T], rwall, [s1.b()])
                w = lambda j: conv[:, j * 14 + tile_idx:j * 14 + tile_idx + 1]
                self.act(dst[:], s1[:, 1:T + 1], AF.Copy, [s1.b(), conv.b()], [dst.b()], scale=w(1))
                self.stt("dve", dst[:], s1[:, 0:T], w(0), dst[:], ALU.mult, ALU.add, [s1.b(), dst.b()], [dst.b()])
                self.stt("dve", dst[:], s1[:, 2:T + 2], w(2), dst[:], ALU.mult, ALU.add, [s1.b(), dst.b()], [dst.b()])

            for s in range(nseq):
                c0s = s * T
                conv_tile(12, c0s, t12)
                self.act(tw[0:64, :], t12[0:64, :], AF.Tanh, [t12.b()], [tw.b()])
                conv_tile(13, c0s, sg)
                self.act(sg[:], sg[:], AF.Sigmoid, [sg.b()], [sg.b()])
                for ct in range(4):
                    self.bg_step()
                    conv_tile(ct, c0s, rc)
                    conv_tile(4 + ct, c0s, s2)
                    self.ts("pool", s3[:], s2[:], pr["kk"][:, ct:ct + 1], None, ALU.mult, None, [s2.b(), pr["kk"].b()], [s3.b()])
                    for x0 in range(0, T, CW):
                        bank = pb[(x0 // CW) % 2]
                        self.act(sq[:], s3[:, x0:x0 + CW], AF.Square, [s3.b()], [sq.b()])
                        self.mm(bank[:, 0:CW], self.bones_b[:], sq[:], True, True, [self.bones_b.b(), sq.b()], [bank.b()])
                        self.act(rstd[:], bank[:, 0:CW], AF.Sqrt, [bank.b(), self.eps12.b()], [rstd.b()], bias=self.eps12[:, 0:1])
                        self.recip(rstd[:], rstd[:], [rstd.b()], [rstd.b()])
                        self.tt("dve", kkn[:, x0:x0 + CW], s3[:, x0:x0 + CW], rstd[:], ALU.mult, [s3.b(), rstd.b()], [kkn.b()])
                    for d in range(2):
                        for x0 in range(0, T, CW):
                            bank = pb[2 + (x0 // CW) % 2]
                            self.mm(bank[:, 0:CW], a2f[64:128, d, ct * 128:(ct + 1) * 128], t12[64:128, x0:x0 + CW], True, True,
                                    [a2f.b(), t12.b()], [bank.b()])
                            self.act(s1[:, 1 + x0:1 + x0 + CW], bank[:, 0:CW], AF.Sigmoid, [bank.b(), pr["a0"].b()], [s1.b()],
                                     bias=pr["a0"][:, d * 4 + ct:d * 4 + ct + 1])
                        self.tt("pool", bA[d][:], kkn[:], s1[:, 1:T + 1], ALU.mult, [kkn.b(), s1.b()], [bA[d].b()])
                        self.ts("dve", s1[:, 1:T + 1], s1[:, 1:T + 1], pr["ka"][:, ct:ct + 1], pr["c1"][:, ct:ct + 1], ALU.mult, ALU.add,
                                [s1.b(), pr["ka"].b(), pr["c1"].b()], [s1.b()])
                        self.tt("dve", KM[d][:], s1[:, 1:T + 1], s2[:], ALU.mult, [s1.b(), s2.b()], [KM[d].b()])
                    conv_tile(8 + ct, c0s, s3)
                    self.tt("pool", s2[:], KM[0][:], KM[1][:], ALU.add, [KM[0].b(), KM[1].b()], [s2.b()])
                    self.stt("dve", s2[:], rc[:], pr["rk"][:, ct:ct + 1], s2[:], ALU.mult, ALU.mult, [rc.b(), s2.b(), pr["rk"].b()], [s2.b()])
                    for x0 in range(0, T, CW):
                        bank = pb[(x0 // CW) % 2]
                        self.mm(bank[:, 0:CW], self.bones_f[:], s2[:, x0:x0 + CW], True, True, [self.bones_f.b(), s2.b()], [bank.b()])
                        self.tt("dve", bon[:, x0:x0 + CW], bank[:, 0:CW], s3[:, x0:x0 + CW], ALU.mult, [bank.b(), s3.b()], [bon.b()])
                    for c in range(nch):
                        bank = pb[2 + c % 2]
                        self.tr(bank[:, 0:128], s3[:, c * 128:(c + 1) * 128], self.ident_f[:], [s3.b(), self.ident_f.b()], [bank.b()])
                        for hh in range(2):
                            self.cp("act" if hh else "dve", VtmP[:, c, hh, hh * 64:(hh + 1) * 64], bank[:, hh * 64:(hh + 1) * 64],
                                    [bank.b()], [VtmP.b()])
                    for d in range(2):
                        self.rwkv_dir(l, g, s, ct, d, T, nch, NB, stk, locals())
                    for x0 in range(0, T, CW):
                        bank = pb[(x0 // CW) % 2]
                        bank2 = pb[2 + (x0 // CW) % 2]
                        bank3 = pb[4 + (x0 // CW) % 2]
                        ys_ = yacc[:, x0:x0 + CW]
                        self.mm(bank[:, 0:CW], self.bones_f[:], ys_, True, True, [self.bones_f.b(), yacc.b()], [bank.b()])
                        self.stt("dve", s2[:, x0:x0 + CW], bank[:, 0:CW], -1.0 / 64, ys_, ALU.mult, ALU.add, [bank.b(), yacc.b()], [s2.b()])
                        self.act(sq[:], s2[:, x0:x0 + CW], AF.Square, [s2.b()], [sq.b()])
                        self.mm(bank2[:, 0:CW], self.bones_b[:], sq[:], True, True, [self.bones_b.b(), sq.b()], [bank2.b()])
                        self.act(rstd[:], bank2[:, 0:CW], AF.Sqrt, [bank2.b(), self.epsgn.b()], [rstd.b()], scale=1.0 / 64, bias=self.epsgn[:, 0:1])
                        self.recip(rstd[:], rstd[:], [rstd.b()], [rstd.b()])
                        self.tt("dve", s2[:, x0:x0 + CW], s2[:, x0:x0 + CW], rstd[:], ALU.mult, [s2.b(), rstd.b()], [s2.b()])
                        self.ts("dve", s2[:, x0:x0 + CW], s2[:, x0:x0 + CW], pr["lng"][:, ct:ct + 1], pr["lnb"][:, ct:ct + 1], ALU.mult, ALU.add,
                                [s2.b(), pr["lng"].b(), pr["lnb"].b()], [s2.b()])
                        self.tt("pool", s2[:, x0:x0 + CW], s2[:, x0:x0 + CW], bon[:, x0:x0 + CW], ALU.add, [s2.b(), bon.b()], [s2.b()])
                        self.mm(bank3[:, 0:CW], g2f[:, ct * 128:(ct + 1) * 128], sg[:, x0:x0 + CW], True, True, [g2f.b(), sg.b()], [bank3.b()])
                        o = ob[(x0 // CW) % 2]
                        self.tt("dve", o[:], bank3[:, 0:CW], s2[:, x0:x0 + CW], ALU.mult, [bank3.b(), s2.b()], [o.b()])
                        cc0 = c0s + x0
                        self.dma(self.MIXT[g][12 + ct, :, cc0:cc0 + CW], o[:], [o.b()], [self.MIXT[g].b(cc0 // 512)])
            self.bg_end()
            self.barrier()

    def rwkv_dir(self, l, g, s, ct, d, T, nch, NB, stk, L):
        pr = self.prm[l]
        pb = self.pb
        is_s = (g == "s")
        tw, w2e, sigT, Ptmp, kkn, bA, KM, rc = L["tw"], L["w2e"], L["sigT"], L["Ptmp"], L["kkn"], L["bA"], L["KM"], L["rc"]
        al, be, ka_, rt, PL = L["al"], L["be"], L["ka_"], L["rt"], L["PL"]
        BTt, KTt, VtmP = L["BTt"], L["KTt"], L["VtmP"]
        Xa, XTa, Xb, XTb, Pf, Pfin, MakT, Wbr, Wkr = (L[k] for k in ("Xa", "XTa", "Xb", "XTb", "Pf", "Pfin", "MakT", "Wbr", "Wkr"))
        Sf, Sb, Sld, RHSs, Upad, Ufull, tS, yacc = (L[k] for k in ("Sf", "Sb", "Sld", "RHSs", "Upad", "Ufull", "tS", "yacc"))
        masks = self.masks
        m_n, m_nt, m_s, m_i = ((4, 5, 0, 2) if d == 0 else (5, 4, 1, 3))
        for cp_ in range(0, nch, 2):
            cs = [c for c in (cp_, cp_ + 1) if c < nch]
            bank = pb[(cp_ // 2) % 2]
            for j, c in enumerate(cs):
                sgt = sigT[c % 2]
                bk2 = pb[2 + c % 2]
                self.mm(bk2[:, 0:128], tw[0:65, c * 128:(c + 1) * 128], w2e[0:65, d, ct * 128:(ct + 1) * 128], True, True,
                        [tw.b(), w2e.b()], [bk2.b()])
                self.act(sgt[:], bk2[:, 0:128], AF.Sigmoid, [bk2.b()], [sgt.b()])
                self.mm(bank[:, j * 256:(j + 1) * 256], sgt[:], self.tri2[:, d, :], True, True, [sgt.b(), self.tri2.b()], [bank.b()])
            n = len(cs)
            pt = Ptmp[(cp_ // 2) % 2]
            bv = bank[:, 0:n * 256].rearrange("p (j x) -> p j x", j=n)
            cols = slice(cp_ * 128, (cp_ + n) * 128)
            v3 = lambda tb_: tb_[:, cols].rearrange("p (j x) -> p j x", j=n)
            self.act(pt[:, 0, 0:n * 128].rearrange("p (j x) -> p j x", j=n), bv[:, :, 0:128], AF.Exp, [bank.b()], [pt.b()])
            self.act(pt[:, 1, 0:n * 128].rearrange("p (j x) -> p j x", j=n), bv[:, :, 0:128], AF.Exp, [bank.b()], [pt.b()], scale=-1.0)
            self.act(pt[:, 2, 0:n * 128].rearrange("p (j x) -> p j x", j=n), bv[:, :, 128:256], AF.Exp, [bank.b()], [pt.b()])
            for j, c in enumerate(cs):
                col = (127 if d == 0 else 0)
                self.cp("pool", PL[:, c:c + 1], pt[:, 0, j * 128 + col:j * 128 + col + 1], [pt.b()], [PL.b()])
            w_ = n * 128
            self.tt("dve", al[:, cols], pt[:, 2, 0:w_], kkn[:, cols], ALU.mult, [pt.b(), kkn.b()], [al.b()])
            self.tt("pool", be[:, cols], pt[:, 1, 0:w_], bA[d][:, cols], ALU.mult, [pt.b(), bA[d].b()], [be.b()])
            self.tt("dve", ka_[:, cols], pt[:, 1, 0:w_], KM[d][:, cols], ALU.mult, [pt.b(), KM[d].b()], [ka_.b()])
            self.tt("pool", rt[:, cols], pt[:, 0, 0:w_], rc[:, cols], ALU.mult, [pt.b(), rc.b()], [rt.b()])
        self.bg_step()
        for c in range(nch):
            bank = pb[c % 2]
            self.mm(bank[:, 0:128], be[:, c * 128:(c + 1) * 128], self.ident_b[:], True, True, [be.b(), self.ident_b.b()], [bank.b()])
            self.mm(bank[:, 128:256], ka_[:, c * 128:(c + 1) * 128], self.ident_b[:], True, True, [ka_.b(), self.ident_b.b()], [bank.b()])
            self.cp("act", BTt[:, c, :], bank[:, 0:128], [bank.b()], [BTt.b()])
            self.cp("dve", KTt[:, c, :], bank[:, 128:256], [bank.b()], [KTt.b()])
        self.bg_step()
        if is_s:
            self.memset("pool", Sld[:], 0.0, [Sld.b()])
            for hh in range(2):
                self.dma(Sld[hh * 64:(hh + 1) * 64, hh * 64:(hh + 1) * 64], self.st0[l, d, ct * 2 + hh], [self.st0.b()], [Sld.b()])
            self.tr(pb[7][:, 0:128], Sld[:], self.ident_f[:], [Sld.b(), self.ident_f.b()], [pb[7].b()])
            self.cp("dve", Sf[:], pb[7][:, 0:128], [pb[7].b()], [Sf.b()])
        else:
            self.memset("pool", Sf[:], 0.0, [Sf.b()])
        self.cp("act", Sb[:], Sf[:], [Sf.b()], [Sb.b()])
        order = list(range(nch)) if d == 0 else list(range(nch - 1, -1, -1))
        for b0 in range(0, nch, NB):
            batch = order[b0:b0 + NB]
            nb_ = len(batch)
            grps = [[(c, hh) for c in batch] for hh in range(2)]
            ngrp = 2
            bk = [0]

            def nbank():
                bk[0] += 1
                return pb[bk[0] % 4]
            m4 = self.masks4
            for gi in range(ngrp):
                g4 = slice(gi * 4, gi * 4 + nb_)
                gin = grps[gi]

                def ops(c, hh):
                    rows = slice(hh * 64, (hh + 1) * 64)
                    cc = slice(c * 128, (c + 1) * 128)
                    return al[rows, cc], be[rows, cc], ka_[rows, cc], rt[rows, cc]
                rb = [al.b(), be.b(), ka_.b(), rt.b()]
                for (dst, mi, sel) in ((Xa, m_n, (1, 0)), (XTa, m_nt, (0, 1)), (MakT, m_s, (2, 0)), (Wbr, m_i, (1, 3)), (Wkr, m_i, (2, 3))):
                    bank = nbank()
                    for j, (c, hh) in enumerate(gin):
                        o_ = ops(c, hh)
                        self.mm(bank[:, j * 128:(j + 1) * 128], o_[sel[0]], o_[sel[1]], True, True, rb, [bank.b()])
                    self.tt("dve", dst[:, g4, :], bank[:, 0:nb_ * 128].rearrange("p (j x) -> p j x", j=nb_), m4[:, mi, 0:nb_, :], ALU.mult,
                            [bank.b(), m4.b()], [dst.b(gi)])
                self.tt("pool", Pf[:, g4, :], Xa[:, g4, :], self.ident4[:, 0:nb_, :], ALU.add, [Xa.b(gi), self.ident4.b()], [Pf.b(gi)])
                self.cp("act", Pfin[:, g4, :], Pf[:, g4, :], [Pf.b(gi)], [Pfin.b(gi)])
            X, XT, Xn, XTn = Xa, XTa, Xb, XTb
            for lev in range(6):
                for gi in range(ngrp):
                    g4 = slice(gi * 4, gi * 4 + nb_)
                    if lev < 5:
                        bA_ = nbank()
                        for j in range(nb_):
                            ii = gi * 4 + j
                            self.mm(bA_[:, j * 128:(j + 1) * 128], XT[:, ii, :], X[:, ii, :], True, True, [X.b(gi), XT.b(gi)], [bA_.b()])
                    bB_ = nbank()
                    for j in range(nb_):
                        ii = gi * 4 + j
                        self.mm(bB_[:, j * 128:(j + 1) * 128], X[:, ii, :], XT[:, ii, :], True, True, [X.b(gi), XT.b(gi)], [bB_.b()])
                    if lev < 5:
                        self.cp("act", Xn[:, g4, :], bA_[:, 0:nb_ * 128].rearrange("p (j x) -> p j x", j=nb_), [bA_.b()], [Xn.b(gi)])
                    self.cp("dve", XTn[:, g4, :], bB_[:, 0:nb_ * 128].rearrange("p (j x) -> p j x", j=nb_), [bB_.b()], [XTn.b(gi)])
                    bC_ = nbank()
                    for j in range(nb_):
                        ii = gi * 4 + j
                        self.mm(bC_[:, j * 128:(j + 1) * 128], XTn[:, ii, :], Pfin[:, ii, :], True, True, [XTn.b(gi), Pfin.b(gi)], [bC_.b()])
                    self.tt("dve", Pf[:, g4, :], Pf[:, g4, :], bC_[:, 0:nb_ * 128].rearrange("p (j x) -> p j x", j=nb_), ALU.add, [Pf.b(gi), bC_.b()], [Pf.b(gi)])
                    self.cp("act", Pfin[:, g4, :], Pf[:, g4, :], [Pf.b(gi)], [Pfin.b(gi)])
                X, XT, Xn, XTn = Xn, XTn, X, XT
            self.bg_step()
            for bi, c in enumerate(batch):
                cc = slice(c * 128, (c + 1) * 128)
                p_rhs, p_u, p_y, p_s = pb[4], pb[5], pb[6], pb[7]
                for hh in range(2):
                    ii = hh * 4 + bi
                    rows = slice(hh * 64, (hh + 1) * 64)
                    vcols = slice(hh * 64, (hh + 1) * 64)
                    self.mm(p_rhs[:, vcols], al[rows, cc], Sb[rows, vcols], True, False, [al.b(), Sb.b()], [p_rhs.b()])
                    self.mm(p_rhs[:, vcols], MakT[:, ii, :], VtmP[:, c, hh, vcols], False, True, [MakT.b(ii // 4), VtmP.b()], [p_rhs.b()])
                self.cp("act", RHSs[:], p_rhs[:, 0:128], [p_rhs.b()], [RHSs.b()])
                for hh in range(2):
                    ii = hh * 4 + bi
                    vcols = slice(hh * 64, (hh + 1) * 64)
                    self.mm(p_u[:, vcols], Pfin[:, ii, :], RHSs[:, vcols], True, True, [Pfin.b(ii // 4), RHSs.b()], [p_u.b()])
                self.ts("dve", Ufull[:], p_u[:, 0:128], -1.0, None, ALU.mult, None, [p_u.b()], [Ufull.b()])
                for hh in range(2):
                    vcols = slice(hh * 64, (hh + 1) * 64)
                    self.cp("act" if hh else "dve", Upad[:, hh, vcols], Ufull[:, vcols], [Ufull.b()], [Upad.b()])
                self.mm(p_y[:, 0:128], Sb[:], rt[:, cc], True, False, [Sb.b(), rt.b()], [p_y.b()])
                for hh in range(2):
                    ii = hh * 4 + bi
                    self.mm(p_y[:, 0:128], Upad[:, hh, :], Wbr[:, ii, :], False, False, [Upad.b(), Wbr.b(ii // 4)], [p_y.b()])
                    self.mm(p_y[:, 0:128], VtmP[:, c, hh, :], Wkr[:, ii, :], False, hh == 1, [VtmP.b(), Wkr.b(ii // 4)], [p_y.b()])
                if d == 0:
                    self.cp("act", yacc[:, cc], p_y[:, 0:128], [p_y.b()], [yacc.b()])
                else:
                    self.tt("dve", yacc[:, cc], yacc[:, cc], p_y[:, 0:128], ALU.add, [yacc.b(), p_y.b()], [yacc.b()])
                self.mm(p_s[:, 0:128], BTt[:, c, :], Ufull[:], True, False, [BTt.b(), Ufull.b()], [p_s.b()])
                for hh in range(2):
                    self.mm(p_s[:, 0:128], KTt[:, c, :], VtmP[:, c, hh, :], False, hh == 1, [KTt.b(), VtmP.b()], [p_s.b()])
                self.stt("dve", tS[:], p_s[:, 0:128], PL[:, c:c + 1], self.bones_f[:], ALU.mult, ALU.mult,
                         [p_s.b(), PL.b(), self.bones_f.b()], [tS.b()])
                self.stt("dve", Sf[:], Sf[:], PL[:, c:c + 1], tS[:], ALU.mult, ALU.add, [Sf.b(), PL.b(), tS.b()], [Sf.b()])
                self.cp("act", Sb[:], Sf[:], [Sf.b()], [Sb.b()])
        self.bg_step()
        if not is_s:
            self.tr(pb[7][:, 0:128], Sf[:], self.ident_f[:], [Sf.b(), self.ident_f.b()], [pb[7].b()])
            self.cp("dve", Sld[:], pb[7][:, 0:128], [pb[7].b()], [Sld.b()])
            for hh in range(2):
                self.dma(self.ns[s, l, d, ct * 2 + hh], Sld[hh * 64:(hh + 1) * 64, hh * 64:(hh + 1) * 64], [Sld.b()], [self.ns.b()])

    def tiles(self):
        out = []
        for g, (Tg, nseq, Tseq) in self.groups.items():
            for c0 in range(0, Tg, 512):
                out.append((g, c0))
        return out

    def build(self):
        stages = self.cfg.get("stages", "all")
        self.declare()
        self.setup_consts()
        if stages != "nocast":
            self.cast_weights()
        if stages == "s0a":
            self.P.emit()
            return self.nc
        self.setup_params()
        if stages == "s0b":
            self.P.emit()
            return self.nc
        self.to_feature_major()
        if stages in ("s0c", "nocast"):
            self.P.emit()
            return self.nc
        X, Y = self.XT, self.XTB
        for l in range(self.DEPTH):
            for (g, c0) in self.tiles():
                if g in self.cfg.get("s1_groups", "sp"):
                    self.stage1_x(l, g, c0, X)
            if stages == "s1":
                break
            for g in self.groups:
                self.stage_fourier(l, g)
                self.stage_attn(l, g)
                if stages == "s2fa":
                    self.zero_rwkv(g)
                else:
                    self.stage_rwkv(l, g)
            if stages == "s2":
                break
            for (g, c0) in self.tiles():
                self.stage3(l, g, c0, X)
            for (g, c0) in self.tiles():
                self.stage4(l, g, c0, X, Y)
            X, Y = Y, X
        if stages in ("all", "s2fa"):
            for (g, c0) in self.tiles():
                self.stage5(g, c0, X)
        self.P.emit()
        return self.nc


def make_in_maps(inputs, cfg, ncores):
    TS, TP, NPS, DEPTH = cfg["TS"], cfg["TP"], cfg["NPS"], cfg["DEPTH"]
    consts = host_consts(TS, TP)
    f = lambda a: np.ascontiguousarray(np.asarray(a, dtype=np.float32))
    shared = {}
    for k in ("w_ada", "b_ada", "norm1_g", "norm2_g", "w_in", "w_out", "q_norm_g", "k_norm_g", "rw_conv",
              "rw_w0", "rw_w2", "rw_a0", "rw_a2", "rw_g2", "rw_kk", "rw_ka", "rw_lnx_g", "rw_lnx_b",
              "ffn_up", "ffn_conv_w", "ffn_conv_b", "ffn_down", "final_norm_g"):
        shared[k] = f(inputs[k])
    shared["rw_rk"] = f(inputs["rw_rk"]).reshape(DEPTH, 512)
    shared.update(consts)
    maps = []
    for b in range(ncores):
        m = dict(shared)
        m["xs"] = f(inputs["x_sample"][b])
        m["xp"] = f(inputs["x_prompt"][NPS * b:NPS * (b + 1)]).reshape(NPS * TP, D)
        m["ck"] = f(inputs["cache_attn_k"][b]).reshape(DEPTH, PAST, 256)
        m["cv"] = f(inputs["cache_attn_v"][b]).reshape(DEPTH, PAST, 256)
        m["st0"] = f(inputs["state_rwkv"][b])
        m["cc"] = np.stack([f(inputs["c"][b]), f(inputs["c_ctx"])], 0)
        maps.append(m)
    return maps


_CACHE = {}


def kernel(**inputs):
    xs = np.asarray(inputs["x_sample"])
    xp = np.asarray(inputs["x_prompt"])
    ncores = xs.shape[0]
    TS, TP = xs.shape[1], xp.shape[1]
    NPS = xp.shape[0] // ncores
    DEPTH = np.asarray(inputs["w_in"]).shape[0]
    DFF = np.asarray(inputs["ffn_down"]).shape[1]
    cfg = dict(TS=TS, TP=TP, NPS=NPS, DEPTH=DEPTH, DFF=DFF, stages="all")
    key = (TS, TP, NPS, DEPTH, DFF)
    if key not in _CACHE:
        kb = KB(cfg)
        _CACHE[key] = kb.build()
    nc = _CACHE[key]
    maps = make_in_maps(inputs, cfg, ncores)
    decl = set()
    for alloc in nc.allocations:
        if isinstance(alloc, mybir.MemoryLocationSet) and alloc.kind == "ExternalInput":
            decl.add(alloc.memorylocations[0].name)
    maps = [{k: v for k, v in m.items() if k in decl} for m in maps]
    res = run_bass_kernel_spmd(nc, maps, core_ids=list(range(ncores)))
    r = res.results
    y_sample = np.stack([np.asarray(r[b]["ys"], np.float32) for b in range(ncores)], 0)
    y_prompt = np.concatenate([np.asarray(r[b]["yp"], np.float32).reshape(NPS, TP, D) for b in range(ncores)], 0)
    nk = np.concatenate([np.asarray(r[b]["nk"], np.float32).reshape(NPS, DEPTH, TP, NKV, 128) for b in range(ncores)], 0)
    nv = np.concatenate([np.asarray(r[b]["nv"], np.float32).reshape(NPS, DEPTH, TP, NKV, 128) for b in range(ncores)], 0)
    ns = np.concatenate([np.asarray(r[b]["ns"], np.float32) for b in range(ncores)], 0)
    return (y_prompt, y_sample, nk, nv, ns)
```

```python
import contextlib
import math
import numpy as np
import ml_dtypes
import concourse.bass as bass
import concourse.mybir as mybir
from concourse.bass_utils import run_bass_kernel_spmd

F32 = mybir.dt.float32
BF16 = mybir.dt.bfloat16
ALU = mybir.AluOpType
AF = mybir.ActivationFunctionType
AX = mybir.AxisListType

EPOCH = 8000
RING = 8


class Buf:
    __slots__ = ("wc", "wd", "rc", "rd", "excl")

    def __init__(self, excl=False):
        self.excl = excl
        self.wc = {}
        self.wd = {}
        self.rc = {}
        self.rd = {}


class Op:
    __slots__ = ("eng", "fn", "waits", "done", "is_dma", "stage")

    def __init__(self, eng, fn, is_dma):
        self.stage = None
        self.eng = eng
        self.fn = fn
        self.waits = {}
        self.done = None
        self.is_dma = is_dma


class Prog:
    ENGS = ("pe", "act", "dve", "pool", "sp")

    def __init__(self, nc):
        self.nc = nc
        self.q = {e: [] for e in self.ENGS}
        self.cnt = {e: 0 for e in self.ENGS}
        self.dcnt = {e: 0 for e in self.ENGS}
        self.sems = {}
        self.semkeys = []
        self.last_dma = {}
        self.pending = {}

    def _semkey(self, key):
        if key not in self.sems:
            self.sems[key] = None
            self.semkeys.append(key)
        return key

    def _add_dep(self, op, dep, raw):
        if dep is None or dep is op:
            return
        if dep.eng == op.eng and not dep.is_dma and not op.is_dma:
            if op.eng == "pe":
                return
        key, val = dep.done
        if op.waits.get(key, 0) < val:
            op.waits[key] = val

    def op(self, eng, fn, reads=(), writes=(), dma=False):
        o = Op(eng, fn, dma)
        o.stage = getattr(self, "stage", None)
        if any(b.excl for b in reads):
            writes = list(writes) + [b for b in reads if b.excl and b not in writes]
            reads = [b for b in reads if not b.excl]
        self.nops = getattr(self, "nops", 0) + 1
        if self.nops > getattr(self, "limit", 1 << 60):
            return o
        pend = self.pending.pop(eng, None)
        if pend:
            for key, val in pend.items():
                if o.waits.get(key, 0) < val:
                    o.waits[key] = val
        for b in reads:
            for d in b.wc.values():
                self._add_dep(o, d, True)
            for lst in b.wd.values():
                for d in lst:
                    self._add_dep(o, d, True)
        for b in writes:
            for d in b.wc.values():
                self._add_dep(o, d, False)
            for lst in b.wd.values():
                for d in lst:
                    self._add_dep(o, d, False)
            for d in b.rc.values():
                self._add_dep(o, d, False)
            for lst in b.rd.values():
                for d in lst:
                    self._add_dep(o, d, False)
        if dma:
            k = self.dcnt[eng]
            self.dcnt[eng] += 1
            slot = k % RING
            key = self._semkey(("d", eng, slot))
            o.done = (key, 16 * (k // RING + 1))
            prev = self.last_dma.get((eng, slot))
            if prev is not None:
                self._add_dep(o, prev, True)
            self.last_dma[(eng, slot)] = o
        else:
            k = self.cnt[eng]
            self.cnt[eng] += 1
            key = self._semkey(("c", eng, k // EPOCH))
            o.done = (key, k % EPOCH + 1)
        for b in reads:
            if dma:
                lst = b.rd.setdefault(eng, [])
                lst.append(o)
                if len(lst) > RING:
                    del lst[0]
            else:
                b.rc[eng] = o
        for b in writes:
            b.rc = {}
            b.rd = {}
            if dma:
                lst = b.wd.setdefault(eng, [])
                lst.append(o)
                if len(lst) > RING:
                    del lst[0]
            else:
                b.wc[eng] = o
        self.q[eng].append(o)
        return o

    def emit(self):
        nc = self.nc
        with contextlib.ExitStack() as st:
            for key in self.semkeys:
                self.sems[key] = st.enter_context(nc.semaphore("s_" + "_".join(str(x) for x in key)))
            block = st.enter_context(nc.Block())
            engmap = {"pe": block.tensor, "act": block.scalar, "dve": block.vector,
                      "pool": block.gpsimd, "sp": block.sync}
            all_ops = self.q

            def make(ename):
                ops = all_ops[ename]

                def body(e):
                    seen = {}
                    for o in ops:
                        for key, val in o.waits.items():
                            if seen.get(key, 0) >= val:
                                continue
                            seen[key] = val
                            e.wait_ge(self.sems[key], val)
                        ins = o.fn(e)
                        if self.annotate and o.stage:
                            ins.annotate(o.stage)
                        key, val = o.done
                        ins.then_inc(self.sems[key], 16 if o.is_dma else 1)
                    if ename == "sp":
                        fin = {}
                        for en in self.ENGS:
                            for o in all_ops[en][-1:]:
                                key, val = o.done
                                fin[key] = max(fin.get(key, 0), val)
                        for o in self.last_dma.values():
                            key, val = o.done
                            fin[key] = max(fin.get(key, 0), val)
                        for key, val in fin.items():
                            if seen.get(key, 0) < val:
                                e.wait_ge(self.sems[key], val)
                return body

            for ename in self.ENGS:
                if all_ops[ename] or ename == "sp":
                    engmap[ename](make(ename))


class TB:
    def __init__(self, h, excl=False):
        self.h = h
        self.bufs = {}
        self.excl = excl

    def __getitem__(self, idx):
        return self.h[idx]

    def b(self, key=None):
        if key not in self.bufs:
            self.bufs[key] = Buf(self.excl)
        return self.bufs[key]


class TBV:
    def __init__(self, ap):
        self.ap = ap
        self.buf = Buf(True)

    def __getitem__(self, idx):
        return self.ap[idx]

    def b(self, key=None):
        return self.buf


class DT:
    def __init__(self, ap):
        self.ap = ap
        self.bufs = {}

    def __getitem__(self, idx):
        return self.ap[idx]

    def b(self, key=None):
        if key not in self.bufs:
            self.bufs[key] = Buf()
        return self.bufs[key]


D = 2048
KT = 16
NH = 8
NKV = 2
PAST = 512
IN_W = 3840
RW_IN = 1792
GRID_W = 64
NORM_EPS = 1e-6
GN_EPS = 64e-5
LWC = -math.exp(-0.5)


def host_consts(TS, TP):
    c = {}
    bf = ml_dtypes.bfloat16
    c["ident_f"] = np.eye(128, dtype=np.float32)
    c["ident_b"] = np.eye(128, dtype=np.float32).astype(bf)
    c["ones_b"] = np.ones((128, 128), np.float32).astype(bf)
    bo = np.zeros((128, 128), np.float32)
    bo[:64, :64] = 1.0
    bo[64:, 64:] = 1.0
    c["bones_b"] = bo.astype(bf)
    c["bones_f"] = bo
    prot = np.zeros((128, 128), np.float32)
    for m in range(128):
        j = m % 64
        if j < 32:
            prot[m + 32, m] = -1.0
        else:
            prot[m - 32, m] = 1.0
    c["prot_f"] = prot
    t = np.arange(TS)
    rows = (t // GRID_W).astype(np.float64)
    cols = (t % GRID_W).astype(np.float64)
    inv = 1.0 / (10000.0 ** (np.arange(0, 64, 2, dtype=np.float64) / 64.0))
    cosT = np.zeros((128, TS), np.float64)
    sinT = np.zeros((128, TS), np.float64)
    for d in range(128):
        pos = rows if d < 64 else cols
        f = inv[(d % 64) % 32]
        ang = np.float32(pos).astype(np.float32) * np.float32(f)
        cosT[d] = np.cos(ang.astype(np.float64))
        sinT[d] = np.sin(ang.astype(np.float64))
    c["cosT"] = cosT.astype(np.float32)
    c["sinT"] = sinT.astype(np.float32)
    i = np.arange(128)
    ang = 2 * np.pi * np.outer(i, i) / 128.0
    c["csC"] = (np.concatenate([np.cos(ang), np.sin(ang)], 1) / np.sqrt(128.0)).astype(np.float32).astype(bf)
    for nm, T in (("S", TS), ("P", TP)):
        i = np.arange(T)
        ang = 2 * np.pi * ((np.outer(i, i)) % T) / float(T)
        c["ct" + nm] = (np.cos(ang) / np.sqrt(T)).astype(np.float32).astype(bf)
        c["nst" + nm] = (-np.sin(ang) / np.sqrt(T)).astype(np.float32).astype(bf)
    idx = np.arange(128)
    su = (idx[:, None] < idx[None, :]).astype(np.float32)
    iu = (idx[:, None] <= idx[None, :]).astype(np.float32)
    sl = su.T.copy()
    il = iu.T.copy()
    c["masks"] = np.stack([su, sl, iu, il, -su, -sl], 0).astype(np.float32)
    c["masks4"] = np.repeat(c["masks"][:, None, :, :], 4, axis=1).transpose(2, 0, 1, 3).copy().astype(np.float32)
    c["ident4"] = np.repeat(np.eye(128, dtype=np.float32)[:, None, :], 4, axis=1).copy()
    c["tri2"] = np.stack([np.concatenate([iu, su], 1), np.concatenate([il, sl], 1)], 0).astype(np.float32) * np.float32(LWC)
    return c


class KB:
    def __init__(self, cfg):
        self.cfg = cfg
        self.TS = cfg["TS"]
        self.TP = cfg["TP"]
        self.NPS = cfg["NPS"]
        self.DFF = cfg["DFF"]
        self.FT = self.DFF // 128
        self.DEPTH = cfg["DEPTH"]
        self.dbg = set(cfg.get("dbg", ()))
        self.nc = bass.Bass("TRN2", target_bir_lowering=False)
        self.P = Prog(self.nc)
        self.P.limit = cfg.get("limit", 1 << 60)
        self.P.annotate = bool(cfg.get("annotate"))
        self.P.stage = "init"
        self.st = contextlib.ExitStack()
        self.dram = {}
        self.uid = 0
        self.bar_uid = 0
        self.groups = {"s": (self.TS, 1, self.TS), "p": (self.NPS * self.TP, self.NPS, self.TP)}

    def din(self, name, shape, dt=F32):
        t = DT(self.nc.dram_tensor(name, list(shape), dt, kind="ExternalInput").ap())
        self.dram[name] = t
        return t

    def dout(self, name, shape, dt=F32):
        t = DT(self.nc.dram_tensor(name, list(shape), dt, kind="ExternalOutput").ap())
        self.dram[name] = t
        return t

    def dscr(self, name, shape, dt):
        kind = "ExternalOutput" if name in self.dbg else "Internal"
        t = DT(self.nc.dram_tensor(name, list(shape), dt, kind=kind).ap())
        self.dram[name] = t
        return t

    def sb(self, name, shape, dt, stack=None):
        self.uid += 1
        h = (stack or self.st).enter_context(self.nc.sbuf_tensor(f"{name}_{self.uid}", list(shape), dt))
        return TB(h)

    def ps(self, name, shape, dt=F32, stack=None):
        self.uid += 1
        h = (stack or self.st).enter_context(self.nc.psum_tensor(f"{name}_{self.uid}", list(shape), dt))
        return TB(h, excl=True)

    def dma(self, out, in_, reads, writes, eng="sp", **kw):
        return self.P.op(eng, lambda e: e.dma_start(out=out, in_=in_, **kw), reads, writes, dma=True)

    def mm(self, out, lhsT, rhs, start, stop, reads, writes):
        return self.P.op("pe", lambda e: e.matmul(out, lhsT=lhsT, rhs=rhs, start=start, stop=stop), reads, writes)

    def tr(self, out, in_, ident, reads, writes):
        return self.P.op("pe", lambda e: e.transpose(out=out, in_=in_, identity=ident), reads, writes)

    def act(self, out, in_, func, reads, writes, **kw):
        return self.P.op("act", lambda e: e.activation(out=out, in_=in_, func=func, **kw), reads, writes)

    def tt(self, eng, out, in0, in1, op, reads, writes):
        return self.P.op(eng, lambda e: e.tensor_tensor(out=out, in0=in0, in1=in1, op=op), reads, writes)

    def ts(self, eng, out, in0, s1, s2, op0, op1, reads, writes):
        if s2 is None:
            return self.P.op(eng, lambda e: e.tensor_scalar(out=out, in0=in0, scalar1=s1, scalar2=None, op0=op0), reads, writes)
        return self.P.op(eng, lambda e: e.tensor_scalar(out=out, in0=in0, scalar1=s1, scalar2=s2, op0=op0, op1=op1), reads, writes)

    def stt(self, eng, out, in0, scalar, in1, op0, op1, reads, writes):
        return self.P.op(eng, lambda e: e.scalar_tensor_tensor(out=out, in0=in0, scalar=scalar, in1=in1, op0=op0, op1=op1), reads, writes)

    def cp(self, eng, out, in_, reads, writes):
        if eng == "act":
            return self.P.op("act", lambda e: e.copy(out=out, in_=in_), reads, writes)
        return self.P.op(eng, lambda e: e.tensor_copy(out=out, in_=in_), reads, writes)

    def memset(self, eng, ap, val, writes):
        return self.P.op(eng, lambda e: e.memset(ap, val), [], writes)

    def recip(self, out, in_, reads, writes):
        return self.P.op("dve", lambda e: e.reciprocal(out=out, in_=in_), reads, writes)

    def barrier(self):
        P = self.P
        fin = {}
        for en in P.ENGS:
            for o in P.q[en][-1:]:
                key, val = o.done
                fin[key] = max(fin.get(key, 0), val)
        for o in P.last_dma.values():
            key, val = o.done
            fin[key] = max(fin.get(key, 0), val)
        for en in P.ENGS:
            d = P.pending.setdefault(en, {})
            for key, val in fin.items():
                d[key] = max(d.get(key, 0), val)

    def declare(self):
        TS, TP, NPS, DEPTH, DFF = self.TS, self.TP, self.NPS, self.DEPTH, self.DFF
        TPG = NPS * TP
        di = self.din
        self.xs = di("xs", [TS, D])
        self.xp = di("xp", [TPG, D])
        self.ck = di("ck", [DEPTH, PAST, 256])
        self.cv = di("cv", [DEPTH, PAST, 256])
        self.st0 = di("st0", [DEPTH, 2, 8, 64, 64])
        self.cc = di("cc", [2, D])
        self.w_ada = di("w_ada", [DEPTH, D, 6 * D])
        self.b_ada = di("b_ada", [DEPTH, 6 * D])
        self.norm1_g = di("norm1_g", [DEPTH, D])
        self.norm2_g = di("norm2_g", [DEPTH, D])
        self.w_in = di("w_in", [DEPTH, D, IN_W])
        self.w_out = di("w_out", [DEPTH, D, D])
        self.q_norm_g = di("q_norm_g", [DEPTH, 128])
        self.k_norm_g = di("k_norm_g", [DEPTH, 128])
        self.rw_conv = di("rw_conv", [DEPTH, 3, RW_IN])
        self.rw_w0 = di("rw_w0", [DEPTH, 2, 512])
        self.rw_w2 = di("rw_w2", [DEPTH, 2, 64, 512])
        self.rw_a0 = di("rw_a0", [DEPTH, 2, 512])
        self.rw_a2 = di("rw_a2", [DEPTH, 2, 64, 512])
        self.rw_g2 = di("rw_g2", [DEPTH, 128, 512])
        self.rw_kk = di("rw_kk", [DEPTH, 512])
        self.rw_ka = di("rw_ka", [DEPTH, 512])
        self.rw_rk = di("rw_rk", [DEPTH, 512])
        self.rw_lnx_g = di("rw_lnx_g", [DEPTH, 512])
        self.rw_lnx_b = di("rw_lnx_b", [DEPTH, 512])
        self.ffn_up = di("ffn_up", [DEPTH, D, 2 * DFF])
        self.ffn_conv_w = di("ffn_conv_w", [DEPTH, 3, 2 * DFF])
        self.ffn_conv_b = di("ffn_conv_b", [DEPTH, 2 * DFF])
        self.ffn_down = di("ffn_down", [DEPTH, DFF, D])
        self.final_norm_g = di("final_norm_g", [D])
        self.c_ident_f = di("ident_f", [128, 128])
        self.c_ident_b = di("ident_b", [128, 128], BF16)
        self.c_ones_b = di("ones_b", [128, 128], BF16)
        self.c_bones_b = di("bones_b", [128, 128], BF16)
        self.c_bones_f = di("bones_f", [128, 128])
        self.c_prot_f = di("prot_f", [128, 128])
        self.c_cosT = di("cosT", [128, TS])
        self.c_sinT = di("sinT", [128, TS])
        self.c_csC = di("csC", [128, 256], BF16)
        self.c_ct = {"s": di("ctS", [TS, TS], BF16), "p": di("ctP", [TP, TP], BF16)}
        self.c_nst = {"s": di("nstS", [TS, TS], BF16), "p": di("nstP", [TP, TP], BF16)}
        self.c_masks = di("masks", [6, 128, 128])
        self.c_tri2 = di("tri2", [2, 128, 256])
        self.c_masks4 = di("masks4", [128, 6, 4, 128])
        self.c_ident4 = di("ident4", [128, 4, 128])
        self.ys = self.dout("ys", [TS, D])
        self.yp = self.dout("yp", [TPG, D])
        self.nk = self.dout("nk", [NPS, DEPTH, TP, 256])
        self.nv = self.dout("nv", [NPS, DEPTH, TP, 256])
        self.ns = self.dout("ns", [NPS, DEPTH, 2, 8, 64, 64])
        ds = self.dscr
        FT = self.FT
        self.Win_t = ds("Win_t", [DEPTH, 30, 128, KT, 128], BF16)
        self.Wv_t = ds("Wv_t", [DEPTH, 128, KT, 256], BF16)
        self.Wout_t = ds("Wout_t", [DEPTH, 16, 128, KT, 128], BF16)
        self.Wup_t = ds("Wup_t", [DEPTH, 2 * FT, 128, KT, 128], BF16)
        self.Wdn_t = ds("Wdn_t", [DEPTH, 16, 128, FT, 128], BF16)
        self.mod_rows = [ds(f"modrows{l}", [2, 6 * D], F32) for l in range(DEPTH)]
        self.XT = {}
        self.XTB = {}
        self.QT = {}
        self.KTs = {}
        self.Vs = {}
        self.RW = {}
        self.AB = {}
        self.MIXT = {}
        for g, (Tg, nseq, Tseq) in self.groups.items():
            self.XT[g] = ds("XT_" + g, [KT, 128, Tg], F32)
            self.XTB[g] = ds("XTB_" + g, [KT, 128, Tg], F32)
            self.QT[g] = ds("QT_" + g, [NH, 128, Tg], BF16)
            self.KTs[g] = ds("KT_" + g, [NKV, 128, Tg], BF16)
            self.Vs[g] = ds("V_" + g, [Tg, 256], BF16)
            self.RW[g] = ds("RW_" + g, [RW_IN, Tg], F32)
            self.AB[g] = ds("AB_" + g, [Tg, 1024], BF16)
            self.MIXT[g] = ds("MIXT_" + g, [KT, 128, Tg], BF16)

    def load_const(self, name, src, shape, dt):
        t = self.sb(name, shape, dt)
        self.dma(t[:], src.ap, [src.b()], [t.b()])
        return t

    def setup_consts(self):
        self.P.stage = "consts"
        TS = self.TS
        self.ident_f = self.load_const("ident_f", self.c_ident_f, [128, 128], F32)
        self.ident_b = self.load_const("ident_b", self.c_ident_b, [128, 128], BF16)
        self.ones_b = self.load_const("ones_b", self.c_ones_b, [128, 128], BF16)
        self.bones_b = self.load_const("bones_b", self.c_bones_b, [128, 128], BF16)
        self.bones_f = self.load_const("bones_f", self.c_bones_f, [128, 128], F32)
        self.prot_f = self.load_const("prot_f", self.c_prot_f, [128, 128], F32)
        self.csC = self.load_const("csC", self.c_csC, [128, 256], BF16)
        self.masks = self.sb("masks", [128, 6, 128], F32)
        self.dma(self.masks[:], self.c_masks.ap.rearrange("m p c -> p m c"), [self.c_masks.b()], [self.masks.b()])
        self.tri2 = self.sb("tri2", [128, 2, 256], F32)
        self.dma(self.tri2[:], self.c_tri2.ap.rearrange("m p c -> p m c"), [self.c_tri2.b()], [self.tri2.b()])
        self.masks4 = self.load_const("masks4", self.c_masks4, [128, 6, 4, 128], F32)
        self.ident4 = self.load_const("ident4", self.c_ident4, [128, 4, 128], F32)
        self.eps6 = self.sb("eps6", [128, 1], F32)
        self.memset("pool", self.eps6[:], NORM_EPS, [self.eps6.b()])
        self.eps12 = self.sb("eps12", [128, 1], F32)
        self.memset("pool", self.eps12[:], 1e-12, [self.eps12.b()])
        self.epsgn = self.sb("epsgn", [128, 1], F32)
        self.memset("pool", self.epsgn[:], GN_EPS, [self.epsgn.b()])
        self.pw = [self.ps(f"pw{i}", [128, 1024], F32) for i in range(4)]
        self.pb = [TBV(self.pw[i // 2].h[:, (i % 2) * 512:(i % 2) * 512 + 512]) for i in range(8)]

    def cast_jobs(self, l, which):
        FT = self.FT
        jobs = []
        if which == "in":
            for c0 in range(0, IN_W, 512):
                wd = min(512, IN_W - c0)
                dsts = []
                for j in range(wd // 128):
                    ot = c0 // 128 + j
                    if ot == 14:
                        dsts.append((self.Wv_t[l], self.Wv_t.b(l), j * 128, 256))
                    elif ot != 15:
                        dsts.append((self.Win_t[l, ot], self.Win_t.b((l, ot)), j * 128, 128))
                jobs.append((self.w_in, l, c0, wd, KT, dsts))
        else:
            for c0 in range(0, D, 512):
                dsts = [(self.Wout_t[l, c0 // 128 + j], self.Wout_t.b((l, c0 // 128 + j)), j * 128, 128) for j in range(4)]
                jobs.append((self.w_out, l, c0, 512, KT, dsts))
            for c0 in range(0, 2 * self.DFF, 512):
                wd = min(512, 2 * self.DFF - c0)
                dsts = [(self.Wup_t[l, c0 // 128 + j], self.Wup_t.b((l, c0 // 128 + j)), j * 128, 128) for j in range(wd // 128)]
                jobs.append((self.ffn_up, l, c0, wd, KT, dsts))
            cw = 128 if FT > 16 else 512
            for c0 in range(0, D, cw):
                dsts = [(self.Wdn_t[l, c0 // 128 + j], self.Wdn_t.b((l, c0 // 128 + j)), j * 128, 128) for j in range(cw // 128)]
                jobs.append((self.ffn_down, l, c0, cw, FT, dsts))
        return jobs

    def bg_begin(self, stk, jobs, engs=("pool",)):
        nel = max(KT * 512, self.FT * (128 if self.FT > 16 else 512))
        self.bg = dict(jobs=list(jobs), i=0, pend=None, engs=engs, n=0,
                       wf=[self.sb("wcf", [128, nel], F32, stk) for _ in range(2)],
                       wb=[self.sb("wcb", [128, nel], BF16, stk) for _ in range(2)])

    def bg_step(self):
        bg = getattr(self, "bg", None)
        if bg is None:
            return
        if bg["pend"] is not None:
            (src, l, c0, wd, ktn, dsts), f, b = bg["pend"]
            fv = f[:, 0:ktn * wd].rearrange("p (kt c) -> p kt c", c=wd)
            off = 0
            for (dap, dbuf, co, w) in dsts:
                view = b[:, off:off + ktn * w].rearrange("p (kt c) -> p kt c", c=w)
                self.cp(bg["engs"][bg["n"] % len(bg["engs"])], view, fv[:, :, co:co + w], [f.b()], [b.b()])
                bg["n"] += 1
                self.dma(dap, view, [b.b()], [dbuf])
                off += ktn * w
            bg["pend"] = None
        if bg["i"] < len(bg["jobs"]):
            job = bg["jobs"][bg["i"]]
            f = bg["wf"][bg["i"] % 2]
            b = bg["wb"][bg["i"] % 2]
            (src, l, c0, wd, ktn, dsts) = job
            fv = f[:, 0:ktn * wd].rearrange("p (kt c) -> p kt c", c=wd)
            self.dma(fv, src[l, :, c0:c0 + wd].rearrange("(kt p) c -> p kt c", p=128), [src.b()], [f.b()])
            bg["pend"] = (job, f, b)
            bg["i"] += 1

    def bg_end(self):
        bg = getattr(self, "bg", None)
        if bg is None:
            return
        while bg["pend"] is not None or bg["i"] < len(bg["jobs"]):
            self.bg_step()
        self.bg = None

    def cast_weights(self):
        self.P.stage = "cast"
        sched = self.cfg.get("bg_cast", True)
        jobs = self.cast_jobs(0, "in")
        self.bg_sched = {}
        if sched:
            r0 = self.cast_jobs(0, "rest")
            h = len(r0) // 2
            self.bg_sched[("attn", 0, "s")] = r0[:h]
            self.bg_sched[("rwkv", 0, "p")] = r0[h:]
            for l in range(1, self.DEPTH):
                self.bg_sched[("rwkv", l - 1, "p")] = self.bg_sched.get(("rwkv", l - 1, "p"), []) + self.cast_jobs(l, "in")
                rl = self.cast_jobs(l, "rest")
                h = len(rl) // 2
                self.bg_sched[("attn", l, "s")] = rl[:h]
                self.bg_sched[("rwkv", l, "p")] = rl[h:]
        else:
            jobs += self.cast_jobs(0, "rest")
            for l in range(1, self.DEPTH):
                jobs += self.cast_jobs(l, "in") + self.cast_jobs(l, "rest")
        with contextlib.ExitStack() as stk:
            self.bg_begin(stk, jobs, engs=("act", "dve", "pool", "dve"))
            self.bg_end()
            self.barrier()

    def load_pp(self, dst, dst_b, src_rows, n, src_b):
        k = self._pp_i = getattr(self, "_pp_i", 0) + 1
        stg = self.pp_stage[k % 2]
        bank = self.pb[6 + (k % 2)]
        self.dma(stg[0:n, :], src_rows, [src_b], [stg.b()])
        self.tr(bank[:, 0:n], stg[0:n, :], self.ident_f[0:n, 0:n], [stg.b(), self.ident_f.b()], [bank.b()])
        self.cp("dve", dst, bank[:, 0:n], [bank.b()], [dst_b])

    def to_feature_major(self):
        self.P.stage = "tofm"
        with contextlib.ExitStack() as stk:
            xin = [self.sb("xin", [128, D], F32, stk) for _ in range(2)]
            xo = [self.sb("xo", [128, KT, 128], F32, stk) for _ in range(2)]
            i = 0
            for g, src in (("s", self.xs), ("p", self.xp)):
                Tg = self.groups[g][0]
                for blk in range(Tg // 128):
                    a = xin[i % 2]
                    o = xo[i % 2]
                    self.dma(a[:], src[blk * 128:(blk + 1) * 128, :], [src.b()], [a.b()])
                    for q in range(4):
                        bank = self.pb[(i * 4 + q) % 4]
                        for j in range(4):
                            kt = q * 4 + j
                            self.tr(bank[:, j * 128:(j + 1) * 128], a[:, kt * 128:(kt + 1) * 128], self.ident_f[:],
                                    [a.b(), self.ident_f.b()], [bank.b()])
                        eng = "act" if q % 2 else "dve"
                        self.cp(eng, o[:, q * 4:(q + 1) * 4, :], bank[:].rearrange("p (j t) -> p j t", j=4), [bank.b()], [o.b()])
                    self.dma(self.XT[g][:, :, blk * 128:(blk + 1) * 128].rearrange("kt p t -> p kt t"), o[:],
                             [o.b()], [self.XT[g].b(blk // 4)])
                    i += 1
            self.barrier()

    def setup_params(self):
        self.P.stage = "params"
        DEPTH, FT = self.DEPTH, self.FT
        self.pp_stage = [self.sb("ppstg", [128, 128], F32) for _ in range(2)]
        self.prm = []
        ccT = self.sb("ccT", [128, 32], F32)
        self.load_pp(ccT[:], ccT.b(), self.cc.ap.rearrange("v (kt p) -> (v kt) p", p=128), 32, self.cc.b())
        sc = self.sb("sc", [128, 32], F32)
        self.act(sc[:], ccT[:], AF.Silu, [ccT.b()], [sc.b()])
        fng = self.sb("fng", [128, KT], F32)
        self.load_pp(fng[:], fng.b(), self.final_norm_g.ap.rearrange("(kt p) -> kt p", p=128), KT, self.final_norm_g.b())
        self.fng = fng
        for l in range(DEPTH):
            pr = {}
            def vec(name, src, n, rows):
                t = self.sb(name, [128, n], F32)
                self.load_pp(t[:], t.b(), rows, n, src.b())
                return t
            pr["n1g"] = vec("n1g", self.norm1_g, KT, self.norm1_g[l].rearrange("(kt p) -> kt p", p=128))
            pr["n2g"] = vec("n2g", self.norm2_g, KT, self.norm2_g[l].rearrange("(kt p) -> kt p", p=128))
            pr["qng"] = vec("qng", self.q_norm_g, 1, self.q_norm_g[l:l + 1, :])
            pr["kng"] = vec("kng", self.k_norm_g, 1, self.k_norm_g[l:l + 1, :])
            pr["rwconv"] = vec("rwconv", self.rw_conv, 42, self.rw_conv[l].rearrange("j (t p) -> (j t) p", p=128))
            for nm, src in (("kk", self.rw_kk), ("ka", self.rw_ka), ("rk", self.rw_rk), ("lng", self.rw_lnx_g), ("lnb", self.rw_lnx_b)):
                pr[nm] = vec(nm, src, 4, src[l].rearrange("(t p) -> t p", p=128))
            pr["a0"] = vec("a0", self.rw_a0, 8, self.rw_a0[l].rearrange("d (t p) -> (d t) p", p=128))
            c1 = self.sb("c1", [128, 4], F32)
            self.ts("dve", c1[:], pr["ka"][:], -1.0, 1.0, ALU.mult, ALU.add, [pr["ka"].b()], [c1.b()])
            pr["c1"] = c1
            nft = 2 * FT
            fcw = self.sb("fcw", [128, 3, nft], F32)
            for j in range(3):
                self.load_pp(fcw[:, j, :], fcw.b(), self.ffn_conv_w[l, j].rearrange("(t p) -> t p", p=128), nft, self.ffn_conv_w.b())
            pr["fcw"] = fcw
            nfcw = self.sb("nfcw", [128, 3, nft], F32)
            self.ts("dve", nfcw[:], fcw[:], -1.0, None, ALU.mult, None, [fcw.b()], [nfcw.b()])
            pr["nfcw"] = nfcw
            pr["fcb"] = vec("fcb", self.ffn_conv_b, nft, self.ffn_conv_b[l].rearrange("(t p) -> t p", p=128))
            bada = vec("bada", self.b_ada, 96, self.b_ada[l].rearrange("(t p) -> t p", p=128))
            mods = [self.sb(f"mod{v}", [128, 96], F32) for v in range(2)]
            modr = self.mod_rows[l]
            with contextlib.ExitStack() as stk:
                wa = [self.sb("wada", [128, KT, 512], F32, stk) for _ in range(2)]
                rowt = [self.sb("modrow", [2, 512], F32, stk) for _ in range(2)]
                for blk in range(24):
                    w = wa[blk % 2]
                    acc = self.pb[4 + blk % 2]
                    self.dma(w[:], self.w_ada[l, :, blk * 512:(blk + 1) * 512].rearrange("(kt p) c -> p kt c", p=128),
                             [self.w_ada.b()], [w.b()])
                    for kt in range(KT):
                        self.mm(acc[0:2, :], sc[:, kt:32:16], w[:, kt, :], kt == 0, kt == KT - 1, [w.b(), sc.b()], [acc.b()])
                    rt_ = rowt[blk % 2]
                    self.cp("dve" if blk % 2 else "act", rt_[:], acc[0:2, :], [acc.b()], [rt_.b()])
                    self.dma(modr[:, blk * 512:(blk + 1) * 512], rt_[:], [rt_.b()], [modr.b()])
                for v in range(2):
                    m = mods[v]
                    self.load_pp(m[:], m.b(), modr[v].rearrange("(t p) -> t p", p=128), 96, modr.b())
                    self.tt("dve", m[:], m[:], bada[:], ALU.add, [m.b(), bada.b()], [m.b()])
                self.barrier()
            pr["mod"] = mods
            for v in range(2):
                for nm, gname, j in (("gs1", "n1g", 1), ("gs2", "n2g", 4)):
                    t = self.sb(f"{nm}_{v}", [128, KT], F32)
                    self.stt("dve", t[:], mods[v][:, j * 16:(j + 1) * 16], 1.0, pr[gname][:], ALU.add, ALU.mult,
                             [mods[v].b(), pr[gname].b()], [t.b()])
                    pr[f"{nm}_{v}"] = t
            self.prm.append(pr)

    def norm_mod(self, xt, chunks, gs, sh_ap_fn, hT, stk, ss_banks):
        Wtot = chunks[-1][1]
        sq = [self.sb("nsq", [128, Wtot], BF16, stk) for _ in range(2)]
        rstd = self.sb("nrstd", [128, Wtot], F32, stk)
        tmp = [self.sb("ntmp", [128, Wtot], F32, stk) for _ in range(2)]
        for kt in range(KT):
            s = sq[kt % 2]
            self.act(s[:], xt[:, kt, :], AF.Square, [xt.b()], [s.b()])
            for ci, (c0, c1) in enumerate(chunks):
                bk = ss_banks[ci]
                self.mm(bk[:, 0:c1 - c0], self.ones_b[:], s[:, c0:c1], kt == 0, kt == KT - 1,
                        [s.b(), self.ones_b.b()], [bk.b()])
        for ci, (c0, c1) in enumerate(chunks):
            bk = ss_banks[ci]
            self.act(rstd[:, c0:c1], bk[:, 0:c1 - c0], AF.Sqrt, [bk.b(), self.eps6.b()], [rstd.b()],
                     scale=1.0 / D, bias=self.eps6[:, 0:1])
        self.recip(rstd[:], rstd[:], [rstd.b()], [rstd.b()])
        for kt in range(KT):
            t = tmp[kt % 2]
            self.tt("pool" if kt % 2 else "dve", t[:], xt[:, kt, :], rstd[:], ALU.mult, [xt.b(), rstd.b()], [t.b()])
            self.act(hT[:, kt, :], t[:], AF.Identity, [t.b(), gs.b()], [hT.b()],
                     scale=gs[:, kt:kt + 1], bias=sh_ap_fn(kt))

    def zero_rwkv(self, g):
        Tg = self.groups[g][0]
        with contextlib.ExitStack() as stk:
            z = self.sb("zz", [128, Tg], BF16, stk)
            self.memset("pool", z[:], 0.0, [z.b()])
            for r in range(12, 16):
                self.dma(self.MIXT[g][r], z[:], [z.b()], [self.MIXT[g].b(i) for i in range((Tg + 511) // 512)])
            self.barrier()

    def stage1_x(self, l, g, c0, X):
        self._X1 = X
        return self.stage1(l, g, c0)

    def stage1(self, l, g, c0):
        self.P.stage = f"s1_{l}_{g}"
        W = 512
        v = 0 if g == "s" else 1
        pr = self.prm[l]
        mod = pr["mod"][v]
        ti = c0 // 512
        pb = self.pb
        is_s = (g == "s")
        with contextlib.ExitStack() as stk:
            xt = self.sb("xt", [128, KT, W], F32, stk)
            hT = self.sb("hT", [128, KT, W], BF16, stk)
            self.dma(xt[:], self._X1[g][:, :, c0:c0 + W].rearrange("kt p t -> p kt t"), [self._X1[g].b(ti)], [xt.b()])
            self.norm_mod(xt, [(0, W)], pr[f"gs1_{v}"], lambda kt: mod[:, kt:kt + 1], hT, stk, [pb[7]])
            wts = [self.sb("wt", [128, KT, 128], BF16, stk) for _ in range(3)]
            wv = self.sb("wv", [128, KT, 256], BF16, stk)
            uT = self.sb("uT", [128, W], BF16, stk)
            abt = self.sb("abt", [128, 4, 256], BF16, stk)
            sqh = self.sb("sqh", [128, W], BF16, stk)
            rq = self.sb("rq", [128, W], F32, stk)
            qn = [self.sb("qn", [128, W], F32, stk) for _ in range(2)]
            t1 = self.sb("t1", [128, W], F32, stk)
            t2 = self.sb("t2", [128, W], F32, stk)
            qr = [self.sb("qr", [128, W], BF16, stk) for _ in range(2)]
            ktok = self.sb("ktok", [128, 4, 128], F32, stk)
            vt = self.sb("vt", [128, 4, 256], BF16, stk)
            vtf = self.sb("vtf", [128, 4, 256], F32, stk)
            rwt = [self.sb("rwt", [128, W], F32, stk) for _ in range(2)]
            if is_s:
                cosb = self.sb("cosb", [128, W], F32, stk)
                sinb = self.sb("sinb", [128, W], F32, stk)
                self.dma(cosb[:], self.c_cosT[:, c0:c0 + W], [self.c_cosT.b()], [cosb.b()])
                self.dma(sinb[:], self.c_sinT[:, c0:c0 + W], [self.c_sinT.b()], [sinb.b()])

            order = list(range(14)) + list(range(16, 30))

            def load_w(oi):
                ot = order[oi]
                w = wts[oi % 3]
                self.dma(w[:], self.Win_t[l, ot], [self.Win_t.b((l, ot))], [w.b()])
            load_w(0)
            load_w(1)
            self.dma(wv[:], self.Wv_t[l], [self.Wv_t.b(l)], [wv.b()])
            for oi, ot in enumerate(order):
                if oi + 2 < len(order):
                    load_w(oi + 2)
                w = wts[oi % 3]
                acc = pb[oi % 3]
                for kt in range(KT):
                    self.mm(acc[:], w[:, kt, :], hT[:, kt, :], kt == 0, kt == KT - 1, [w.b(), hT.b()], [acc.b()])
                skip = self.cfg.get("s1_skip", "")
                if ("f" in skip and ot < 4) or ("q" in skip and 4 <= ot < 14) or ("r" in skip and ot >= 16):
                    continue
                if ot < 4:
                    self.cp("act", uT[:], acc[:], [acc.b()], [uT.b()])
                    for sub in range(4):
                        reg = pb[5][:, (sub % 2) * 256:(sub % 2) * 256 + 256]
                        self.mm(reg, uT[:, sub * 128:(sub + 1) * 128], self.csC[:], True, True,
                                [uT.b(), self.csC.b()], [pb[5].b()])
                        self.cp("dve", abt[:, sub, :], reg, [pb[5].b()], [abt.b()])
                    self.dma(self.AB[g][c0:c0 + W, ot * 256:(ot + 1) * 256].rearrange("(s p) c -> p s c", p=128), abt[:],
                             [abt.b()], [self.AB[g].b(ti)])
                elif ot < 14:
                    isk = ot >= 12
                    hd = ot - 12 if isk else ot - 4
                    gq = pr["kng"] if isk else pr["qng"]
                    q_n = qn[oi % 2]
                    q_r = qr[oi % 2]
                    self.act(sqh[:], acc[:], AF.Square, [acc.b()], [sqh.b()])
                    self.mm(pb[3][:], self.ones_b[:], sqh[:], True, True, [sqh.b(), self.ones_b.b()], [pb[3].b()])
                    self.act(rq[:], pb[3][:], AF.Sqrt, [pb[3].b(), self.eps6.b()], [rq.b()], scale=1.0 / 128, bias=self.eps6[:, 0:1])
                    self.recip(rq[:], rq[:], [rq.b()], [rq.b()])
                    self.stt("dve", q_n[:], acc[:], gq[:, 0:1], rq[:], ALU.mult, ALU.mult, [acc.b(), gq.b(), rq.b()], [q_n.b()])
                    if isk and not is_s:
                        for sub in range(4):
                            self.tr(pb[5][:, sub * 128:(sub + 1) * 128], q_n[:, sub * 128:(sub + 1) * 128], self.ident_f[:],
                                    [q_n.b(), self.ident_f.b()], [pb[5].b()])
                        self.cp("dve", ktok[:], pb[5][:].rearrange("p (s c) -> p s c", s=4), [pb[5].b()], [ktok.b()])
                        for sub in range(4):
                            tok = sub * 128
                            sq_, off = tok // self.TP, tok % self.TP
                            self.dma(self.nk[sq_, l, off:off + 128, hd * 128:(hd + 1) * 128], ktok[:, sub, :],
                                     [ktok.b()], [self.nk.b()])
                    if is_s:
                        self.mm(pb[4][:], self.prot_f[:], q_n[:], True, True, [self.prot_f.b(), q_n.b()], [pb[4].b()])
                        self.tt("pool", t1[:], q_n[:], cosb[:], ALU.mult, [q_n.b(), cosb.b()], [t1.b()])
                        self.tt("dve", t2[:], pb[4][:], sinb[:], ALU.mult, [pb[4].b(), sinb.b()], [t2.b()])
                        self.tt("pool", q_r[:], t1[:], t2[:], ALU.add, [t1.b(), t2.b()], [q_r.b()])
                    else:
                        self.cp("pool", q_r[:], q_n[:], [q_n.b()], [q_r.b()])
                    dst = self.KTs[g] if isk else self.QT[g]
                    self.dma(dst[hd, :, c0:c0 + W], q_r[:], [q_r.b()], [dst.b(ti)])
                else:
                    r = rwt[oi % 2]
                    self.cp("act" if oi % 2 else "dve", r[:], acc[:], [acc.b()], [r.b()])
                    self.dma(self.RW[g][(ot - 16) * 128:(ot - 15) * 128, c0:c0 + W], r[:], [r.b()], [self.RW[g].b(ti)])
                if oi == self.cfg.get("v_at", 13) and "v" not in skip:
                    for sub in range(4):
                        reg = pb[6][:, (sub % 2) * 256:(sub % 2) * 256 + 256]
                        for kt in range(KT):
                            self.mm(reg, hT[:, kt, sub * 128:(sub + 1) * 128], wv[:, kt, :], kt == 0, kt == KT - 1,
                                    [hT.b(), wv.b()], [pb[6].b()])
                        self.cp("act", vt[:, sub, :], reg, [pb[6].b()], [vt.b()])
                        if not is_s:
                            self.cp(self.cfg.get("vtf_eng", "dve"), vtf[:, sub, :], reg, [pb[6].b()], [vtf.b()])
                    self.dma(self.Vs[g][c0:c0 + W, :].rearrange("(s p) c -> p s c", p=128), vt[:], [vt.b()], [self.Vs[g].b(ti)])
                    if not is_s:
                        for sub in range(4):
                            tok = sub * 128
                            sq_, off = tok // self.TP, tok % self.TP
                            self.dma(self.nv[sq_, l, off:off + 128, :], vtf[:, sub, :], [vtf.b()], [self.nv.b()])
            self.barrier()

    def stage_fourier(self, l, g):
        self.P.stage = f"four_{l}_{g}"
        Tg, nseq, Tseq = self.groups[g]
        nch = Tseq // 128
        TW = min(512, Tseq)
        pb = self.pb
        allab = [self.AB[g].b(i) for i in range((Tg + 511) // 512)]
        with contextlib.ExitStack() as stk:
            ab = self.sb("fab", [128, nch, 1024], BF16, stk)
            ctb = [self.sb("fct", [128, nch, TW], BF16, stk) for _ in range(2)]
            nsb = [self.sb("fns", [128, nch, TW], BF16, stk) for _ in range(2)]
            fo = [self.sb("ffo", [128, TW], BF16, stk) for _ in range(2)]
            n = 0
            for s in range(nseq):
                self.dma(ab[:], self.AB[g][s * Tseq:(s + 1) * Tseq, :].rearrange("(c p) x -> p c x", p=128), allab, [ab.b()])
                for ti, t0 in enumerate(range(0, Tseq, TW)):
                    cb = ctb[ti % 2]
                    sbb = nsb[ti % 2]
                    self.dma(cb[:], self.c_ct[g][:, t0:t0 + TW].rearrange("(c p) t -> p c t", p=128), [self.c_ct[g].b()], [cb.b()])
                    self.dma(sbb[:], self.c_nst[g][:, t0:t0 + TW].rearrange("(c p) t -> p c t", p=128), [self.c_nst[g].b()], [sbb.b()])
                    for grp in range(4):
                        acc = pb[n % 4]
                        for c in range(nch):
                            self.mm(acc[:, 0:TW], ab[:, c, grp * 256:grp * 256 + 128], cb[:, c, :], c == 0, False,
                                    [ab.b(), cb.b()], [acc.b()])
                            self.mm(acc[:, 0:TW], ab[:, c, grp * 256 + 128:grp * 256 + 256], sbb[:, c, :], False, c == nch - 1,
                                    [ab.b(), sbb.b()], [acc.b()])
                        f = fo[n % 2]
                        self.cp("act" if n % 2 else "dve", f[:], acc[:, 0:TW], [acc.b()], [f.b()])
                        c0 = s * Tseq + t0
                        self.dma(self.MIXT[g][grp, :, c0:c0 + TW], f[:], [f.b()], [self.MIXT[g].b(c0 // 512)])
                        n += 1
            self.barrier()

    def stage_attn(self, l, g):
        self.P.stage = f"attn_{l}_{g}"
        Tg, nseq, Tseq = self.groups[g]
        is_s = (g == "s")
        Stot = Tseq + (PAST if is_s else 0)
        nck = Stot // 128
        QW = min(512, Tseq)
        pb = self.pb
        ntile = (Tg + 511) // 512
        scale = 128.0 ** -0.5
        with contextlib.ExitStack() as stk:
            kT = self.sb("akT", [128, NKV, Stot], BF16, stk)
            vv = self.sb("avv", [128, nck, 256], BF16, stk)
            qT = [self.sb("aqT", [128, Tseq], BF16, stk) for _ in range(2)]
            pT = [self.sb("apT", [128, QW], BF16, stk) for _ in range(3)]
            rden = self.sb("arden", [128, QW], F32, stk)
            ob = [self.sb("aob", [128, QW], BF16, stk) for _ in range(2)]
            if is_s:
                ckf = self.sb("ackf", [128, PAST // 128, 256], F32, stk)
                cvf = self.sb("acvf", [128, PAST // 128, 256], F32, stk)
            nq = 0
            bgj = self.bg_sched.pop(("attn", l, g), None)
            if bgj:
                self.bg_begin(stk, bgj, engs=("dve", "pool", "dve"))
            for s in range(nseq):
                c0s = s * Tseq
                for kv in range(NKV):
                    self.dma(kT[:, kv, 0:Tseq], self.KTs[g][kv, :, c0s:c0s + Tseq], [self.KTs[g].b(i) for i in range(ntile)], [kT.b()])
                self.dma(vv[:, 0:Tseq // 128, :], self.Vs[g][c0s:c0s + Tseq, :].rearrange("(c p) x -> p c x", p=128),
                         [self.Vs[g].b(i) for i in range(ntile)], [vv.b()])
                if is_s:
                    self.dma(ckf[:], self.ck[l].rearrange("(c p) x -> p c x", p=128), [self.ck.b()], [ckf.b()])
                    self.dma(cvf[:], self.cv[l].rearrange("(c p) x -> p c x", p=128), [self.cv.b()], [cvf.b()])
                    self.cp("pool", vv[:, Tseq // 128:nck, :], cvf[:], [cvf.b()], [vv.b()])
                    for kv in range(NKV):
                        bank = pb[kv]
                        for c in range(PAST // 128):
                            self.tr(bank[:, c * 128:(c + 1) * 128], ckf[:, c, kv * 128:(kv + 1) * 128], self.ident_f[:],
                                    [ckf.b(), self.ident_f.b()], [bank.b()])
                        self.cp("act", kT[:, kv, Tseq:Stot], bank[:, 0:PAST], [bank.b()], [kT.b()])
                for h in range(NH):
                    kv = h // (NH // NKV)
                    q = qT[h % 2]
                    self.dma(q[:], self.QT[g][h, :, c0s:c0s + Tseq], [self.QT[g].b(i) for i in range(ntile)], [q.b()])
                    for q0 in range(0, Tseq, QW):
                        self.bg_step()
                        oacc = pb[4 + nq % 2]
                        dacc = pb[6 + nq % 2]
                        def score(c):
                            sT = pb[c % 3]
                            p = pT[c % 3]
                            self.mm(sT[:, 0:QW], kT[:, kv, c * 128:(c + 1) * 128], q[:, q0:q0 + QW], True, True,
                                    [kT.b(), q.b()], [sT.b()])
                            self.act(p[:], sT[:, 0:QW], AF.Exp, [sT.b()], [p.b()], scale=scale)
                        pipe = self.cfg.get("attn_pipe", True)
                        if pipe:
                            score(0)
                        for c in range(nck):
                            if not pipe:
                                score(c)
                            elif c + 1 < nck:
                                score(c + 1)
                            p = pT[c % 3]
                            self.mm(oacc[:, 0:QW], vv[:, c, kv * 128:(kv + 1) * 128], p[:], c == 0, c == nck - 1,
                                    [vv.b(), p.b()], [oacc.b()])
                            self.mm(dacc[:, 0:QW], self.ones_b[:], p[:], c == 0, c == nck - 1,
                                    [self.ones_b.b(), p.b()], [dacc.b()])
                        self.recip(rden[:], dacc[:, 0:QW], [dacc.b()], [rden.b()])
                        o = ob[nq % 2]
                        self.tt("dve", o[:], oacc[:, 0:QW], rden[:], ALU.mult, [oacc.b(), rden.b()], [o.b()])
                        cc0 = c0s + q0
                        self.dma(self.MIXT[g][4 + h, :, cc0:cc0 + QW], o[:], [o.b()], [self.MIXT[g].b(cc0 // 512)])
                        nq += 1
            self.bg_end()
            self.barrier()

    def stage3(self, l, g, c0, X):
        self.P.stage = f"s3_{l}_{g}"
        W = 512
        v = 0 if g == "s" else 1
        pr = self.prm[l]
        mod = pr["mod"][v]
        ti = c0 // 512
        pb = self.pb
        with contextlib.ExitStack() as stk:
            xt = self.sb("xt3", [128, KT, W], F32, stk)
            mx = self.sb("mx3", [128, KT, W], BF16, stk)
            wts = [self.sb("wt3", [128, KT, 128], BF16, stk) for _ in range(3)]
            self.dma(xt[:], X[g][:, :, c0:c0 + W].rearrange("kt p t -> p kt t"), [X[g].b(ti)], [xt.b()])
            self.dma(mx[:], self.MIXT[g][:, :, c0:c0 + W].rearrange("kt p t -> p kt t"), [self.MIXT[g].b(ti)], [mx.b()])

            def load_w(ot):
                w = wts[ot % 3]
                self.dma(w[:], self.Wout_t[l, ot], [self.Wout_t.b((l, ot))], [w.b()])
            load_w(0)
            load_w(1)
            for ot in range(KT):
                if ot + 2 < KT:
                    load_w(ot + 2)
                w = wts[ot % 3]
                acc = pb[ot % 4]
                for kt in range(KT):
                    self.mm(acc[:], w[:, kt, :], mx[:, kt, :], kt == 0, kt == KT - 1, [w.b(), mx.b()], [acc.b()])
                self.stt("dve", xt[:, ot, :], acc[:], mod[:, 32 + ot:33 + ot], xt[:, ot, :], ALU.mult, ALU.add,
                         [acc.b(), xt.b(), xt.b(ot)], [xt.b(ot)])
            self.dma(X[g][:, :, c0:c0 + W].rearrange("kt p t -> p kt t"), xt[:], [xt.b(ot) for ot in range(KT)] + [xt.b()], [X[g].b(ti)])
            self.barrier()

    def stage4(self, l, g, c0, X, Y):
        self.P.stage = f"s4_{l}_{g}"
        W = 512
        Tg, nseq, Tseq = self.groups[g]
        v = 0 if g == "s" else 1
        pr = self.prm[l]
        mod = pr["mod"][v]
        ti = c0 // 512
        nti = (Tg + 511) // 512
        pb, pw = self.pb, self.pw
        FT = self.FT
        fcw, fcb, nfcw = pr["fcw"], pr["fcb"], pr["nfcw"]
        with contextlib.ExitStack() as stk:
            xt = self.sb("xt4", [128, KT, W + 2], F32, stk)
            hT = self.sb("hT4", [128, KT, W + 2], BF16, stk)
            actT = self.sb("act4", [128, FT, W], BF16, stk)
            wts = [self.sb("wt4", [128, KT, 128], BF16, stk) for _ in range(3)]
            wdn = [self.sb("wd4", [128, FT, 128], BF16, stk) for _ in range(2)]
            ua = self.sb("ua4", [128, W], F32, stk)
            ug = self.sb("ug4", [128, W], F32, stk)
            sg = self.sb("sg4", [128, W], F32, stk)
            self.dma(xt[:, :, 0:W], X[g][:, :, c0:c0 + W].rearrange("kt p t -> p kt t"), [X[g].b(ti)], [xt.b()])
            lzero = (c0 % Tseq == 0)
            rzero = ((c0 + W) % Tseq == 0)
            if lzero:
                self.memset("pool", xt[:, :, W:W + 1], 0.0, [xt.b()])
            else:
                self.dma(xt[:, :, W:W + 1], X[g][:, :, c0 - 1:c0].rearrange("kt p t -> p kt t"), [X[g].b(ti - 1)], [xt.b()],
                         allow_slow_non_contiguous=True)
            if rzero:
                self.memset("pool", xt[:, :, W + 1:W + 2], 0.0, [xt.b()])
            else:
                self.dma(xt[:, :, W + 1:W + 2], X[g][:, :, c0 + W:c0 + W + 1].rearrange("kt p t -> p kt t"), [X[g].b(ti + 1)], [xt.b()],
                         allow_slow_non_contiguous=True)
            self.norm_mod(xt, [(0, W), (W, W + 2)], pr[f"gs2_{v}"], lambda kt: mod[:, 48 + kt:49 + kt], hT, stk, [pb[7], pb[6]])
            if lzero:
                self.memset("pool", hT[:, :, W:W + 1], 0.0, [hT.b()])
            if rzero:
                self.memset("pool", hT[:, :, W + 1:W + 2], 0.0, [hT.b()])
            inner = [b for b in range(Tseq, W, Tseq)] if Tseq < W else []

            def load_w(j):
                i, half = j // 2, j % 2
                w = wts[j % 3]
                ot = i + half * FT
                self.dma(w[:], self.Wup_t[l, ot], [self.Wup_t.b((l, ot))], [w.b()])
            load_w(0)
            load_w(1)
            for j in range(2 * FT):
                if j + 2 < 2 * FT:
                    load_w(j + 2)
                i, half = j // 2, j % 2
                ot = i + half * FT
                w = wts[j % 3]
                wide = j % 3
                A, Bk = pb[2 * wide], pb[2 * wide + 1]
                for kt in range(KT):
                    self.mm(A[:], w[:, kt, :], hT[:, kt, 0:W], kt == 0, kt == KT - 1, [w.b(), hT.b()], [A.b()])
                for kt in range(KT):
                    self.mm(Bk[:, 0:2], w[:, kt, :], hT[:, kt, W:W + 2], kt == 0, kt == KT - 1, [w.b(), hT.b()], [Bk.b()])
                u = ug if half else ua
                w0 = fcw[:, 0, ot:ot + 1]
                w1 = fcw[:, 1, ot:ot + 1]
                w2 = fcw[:, 2, ot:ot + 1]
                self.act(u[:], A[:], AF.Identity, [A.b(), fcw.b(), fcb.b()], [u.b()], scale=w1, bias=fcb[:, ot:ot + 1])
                self.stt("dve", u[:, 1:W], A[:, 0:W - 1], w0, u[:, 1:W], ALU.mult, ALU.add, [A.b(), u.b()], [u.b()])
                self.stt("dve", u[:, 0:W - 1], A[:, 1:W], w2, u[:, 0:W - 1], ALU.mult, ALU.add, [A.b(), u.b()], [u.b()])
                self.stt("dve", u[:, 0:1], Bk[:, 0:1], w0, u[:, 0:1], ALU.mult, ALU.add, [Bk.b(), u.b()], [u.b()])
                self.stt("dve", u[:, W - 1:W], Bk[:, 1:2], w2, u[:, W - 1:W], ALU.mult, ALU.add, [Bk.b(), u.b()], [u.b()])
                for bnd in inner:
                    self.stt("dve", u[:, bnd - 1:bnd], A[:, bnd:bnd + 1], nfcw[:, 2, ot:ot + 1], u[:, bnd - 1:bnd], ALU.mult, ALU.add,
                             [A.b(), u.b(), nfcw.b()], [u.b()])
                    self.stt("dve", u[:, bnd:bnd + 1], A[:, bnd - 1:bnd], nfcw[:, 0, ot:ot + 1], u[:, bnd:bnd + 1], ALU.mult, ALU.add,
                             [A.b(), u.b(), nfcw.b()], [u.b()])
                if half:
                    self.act(sg[:], ug[:], AF.Silu, [ug.b()], [sg.b()])
                    self.tt("pool", actT[:, i, :], sg[:], ua[:], ALU.mult, [sg.b(), ua.b()], [actT.b()])
            self.dma(wdn[0][:], self.Wdn_t[l, 0], [self.Wdn_t.b((l, 0))], [wdn[0].b()])
            for ot in range(KT):
                if ot + 1 < KT:
                    self.dma(wdn[(ot + 1) % 2][:], self.Wdn_t[l, ot + 1], [self.Wdn_t.b((l, ot + 1))], [wdn[(ot + 1) % 2].b()])
                w = wdn[ot % 2]
                acc = pb[6 + ot % 2]
                for kt in range(FT):
                    self.mm(acc[:], w[:, kt, :], actT[:, kt, :], kt == 0, kt == FT - 1, [w.b(), actT.b()], [acc.b()])
                self.stt("dve", xt[:, ot, 0:W], acc[:], mod[:, 80 + ot:81 + ot], xt[:, ot, 0:W], ALU.mult, ALU.add,
                         [acc.b(), xt.b(), xt.b(ot)], [xt.b(ot)])
            self.dma(Y[g][:, :, c0:c0 + W].rearrange("kt p t -> p kt t"), xt[:, :, 0:W], [xt.b(ot) for ot in range(KT)] + [xt.b()], [Y[g].b(ti)])
            self.barrier()

    def stage5(self, g, c0, X):
        self.P.stage = f"s5_{g}"
        W = 512
        ti = c0 // 512
        pb = self.pb
        out = self.ys if g == "s" else self.yp
        with contextlib.ExitStack() as stk:
            xt = self.sb("xt5", [128, KT, W], F32, stk)
            xn = self.sb("xn5", [128, KT, W], F32, stk)
            sq = [self.sb("sq5", [128, W], BF16, stk) for _ in range(2)]
            rstd = self.sb("rstd5", [128, W], F32, stk)
            yt = [self.sb("yt5", [128, D], F32, stk) for _ in range(2)]
            self.dma(xt[:], X[g][:, :, c0:c0 + W].rearrange("kt p t -> p kt t"), [X[g].b(ti)], [xt.b()])
            for kt in range(KT):
                s = sq[kt % 2]
                self.act(s[:], xt[:, kt, :], AF.Square, [xt.b()], [s.b()])
                self.mm(pb[7][:], self.ones_b[:], s[:], kt == 0, kt == KT - 1, [s.b(), self.ones_b.b()], [pb[7].b()])
            self.act(rstd[:], pb[7][:], AF.Sqrt, [pb[7].b(), self.eps6.b()], [rstd.b()], scale=1.0 / D, bias=self.eps6[:, 0:1])
            self.recip(rstd[:], rstd[:], [rstd.b()], [rstd.b()])
            for kt in range(KT):
                self.stt("dve", xn[:, kt, :], xt[:, kt, :], self.fng[:, kt:kt + 1], rstd[:], ALU.mult, ALU.mult,
                         [xt.b(), rstd.b(), self.fng.b()], [xn.b()])
            for sub in range(4):
                y = yt[sub % 2]
                for q in range(4):
                    bank = pb[q]
                    for j in range(4):
                        kt = q * 4 + j
                        self.tr(bank[:, j * 128:(j + 1) * 128], xn[:, kt, sub * 128:(sub + 1) * 128], self.ident_f[:],
                                [xn.b(), self.ident_f.b()], [bank.b()])
                    self.cp("act" if q % 2 else "dve", y[:, q * 512:(q + 1) * 512], bank[:], [bank.b()], [y.b()])
                self.dma(out[c0 + sub * 128:c0 + (sub + 1) * 128, :], y[:], [y.b()], [out.b()])
            self.barrier()

    def stage_rwkv(self, l, g):
        self.P.stage = f"rwkv_{l}_{g}"
        Tg, nseq, Tseq = self.groups[g]
        is_s = (g == "s")
        T = Tseq
        nch = T // 128
        NB = 4
        pr = self.prm[l]
        pb = self.pb
        ntile = (Tg + 511) // 512
        rwall = [self.RW[g].b(i) for i in range(ntile)]
        conv = pr["rwconv"]
        CW = min(512, T)
        with contextlib.ExitStack() as stk:
            sbt = lambda name, shape, dt: self.sb(name, shape, dt, stk)
            w2e = sbt("w2e", [65, 2, 512], F32)
            a2f = sbt("a2f", [128, 2, 512], F32)
            g2f = sbt("g2f", [128, 512], F32)
            self.dma(w2e[0:64, :, :], self.rw_w2[l].rearrange("d k c -> k d c"), [self.rw_w2.b()], [w2e.b()])
            self.dma(w2e[64:65, :, :], self.rw_w0[l:l + 1], [self.rw_w0.b()], [w2e.b()])
            self.dma(a2f[64:128, :, :], self.rw_a2[l].rearrange("d k c -> k d c"), [self.rw_a2.b()], [a2f.b()])
            self.dma(g2f[:], self.rw_g2[l], [self.rw_g2.b()], [g2f.b()])
            t12 = sbt("t12", [128, T], F32)
            tw = sbt("tw", [65, T], F32)
            sg = sbt("sg", [128, T], F32)
            rc = sbt("rc", [128, T], F32)
            kkn = sbt("kkn", [128, T], F32)
            bA = [sbt("bA", [128, T], BF16) for _ in range(2)]
            KM = [sbt("KM", [128, T], BF16) for _ in range(2)]
            bon = sbt("bon", [128, T], F32)
            yacc = sbt("yacc", [128, T], F32)
            s1 = sbt("scr1", [128, T + 2], F32)
            s2 = sbt("scr2", [128, T], F32)
            s3 = sbt("scr3", [128, T], F32)
            al = sbt("al", [128, T], BF16)
            be = sbt("be", [128, T], BF16)
            ka_ = sbt("kap", [128, T], BF16)
            rt = sbt("rt", [128, T], BF16)
            VtmP = sbt("VtmP", [128, nch, 2, 128], BF16)
            BTt = sbt("BTt", [128, nch, 128], BF16)
            KTt = sbt("KTt", [128, nch, 128], BF16)
            PL = sbt("PL", [128, nch], F32)
            sigT = [sbt("sigT", [128, 128], F32) for _ in range(2)]
            Ptmp = [sbt("Ptmp", [128, 3, 256], F32) for _ in range(2)]
            NI = NB * 2
            Xa = sbt("Xa", [128, NI, 128], BF16)
            XTa = sbt("XTa", [128, NI, 128], BF16)
            Xb = sbt("Xb", [128, NI, 128], BF16)
            XTb = sbt("XTb", [128, NI, 128], BF16)
            Pf = sbt("Pf", [128, NI, 128], F32)
            Pfin = sbt("Pfin", [128, NI, 128], BF16)
            MakT = sbt("MakT", [128, NI, 128], BF16)
            Wbr = sbt("Wbr", [128, NI, 128], BF16)
            Wkr = sbt("Wkr", [128, NI, 128], BF16)
            Sf = sbt("Sf", [128, 128], F32)
            Sb = sbt("Sb", [128, 128], BF16)
            Sld = sbt("Sld", [128, 128], F32)
            RHSs = sbt("RHSs", [128, 128], BF16)
            Upad = sbt("Upad", [128, 2, 128], BF16)
            Ufull = sbt("Ufull", [128, 128], BF16)
            tS = sbt("tS", [128, 128], F32)
            sq = sbt("sqr", [128, CW], BF16)
            rstd = sbt("rstdr", [128, CW], F32)
            ob = [sbt("obr", [128, CW], BF16) for _ in range(2)]

            bgj = self.bg_sched.pop(("rwkv", l, g), None)
            if bgj:
                self.bg_begin(stk, bgj, engs=("act", "dve", "pool", "act", "dve"))
            self.memset("pool", VtmP[:], 0.0, [VtmP.b()])
            self.memset("pool", Upad[:], 0.0, [Upad.b()])
            self.memset("pool", tw[64:65, :], 1.0, [tw.b()])
            self.memset("pool", s1[:, 0:1], 0.0, [s1.b()])
            self.memset("pool", s1[:, T + 1:T + 2], 0.0, [s1.b()])

            def conv_tile(tile_idx, c0s, dst):
                self.dma(s1[:, 1:T + 1], self.RW[g][tile_idx * 128:(tile_idx + 1) * 128, c0s:c0s + T], rwall, [s1.b()])
                w = lambda j: conv[:, j * 14 + tile_idx:j * 14 + tile_idx + 1]
                self.act(dst[:], s1[:, 1:T + 1], AF.Copy, [s1.b(), conv.b()], [dst.b()], scale=w(1))
                self.stt("dve", dst[:], s1[:, 0:T], w(0), dst[:], ALU.mult, ALU.add, [s1.b(), dst.b()], [dst.b()])
                self.stt("dve", dst[:], s1[:, 2:T + 2], w(2), dst[:], ALU.mult, ALU.add, [s1.b(), dst.b()], [dst.b()])

            for s in range(nseq):
                c0s = s * T
                conv_tile(12, c0s, t12)
                self.act(tw[0:64, :], t12[0:64, :], AF.Tanh, [t12.b()], [tw.b()])
                conv_tile(13, c0s, sg)
                self.act(sg[:], sg[:], AF.Sigmoid, [sg.b()], [sg.b()])
                for ct in range(4):
                    self.bg_step()
                    conv_tile(ct, c0s, rc)
                    conv_tile(4 + ct, c0s, s2)
                    self.ts("pool", s3[:], s2[:], pr["kk"][:, ct:ct + 1], None, ALU.mult, None, [s2.b(), pr["kk"].b()], [s3.b()])
                    for x0 in range(0, T, CW):
                        bank = pb[(x0 // CW) % 2]
                        self.act(sq[:], s3[:, x0:x0 + CW], AF.Square, [s3.b()], [sq.b()])
                        self.mm(bank[:, 0:CW], self.bones_b[:], sq[:], True, True, [self.bones_b.b(), sq.b()], [bank.b()])
                        self.act(rstd[:], bank[:, 0:CW], AF.Sqrt, [bank.b(), self.eps12.b()], [rstd.b()], bias=self.eps12[:, 0:1])
                        self.recip(rstd[:], rstd[:], [rstd.b()], [rstd.b()])
                        self.tt("dve", kkn[:, x0:x0 + CW], s3[:, x0:x0 + CW], rstd[:], ALU.mult, [s3.b(), rstd.b()], [kkn.b()])
                    for d in range(2):
                        for x0 in range(0, T, CW):
                            bank = pb[2 + (x0 // CW) % 2]
                            self.mm(bank[:, 0:CW], a2f[64:128, d, ct * 128:(ct + 1) * 128], t12[64:128, x0:x0 + CW], True, True,
                                    [a2f.b(), t12.b()], [bank.b()])
                            self.act(s1[:, 1 + x0:1 + x0 + CW], bank[:, 0:CW], AF.Sigmoid, [bank.b(), pr["a0"].b()], [s1.b()],
                                     bias=pr["a0"][:, d * 4 + ct:d * 4 + ct + 1])
                        self.tt("pool", bA[d][:], kkn[:], s1[:, 1:T + 1], ALU.mult, [kkn.b(), s1.b()], [bA[d].b()])
                        self.ts("dve", s1[:, 1:T + 1], s1[:, 1:T + 1], pr["ka"][:, ct:ct + 1], pr["c1"][:, ct:ct + 1], ALU.mult, ALU.add,
                                [s1.b(), pr["ka"].b(), pr["c1"].b()], [s1.b()])
                        self.tt("dve", KM[d][:], s1[:, 1:T + 1], s2[:], ALU.mult, [s1.b(), s2.b()], [KM[d].b()])
                    conv_tile(8 + ct, c0s, s3)
                    self.tt("pool", s2[:], KM[0][:], KM[1][:], ALU.add, [KM[0].b(), KM[1].b()], [s2.b()])
                    self.stt("dve", s2[:], rc[:], pr["rk"][:, ct:ct + 1], s2[:], ALU.mult, ALU.mult, [rc.b(), s2.b(), pr["rk"].b()], [s2.b()])
                    for x0 in range(0, T, CW):
                        bank = pb[(x0 // CW) % 2]
                        self.mm(bank[:, 0:CW], self.bones_f[:], s2[:, x0:x0 + CW], True, True, [self.bones_f.b(), s2.b()], [bank.b()])
                        self.tt("dve", bon[:, x0:x0 + CW], bank[:, 0:CW], s3[:, x0:x0 + CW], ALU.mult, [bank.b(), s3.b()], [bon.b()])
                    for c in range(nch):
                        bank = pb[2 + c % 2]
                        self.tr(bank[:, 0:128], s3[:, c * 128:(c + 1) * 128], self.ident_f[:], [s3.b(), self.ident_f.b()], [bank.b()])
                        for hh in range(2):
                            self.cp("act" if hh else "dve", VtmP[:, c, hh, hh * 64:(hh + 1) * 64], bank[:, hh * 64:(hh + 1) * 64],
                                    [bank.b()], [VtmP.b()])
                    for d in range(2):
                        self.rwkv_dir(l, g, s, ct, d, T, nch, NB, stk, locals())
                    for x0 in range(0, T, CW):
                        bank = pb[(x0 // CW) % 2]
                        bank2 = pb[2 + (x0 // CW) % 2]
                        bank3 = pb[4 + (x0 // CW) % 2]
                        ys_ = yacc[:, x0:x0 + CW]
                        self.mm(bank[:, 0:CW], self.bones_f[:], ys_, True, True, [self.bones_f.b(), yacc.b()], [bank.b()])
                        self.stt("dve", s2[:, x0:x0 + CW], bank[:, 0:CW], -1.0 / 64, ys_, ALU.mult, ALU.add, [bank.b(), yacc.b()], [s2.b()])
                        self.act(sq[:], s2[:, x0:x0 + CW], AF.Square, [s2.b()], [sq.b()])
                        self.mm(bank2[:, 0:CW], self.bones_b[:], sq[:], True, True, [self.bones_b.b(), sq.b()], [bank2.b()])
                        self.act(rstd[:], bank2[:, 0:CW], AF.Sqrt, [bank2.b(), self.epsgn.b()], [rstd.b()], scale=1.0 / 64, bias=self.epsgn[:, 0:1])
                        self.recip(rstd[:], rstd[:], [rstd.b()], [rstd.b()])
                        self.tt("dve", s2[:, x0:x0 + CW], s2[:, x0:x0 + CW], rstd[:], ALU.mult, [s2.b(), rstd.b()], [s2.b()])
                        self.ts("dve", s2[:, x0:x0 + CW], s2[:, x0:x0 + CW], pr["lng"][:, ct:ct + 1], pr["lnb"][:, ct:ct + 1], ALU.mult, ALU.add,
                                [s2.b(), pr["lng"].b(), pr["lnb"].b()], [s2.b()])
                        self.tt("pool", s2[:, x0:x0 + CW], s2[:, x0:x0 + CW], bon[:, x0:x0 + CW], ALU.add, [s2.b(), bon.b()], [s2.b()])
                        self.mm(bank3[:, 0:CW], g2f[:, ct * 128:(ct + 1) * 128], sg[:, x0:x0 + CW], True, True, [g2f.b(), sg.b()], [bank3.b()])
                        o = ob[(x0 // CW) % 2]
                        self.tt("dve", o[:], bank3[:, 0:CW], s2[:, x0:x0 + CW], ALU.mult, [bank3.b(), s2.b()], [o.b()])
                        cc0 = c0s + x0
                        self.dma(self.MIXT[g][12 + ct, :, cc0:cc0 + CW], o[:], [o.b()], [self.MIXT[g].b(cc0 // 512)])
            self.bg_end()
            self.barrier()

    def rwkv_dir(self, l, g, s, ct, d, T, nch, NB, stk, L):
        pr = self.prm[l]
        pb = self.pb
        is_s = (g == "s")
        tw, w2e, sigT, Ptmp, kkn, bA, KM, rc = L["tw"], L["w2e"], L["sigT"], L["Ptmp"], L["kkn"], L["bA"], L["KM"], L["rc"]
        al, be, ka_, rt, PL = L["al"], L["be"], L["ka_"], L["rt"], L["PL"]
        BTt, KTt, VtmP = L["BTt"], L["KTt"], L["VtmP"]
        Xa, XTa, Xb, XTb, Pf, Pfin, MakT, Wbr, Wkr = (L[k] for k in ("Xa", "XTa", "Xb", "XTb", "Pf", "Pfin", "MakT", "Wbr", "Wkr"))
        Sf, Sb, Sld, RHSs, Upad, Ufull, tS, yacc = (L[k] for k in ("Sf", "Sb", "Sld", "RHSs", "Upad", "Ufull", "tS", "yacc"))
        masks = self.masks
        m_n, m_nt, m_s, m_i = ((4, 5, 0, 2) if d == 0 else (5, 4, 1, 3))
        for cp_ in range(0, nch, 2):
            cs = [c for c in (cp_, cp_ + 1) if c < nch]
            bank = pb[(cp_ // 2) % 2]
            for j, c in enumerate(cs):
                sgt = sigT[c % 2]
                bk2 = pb[2 + c % 2]
                self.mm(bk2[:, 0:128], tw[0:65, c * 128:(c + 1) * 128], w2e[0:65, d, ct * 128:(ct + 1) * 128], True, True,
                        [tw.b(), w2e.b()], [bk2.b()])
                self.act(sgt[:], bk2[:, 0:128], AF.Sigmoid, [bk2.b()], [sgt.b()])
                self.mm(bank[:, j * 256:(j + 1) * 256], sgt[:], self.tri2[:, d, :], True, True, [sgt.b(), self.tri2.b()], [bank.b()])
            n = len(cs)
            pt = Ptmp[(cp_ // 2) % 2]
            bv = bank[:, 0:n * 256].rearrange("p (j x) -> p j x", j=n)
            cols = slice(cp_ * 128, (cp_ + n) * 128)
            v3 = lambda tb_: tb_[:, cols].rearrange("p (j x) -> p j x", j=n)
            self.act(pt[:, 0, 0:n * 128].rearrange("p (j x) -> p j x", j=n), bv[:, :, 0:128], AF.Exp, [bank.b()], [pt.b()])
            self.act(pt[:, 1, 0:n * 128].rearrange("p (j x) -> p j x", j=n), bv[:, :, 0:128], AF.Exp, [bank.b()], [pt.b()], scale=-1.0)
            self.act(pt[:, 2, 0:n * 128].rearrange("p (j x) -> p j x", j=n), bv[:, :, 128:256], AF.Exp, [bank.b()], [pt.b()])
            for j, c in enumerate(cs):
                col = (127 if d == 0 else 0)
                self.cp("pool", PL[:, c:c + 1], pt[:, 0, j * 128 + col:j * 128 + col + 1], [pt.b()], [PL.b()])
            w_ = n * 128
            self.tt("dve", al[:, cols], pt[:, 2, 0:w_], kkn[:, cols], ALU.mult, [pt.b(), kkn.b()], [al.b()])
            self.tt("pool", be[:, cols], pt[:, 1, 0:w_], bA[d][:, cols], ALU.mult, [pt.b(), bA[d].b()], [be.b()])
            self.tt("dve", ka_[:, cols], pt[:, 1, 0:w_], KM[d][:, cols], ALU.mult, [pt.b(), KM[d].b()], [ka_.b()])
            self.tt("pool", rt[:, cols], pt[:, 0, 0:w_], rc[:, cols], ALU.mult, [pt.b(), rc.b()], [rt.b()])
        self.bg_step()
        for c in range(nch):
            bank = pb[c % 2]
            self.mm(bank[:, 0:128], be[:, c * 128:(c + 1) * 128], self.ident_b[:], True, True, [be.b(), self.ident_b.b()], [bank.b()])
            self.mm(bank[:, 128:256], ka_[:, c * 128:(c + 1) * 128], self.ident_b[:], True, True, [ka_.b(), self.ident_b.b()], [bank.b()])
            self.cp("act", BTt[:, c, :], bank[:, 0:128], [bank.b()], [BTt.b()])
            self.cp("dve", KTt[:, c, :], bank[:, 128:256], [bank.b()], [KTt.b()])
        self.bg_step()
        if is_s:
            self.memset("pool", Sld[:], 0.0, [Sld.b()])
            for hh in range(2):
                self.dma(Sld[hh * 64:(hh + 1) * 64, hh * 64:(hh + 1) * 64], self.st0[l, d, ct * 2 + hh], [self.st0.b()], [Sld.b()])
            self.tr(pb[7][:, 0:128], Sld[:], self.ident_f[:], [Sld.b(), self.ident_f.b()], [pb[7].b()])
            self.cp("dve", Sf[:], pb[7][:, 0:128], [pb[7].b()], [Sf.b()])
        else:
            self.memset("pool", Sf[:], 0.0, [Sf.b()])
        self.cp("act", Sb[:], Sf[:], [Sf.b()], [Sb.b()])
        order = list(range(nch)) if d == 0 else list(range(nch - 1, -1, -1))
        for b0 in range(0, nch, NB):
            batch = order[b0:b0 + NB]
            nb_ = len(batch)
            grps = [[(c, hh) for c in batch] for hh in range(2)]
            ngrp = 2
            bk = [0]

            def nbank():
                bk[0] += 1
                return pb[bk[0] % 4]
            m4 = self.masks4
            for gi in range(ngrp):
                g4 = slice(gi * 4, gi * 4 + nb_)
                gin = grps[gi]

                def ops(c, hh):
                    rows = slice(hh * 64, (hh + 1) * 64)
                    cc = slice(c * 128, (c + 1) * 128)
                    return al[rows, cc], be[rows, cc], ka_[rows, cc], rt[rows, cc]
                rb = [al.b(), be.b(), ka_.b(), rt.b()]
                for (dst, mi, sel) in ((Xa, m_n, (1, 0)), (XTa, m_nt, (0, 1)), (MakT, m_s, (2, 0)), (Wbr, m_i, (1, 3)), (Wkr, m_i, (2, 3))):
                    bank = nbank()
                    for j, (c, hh) in enumerate(gin):
                        o_ = ops(c, hh)
                        self.mm(bank[:, j * 128:(j + 1) * 128], o_[sel[0]], o_[sel[1]], True, True, rb, [bank.b()])
                    self.tt("dve", dst[:, g4, :], bank[:, 0:nb_ * 128].rearrange("p (j x) -> p j x", j=nb_), m4[:, mi, 0:nb_, :], ALU.mult,
                            [bank.b(), m4.b()], [dst.b(gi)])
                self.tt("pool", Pf[:, g4, :], Xa[:, g4, :], self.ident4[:, 0:nb_, :], ALU.add, [Xa.b(gi), self.ident4.b()], [Pf.b(gi)])
                self.cp("act", Pfin[:, g4, :], Pf[:, g4, :], [Pf.b(gi)], [Pfin.b(gi)])
            X, XT, Xn, XTn = Xa, XTa, Xb, XTb
            for lev in range(6):
                for gi in range(ngrp):
                    g4 = slice(gi * 4, gi * 4 + nb_)
                    if lev < 5:
                        bA_ = nbank()
                        for j in range(nb_):
                            ii = gi * 4 + j
                            self.mm(bA_[:, j * 128:(j + 1) * 128], XT[:, ii, :], X[:, ii, :], True, True, [X.b(gi), XT.b(gi)], [bA_.b()])
                    bB_ = nbank()
                    for j in range(nb_):
                        ii = gi * 4 + j
                        self.mm(bB_[:, j * 128:(j + 1) * 128], X[:, ii, :], XT[:, ii, :], True, True, [X.b(gi), XT.b(gi)], [bB_.b()])
                    if lev < 5:
                        self.cp("act", Xn[:, g4, :], bA_[:, 0:nb_ * 128].rearrange("p (j x) -> p j x", j=nb_), [bA_.b()], [Xn.b(gi)])
                    self.cp("dve", XTn[:, g4, :], bB_[:, 0:nb_ * 128].rearrange("p (j x) -> p j x", j=nb_), [bB_.b()], [XTn.b(gi)])
                    bC_ = nbank()
                    for j in range(nb_):
                        ii = gi * 4 + j
                        self.mm(bC_[:, j * 128:(j + 1) * 128], XTn[:, ii, :], Pfin[:, ii, :], True, True, [XTn.b(gi), Pfin.b(gi)], [bC_.b()])
                    self.tt("dve", Pf[:, g4, :], Pf[:, g4, :], bC_[:, 0:nb_ * 128].rearrange("p (j x) -> p j x", j=nb_), ALU.add, [Pf.b(gi), bC_.b()], [Pf.b(gi)])
                    self.cp("act", Pfin[:, g4, :], Pf[:, g4, :], [Pf.b(gi)], [Pfin.b(gi)])
                X, XT, Xn, XTn = Xn, XTn, X, XT
            self.bg_step()
            for bi, c in enumerate(batch):
                cc = slice(c * 128, (c + 1) * 128)
                p_rhs, p_u, p_y, p_s = pb[4], pb[5], pb[6], pb[7]
                for hh in range(2):
                    ii = hh * 4 + bi
                    rows = slice(hh * 64, (hh + 1) * 64)
                    vcols = slice(hh * 64, (hh + 1) * 64)
                    self.mm(p_rhs[:, vcols], al[rows, cc], Sb[rows, vcols], True, False, [al.b(), Sb.b()], [p_rhs.b()])
                    self.mm(p_rhs[:, vcols], MakT[:, ii, :], VtmP[:, c, hh, vcols], False, True, [MakT.b(ii // 4), VtmP.b()], [p_rhs.b()])
                self.cp("act", RHSs[:], p_rhs[:, 0:128], [p_rhs.b()], [RHSs.b()])
                for hh in range(2):
                    ii = hh * 4 + bi
                    vcols = slice(hh * 64, (hh + 1) * 64)
                    self.mm(p_u[:, vcols], Pfin[:, ii, :], RHSs[:, vcols], True, True, [Pfin.b(ii // 4), RHSs.b()], [p_u.b()])
                self.ts("dve", Ufull[:], p_u[:, 0:128], -1.0, None, ALU.mult, None, [p_u.b()], [Ufull.b()])
                for hh in range(2):
                    vcols = slice(hh * 64, (hh + 1) * 64)
                    self.cp("act" if hh else "dve", Upad[:, hh, vcols], Ufull[:, vcols], [Ufull.b()], [Upad.b()])
                self.mm(p_y[:, 0:128], Sb[:], rt[:, cc], True, False, [Sb.b(), rt.b()], [p_y.b()])
                for hh in range(2):
                    ii = hh * 4 + bi
                    self.mm(p_y[:, 0:128], Upad[:, hh, :], Wbr[:, ii, :], False, False, [Upad.b(), Wbr.b(ii // 4)], [p_y.b()])
                    self.mm(p_y[:, 0:128], VtmP[:, c, hh, :], Wkr[:, ii, :], False, hh == 1, [VtmP.b(), Wkr.b(ii // 4)], [p_y.b()])
                if d == 0:
                    self.cp("act", yacc[:, cc], p_y[:, 0:128], [p_y.b()], [yacc.b()])
                else:
                    self.tt("dve", yacc[:, cc], yacc[:, cc], p_y[:, 0:128], ALU.add, [yacc.b(), p_y.b()], [yacc.b()])
                self.mm(p_s[:, 0:128], BTt[:, c, :], Ufull[:], True, False, [BTt.b(), Ufull.b()], [p_s.b()])
                for hh in range(2):
                    self.mm(p_s[:, 0:128], KTt[:, c, :], VtmP[:, c, hh, :], False, hh == 1, [KTt.b(), VtmP.b()], [p_s.b()])
                self.stt("dve", tS[:], p_s[:, 0:128], PL[:, c:c + 1], self.bones_f[:], ALU.mult, ALU.mult,
                         [p_s.b(), PL.b(), self.bones_f.b()], [tS.b()])
                self.stt("dve", Sf[:], Sf[:], PL[:, c:c + 1], tS[:], ALU.mult, ALU.add, [Sf.b(), PL.b(), tS.b()], [Sf.b()])
                self.cp("act", Sb[:], Sf[:], [Sf.b()], [Sb.b()])
        self.bg_step()
        if not is_s:
            self.tr(pb[7][:, 0:128], Sf[:], self.ident_f[:], [Sf.b(), self.ident_f.b()], [pb[7].b()])
            self.cp("dve", Sld[:], pb[7][:, 0:128], [pb[7].b()], [Sld.b()])
            for hh in range(2):
                self.dma(self.ns[s, l, d, ct * 2 + hh], Sld[hh * 64:(hh + 1) * 64, hh * 64:(hh + 1) * 64], [Sld.b()], [self.ns.b()])

    def tiles(self):
        out = []
        for g, (Tg, nseq, Tseq) in self.groups.items():
            for c0 in range(0, Tg, 512):
                out.append((g, c0))
        return out

    def build(self):
        stages = self.cfg.get("stages", "all")
        self.declare()
        self.setup_consts()
        if stages != "nocast":
            self.cast_weights()
        if stages == "s0a":
            self.P.emit()
            return self.nc
        self.setup_params()
        if stages == "s0b":
            self.P.emit()
            return self.nc
        self.to_feature_major()
        if stages in ("s0c", "nocast"):
            self.P.emit()
            return self.nc
        X, Y = self.XT, self.XTB
        for l in range(self.DEPTH):
            for (g, c0) in self.tiles():
                if g in self.cfg.get("s1_groups", "sp"):
                    self.stage1_x(l, g, c0, X)
            if stages == "s1":
                break
            for g in self.groups:
                self.stage_fourier(l, g)
                self.stage_attn(l, g)
                if stages == "s2fa":
                    self.zero_rwkv(g)
                else:
                    self.stage_rwkv(l, g)
            if stages == "s2":
                break
            for (g, c0) in self.tiles():
                self.stage3(l, g, c0, X)
            for (g, c0) in self.tiles():
                self.stage4(l, g, c0, X, Y)
            X, Y = Y, X
        if stages in ("all", "s2fa"):
            for (g, c0) in self.tiles():
                self.stage5(g, c0, X)
        self.P.emit()
        return self.nc


def make_in_maps(inputs, cfg, ncores):
    TS, TP, NPS, DEPTH = cfg["TS"], cfg["TP"], cfg["NPS"], cfg["DEPTH"]
    consts = host_consts(TS, TP)
    f = lambda a: np.ascontiguousarray(np.asarray(a, dtype=np.float32))
    shared = {}
    for k in ("w_ada", "b_ada", "norm1_g", "norm2_g", "w_in", "w_out", "q_norm_g", "k_norm_g", "rw_conv",
              "rw_w0", "rw_w2", "rw_a0", "rw_a2", "rw_g2", "rw_kk", "rw_ka", "rw_lnx_g", "rw_lnx_b",
              "ffn_up", "ffn_conv_w", "ffn_conv_b", "ffn_down", "final_norm_g"):
        shared[k] = f(inputs[k])
    shared["rw_rk"] = f(inputs["rw_rk"]).reshape(DEPTH, 512)
    shared.update(consts)
    maps = []
    for b in range(ncores):
        m = dict(shared)
        m["xs"] = f(inputs["x_sample"][b])
        m["xp"] = f(inputs["x_prompt"][NPS * b:NPS * (b + 1)]).reshape(NPS * TP, D)
        m["ck"] = f(inputs["cache_attn_k"][b]).reshape(DEPTH, PAST, 256)
        m["cv"] = f(inputs["cache_attn_v"][b]).reshape(DEPTH, PAST, 256)
        m["st0"] = f(inputs["state_rwkv"][b])
        m["cc"] = np.stack([f(inputs["c"][b]), f(inputs["c_ctx"])], 0)
        maps.append(m)
    return maps


_CACHE = {}


def kernel(**inputs):
    xs = np.asarray(inputs["x_sample"])
    xp = np.asarray(inputs["x_prompt"])
    ncores = xs.shape[0]
    TS, TP = xs.shape[1], xp.shape[1]
    NPS = xp.shape[0] // ncores
    DEPTH = np.asarray(inputs["w_in"]).shape[0]
    DFF = np.asarray(inputs["ffn_down"]).shape[1]
    cfg = dict(TS=TS, TP=TP, NPS=NPS, DEPTH=DEPTH, DFF=DFF, stages="all")
    key = (TS, TP, NPS, DEPTH, DFF)
    if key not in _CACHE:
        kb = KB(cfg)
        _CACHE[key] = kb.build()
    nc = _CACHE[key]
    maps = make_in_maps(inputs, cfg, ncores)
    decl = set()
    for alloc in nc.allocations:
        if isinstance(alloc, mybir.MemoryLocationSet) and alloc.kind == "ExternalInput":
            decl.add(alloc.memorylocations[0].name)
    maps = [{k: v for k, v in m.items() if k in decl} for m in maps]
    res = run_bass_kernel_spmd(nc, maps, core_ids=list(range(ncores)))
    r = res.results
    y_sample = np.stack([np.asarray(r[b]["ys"], np.float32) for b in range(ncores)], 0)
    y_prompt = np.concatenate([np.asarray(r[b]["yp"], np.float32).reshape(NPS, TP, D) for b in range(ncores)], 0)
    nk = np.concatenate([np.asarray(r[b]["nk"], np.float32).reshape(NPS, DEPTH, TP, NKV, 128) for b in range(ncores)], 0)
    nv = np.concatenate([np.asarray(r[b]["nv"], np.float32).reshape(NPS, DEPTH, TP, NKV, 128) for b in range(ncores)], 0)
    ns = np.concatenate([np.asarray(r[b]["ns"], np.float32) for b in range(ncores)], 0)
    return (y_prompt, y_sample, nk, nv, ns)
```

```python
import contextlib
import math
import numpy as np
import ml_dtypes
import concourse.bass as bass
import concourse.mybir as mybir
from concourse.bass_utils import run_bass_kernel_spmd

F32 = mybir.dt.float32
BF16 = mybir.dt.bfloat16
ALU = mybir.AluOpType
AF = mybir.ActivationFunctionType
AX = mybir.AxisListType

EPOCH = 8000
RING = 8


class Buf:
    __slots__ = ("wc", "wd", "rc", "rd", "excl")

    def __init__(self, excl=False):
        self.excl = excl
        self.wc = {}
        self.wd = {}
        self.rc = {}
        self.rd = {}


class Op:
    __slots__ = ("eng", "fn", "waits", "done", "is_dma", "stage")

    def __init__(self, eng, fn, is_dma):
        self.stage = None
        self.eng = eng
        self.fn = fn
        self.waits = {}
        self.done = None
        self.is_dma = is_dma


class Prog:
    ENGS = ("pe", "act", "dve", "pool", "sp")

    def __init__(self, nc):
        self.nc = nc
        self.q = {e: [] for e in self.ENGS}
        self.cnt = {e: 0 for e in self.ENGS}
        self.dcnt = {e: 0 for e in self.ENGS}
        self.sems = {}
        self.semkeys = []
        self.last_dma = {}
        self.pending = {}

    def _semkey(self, key):
        if key not in self.sems:
            self.sems[key] = None
            self.semkeys.append(key)
        return key

    def _add_dep(self, op, dep, raw):
        if dep is None or dep is op:
            return
        if dep.eng == op.eng and not dep.is_dma and not op.is_dma:
            if op.eng == "pe":
                return
        key, val = dep.done
        if op.waits.get(key, 0) < val:
            op.waits[key] = val

    def op(self, eng, fn, reads=(), writes=(), dma=False):
        o = Op(eng, fn, dma)
        o.stage = getattr(self, "stage", None)
        if any(b.excl for b in reads):
            writes = list(writes) + [b for b in reads if b.excl and b not in writes]
            reads = [b for b in reads if not b.excl]
        self.nops = getattr(self, "nops", 0) + 1
        if self.nops > getattr(self, "limit", 1 << 60):
            return o
        pend = self.pending.pop(eng, None)
        if pend:
            for key, val in pend.items():
                if o.waits.get(key, 0) < val:
                    o.waits[key] = val
        for b in reads:
            for d in b.wc.values():
                self._add_dep(o, d, True)
            for lst in b.wd.values():
                for d in lst:
                    self._add_dep(o, d, True)
        for b in writes:
            for d in b.wc.values():
                self._add_dep(o, d, False)
            for lst in b.wd.values():
                for d in lst:
                    self._add_dep(o, d, False)
            for d in b.rc.values():
                self._add_dep(o, d, False)
            for lst in b.rd.values():
                for d in lst:
                    self._add_dep(o, d, False)
        if dma:
            k = self.dcnt[eng]
            self.dcnt[eng] += 1
            slot = k % RING
            key = self._semkey(("d", eng, slot))
            o.done = (key, 16 * (k // RING + 1))
            prev = self.last_dma.get((eng, slot))
            if prev is not None:
                self._add_dep(o, prev, True)
            self.last_dma[(eng, slot)] = o
        else:
            k = self.cnt[eng]
            self.cnt[eng] += 1
            key = self._semkey(("c", eng, k // EPOCH))
            o.done = (key, k % EPOCH + 1)
        for b in reads:
            if dma:
                lst = b.rd.setdefault(eng, [])
                lst.append(o)
                if len(lst) > RING:
                    del lst[0]
            else:
                b.rc[eng] = o
        for b in writes:
            b.rc = {}
            b.rd = {}
            if dma:
                lst = b.wd.setdefault(eng, [])
                lst.append(o)
                if len(lst) > RING:
                    del lst[0]
            else:
                b.wc[eng] = o
        self.q[eng].append(o)
        return o

    def emit(self):
        nc = self.nc
        with contextlib.ExitStack() as st:
            for key in self.semkeys:
                self.sems[key] = st.enter_context(nc.semaphore("s_" + "_".join(str(x) for x in key)))
            block = st.enter_context(nc.Block())
            engmap = {"pe": block.tensor, "act": block.scalar, "dve": block.vector,
                      "pool": block.gpsimd, "sp": block.sync}
            all_ops = self.q

            def make(ename):
                ops = all_ops[ename]

                def body(e):
                    seen = {}
                    for o in ops:
                        for key, val in o.waits.items():
                            if seen.get(key, 0) >= val:
                                continue
                            seen[key] = val
                            e.wait_ge(self.sems[key], val)
                        ins = o.fn(e)
                        if self.annotate and o.stage:
                            ins.annotate(o.stage)
                        key, val = o.done
                        ins.then_inc(self.sems[key], 16 if o.is_dma else 1)
                    if ename == "sp":
                        fin = {}
                        for en in self.ENGS:
                            for o in all_ops[en][-1:]:
                                key, val = o.done
                                fin[key] = max(fin.get(key, 0), val)
                        for o in self.last_dma.values():
                            key, val = o.done
                            fin[key] = max(fin.get(key, 0), val)
                        for key, val in fin.items():
                            if seen.get(key, 0) < val:
                                e.wait_ge(self.sems[key], val)
                return body

            for ename in self.ENGS:
                if all_ops[ename] or ename == "sp":
                    engmap[ename](make(ename))


class TB:
    def __init__(self, h, excl=False):
        self.h = h
        self.bufs = {}
        self.excl = excl

    def __getitem__(self, idx):
        return self.h[idx]

    def b(self, key=None):
        if key not in self.bufs:
            self.bufs[key] = Buf(self.excl)
        return self.bufs[key]


class TBV:
    def __init__(self, ap):
        self.ap = ap
        self.buf = Buf(True)

    def __getitem__(self, idx):
        return self.ap[idx]

    def b(self, key=None):
        return self.buf


class DT:
    def __init__(self, ap):
        self.ap = ap
        self.bufs = {}

    def __getitem__(self, idx):
        return self.ap[idx]

    def b(self, key=None):
        if key not in self.bufs:
            self.bufs[key] = Buf()
        return self.bufs[key]


D = 2048
KT = 16
NH = 8
NKV = 2
PAST = 512
IN_W = 3840
RW_IN = 1792
GRID_W = 64
NORM_EPS = 1e-6
GN_EPS = 64e-5
LWC = -math.exp(-0.5)


def host_consts(TS, TP):
    c = {}
    bf = ml_dtypes.bfloat16
    c["ident_f"] = np.eye(128, dtype=np.float32)
    c["ident_b"] = np.eye(128, dtype=np.float32).astype(bf)
    c["ones_b"] = np.ones((128, 128), np.float32).astype(bf)
    bo = np.zeros((128, 128), np.float32)
    bo[:64, :64] = 1.0
    bo[64:, 64:] = 1.0
    c["bones_b"] = bo.astype(bf)
    c["bones_f"] = bo
    prot = np.zeros((128, 128), np.float32)
    for m in range(128):
        j = m % 64
        if j < 32:
            prot[m + 32, m] = -1.0
        else:
            prot[m - 32, m] = 1.0
    c["prot_f"] = prot
    t = np.arange(TS)
    rows = (t // GRID_W).astype(np.float64)
    cols = (t % GRID_W).astype(np.float64)
    inv = 1.0 / (10000.0 ** (np.arange(0, 64, 2, dtype=np.float64) / 64.0))
    cosT = np.zeros((128, TS), np.float64)
    sinT = np.zeros((128, TS), np.float64)
    for d in range(128):
        pos = rows if d < 64 else cols
        f = inv[(d % 64) % 32]
        ang = np.float32(pos).astype(np.float32) * np.float32(f)
        cosT[d] = np.cos(ang.astype(np.float64))
        sinT[d] = np.sin(ang.astype(np.float64))
    c["cosT"] = cosT.astype(np.float32)
    c["sinT"] = sinT.astype(np.float32)
    i = np.arange(128)
    ang = 2 * np.pi * np.outer(i, i) / 128.0
    c["csC"] = (np.concatenate([np.cos(ang), np.sin(ang)], 1) / np.sqrt(128.0)).astype(np.float32).astype(bf)
    for nm, T in (("S", TS), ("P", TP)):
        i = np.arange(T)
        ang = 2 * np.pi * ((np.outer(i, i)) % T) / float(T)
        c["ct" + nm] = (np.cos(ang) / np.sqrt(T)).astype(np.float32).astype(bf)
        c["nst" + nm] = (-np.sin(ang) / np.sqrt(T)).astype(np.float32).astype(bf)
    idx = np.arange(128)
    su = (idx[:, None] < idx[None, :]).astype(np.float32)
    iu = (idx[:, None] <= idx[None, :]).astype(np.float32)
    sl = su.T.copy()
    il = iu.T.copy()
    c["masks"] = np.stack([su, sl, iu, il, -su, -sl], 0).astype(np.float32)
    c["masks4"] = np.repeat(c["masks"][:, None, :, :], 4, axis=1).transpose(2, 0, 1, 3).copy().astype(np.float32)
    c["ident4"] = np.repeat(np.eye(128, dtype=np.float32)[:, None, :], 4, axis=1).copy()
    c["tri2"] = np.stack([np.concatenate([iu, su], 1), np.concatenate([il, sl], 1)], 0).astype(np.float32) * np.float32(LWC)
    return c


class KB:
    def __init__(self, cfg):
        self.cfg = cfg
        self.TS = cfg["TS"]
        self.TP = cfg["TP"]
        self.NPS = cfg["NPS"]
        self.DFF = cfg["DFF"]
        self.FT = self.DFF // 128
        self.DEPTH = cfg["DEPTH"]
        self.dbg = set(cfg.get("dbg", ()))
        self.nc = bass.Bass("TRN2", target_bir_lowering=False)
        self.P = Prog(self.nc)
        self.P.limit = cfg.get("limit", 1 << 60)
        self.P.annotate = bool(cfg.get("annotate"))
        self.P.stage = "init"
        self.st = contextlib.ExitStack()
        self.dram = {}
        self.uid = 0
        self.bar_uid = 0
        self.groups = {"s": (self.TS, 1, self.TS), "p": (self.NPS * self.TP, self.NPS, self.TP)}

    def din(self, name, shape, dt=F32):
        t = DT(self.nc.dram_tensor(name, list(shape), dt, kind="ExternalInput").ap())
        self.dram[name] = t
        return t

    def dout(self, name, shape, dt=F32):
        t = DT(self.nc.dram_tensor(name, list(shape), dt, kind="ExternalOutput").ap())
        self.dram[name] = t
        return t

    def dscr(self, name, shape, dt):
        kind = "ExternalOutput" if name in self.dbg else "Internal"
        t = DT(self.nc.dram_tensor(name, list(shape), dt, kind=kind).ap())
        self.dram[name] = t
        return t

    def sb(self, name, shape, dt, stack=None):
        self.uid += 1
        h = (stack or self.st).enter_context(self.nc.sbuf_tensor(f"{name}_{self.uid}", list(shape), dt))
        return TB(h)

    def ps(self, name, shape, dt=F32, stack=None):
        self.uid += 1
        h = (stack or self.st).enter_context(self.nc.psum_tensor(f"{name}_{self.uid}", list(shape), dt))
        return TB(h, excl=True)

    def dma(self, out, in_, reads, writes, eng="sp", **kw):
        return self.P.op(eng, lambda e: e.dma_start(out=out, in_=in_, **kw), reads, writes, dma=True)

    def mm(self, out, lhsT, rhs, start, stop, reads, writes):
        return self.P.op("pe", lambda e: e.matmul(out, lhsT=lhsT, rhs=rhs, start=start, stop=stop), reads, writes)

    def tr(self, out, in_, ident, reads, writes):
        return self.P.op("pe", lambda e: e.transpose(out=out, in_=in_, identity=ident), reads, writes)

    def act(self, out, in_, func, reads, writes, **kw):
        return self.P.op("act", lambda e: e.activation(out=out, in_=in_, func=func, **kw), reads, writes)

    def tt(self, eng, out, in0, in1, op, reads, writes):
        return self.P.op(eng, lambda e: e.tensor_tensor(out=out, in0=in0, in1=in1, op=op), reads, writes)

    def ts(self, eng, out, in0, s1, s2, op0, op1, reads, writes):
        if s2 is None:
            return self.P.op(eng, lambda e: e.tensor_scalar(out=out, in0=in0, scalar1=s1, scalar2=None, op0=op0), reads, writes)
        return self.P.op(eng, lambda e: e.tensor_scalar(out=out, in0=in0, scalar1=s1, scalar2=s2, op0=op0, op1=op1), reads, writes)

    def stt(self, eng, out, in0, scalar, in1, op0, op1, reads, writes):
        return self.P.op(eng, lambda e: e.scalar_tensor_tensor(out=out, in0=in0, scalar=scalar, in1=in1, op0=op0, op1=op1), reads, writes)

    def cp(self, eng, out, in_, reads, writes):
        if eng == "act":
            return self.P.op("act", lambda e: e.copy(out=out, in_=in_), reads, writes)
        return self.P.op(eng, lambda e: e.tensor_copy(out=out, in_=in_), reads, writes)

    def memset(self, eng, ap, val, writes):
        return self.P.op(eng, lambda e: e.memset(ap, val), [], writes)

    def recip(self, out, in_, reads, writes):
        return self.P.op("dve", lambda e: e.reciprocal(out=out, in_=in_), reads, writes)

    def barrier(self):
        P = self.P
        fin = {}
        for en in P.ENGS:
            for o in P.q[en][-1:]:
                key, val = o.done
                fin[key] = max(fin.get(key, 0), val)
        for o in P.last_dma.values():
            key, val = o.done
            fin[key] = max(fin.get(key, 0), val)
        for en in P.ENGS:
            d = P.pending.setdefault(en, {})
            for key, val in fin.items():
                d[key] = max(d.get(key, 0), val)

    def declare(self):
        TS, TP, NPS, DEPTH, DFF = self.TS, self.TP, self.NPS, self.DEPTH, self.DFF
        TPG = NPS * TP
        di = self.din
        self.xs = di("xs", [TS, D])
        self.xp = di("xp", [TPG, D])
        self.ck = di("ck", [DEPTH, PAST, 256])
        self.cv = di("cv", [DEPTH, PAST, 256])
        self.st0 = di("st0", [DEPTH, 2, 8, 64, 64])
        self.cc = di("cc", [2, D])
        self.w_ada = di("w_ada", [DEPTH, D, 6 * D])
        self.b_ada = di("b_ada", [DEPTH, 6 * D])
        self.norm1_g = di("norm1_g", [DEPTH, D])
        self.norm2_g = di("norm2_g", [DEPTH, D])
        self.w_in = di("w_in", [DEPTH, D, IN_W])
        self.w_out = di("w_out", [DEPTH, D, D])
        self.q_norm_g = di("q_norm_g", [DEPTH, 128])
        self.k_norm_g = di("k_norm_g", [DEPTH, 128])
        self.rw_conv = di("rw_conv", [DEPTH, 3, RW_IN])
        self.rw_w0 = di("rw_w0", [DEPTH, 2, 512])
        self.rw_w2 = di("rw_w2", [DEPTH, 2, 64, 512])
        self.rw_a0 = di("rw_a0", [DEPTH, 2, 512])
        self.rw_a2 = di("rw_a2", [DEPTH, 2, 64, 512])
        self.rw_g2 = di("rw_g2", [DEPTH, 128, 512])
        self.rw_kk = di("rw_kk", [DEPTH, 512])
        self.rw_ka = di("rw_ka", [DEPTH, 512])
        self.rw_rk = di("rw_rk", [DEPTH, 512])
        self.rw_lnx_g = di("rw_lnx_g", [DEPTH, 512])
        self.rw_lnx_b = di("rw_lnx_b", [DEPTH, 512])
        self.ffn_up = di("ffn_up", [DEPTH, D, 2 * DFF])
        self.ffn_conv_w = di("ffn_conv_w", [DEPTH, 3, 2 * DFF])
        self.ffn_conv_b = di("ffn_conv_b", [DEPTH, 2 * DFF])
        self.ffn_down = di("ffn_down", [DEPTH, DFF, D])
        self.final_norm_g = di("final_norm_g", [D])
        self.c_ident_f = di("ident_f", [128, 128])
        self.c_ident_b = di("ident_b", [128, 128], BF16)
        self.c_ones_b = di("ones_b", [128, 128], BF16)
        self.c_bones_b = di("bones_b", [128, 128], BF16)
        self.c_bones_f = di("bones_f", [128, 128])
        self.c_prot_f = di("prot_f", [128, 128])
        self.c_cosT = di("cosT", [128, TS])
        self.c_sinT = di("sinT", [128, TS])
        self.c_csC = di("csC", [128, 256], BF16)
        self.c_ct = {"s": di("ctS", [TS, TS], BF16), "p": di("ctP", [TP, TP], BF16)}
        self.c_nst = {"s": di("nstS", [TS, TS], BF16), "p": di("nstP", [TP, TP], BF16)}
        self.c_masks = di("masks", [6, 128, 128])
        self.c_tri2 = di("tri2", [2, 128, 256])
        self.c_masks4 = di("masks4", [128, 6, 4, 128])
        self.c_ident4 = di("ident4", [128, 4, 128])
        self.ys = self.dout("ys", [TS, D])
        self.yp = self.dout("yp", [TPG, D])
        self.nk = self.dout("nk", [NPS, DEPTH, TP, 256])
        self.nv = self.dout("nv", [NPS, DEPTH, TP, 256])
        self.ns = self.dout("ns", [NPS, DEPTH, 2, 8, 64, 64])
        ds = self.dscr
        FT = self.FT
        self.Win_t = ds("Win_t", [DEPTH, 30, 128, KT, 128], BF16)
        self.Wv_t = ds("Wv_t", [DEPTH, 128, KT, 256], BF16)
        self.Wout_t = ds("Wout_t", [DEPTH, 16, 128, KT, 128], BF16)
        self.Wup_t = ds("Wup_t", [DEPTH, 2 * FT, 128, KT, 128], BF16)
        self.Wdn_t = ds("Wdn_t", [DEPTH, 16, 128, FT, 128], BF16)
        self.mod_rows = [ds(f"modrows{l}", [2, 6 * D], F32) for l in range(DEPTH)]
        self.XT = {}
        self.XTB = {}
        self.QT = {}
        self.KTs = {}
        self.Vs = {}
        self.RW = {}
        self.AB = {}
        self.MIXT = {}
        for g, (Tg, nseq, Tseq) in self.groups.items():
            self.XT[g] = ds("XT_" + g, [KT, 128, Tg], F32)
            self.XTB[g] = ds("XTB_" + g, [KT, 128, Tg], F32)
            self.QT[g] = ds("QT_" + g, [NH, 128, Tg], BF16)
            self.KTs[g] = ds("KT_" + g, [NKV, 128, Tg], BF16)
            self.Vs[g] = ds("V_" + g, [Tg, 256], BF16)
            self.RW[g] = ds("RW_" + g, [RW_IN, Tg], F32)
            self.AB[g] = ds("AB_" + g, [Tg, 1024], BF16)
            self.MIXT[g] = ds("MIXT_" + g, [KT, 128, Tg], BF16)

    def load_const(self, name, src, shape, dt):
        t = self.sb(name, shape, dt)
        self.dma(t[:], src.ap, [src.b()], [t.b()])
        return t

    def setup_consts(self):
        self.P.stage = "consts"
        TS = self.TS
        self.ident_f = self.load_const("ident_f", self.c_ident_f, [128, 128], F32)
        self.ident_b = self.load_const("ident_b", self.c_ident_b, [128, 128], BF16)
        self.ones_b = self.load_const("ones_b", self.c_ones_b, [128, 128], BF16)
        self.bones_b = self.load_const("bones_b", self.c_bones_b, [128, 128], BF16)
        self.bones_f = self.load_const("bones_f", self.c_bones_f, [128, 128], F32)
        self.prot_f = self.load_const("prot_f", self.c_prot_f, [128, 128], F32)
        self.csC = self.load_const("csC", self.c_csC, [128, 256], BF16)
        self.masks = self.sb("masks", [128, 6, 128], F32)
        self.dma(self.masks[:], self.c_masks.ap.rearrange("m p c -> p m c"), [self.c_masks.b()], [self.masks.b()])
        self.tri2 = self.sb("tri2", [128, 2, 256], F32)
        self.dma(self.tri2[:], self.c_tri2.ap.rearrange("m p c -> p m c"), [self.c_tri2.b()], [self.tri2.b()])
        self.masks4 = self.load_const("masks4", self.c_masks4, [128, 6, 4, 128], F32)
        self.ident4 = self.load_const("ident4", self.c_ident4, [128, 4, 128], F32)
        self.eps6 = self.sb("eps6", [128, 1], F32)
        self.memset("pool", self.eps6[:], NORM_EPS, [self.eps6.b()])
        self.eps12 = self.sb("eps12", [128, 1], F32)
        self.memset("pool", self.eps12[:], 1e-12, [self.eps12.b()])
        self.epsgn = self.sb("epsgn", [128, 1], F32)
        self.memset("pool", self.epsgn[:], GN_EPS, [self.epsgn.b()])
        self.pw = [self.ps(f"pw{i}", [128, 1024], F32) for i in range(4)]
        self.pb = [TBV(self.pw[i // 2].h[:, (i % 2) * 512:(i % 2) * 512 + 512]) for i in range(8)]

    def cast_jobs(self, l, which):
        FT = self.FT
        jobs = []
        if which == "in":
            for c0 in range(0, IN_W, 512):
                wd = min(512, IN_W - c0)
                dsts = []
                for j in range(wd // 128):
                    ot = c0 // 128 + j
                    if ot == 14:
                        dsts.append((self.Wv_t[l], self.Wv_t.b(l), j * 128, 256))
                    elif ot != 15:
                        dsts.append((self.Win_t[l, ot], self.Win_t.b((l, ot)), j * 128, 128))
                jobs.append((self.w_in, l, c0, wd, KT, dsts))
        else:
            for c0 in range(0, D, 512):
                dsts = [(self.Wout_t[l, c0 // 128 + j], self.Wout_t.b((l, c0 // 128 + j)), j * 128, 128) for j in range(4)]
                jobs.append((self.w_out, l, c0, 512, KT, dsts))
            for c0 in range(0, 2 * self.DFF, 512):
                wd = min(512, 2 * self.DFF - c0)
                dsts = [(self.Wup_t[l, c0 // 128 + j], self.Wup_t.b((l, c0 // 128 + j)), j * 128, 128) for j in range(wd // 128)]
                jobs.append((self.ffn_up, l, c0, wd, KT, dsts))
            cw = 128 if FT > 16 else 512
            for c0 in range(0, D, cw):
                dsts = [(self.Wdn_t[l, c0 // 128 + j], self.Wdn_t.b((l, c0 // 128 + j)), j * 128, 128) for j in range(cw // 128)]
                jobs.append((self.ffn_down, l, c0, cw, FT, dsts))
        return jobs

    def bg_begin(self, stk, jobs, engs=("pool",)):
        nel = max(KT * 512, self.FT * (128 if self.FT > 16 else 512))
        self.bg = dict(jobs=list(jobs), i=0, pend=None, engs=engs, n=0,
                       wf=[self.sb("wcf", [128, nel], F32, stk) for _ in range(2)],
                       wb=[self.sb("wcb", [128, nel], BF16, stk) for _ in range(2)])

    def bg_step(self):
        bg = getattr(self, "bg", None)
        if bg is None:
            return
        if bg["pend"] is not None:
            (src, l, c0, wd, ktn, dsts), f, b = bg["pend"]
            fv = f[:, 0:ktn * wd].rearrange("p (kt c) -> p kt c", c=wd)
            off = 0
            for (dap, dbuf, co, w) in dsts:
                view = b[:, off:off + ktn * w].rearrange("p (kt c) -> p kt c", c=w)
                self.cp(bg["engs"][bg["n"] % len(bg["engs"])], view, fv[:, :, co:co + w], [f.b()], [b.b()])
                bg["n"] += 1
                self.dma(dap, view, [b.b()], [dbuf])
                off += ktn * w
            bg["pend"] = None
        if bg["i"] < len(bg["jobs"]):
            job = bg["jobs"][bg["i"]]
            f = bg["wf"][bg["i"] % 2]
            b = bg["wb"][bg["i"] % 2]
            (src, l, c0, wd, ktn, dsts) = job
            fv = f[:, 0:ktn * wd].rearrange("p (kt c) -> p kt c", c=wd)
            self.dma(fv, src[l, :, c0:c0 + wd].rearrange("(kt p) c -> p kt c", p=128), [src.b()], [f.b()])
            bg["pend"] = (job, f, b)
            bg["i"] += 1

    def bg_end(self):
        bg = getattr(self, "bg", None)
        if bg is None:
            return
        while bg["pend"] is not None or bg["i"] < len(bg["jobs"]):
            self.bg_step()
        self.bg = None

    def cast_weights(self):
        self.P.stage = "cast"
        sched = self.cfg.get("bg_cast", True)
        jobs = self.cast_jobs(0, "in")
        self.bg_sched = {}
        if sched:
            r0 = self.cast_jobs(0, "rest")
            h = len(r0) // 2
            self.bg_sched[("attn", 0, "s")] = r0[:h]
            self.bg_sched[("rwkv", 0, "p")] = r0[h:]
            for l in range(1, self.DEPTH):
                self.bg_sched[("rwkv", l - 1, "p")] = self.bg_sched.get(("rwkv", l - 1, "p"), []) + self.cast_jobs(l, "in")
                rl = self.cast_jobs(l, "rest")
                h = len(rl) // 2
                self.bg_sched[("attn", l, "s")] = rl[:h]
                self.bg_sched[("rwkv", l, "p")] = rl[h:]
        else:
            jobs += self.cast_jobs(0, "rest")
            for l in range(1, self.DEPTH):
                jobs += self.cast_jobs(l, "in") + self.cast_jobs(l, "rest")
        with contextlib.ExitStack() as stk:
            self.bg_begin(stk, jobs, engs=("act", "dve", "pool", "dve"))
            self.bg_end()
            self.barrier()

    def load_pp(self, dst, dst_b, src_rows, n, src_b):
        k = self._pp_i = getattr(self, "_pp_i", 0) + 1
        stg = self.pp_stage[k % 2]
        bank = self.pb[6 + (k % 2)]
        self.dma(stg[0:n, :], src_rows, [src_b], [stg.b()])
        self.tr(bank[:, 0:n], stg[0:n, :], self.ident_f[0:n, 0:n], [stg.b(), self.ident_f.b()], [bank.b()])
        self.cp("dve", dst, bank[:, 0:n], [bank.b()], [dst_b])

    def to_feature_major(self):
        self.P.stage = "tofm"
        with contextlib.ExitStack() as stk:
            xin = [self.sb("xin", [128, D], F32, stk) for _ in range(2)]
            xo = [self.sb("xo", [128, KT, 128], F32, stk) for _ in range(2)]
            i = 0
            for g, src in (("s", self.xs), ("p", self.xp)):
                Tg = self.groups[g][0]
                for blk in range(Tg // 128):
                    a = xin[i % 2]
                    o = xo[i % 2]
                    self.dma(a[:], src[blk * 128:(blk + 1) * 128, :], [src.b()], [a.b()])
                    for q in range(4):
                        bank = self.pb[(i * 4 + q) % 4]
                        for j in range(4):
                            kt = q * 4 + j
                            self.tr(bank[:, j * 128:(j + 1) * 128], a[:, kt * 128:(kt + 1) * 128], self.ident_f[:],
                                    [a.b(), self.ident_f.b()], [bank.b()])
                        eng = "act" if q % 2 else "dve"
                        self.cp(eng, o[:, q * 4:(q + 1) * 4, :], bank[:].rearrange("p (j t) -> p j t", j=4), [bank.b()], [o.b()])
                    self.dma(self.XT[g][:, :, blk * 128:(blk + 1) * 128].rearrange("kt p t -> p kt t"), o[:],
                             [o.b()], [self.XT[g].b(blk // 4)])
                    i += 1
            self.barrier()

    def setup_params(self):
        self.P.stage = "params"
        DEPTH, FT = self.DEPTH, self.FT
        self.pp_stage = [self.sb("ppstg", [128, 128], F32) for _ in range(2)]
        self.prm = []
        ccT = self.sb("ccT", [128, 32], F32)
        self.load_pp(ccT[:], ccT.b(), self.cc.ap.rearrange("v (kt p) -> (v kt) p", p=128), 32, self.cc.b())
        sc = self.sb("sc", [128, 32], F32)
        self.act(sc[:], ccT[:], AF.Silu, [ccT.b()], [sc.b()])
        fng = self.sb("fng", [128, KT], F32)
        self.load_pp(fng[:], fng.b(), self.final_norm_g.ap.rearrange("(kt p) -> kt p", p=128), KT, self.final_norm_g.b())
        self.fng = fng
        for l in range(DEPTH):
            pr = {}
            def vec(name, src, n, rows):
                t = self.sb(name, [128, n], F32)
                self.load_pp(t[:], t.b(), rows, n, src.b())
                return t
            pr["n1g"] = vec("n1g", self.norm1_g, KT, self.norm1_g[l].rearrange("(kt p) -> kt p", p=128))
            pr["n2g"] = vec("n2g", self.norm2_g, KT, self.norm2_g[l].rearrange("(kt p) -> kt p", p=128))
            pr["qng"] = vec("qng", self.q_norm_g, 1, self.q_norm_g[l:l + 1, :])
            pr["kng"] = vec("kng", self.k_norm_g, 1, self.k_norm_g[l:l + 1, :])
            pr["rwconv"] = vec("rwconv", self.rw_conv, 42, self.rw_conv[l].rearrange("j (t p) -> (j t) p", p=128))
            for nm, src in (("kk", self.rw_kk), ("ka", self.rw_ka), ("rk", self.rw_rk), ("lng", self.rw_lnx_g), ("lnb", self.rw_lnx_b)):
                pr[nm] = vec(nm, src, 4, src[l].rearrange("(t p) -> t p", p=128))
            pr["a0"] = vec("a0", self.rw_a0, 8, self.rw_a0[l].rearrange("d (t p) -> (d t) p", p=128))
            c1 = self.sb("c1", [128, 4], F32)
            self.ts("dve", c1[:], pr["ka"][:], -1.0, 1.0, ALU.mult, ALU.add, [pr["ka"].b()], [c1.b()])
            pr["c1"] = c1
            nft = 2 * FT
            fcw = self.sb("fcw", [128, 3, nft], F32)
            for j in range(3):
                self.load_pp(fcw[:, j, :], fcw.b(), self.ffn_conv_w[l, j].rearrange("(t p) -> t p", p=128), nft, self.ffn_conv_w.b())
            pr["fcw"] = fcw
            nfcw = self.sb("nfcw", [128, 3, nft], F32)
            self.ts("dve", nfcw[:], fcw[:], -1.0, None, ALU.mult, None, [fcw.b()], [nfcw.b()])
            pr["nfcw"] = nfcw
            pr["fcb"] = vec("fcb", self.ffn_conv_b, nft, self.ffn_conv_b[l].rearrange("(t p) -> t p", p=128))
            bada = vec("bada", self.b_ada, 96, self.b_ada[l].rearrange("(t p) -> t p", p=128))
            mods = [self.sb(f"mod{v}", [128, 96], F32) for v in range(2)]
            modr = self.mod_rows[l]
            with contextlib.ExitStack() as stk:
                wa = [self.sb("wada", [128, KT, 1024], F32, stk) for _ in range(2)]
                rowt = [self.sb("modrow", [2, 512], F32, stk) for _ in range(2)]
                for blk in range(12):
                    w = wa[blk % 2]
                    self.dma(w[:], self.w_ada[l, :, blk * 1024:(blk + 1) * 1024].rearrange("(kt p) c -> p kt c", p=128),
                             [self.w_ada.b()], [w.b()], eng=("sp", "act")[blk % 2])
                    for hf in range(2):
                        cb = blk * 2 + hf
                        acc = self.pb[4 + cb % 2]
                        for kt in range(KT):
                            self.mm(acc[0:2, :], sc[:, kt:32:16], w[:, kt, hf * 512:(hf + 1) * 512], kt == 0, kt == KT - 1,
                                    [w.b(), sc.b()], [acc.b()])
                        rt_ = rowt[cb % 2]
                        self.cp("dve" if cb % 2 else "act", rt_[:], acc[0:2, :], [acc.b()], [rt_.b()])
                        self.dma(modr[:, cb * 512:(cb + 1) * 512], rt_[:], [rt_.b()], [modr.b()])
                for v in range(2):
                    m = mods[v]
                    self.load_pp(m[:], m.b(), modr[v].rearrange("(t p) -> t p", p=128), 96, modr.b())
                    self.tt("dve", m[:], m[:], bada[:], ALU.add, [m.b(), bada.b()], [m.b()])
                self.barrier()
            pr["mod"] = mods
            for v in range(2):
                for nm, gname, j in (("gs1", "n1g", 1), ("gs2", "n2g", 4)):
                    t = self.sb(f"{nm}_{v}", [128, KT], F32)
                    self.stt("dve", t[:], mods[v][:, j * 16:(j + 1) * 16], 1.0, pr[gname][:], ALU.add, ALU.mult,
                             [mods[v].b(), pr[gname].b()], [t.b()])
                    pr[f"{nm}_{v}"] = t
            self.prm.append(pr)

    def norm_mod(self, xt, chunks, gs, sh_ap_fn, hT, stk, ss_banks):
        Wtot = chunks[-1][1]
        sq = [self.sb("nsq", [128, Wtot], BF16, stk) for _ in range(2)]
        rstd = self.sb("nrstd", [128, Wtot], F32, stk)
        tmp = [self.sb("ntmp", [128, Wtot], F32, stk) for _ in range(2)]
        for kt in range(KT):
            s = sq[kt % 2]
            self.act(s[:], xt[:, kt, :], AF.Square, [xt.b()], [s.b()])
            for ci, (c0, c1) in enumerate(chunks):
                bk = ss_banks[ci]
                self.mm(bk[:, 0:c1 - c0], self.ones_b[:], s[:, c0:c1], kt == 0, kt == KT - 1,
                        [s.b(), self.ones_b.b()], [bk.b()])
        for ci, (c0, c1) in enumerate(chunks):
            bk = ss_banks[ci]
            self.act(rstd[:, c0:c1], bk[:, 0:c1 - c0], AF.Sqrt, [bk.b(), self.eps6.b()], [rstd.b()],
                     scale=1.0 / D, bias=self.eps6[:, 0:1])
        self.recip(rstd[:], rstd[:], [rstd.b()], [rstd.b()])
        for kt in range(KT):
            t = tmp[kt % 2]
            self.tt("pool" if kt % 2 else "dve", t[:], xt[:, kt, :], rstd[:], ALU.mult, [xt.b(), rstd.b()], [t.b()])
            self.act(hT[:, kt, :], t[:], AF.Identity, [t.b(), gs.b()], [hT.b()],
                     scale=gs[:, kt:kt + 1], bias=sh_ap_fn(kt))

    def zero_rwkv(self, g):
        Tg = self.groups[g][0]
        with contextlib.ExitStack() as stk:
            z = self.sb("zz", [128, Tg], BF16, stk)
            self.memset("pool", z[:], 0.0, [z.b()])
            for r in range(12, 16):
                self.dma(self.MIXT[g][r], z[:], [z.b()], [self.MIXT[g].b(i) for i in range((Tg + 511) // 512)])
            self.barrier()

    def stage1_x(self, l, g, c0, X):
        self._X1 = X
        return self.stage1(l, g, c0)

    def stage1(self, l, g, c0):
        self.P.stage = f"s1_{l}_{g}"
        W = 512
        v = 0 if g == "s" else 1
        pr = self.prm[l]
        mod = pr["mod"][v]
        ti = c0 // 512
        pb = self.pb
        is_s = (g == "s")
        with contextlib.ExitStack() as stk:
            xt = self.sb("xt", [128, KT, W], F32, stk)
            hT = self.sb("hT", [128, KT, W], BF16, stk)
            self.dma(xt[:], self._X1[g][:, :, c0:c0 + W].rearrange("kt p t -> p kt t"), [self._X1[g].b(ti)], [xt.b()])
            self.norm_mod(xt, [(0, W)], pr[f"gs1_{v}"], lambda kt: mod[:, kt:kt + 1], hT, stk, [pb[7]])
            wts = [self.sb("wt", [128, KT, 128], BF16, stk) for _ in range(3)]
            wv = self.sb("wv", [128, KT, 256], BF16, stk)
            uT = self.sb("uT", [128, W], BF16, stk)
            abt = self.sb("abt", [128, 4, 256], BF16, stk)
            sqh = self.sb("sqh", [128, W], BF16, stk)
            rq = self.sb("rq", [128, W], F32, stk)
            qn = [self.sb("qn", [128, W], F32, stk) for _ in range(2)]
            t1 = self.sb("t1", [128, W], F32, stk)
            t2 = self.sb("t2", [128, W], F32, stk)
            qr = [self.sb("qr", [128, W], BF16, stk) for _ in range(2)]
            ktok = self.sb("ktok", [128, 4, 128], F32, stk)
            vt = self.sb("vt", [128, 4, 256], BF16, stk)
            vtf = self.sb("vtf", [128, 4, 256], F32, stk)
            rwt = [self.sb("rwt", [128, W], F32, stk) for _ in range(2)]
            if is_s:
                cosb = self.sb("cosb", [128, W], F32, stk)
                sinb = self.sb("sinb", [128, W], F32, stk)
                self.dma(cosb[:], self.c_cosT[:, c0:c0 + W], [self.c_cosT.b()], [cosb.b()])
                self.dma(sinb[:], self.c_sinT[:, c0:c0 + W], [self.c_sinT.b()], [sinb.b()])

            order = list(range(14)) + list(range(16, 30))

            def load_w(oi):
                ot = order[oi]
                w = wts[oi % 3]
                self.dma(w[:], self.Win_t[l, ot], [self.Win_t.b((l, ot))], [w.b()])
            load_w(0)
            load_w(1)
            self.dma(wv[:], self.Wv_t[l], [self.Wv_t.b(l)], [wv.b()])
            for oi, ot in enumerate(order):
                if oi + 2 < len(order):
                    load_w(oi + 2)
                w = wts[oi % 3]
                acc = pb[oi % 3]
                for kt in range(KT):
                    self.mm(acc[:], w[:, kt, :], hT[:, kt, :], kt == 0, kt == KT - 1, [w.b(), hT.b()], [acc.b()])
                skip = self.cfg.get("s1_skip", "")
                if ("f" in skip and ot < 4) or ("q" in skip and 4 <= ot < 14) or ("r" in skip and ot >= 16):
                    continue
                if ot < 4:
                    self.cp("act", uT[:], acc[:], [acc.b()], [uT.b()])
                    for sub in range(4):
                        reg = pb[5][:, (sub % 2) * 256:(sub % 2) * 256 + 256]
                        self.mm(reg, uT[:, sub * 128:(sub + 1) * 128], self.csC[:], True, True,
                                [uT.b(), self.csC.b()], [pb[5].b()])
                        self.cp("dve", abt[:, sub, :], reg, [pb[5].b()], [abt.b()])
                    self.dma(self.AB[g][c0:c0 + W, ot * 256:(ot + 1) * 256].rearrange("(s p) c -> p s c", p=128), abt[:],
                             [abt.b()], [self.AB[g].b(ti)])
                elif ot < 14:
                    isk = ot >= 12
                    hd = ot - 12 if isk else ot - 4
                    gq = pr["kng"] if isk else pr["qng"]
                    q_n = qn[oi % 2]
                    q_r = qr[oi % 2]
                    self.act(sqh[:], acc[:], AF.Square, [acc.b()], [sqh.b()])
                    self.mm(pb[3][:], self.ones_b[:], sqh[:], True, True, [sqh.b(), self.ones_b.b()], [pb[3].b()])
                    self.act(rq[:], pb[3][:], AF.Sqrt, [pb[3].b(), self.eps6.b()], [rq.b()], scale=1.0 / 128, bias=self.eps6[:, 0:1])
                    self.recip(rq[:], rq[:], [rq.b()], [rq.b()])
                    self.stt("dve", q_n[:], acc[:], gq[:, 0:1], rq[:], ALU.mult, ALU.mult, [acc.b(), gq.b(), rq.b()], [q_n.b()])
                    if isk and not is_s:
                        for sub in range(4):
                            self.tr(pb[5][:, sub * 128:(sub + 1) * 128], q_n[:, sub * 128:(sub + 1) * 128], self.ident_f[:],
                                    [q_n.b(), self.ident_f.b()], [pb[5].b()])
                        self.cp("dve", ktok[:], pb[5][:].rearrange("p (s c) -> p s c", s=4), [pb[5].b()], [ktok.b()])
                        for sub in range(4):
                            tok = sub * 128
                            sq_, off = tok // self.TP, tok % self.TP
                            self.dma(self.nk[sq_, l, off:off + 128, hd * 128:(hd + 1) * 128], ktok[:, sub, :],
                                     [ktok.b()], [self.nk.b()])
                    if is_s:
                        self.mm(pb[4][:], self.prot_f[:], q_n[:], True, True, [self.prot_f.b(), q_n.b()], [pb[4].b()])
                        self.tt("pool", t1[:], q_n[:], cosb[:], ALU.mult, [q_n.b(), cosb.b()], [t1.b()])
                        self.tt("dve", t2[:], pb[4][:], sinb[:], ALU.mult, [pb[4].b(), sinb.b()], [t2.b()])
                        self.tt("pool", q_r[:], t1[:], t2[:], ALU.add, [t1.b(), t2.b()], [q_r.b()])
                    else:
                        self.cp("pool", q_r[:], q_n[:], [q_n.b()], [q_r.b()])
                    dst = self.KTs[g] if isk else self.QT[g]
                    self.dma(dst[hd, :, c0:c0 + W], q_r[:], [q_r.b()], [dst.b(ti)])
                else:
                    r = rwt[oi % 2]
                    self.cp("act" if oi % 2 else "dve", r[:], acc[:], [acc.b()], [r.b()])
                    self.dma(self.RW[g][(ot - 16) * 128:(ot - 15) * 128, c0:c0 + W], r[:], [r.b()], [self.RW[g].b(ti)])
                if oi == self.cfg.get("v_at", 13) and "v" not in skip:
                    for sub in range(4):
                        reg = pb[6][:, (sub % 2) * 256:(sub % 2) * 256 + 256]
                        for kt in range(KT):
                            self.mm(reg, hT[:, kt, sub * 128:(sub + 1) * 128], wv[:, kt, :], kt == 0, kt == KT - 1,
                                    [hT.b(), wv.b()], [pb[6].b()])
                        self.cp("act", vt[:, sub, :], reg, [pb[6].b()], [vt.b()])
                        if not is_s:
                            self.cp(self.cfg.get("vtf_eng", "dve"), vtf[:, sub, :], reg, [pb[6].b()], [vtf.b()])
                    self.dma(self.Vs[g][c0:c0 + W, :].rearrange("(s p) c -> p s c", p=128), vt[:], [vt.b()], [self.Vs[g].b(ti)])
                    if not is_s:
                        for sub in range(4):
                            tok = sub * 128
                            sq_, off = tok // self.TP, tok % self.TP
                            self.dma(self.nv[sq_, l, off:off + 128, :], vtf[:, sub, :], [vtf.b()], [self.nv.b()])
            self.barrier()

    def stage_fourier(self, l, g):
        self.P.stage = f"four_{l}_{g}"
        Tg, nseq, Tseq = self.groups[g]
        nch = Tseq // 128
        TW = min(512, Tseq)
        pb = self.pb
        allab = [self.AB[g].b(i) for i in range((Tg + 511) // 512)]
        with contextlib.ExitStack() as stk:
            ab = self.sb("fab", [128, nch, 1024], BF16, stk)
            ctb = [self.sb("fct", [128, nch, TW], BF16, stk) for _ in range(2)]
            nsb = [self.sb("fns", [128, nch, TW], BF16, stk) for _ in range(2)]
            fo = [self.sb("ffo", [128, TW], BF16, stk) for _ in range(2)]
            n = 0
            for s in range(nseq):
                self.dma(ab[:], self.AB[g][s * Tseq:(s + 1) * Tseq, :].rearrange("(c p) x -> p c x", p=128), allab, [ab.b()])
                for ti, t0 in enumerate(range(0, Tseq, TW)):
                    cb = ctb[ti % 2]
                    sbb = nsb[ti % 2]
                    self.dma(cb[:], self.c_ct[g][:, t0:t0 + TW].rearrange("(c p) t -> p c t", p=128), [self.c_ct[g].b()], [cb.b()])
                    self.dma(sbb[:], self.c_nst[g][:, t0:t0 + TW].rearrange("(c p) t -> p c t", p=128), [self.c_nst[g].b()], [sbb.b()])
                    for grp in range(4):
                        acc = pb[n % 4]
                        for c in range(nch):
                            self.mm(acc[:, 0:TW], ab[:, c, grp * 256:grp * 256 + 128], cb[:, c, :], c == 0, False,
                                    [ab.b(), cb.b()], [acc.b()])
                            self.mm(acc[:, 0:TW], ab[:, c, grp * 256 + 128:grp * 256 + 256], sbb[:, c, :], False, c == nch - 1,
                                    [ab.b(), sbb.b()], [acc.b()])
                        f = fo[n % 2]
                        self.cp("act" if n % 2 else "dve", f[:], acc[:, 0:TW], [acc.b()], [f.b()])
                        c0 = s * Tseq + t0
                        self.dma(self.MIXT[g][grp, :, c0:c0 + TW], f[:], [f.b()], [self.MIXT[g].b(c0 // 512)])
                        n += 1
            self.barrier()

    def stage_attn(self, l, g):
        self.P.stage = f"attn_{l}_{g}"
        Tg, nseq, Tseq = self.groups[g]
        is_s = (g == "s")
        Stot = Tseq + (PAST if is_s else 0)
        nck = Stot // 128
        QW = min(512, Tseq)
        pb = self.pb
        ntile = (Tg + 511) // 512
        scale = 128.0 ** -0.5
        with contextlib.ExitStack() as stk:
            kT = self.sb("akT", [128, NKV, Stot], BF16, stk)
            vv = self.sb("avv", [128, nck, 256], BF16, stk)
            qT = [self.sb("aqT", [128, Tseq], BF16, stk) for _ in range(2)]
            pT = [self.sb("apT", [128, QW], BF16, stk) for _ in range(3)]
            rden = self.sb("arden", [128, QW], F32, stk)
            ob = [self.sb("aob", [128, QW], BF16, stk) for _ in range(2)]
            if is_s:
                ckf = self.sb("ackf", [128, PAST // 128, 256], F32, stk)
                cvf = self.sb("acvf", [128, PAST // 128, 256], F32, stk)
            nq = 0
            bgj = self.bg_sched.pop(("attn", l, g), None)
            if bgj:
                self.bg_begin(stk, bgj, engs=("dve", "pool", "dve"))
            for s in range(nseq):
                c0s = s * Tseq
                for kv in range(NKV):
                    self.dma(kT[:, kv, 0:Tseq], self.KTs[g][kv, :, c0s:c0s + Tseq], [self.KTs[g].b(i) for i in range(ntile)], [kT.b()])
                self.dma(vv[:, 0:Tseq // 128, :], self.Vs[g][c0s:c0s + Tseq, :].rearrange("(c p) x -> p c x", p=128),
                         [self.Vs[g].b(i) for i in range(ntile)], [vv.b()])
                if is_s:
                    self.dma(ckf[:], self.ck[l].rearrange("(c p) x -> p c x", p=128), [self.ck.b()], [ckf.b()])
                    self.dma(cvf[:], self.cv[l].rearrange("(c p) x -> p c x", p=128), [self.cv.b()], [cvf.b()])
                    self.cp("pool", vv[:, Tseq // 128:nck, :], cvf[:], [cvf.b()], [vv.b()])
                    for kv in range(NKV):
                        bank = pb[kv]
                        for c in range(PAST // 128):
                            self.tr(bank[:, c * 128:(c + 1) * 128], ckf[:, c, kv * 128:(kv + 1) * 128], self.ident_f[:],
                                    [ckf.b(), self.ident_f.b()], [bank.b()])
                        self.cp("act", kT[:, kv, Tseq:Stot], bank[:, 0:PAST], [bank.b()], [kT.b()])
                for h in range(NH):
                    kv = h // (NH // NKV)
                    q = qT[h % 2]
                    self.dma(q[:], self.QT[g][h, :, c0s:c0s + Tseq], [self.QT[g].b(i) for i in range(ntile)], [q.b()])
                    for q0 in range(0, Tseq, QW):
                        self.bg_step()
                        oacc = pb[4 + nq % 2]
                        dacc = pb[6 + nq % 2]
                        def score(c):
                            sT = pb[c % 3]
                            p = pT[c % 3]
                            self.mm(sT[:, 0:QW], kT[:, kv, c * 128:(c + 1) * 128], q[:, q0:q0 + QW], True, True,
                                    [kT.b(), q.b()], [sT.b()])
                            self.act(p[:], sT[:, 0:QW], AF.Exp, [sT.b()], [p.b()], scale=scale)
                        pipe = self.cfg.get("attn_pipe", True)
                        if pipe:
                            score(0)
                        for c in range(nck):
                            if not pipe:
                                score(c)
                            elif c + 1 < nck:
                                score(c + 1)
                            p = pT[c % 3]
                            self.mm(oacc[:, 0:QW], vv[:, c, kv * 128:(kv + 1) * 128], p[:], c == 0, c == nck - 1,
                                    [vv.b(), p.b()], [oacc.b()])
                            self.mm(dacc[:, 0:QW], self.ones_b[:], p[:], c == 0, c == nck - 1,
                                    [self.ones_b.b(), p.b()], [dacc.b()])
                        self.recip(rden[:], dacc[:, 0:QW], [dacc.b()], [rden.b()])
                        o = ob[nq % 2]
                        self.tt("dve", o[:], oacc[:, 0:QW], rden[:], ALU.mult, [oacc.b(), rden.b()], [o.b()])
                        cc0 = c0s + q0
                        self.dma(self.MIXT[g][4 + h, :, cc0:cc0 + QW], o[:], [o.b()], [self.MIXT[g].b(cc0 // 512)])
                        nq += 1
            self.bg_end()
            self.barrier()

    def stage3(self, l, g, c0, X):
        self.P.stage = f"s3_{l}_{g}"
        W = 512
        v = 0 if g == "s" else 1
        pr = self.prm[l]
        mod = pr["mod"][v]
        ti = c0 // 512
        pb = self.pb
        with contextlib.ExitStack() as stk:
            xt = self.sb("xt3", [128, KT, W], F32, stk)
            mx = self.sb("mx3", [128, KT, W], BF16, stk)
            wts = [self.sb("wt3", [128, KT, 128], BF16, stk) for _ in range(3)]
            self.dma(xt[:], X[g][:, :, c0:c0 + W].rearrange("kt p t -> p kt t"), [X[g].b(ti)], [xt.b()])
            self.dma(mx[:], self.MIXT[g][:, :, c0:c0 + W].rearrange("kt p t -> p kt t"), [self.MIXT[g].b(ti)], [mx.b()])

            def load_w(ot):
                w = wts[ot % 3]
                self.dma(w[:], self.Wout_t[l, ot], [self.Wout_t.b((l, ot))], [w.b()])
            load_w(0)
            load_w(1)
            for ot in range(KT):
                if ot + 2 < KT:
                    load_w(ot + 2)
                w = wts[ot % 3]
                acc = pb[ot % 4]
                for kt in range(KT):
                    self.mm(acc[:], w[:, kt, :], mx[:, kt, :], kt == 0, kt == KT - 1, [w.b(), mx.b()], [acc.b()])
                self.stt("dve", xt[:, ot, :], acc[:], mod[:, 32 + ot:33 + ot], xt[:, ot, :], ALU.mult, ALU.add,
                         [acc.b(), xt.b(), xt.b(ot)], [xt.b(ot)])
            self.dma(X[g][:, :, c0:c0 + W].rearrange("kt p t -> p kt t"), xt[:], [xt.b(ot) for ot in range(KT)] + [xt.b()], [X[g].b(ti)])
            self.barrier()

    def stage4(self, l, g, c0, X, Y):
        self.P.stage = f"s4_{l}_{g}"
        W = 512
        Tg, nseq, Tseq = self.groups[g]
        v = 0 if g == "s" else 1
        pr = self.prm[l]
        mod = pr["mod"][v]
        ti = c0 // 512
        nti = (Tg + 511) // 512
        pb, pw = self.pb, self.pw
        FT = self.FT
        fcw, fcb, nfcw = pr["fcw"], pr["fcb"], pr["nfcw"]
        with contextlib.ExitStack() as stk:
            xt = self.sb("xt4", [128, KT, W + 2], F32, stk)
            hT = self.sb("hT4", [128, KT, W + 2], BF16, stk)
            actT = self.sb("act4", [128, FT, W], BF16, stk)
            wts = [self.sb("wt4", [128, KT, 128], BF16, stk) for _ in range(3)]
            wdn = [self.sb("wd4", [128, FT, 128], BF16, stk) for _ in range(2)]
            ua = self.sb("ua4", [128, W], F32, stk)
            ug = self.sb("ug4", [128, W], F32, stk)
            sg = self.sb("sg4", [128, W], F32, stk)
            self.dma(xt[:, :, 0:W], X[g][:, :, c0:c0 + W].rearrange("kt p t -> p kt t"), [X[g].b(ti)], [xt.b()])
            lzero = (c0 % Tseq == 0)
            rzero = ((c0 + W) % Tseq == 0)
            if lzero:
                self.memset("pool", xt[:, :, W:W + 1], 0.0, [xt.b()])
            else:
                self.dma(xt[:, :, W:W + 1], X[g][:, :, c0 - 1:c0].rearrange("kt p t -> p kt t"), [X[g].b(ti - 1)], [xt.b()],
                         allow_slow_non_contiguous=True)
            if rzero:
                self.memset("pool", xt[:, :, W + 1:W + 2], 0.0, [xt.b()])
            else:
                self.dma(xt[:, :, W + 1:W + 2], X[g][:, :, c0 + W:c0 + W + 1].rearrange("kt p t -> p kt t"), [X[g].b(ti + 1)], [xt.b()],
                         allow_slow_non_contiguous=True)
            self.norm_mod(xt, [(0, W), (W, W + 2)], pr[f"gs2_{v}"], lambda kt: mod[:, 48 + kt:49 + kt], hT, stk, [pb[7], pb[6]])
            if lzero:
                self.memset("pool", hT[:, :, W:W + 1], 0.0, [hT.b()])
            if rzero:
                self.memset("pool", hT[:, :, W + 1:W + 2], 0.0, [hT.b()])
            inner = [b for b in range(Tseq, W, Tseq)] if Tseq < W else []

            def load_w(j):
                i, half = j // 2, j % 2
                w = wts[j % 3]
                ot = i + half * FT
                self.dma(w[:], self.Wup_t[l, ot], [self.Wup_t.b((l, ot))], [w.b()])
            load_w(0)
            load_w(1)
            for j in range(2 * FT):
                if j + 2 < 2 * FT:
                    load_w(j + 2)
                i, half = j // 2, j % 2
                ot = i + half * FT
                w = wts[j % 3]
                wide = j % 3
                A, Bk = pb[2 * wide], pb[2 * wide + 1]
                for kt in range(KT):
                    self.mm(A[:], w[:, kt, :], hT[:, kt, 0:W], kt == 0, kt == KT - 1, [w.b(), hT.b()], [A.b()])
                for kt in range(KT):
                    self.mm(Bk[:, 0:2], w[:, kt, :], hT[:, kt, W:W + 2], kt == 0, kt == KT - 1, [w.b(), hT.b()], [Bk.b()])
                u = ug if half else ua
                w0 = fcw[:, 0, ot:ot + 1]
                w1 = fcw[:, 1, ot:ot + 1]
                w2 = fcw[:, 2, ot:ot + 1]
                self.act(u[:], A[:], AF.Identity, [A.b(), fcw.b(), fcb.b()], [u.b()], scale=w1, bias=fcb[:, ot:ot + 1])
                self.stt("dve", u[:, 1:W], A[:, 0:W - 1], w0, u[:, 1:W], ALU.mult, ALU.add, [A.b(), u.b()], [u.b()])
                self.stt("dve", u[:, 0:W - 1], A[:, 1:W], w2, u[:, 0:W - 1], ALU.mult, ALU.add, [A.b(), u.b()], [u.b()])
                self.stt("dve", u[:, 0:1], Bk[:, 0:1], w0, u[:, 0:1], ALU.mult, ALU.add, [Bk.b(), u.b()], [u.b()])
                self.stt("dve", u[:, W - 1:W], Bk[:, 1:2], w2, u[:, W - 1:W], ALU.mult, ALU.add, [Bk.b(), u.b()], [u.b()])
                for bnd in inner:
                    self.stt("dve", u[:, bnd - 1:bnd], A[:, bnd:bnd + 1], nfcw[:, 2, ot:ot + 1], u[:, bnd - 1:bnd], ALU.mult, ALU.add,
                             [A.b(), u.b(), nfcw.b()], [u.b()])
                    self.stt("dve", u[:, bnd:bnd + 1], A[:, bnd - 1:bnd], nfcw[:, 0, ot:ot + 1], u[:, bnd:bnd + 1], ALU.mult, ALU.add,
                             [A.b(), u.b(), nfcw.b()], [u.b()])
                if half:
                    self.act(sg[:], ug[:], AF.Silu, [ug.b()], [sg.b()])
                    self.tt("pool", actT[:, i, :], sg[:], ua[:], ALU.mult, [sg.b(), ua.b()], [actT.b()])
            self.dma(wdn[0][:], self.Wdn_t[l, 0], [self.Wdn_t.b((l, 0))], [wdn[0].b()])
            for ot in range(KT):
                if ot + 1 < KT:
                    self.dma(wdn[(ot + 1) % 2][:], self.Wdn_t[l, ot + 1], [self.Wdn_t.b((l, ot + 1))], [wdn[(ot + 1) % 2].b()])
                w = wdn[ot % 2]
                acc = pb[6 + ot % 2]
                for kt in range(FT):
                    self.mm(acc[:], w[:, kt, :], actT[:, kt, :], kt == 0, kt == FT - 1, [w.b(), actT.b()], [acc.b()])
                self.stt("dve", xt[:, ot, 0:W], acc[:], mod[:, 80 + ot:81 + ot], xt[:, ot, 0:W], ALU.mult, ALU.add,
                         [acc.b(), xt.b(), xt.b(ot)], [xt.b(ot)])
            self.dma(Y[g][:, :, c0:c0 + W].rearrange("kt p t -> p kt t"), xt[:, :, 0:W], [xt.b(ot) for ot in range(KT)] + [xt.b()], [Y[g].b(ti)])
            self.barrier()

    def stage5(self, g, c0, X):
        self.P.stage = f"s5_{g}"
        W = 512
        ti = c0 // 512
        pb = self.pb
        out = self.ys if g == "s" else self.yp
        with contextlib.ExitStack() as stk:
            xt = self.sb("xt5", [128, KT, W], F32, stk)
            xn = self.sb("xn5", [128, KT, W], F32, stk)
            sq = [self.sb("sq5", [128, W], BF16, stk) for _ in range(2)]
            rstd = self.sb("rstd5", [128, W], F32, stk)
            yt = [self.sb("yt5", [128, D], F32, stk) for _ in range(2)]
            self.dma(xt[:], X[g][:, :, c0:c0 + W].rearrange("kt p t -> p kt t"), [X[g].b(ti)], [xt.b()])
            for kt in range(KT):
                s = sq[kt % 2]
                self.act(s[:], xt[:, kt, :], AF.Square, [xt.b()], [s.b()])
                self.mm(pb[7][:], self.ones_b[:], s[:], kt == 0, kt == KT - 1, [s.b(), self.ones_b.b()], [pb[7].b()])
            self.act(rstd[:], pb[7][:], AF.Sqrt, [pb[7].b(), self.eps6.b()], [rstd.b()], scale=1.0 / D, bias=self.eps6[:, 0:1])
            self.recip(rstd[:], rstd[:], [rstd.b()], [rstd.b()])
            for kt in range(KT):
                self.stt("dve", xn[:, kt, :], xt[:, kt, :], self.fng[:, kt:kt + 1], rstd[:], ALU.mult, ALU.mult,
                         [xt.b(), rstd.b(), self.fng.b()], [xn.b()])
            for sub in range(4):
                y = yt[sub % 2]
                for q in range(4):
                    bank = pb[q]
                    for j in range(4):
                        kt = q * 4 + j
                        self.tr(bank[:, j * 128:(j + 1) * 128], xn[:, kt, sub * 128:(sub + 1) * 128], self.ident_f[:],
                                [xn.b(), self.ident_f.b()], [bank.b()])
                    self.cp("act" if q % 2 else "dve", y[:, q * 512:(q + 1) * 512], bank[:], [bank.b()], [y.b()])
                self.dma(out[c0 + sub * 128:c0 + (sub + 1) * 128, :], y[:], [y.b()], [out.b()])
            self.barrier()

    def stage_rwkv(self, l, g):
        self.P.stage = f"rwkv_{l}_{g}"
        Tg, nseq, Tseq = self.groups[g]
        is_s = (g == "s")
        T = Tseq
        nch = T // 128
        NB = 4
        pr = self.prm[l]
        pb = self.pb
        ntile = (Tg + 511) // 512
        rwall = [self.RW[g].b(i) for i in range(ntile)]
        conv = pr["rwconv"]
        CW = min(512, T)
        with contextlib.ExitStack() as stk:
            sbt = lambda name, shape, dt: self.sb(name, shape, dt, stk)
            w2e = sbt("w2e", [65, 2, 512], F32)
            a2f = sbt("a2f", [128, 2, 512], F32)
            g2f = sbt("g2f", [128, 512], F32)
            self.dma(w2e[0:64, :, :], self.rw_w2[l].rearrange("d k c -> k d c"), [self.rw_w2.b()], [w2e.b()])
            self.dma(w2e[64:65, :, :], self.rw_w0[l:l + 1], [self.rw_w0.b()], [w2e.b()])
            self.dma(a2f[64:128, :, :], self.rw_a2[l].rearrange("d k c -> k d c"), [self.rw_a2.b()], [a2f.b()])
            self.dma(g2f[:], self.rw_g2[l], [self.rw_g2.b()], [g2f.b()])
            t12 = sbt("t12", [128, T], F32)
            tw = sbt("tw", [65, T], F32)
            sg = sbt("sg", [128, T], F32)
            rc = sbt("rc", [128, T], F32)
            kkn = sbt("kkn", [128, T], F32)
            bA = [sbt("bA", [128, T], BF16) for _ in range(2)]
            KM = [sbt("KM", [128, T], BF16) for _ in range(2)]
            bon = sbt("bon", [128, T], F32)
            yacc = sbt("yacc", [128, T], F32)
            s1 = sbt("scr1", [128, T + 2], F32)
            s2 = sbt("scr2", [128, T], F32)
            s3 = sbt("scr3", [128, T], F32)
            al = sbt("al", [128, T], BF16)
            be = sbt("be", [128, T], BF16)
            ka_ = sbt("kap", [128, T], BF16)
            rt = sbt("rt", [128, T], BF16)
            VtmP = sbt("VtmP", [128, nch, 2, 128], BF16)
            BTt = sbt("BTt", [128, nch, 128], BF16)
            KTt = sbt("KTt", [128, nch, 128], BF16)
            PL = sbt("PL", [128, nch], F32)
            sigT = [sbt("sigT", [128, 128], F32) for _ in range(2)]
            Ptmp = [sbt("Ptmp", [128, 3, 256], F32) for _ in range(2)]
            NI = NB * 2
            Xa = sbt("Xa", [128, NI, 128], BF16)
            XTa = sbt("XTa", [128, NI, 128], BF16)
            Xb = sbt("Xb", [128, NI, 128], BF16)
            XTb = sbt("XTb", [128, NI, 128], BF16)
            Pf = sbt("Pf", [128, NI, 128], F32)
            Pfin = sbt("Pfin", [128, NI, 128], BF16)
            MakT = sbt("MakT", [128, NI, 128], BF16)
            Wbr = sbt("Wbr", [128, NI, 128], BF16)
            Wkr = sbt("Wkr", [128, NI, 128], BF16)
            Sf = sbt("Sf", [128, 128], F32)
            Sb = sbt("Sb", [128, 128], BF16)
            Sld = sbt("Sld", [128, 128], F32)
            RHSs = sbt("RHSs", [128, 128], BF16)
            Upad = sbt("Upad", [128, 2, 128], BF16)
            Ufull = sbt("Ufull", [128, 128], BF16)
            tS = sbt("tS", [128, 128], F32)
            sq = sbt("sqr", [128, CW], BF16)
            rstd = sbt("rstdr", [128, CW], F32)
            ob = [sbt("obr", [128, CW], BF16) for _ in range(2)]

            bgj = self.bg_sched.pop(("rwkv", l, g), None)
            if bgj:
                self.bg_begin(stk, bgj, engs=("act", "dve", "pool", "act", "dve"))
            self.memset("pool", VtmP[:], 0.0, [VtmP.b()])
            self.memset("pool", Upad[:], 0.0, [Upad.b()])
            self.memset("pool", tw[64:65, :], 1.0, [tw.b()])
            self.memset("pool", s1[:, 0:1], 0.0, [s1.b()])
            self.memset("pool", s1[:, T + 1:T + 2], 0.0, [s1.b()])

            def conv_tile(tile_idx, c0s, dst):
                self.dma(s1[:, 1:T + 1], self.RW[g][tile_idx * 128:(tile_idx + 1) * 128, c0s:c0s + T], rwall, [s1.b()])
                w = lambda j: conv[:, j * 14 + tile_idx:j * 14 + tile_idx + 1]
                self.act(dst[:], s1[:, 1:T + 1], AF.Copy, [s1.b(), conv.b()], [dst.b()], scale=w(1))
                self.stt("dve", dst[:], s1[:, 0:T], w(0), dst[:], ALU.mult, ALU.add, [s1.b(), dst.b()], [dst.b()])
                self.stt("dve", dst[:], s1[:, 2:T + 2], w(2), dst[:], ALU.mult, ALU.add, [s1.b(), dst.b()], [dst.b()])

            for s in range(nseq):
                c0s = s * T
                conv_tile(12, c0s, t12)
                self.act(tw[0:64, :], t12[0:64, :], AF.Tanh, [t12.b()], [tw.b()])
                conv_tile(13, c0s, sg)
                self.act(sg[:], sg[:], AF.Sigmoid, [sg.b()], [sg.b()])
                for ct in range(4):
                    self.bg_step()
                    conv_tile(ct, c0s, rc)
                    conv_tile(4 + ct, c0s, s2)
                    self.ts("pool", s3[:], s2[:], pr["kk"][:, ct:ct + 1], None, ALU.mult, None, [s2.b(), pr["kk"].b()], [s3.b()])
                    for x0 in range(0, T, CW):
                        bank = pb[(x0 // CW) % 2]
                        self.act(sq[:], s3[:, x0:x0 + CW], AF.Square, [s3.b()], [sq.b()])
                        self.mm(bank[:, 0:CW], self.bones_b[:], sq[:], True, True, [self.bones_b.b(), sq.b()], [bank.b()])
                        self.act(rstd[:], bank[:, 0:CW], AF.Sqrt, [bank.b(), self.eps12.b()], [rstd.b()], bias=self.eps12[:, 0:1])
                        self.recip(rstd[:], rstd[:], [rstd.b()], [rstd.b()])
                        self.tt("dve", kkn[:, x0:x0 + CW], s3[:, x0:x0 + CW], rstd[:], ALU.mult, [s3.b(), rstd.b()], [kkn.b()])
                    for d in range(2):
                        for x0 in range(0, T, CW):
                            bank = pb[2 + (x0 // CW) % 2]
                            self.mm(bank[:, 0:CW], a2f[64:128, d, ct * 128:(ct + 1) * 128], t12[64:128, x0:x0 + CW], True, True,
                                    [a2f.b(), t12.b()], [bank.b()])
                            self.act(s1[:, 1 + x0:1 + x0 + CW], bank[:, 0:CW], AF.Sigmoid, [bank.b(), pr["a0"].b()], [s1.b()],
                                     bias=pr["a0"][:, d * 4 + ct:d * 4 + ct + 1])
                        self.tt("pool", bA[d][:], kkn[:], s1[:, 1:T + 1], ALU.mult, [kkn.b(), s1.b()], [bA[d].b()])
                        self.ts("dve", s1[:, 1:T + 1], s1[:, 1:T + 1], pr["ka"][:, ct:ct + 1], pr["c1"][:, ct:ct + 1], ALU.mult, ALU.add,
                                [s1.b(), pr["ka"].b(), pr["c1"].b()], [s1.b()])
                        self.tt("dve", KM[d][:], s1[:, 1:T + 1], s2[:], ALU.mult, [s1.b(), s2.b()], [KM[d].b()])
                    conv_tile(8 + ct, c0s, s3)
                    self.tt("pool", s2[:], KM[0][:], KM[1][:], ALU.add, [KM[0].b(), KM[1].b()], [s2.b()])
                    self.stt("dve", s2[:], rc[:], pr["rk"][:, ct:ct + 1], s2[:], ALU.mult, ALU.mult, [rc.b(), s2.b(), pr["rk"].b()], [s2.b()])
                    for x0 in range(0, T, CW):
                        bank = pb[(x0 // CW) % 2]
                        self.mm(bank[:, 0:CW], self.bones_f[:], s2[:, x0:x0 + CW], True, True, [self.bones_f.b(), s2.b()], [bank.b()])
                        self.tt("dve", bon[:, x0:x0 + CW], bank[:, 0:CW], s3[:, x0:x0 + CW], ALU.mult, [bank.b(), s3.b()], [bon.b()])
                    for c in range(nch):
                        bank = pb[2 + c % 2]
                        self.tr(bank[:, 0:128], s3[:, c * 128:(c + 1) * 128], self.ident_f[:], [s3.b(), self.ident_f.b()], [bank.b()])
                        for hh in range(2):
                            self.cp("act" if hh else "dve", VtmP[:, c, hh, hh * 64:(hh + 1) * 64], bank[:, hh * 64:(hh + 1) * 64],
                                    [bank.b()], [VtmP.b()])
                    for d in range(2):
                        self.rwkv_dir(l, g, s, ct, d, T, nch, NB, stk, locals())
                    for x0 in range(0, T, CW):
                        bank = pb[(x0 // CW) % 2]
                        bank2 = pb[2 + (x0 // CW) % 2]
                        bank3 = pb[4 + (x0 // CW) % 2]
                        ys_ = yacc[:, x0:x0 + CW]
                        self.mm(bank[:, 0:CW], self.bones_f[:], ys_, True, True, [self.bones_f.b(), yacc.b()], [bank.b()])
                        self.stt("dve", s2[:, x0:x0 + CW], bank[:, 0:CW], -1.0 / 64, ys_, ALU.mult, ALU.add, [bank.b(), yacc.b()], [s2.b()])
                        self.act(sq[:], s2[:, x0:x0 + CW], AF.Square, [s2.b()], [sq.b()])
                        self.mm(bank2[:, 0:CW], self.bones_b[:], sq[:], True, True, [self.bones_b.b(), sq.b()], [bank2.b()])
                        self.act(rstd[:], bank2[:, 0:CW], AF.Sqrt, [bank2.b(), self.epsgn.b()], [rstd.b()], scale=1.0 / 64, bias=self.epsgn[:, 0:1])
                        self.recip(rstd[:], rstd[:], [rstd.b()], [rstd.b()])
                        self.tt("dve", s2[:, x0:x0 + CW], s2[:, x0:x0 + CW], rstd[:], ALU.mult, [s2.b(), rstd.b()], [s2.b()])
                        self.ts("dve", s2[:, x0:x0 + CW], s2[:, x0:x0 + CW], pr["lng"][:, ct:ct + 1], pr["lnb"][:, ct:ct + 1], ALU.mult, ALU.add,
                                [s2.b(), pr["lng"].b(), pr["lnb"].b()], [s2.b()])
                        self.tt("pool", s2[:, x0:x0 + CW], s2[:, x0:x0 + CW], bon[:, x0:x0 + CW], ALU.add, [s2.b(), bon.b()], [s2.b()])
                        self.mm(bank3[:, 0:CW], g2f[:, ct * 128:(ct + 1) * 128], sg[:, x0:x0 + CW], True, True, [g2f.b(), sg.b()], [bank3.b()])
                        o = ob[(x0 // CW) % 2]
                        self.tt("dve", o[:], bank3[:, 0:CW], s2[:, x0:x0 + CW], ALU.mult, [bank3.b(), s2.b()], [o.b()])
                        cc0 = c0s + x0
                        self.dma(self.MIXT[g][12 + ct, :, cc0:cc0 + CW], o[:], [o.b()], [self.MIXT[g].b(cc0 // 512)])
            self.bg_end()
            self.barrier()

    def rwkv_dir(self, l, g, s, ct, d, T, nch, NB, stk, L):
        pr = self.prm[l]
        pb = self.pb
        is_s = (g == "s")
        tw, w2e, sigT, Ptmp, kkn, bA, KM, rc = L["tw"], L["w2e"], L["sigT"], L["Ptmp"], L["kkn"], L["bA"], L["KM"], L["rc"]
        al, be, ka_, rt, PL = L["al"], L["be"], L["ka_"], L["rt"], L["PL"]
        BTt, KTt, VtmP = L["BTt"], L["KTt"], L["VtmP"]
        Xa, XTa, Xb, XTb, Pf, Pfin, MakT, Wbr, Wkr = (L[k] for k in ("Xa", "XTa", "Xb", "XTb", "Pf", "Pfin", "MakT", "Wbr", "Wkr"))
        Sf, Sb, Sld, RHSs, Upad, Ufull, tS, yacc = (L[k] for k in ("Sf", "Sb", "Sld", "RHSs", "Upad", "Ufull", "tS", "yacc"))
        masks = self.masks
        m_n, m_nt, m_s, m_i = ((4, 5, 0, 2) if d == 0 else (5, 4, 1, 3))
        for cp_ in range(0, nch, 2):
            cs = [c for c in (cp_, cp_ + 1) if c < nch]
            bank = pb[(cp_ // 2) % 2]
            for j, c in enumerate(cs):
                sgt = sigT[c % 2]
                bk2 = pb[2 + c % 2]
                self.mm(bk2[:, 0:128], tw[0:65, c * 128:(c + 1) * 128], w2e[0:65, d, ct * 128:(ct + 1) * 128], True, True,
                        [tw.b(), w2e.b()], [bk2.b()])
                self.act(sgt[:], bk2[:, 0:128], AF.Sigmoid, [bk2.b()], [sgt.b()])
                self.mm(bank[:, j * 256:(j + 1) * 256], sgt[:], self.tri2[:, d, :], True, True, [sgt.b(), self.tri2.b()], [bank.b()])
            n = len(cs)
            pt = Ptmp[(cp_ // 2) % 2]
            bv = bank[:, 0:n * 256].rearrange("p (j x) -> p j x", j=n)
            cols = slice(cp_ * 128, (cp_ + n) * 128)
            v3 = lambda tb_: tb_[:, cols].rearrange("p (j x) -> p j x", j=n)
            self.act(pt[:, 0, 0:n * 128].rearrange("p (j x) -> p j x", j=n), bv[:, :, 0:128], AF.Exp, [bank.b()], [pt.b()])
            self.act(pt[:, 1, 0:n * 128].rearrange("p (j x) -> p j x", j=n), bv[:, :, 0:128], AF.Exp, [bank.b()], [pt.b()], scale=-1.0)
            self.act(pt[:, 2, 0:n * 128].rearrange("p (j x) -> p j x", j=n), bv[:, :, 128:256], AF.Exp, [bank.b()], [pt.b()])
            for j, c in enumerate(cs):
                col = (127 if d == 0 else 0)
                self.cp("pool", PL[:, c:c + 1], pt[:, 0, j * 128 + col:j * 128 + col + 1], [pt.b()], [PL.b()])
            w_ = n * 128
            self.tt("dve", al[:, cols], pt[:, 2, 0:w_], kkn[:, cols], ALU.mult, [pt.b(), kkn.b()], [al.b()])
            self.tt("pool", be[:, cols], pt[:, 1, 0:w_], bA[d][:, cols], ALU.mult, [pt.b(), bA[d].b()], [be.b()])
            self.tt("dve", ka_[:, cols], pt[:, 1, 0:w_], KM[d][:, cols], ALU.mult, [pt.b(), KM[d].b()], [ka_.b()])
            self.tt("pool", rt[:, cols], pt[:, 0, 0:w_], rc[:, cols], ALU.mult, [pt.b(), rc.b()], [rt.b()])
        self.bg_step()
        for c in range(nch):
            bank = pb[c % 2]
            self.mm(bank[:, 0:128], be[:, c * 128:(c + 1) * 128], self.ident_b[:], True, True, [be.b(), self.ident_b.b()], [bank.b()])
            self.mm(bank[:, 128:256], ka_[:, c * 128:(c + 1) * 128], self.ident_b[:], True, True, [ka_.b(), self.ident_b.b()], [bank.b()])
            self.cp("act", BTt[:, c, :], bank[:, 0:128], [bank.b()], [BTt.b()])
            self.cp("dve", KTt[:, c, :], bank[:, 128:256], [bank.b()], [KTt.b()])
        self.bg_step()
        if is_s:
            self.memset("pool", Sld[:], 0.0, [Sld.b()])
            for hh in range(2):
                self.dma(Sld[hh * 64:(hh + 1) * 64, hh * 64:(hh + 1) * 64], self.st0[l, d, ct * 2 + hh], [self.st0.b()], [Sld.b()])
            self.tr(pb[7][:, 0:128], Sld[:], self.ident_f[:], [Sld.b(), self.ident_f.b()], [pb[7].b()])
            self.cp("dve", Sf[:], pb[7][:, 0:128], [pb[7].b()], [Sf.b()])
        else:
            self.memset("pool", Sf[:], 0.0, [Sf.b()])
        self.cp("act", Sb[:], Sf[:], [Sf.b()], [Sb.b()])
        order = list(range(nch)) if d == 0 else list(range(nch - 1, -1, -1))
        for b0 in range(0, nch, NB):
            batch = order[b0:b0 + NB]
            nb_ = len(batch)
            grps = [[(c, hh) for c in batch] for hh in range(2)]
            ngrp = 2
            bk = [0]

            def nbank():
                bk[0] += 1
                return pb[bk[0] % 4]
            m4 = self.masks4
            for gi in range(ngrp):
                g4 = slice(gi * 4, gi * 4 + nb_)
                gin = grps[gi]

                def ops(c, hh):
                    rows = slice(hh * 64, (hh + 1) * 64)
                    cc = slice(c * 128, (c + 1) * 128)
                    return al[rows, cc], be[rows, cc], ka_[rows, cc], rt[rows, cc]
                rb = [al.b(), be.b(), ka_.b(), rt.b()]
                for (dst, mi, sel) in ((Xa, m_n, (1, 0)), (XTa, m_nt, (0, 1)), (MakT, m_s, (2, 0)), (Wbr, m_i, (1, 3)), (Wkr, m_i, (2, 3))):
                    bank = nbank()
                    for j, (c, hh) in enumerate(gin):
                        o_ = ops(c, hh)
                        self.mm(bank[:, j * 128:(j + 1) * 128], o_[sel[0]], o_[sel[1]], True, True, rb, [bank.b()])
                    self.tt("dve", dst[:, g4, :], bank[:, 0:nb_ * 128].rearrange("p (j x) -> p j x", j=nb_), m4[:, mi, 0:nb_, :], ALU.mult,
                            [bank.b(), m4.b()], [dst.b(gi)])
                self.tt("pool", Pf[:, g4, :], Xa[:, g4, :], self.ident4[:, 0:nb_, :], ALU.add, [Xa.b(gi), self.ident4.b()], [Pf.b(gi)])
                self.cp("act", Pfin[:, g4, :], Pf[:, g4, :], [Pf.b(gi)], [Pfin.b(gi)])
            X, XT, Xn, XTn = Xa, XTa, Xb, XTb
            for lev in range(6):
                for gi in range(ngrp):
                    g4 = slice(gi * 4, gi * 4 + nb_)
                    if lev < 5:
                        bA_ = nbank()
                        for j in range(nb_):
                            ii = gi * 4 + j
                            self.mm(bA_[:, j * 128:(j + 1) * 128], XT[:, ii, :], X[:, ii, :], True, True, [X.b(gi), XT.b(gi)], [bA_.b()])
                    bB_ = nbank()
                    for j in range(nb_):
                        ii = gi * 4 + j
                        self.mm(bB_[:, j * 128:(j + 1) * 128], X[:, ii, :], XT[:, ii, :], True, True, [X.b(gi), XT.b(gi)], [bB_.b()])
                    if lev < 5:
                        self.cp("act", Xn[:, g4, :], bA_[:, 0:nb_ * 128].rearrange("p (j x) -> p j x", j=nb_), [bA_.b()], [Xn.b(gi)])
                    self.cp("dve", XTn[:, g4, :], bB_[:, 0:nb_ * 128].rearrange("p (j x) -> p j x", j=nb_), [bB_.b()], [XTn.b(gi)])
                    bC_ = nbank()
                    for j in range(nb_):
                        ii = gi * 4 + j
                        self.mm(bC_[:, j * 128:(j + 1) * 128], XTn[:, ii, :], Pfin[:, ii, :], True, True, [XTn.b(gi), Pfin.b(gi)], [bC_.b()])
                    self.tt("dve", Pf[:, g4, :], Pf[:, g4, :], bC_[:, 0:nb_ * 128].rearrange("p (j x) -> p j x", j=nb_), ALU.add, [Pf.b(gi), bC_.b()], [Pf.b(gi)])
                    self.cp("act", Pfin[:, g4, :], Pf[:, g4, :], [Pf.b(gi)], [Pfin.b(gi)])
                X, XT, Xn, XTn = Xn, XTn, X, XT
            self.bg_step()
            for bi, c in enumerate(batch):
                cc = slice(c * 128, (c + 1) * 128)
                p_rhs, p_u, p_y, p_s = pb[4], pb[5], pb[6], pb[7]
                for hh in range(2):
                    ii = hh * 4 + bi
                    rows = slice(hh * 64, (hh + 1) * 64)
                    vcols = slice(hh * 64, (hh + 1) * 64)
                    self.mm(p_rhs[:, vcols], al[rows, cc], Sb[rows, vcols], True, False, [al.b(), Sb.b()], [p_rhs.b()])
                    self.mm(p_rhs[:, vcols], MakT[:, ii, :], VtmP[:, c, hh, vcols], False, True, [MakT.b(ii // 4), VtmP.b()], [p_rhs.b()])
                self.cp("act", RHSs[:], p_rhs[:, 0:128], [p_rhs.b()], [RHSs.b()])
                for hh in range(2):
                    ii = hh * 4 + bi
                    vcols = slice(hh * 64, (hh + 1) * 64)
                    self.mm(p_u[:, vcols], Pfin[:, ii, :], RHSs[:, vcols], True, True, [Pfin.b(ii // 4), RHSs.b()], [p_u.b()])
                self.ts("dve", Ufull[:], p_u[:, 0:128], -1.0, None, ALU.mult, None, [p_u.b()], [Ufull.b()])
                for hh in range(2):
                    vcols = slice(hh * 64, (hh + 1) * 64)
                    self.cp("act" if hh else "dve", Upad[:, hh, vcols], Ufull[:, vcols], [Ufull.b()], [Upad.b()])
                self.mm(p_y[:, 0:128], Sb[:], rt[:, cc], True, False, [Sb.b(), rt.b()], [p_y.b()])
                for hh in range(2):
                    ii = hh * 4 + bi
                    self.mm(p_y[:, 0:128], Upad[:, hh, :], Wbr[:, ii, :], False, False, [Upad.b(), Wbr.b(ii // 4)], [p_y.b()])
                    self.mm(p_y[:, 0:128], VtmP[:, c, hh, :], Wkr[:, ii, :], False, hh == 1, [VtmP.b(), Wkr.b(ii // 4)], [p_y.b()])
                if d == 0:
                    self.cp("act", yacc[:, cc], p_y[:, 0:128], [p_y.b()], [yacc.b()])
                else:
                    self.tt("dve", yacc[:, cc], yacc[:, cc], p_y[:, 0:128], ALU.add, [yacc.b(), p_y.b()], [yacc.b()])
                self.mm(p_s[:, 0:128], BTt[:, c, :], Ufull[:], True, False, [BTt.b(), Ufull.b()], [p_s.b()])
                for hh in range(2):
                    self.mm(p_s[:, 0:128], KTt[:, c, :], VtmP[:, c, hh, :], False, hh == 1, [KTt.b(), VtmP.b()], [p_s.b()])
                self.stt("dve", tS[:], p_s[:, 0:128], PL[:, c:c + 1], self.bones_f[:], ALU.mult, ALU.mult,
                         [p_s.b(), PL.b(), self.bones_f.b()], [tS.b()])
                self.stt("dve", Sf[:], Sf[:], PL[:, c:c + 1], tS[:], ALU.mult, ALU.add, [Sf.b(), PL.b(), tS.b()], [Sf.b()])
                self.cp("act", Sb[:], Sf[:], [Sf.b()], [Sb.b()])
        self.bg_step()
        if not is_s:
            self.tr(pb[7][:, 0:128], Sf[:], self.ident_f[:], [Sf.b(), self.ident_f.b()], [pb[7].b()])
            self.cp("dve", Sld[:], pb[7][:, 0:128], [pb[7].b()], [Sld.b()])
            for hh in range(2):
                self.dma(self.ns[s, l, d, ct * 2 + hh], Sld[hh * 64:(hh + 1) * 64, hh * 64:(hh + 1) * 64], [Sld.b()], [self.ns.b()])

    def tiles(self):
        out = []
        for g, (Tg, nseq, Tseq) in self.groups.items():
            for c0 in range(0, Tg, 512):
                out.append((g, c0))
        return out

    def build(self):
        stages = self.cfg.get("stages", "all")
        self.declare()
        self.setup_consts()
        if stages != "nocast":
            self.cast_weights()
        if stages == "s0a":
            self.P.emit()
            return self.nc
        self.setup_params()
        if stages == "s0b":
            self.P.emit()
            return self.nc
        self.to_feature_major()
        if stages in ("s0c", "nocast"):
            self.P.emit()
            return self.nc
        X, Y = self.XT, self.XTB
        for l in range(self.DEPTH):
            for (g, c0) in self.tiles():
                if g in self.cfg.get("s1_groups", "sp"):
                    self.stage1_x(l, g, c0, X)
            if stages == "s1":
                break
            for g in self.groups:
                self.stage_fourier(l, g)
                self.stage_attn(l, g)
                if stages == "s2fa":
                    self.zero_rwkv(g)
                else:
                    self.stage_rwkv(l, g)
            if stages == "s2":
                break
            for (g, c0) in self.tiles():
                self.stage3(l, g, c0, X)
            for (g, c0) in self.tiles():
                self.stage4(l, g, c0, X, Y)
            X, Y = Y, X
        if stages in ("all", "s2fa"):
            for (g, c0) in self.tiles():
                self.stage5(g, c0, X)
        self.P.emit()
        return self.nc


def make_in_maps(inputs, cfg, ncores):
    TS, TP, NPS, DEPTH = cfg["TS"], cfg["TP"], cfg["NPS"], cfg["DEPTH"]
    consts = host_consts(TS, TP)
    f = lambda a: np.ascontiguousarray(np.asarray(a, dtype=np.float32))
    shared = {}
    for k in ("w_ada", "b_ada", "norm1_g", "norm2_g", "w_in", "w_out", "q_norm_g", "k_norm_g", "rw_conv",
              "rw_w0", "rw_w2", "rw_a0", "rw_a2", "rw_g2", "rw_kk", "rw_ka", "rw_lnx_g", "rw_lnx_b",
              "ffn_up", "ffn_conv_w", "ffn_conv_b", "ffn_down", "final_norm_g"):
        shared[k] = f(inputs[k])
    shared["rw_rk"] = f(inputs["rw_rk"]).reshape(DEPTH, 512)
    shared.update(consts)
    maps = []
    for b in range(ncores):
        m = dict(shared)
        m["xs"] = f(inputs["x_sample"][b])
        m["xp"] = f(inputs["x_prompt"][NPS * b:NPS * (b + 1)]).reshape(NPS * TP, D)
        m["ck"] = f(inputs["cache_attn_k"][b]).reshape(DEPTH, PAST, 256)
        m["cv"] = f(inputs["cache_attn_v"][b]).reshape(DEPTH, PAST, 256)
        m["st0"] = f(inputs["state_rwkv"][b])
        m["cc"] = np.stack([f(inputs["c"][b]), f(inputs["c_ctx"])], 0)
        maps.append(m)
    return maps


_CACHE = {}


def kernel(**inputs):
    xs = np.asarray(inputs["x_sample"])
    xp = np.asarray(inputs["x_prompt"])
    ncores = xs.shape[0]
    TS, TP = xs.shape[1], xp.shape[1]
    NPS = xp.shape[0] // ncores
    DEPTH = np.asarray(inputs["w_in"]).shape[0]
    DFF = np.asarray(inputs["ffn_down"]).shape[1]
    cfg = dict(TS=TS, TP=TP, NPS=NPS, DEPTH=DEPTH, DFF=DFF, stages="all")
    key = (TS, TP, NPS, DEPTH, DFF)
    if key not in _CACHE:
        kb = KB(cfg)
        _CACHE[key] = kb.build()
    nc = _CACHE[key]
    maps = make_in_maps(inputs, cfg, ncores)
    decl = set()
    for alloc in nc.allocations:
        if isinstance(alloc, mybir.MemoryLocationSet) and alloc.kind == "ExternalInput":
            decl.add(alloc.memorylocations[0].name)
    maps = [{k: v for k, v in m.items() if k in decl} for m in maps]
    res = run_bass_kernel_spmd(nc, maps, core_ids=list(range(ncores)))
    r = res.results
    y_sample = np.stack([np.asarray(r[b]["ys"], np.float32) for b in range(ncores)], 0)
    y_prompt = np.concatenate([np.asarray(r[b]["yp"], np.float32).reshape(NPS, TP, D) for b in range(ncores)], 0)
    nk = np.concatenate([np.asarray(r[b]["nk"], np.float32).reshape(NPS, DEPTH, TP, NKV, 128) for b in range(ncores)], 0)
    nv = np.concatenate([np.asarray(r[b]["nv"], np.float32).reshape(NPS, DEPTH, TP, NKV, 128) for b in range(ncores)], 0)
    ns = np.concatenate([np.asarray(r[b]["ns"], np.float32) for b in range(ncores)], 0)
    return (y_prompt, y_sample, nk, nv, ns)
```

```python
import contextlib
import math
import numpy as np
import ml_dtypes
import concourse.bass as bass
import concourse.mybir as mybir
from concourse.bass_utils import run_bass_kernel_spmd

F32 = mybir.dt.float32
BF16 = mybir.dt.bfloat16
ALU = mybir.AluOpType
AF = mybir.ActivationFunctionType
AX = mybir.AxisListType

EPOCH = 8000
RING = 8


class Buf:
    __slots__ = ("wc", "wd", "rc", "rd", "excl")

    def __init__(self, excl=False):
        self.excl = excl
        self.wc = {}
        self.wd = {}
        self.rc = {}
        self.rd = {}


class Op:
    __slots__ = ("eng", "fn", "waits", "done", "is_dma", "stage")

    def __init__(self, eng, fn, is_dma):
        self.stage = None
        self.eng = eng
        self.fn = fn
        self.waits = {}
        self.done = None
        self.is_dma = is_dma


class Prog:
    ENGS = ("pe", "act", "dve", "pool", "sp")

    def __init__(self, nc):
        self.nc = nc
        self.q = {e: [] for e in self.ENGS}
        self.cnt = {e: 0 for e in self.ENGS}
        self.dcnt = {e: 0 for e in self.ENGS}
        self.sems = {}
        self.semkeys = []
        self.last_dma = {}
        self.pending = {}

    def _semkey(self, key):
        if key not in self.sems:
            self.sems[key] = None
            self.semkeys.append(key)
        return key

    def _add_dep(self, op, dep, raw):
        if dep is None or dep is op:
            return
        if dep.eng == op.eng and not dep.is_dma and not op.is_dma:
            if op.eng == "pe":
                return
        key, val = dep.done
        if op.waits.get(key, 0) < val:
            op.waits[key] = val

    def op(self, eng, fn, reads=(), writes=(), dma=False):
        o = Op(eng, fn, dma)
        o.stage = getattr(self, "stage", None)
        if any(b.excl for b in reads):
            writes = list(writes) + [b for b in reads if b.excl and b not in writes]
            reads = [b for b in reads if not b.excl]
        self.nops = getattr(self, "nops", 0) + 1
        if self.nops > getattr(self, "limit", 1 << 60):
            return o
        pend = self.pending.pop(eng, None)
        if pend:
            for key, val in pend.items():
                if o.waits.get(key, 0) < val:
                    o.waits[key] = val
        for b in reads:
            for d in b.wc.values():
                self._add_dep(o, d, True)
            for lst in b.wd.values():
                for d in lst:
                    self._add_dep(o, d, True)
        for b in writes:
            for d in b.wc.values():
                self._add_dep(o, d, False)
            for lst in b.wd.values():
                for d in lst:
                    self._add_dep(o, d, False)
            for d in b.rc.values():
                self._add_dep(o, d, False)
            for lst in b.rd.values():
                for d in lst:
                    self._add_dep(o, d, False)
        if dma:
            k = self.dcnt[eng]
            self.dcnt[eng] += 1
            slot = k % RING
            key = self._semkey(("d", eng, slot))
            o.done = (key, 16 * (k // RING + 1))
            prev = self.last_dma.get((eng, slot))
            if prev is not None:
                self._add_dep(o, prev, True)
            self.last_dma[(eng, slot)] = o
        else:
            k = self.cnt[eng]
            self.cnt[eng] += 1
            key = self._semkey(("c", eng, k // EPOCH))
            o.done = (key, k % EPOCH + 1)
        for b in reads:
            if dma:
                lst = b.rd.setdefault(eng, [])
                lst.append(o)
                if len(lst) > RING:
                    del lst[0]
            else:
                b.rc[eng] = o
        for b in writes:
            b.rc = {}
            b.rd = {}
            if dma:
                lst = b.wd.setdefault(eng, [])
                lst.append(o)
                if len(lst) > RING:
                    del lst[0]
            else:
                b.wc[eng] = o
        self.q[eng].append(o)
        return o

    def emit(self):
        nc = self.nc
        with contextlib.ExitStack() as st:
            for key in self.semkeys:
                self.sems[key] = st.enter_context(nc.semaphore("s_" + "_".join(str(x) for x in key)))
            block = st.enter_context(nc.Block())
            engmap = {"pe": block.tensor, "act": block.scalar, "dve": block.vector,
                      "pool": block.gpsimd, "sp": block.sync}
            all_ops = self.q

            def make(ename):
                ops = all_ops[ename]

                def body(e):
                    seen = {}
                    for o in ops:
                        for key, val in o.waits.items():
                            if seen.get(key, 0) >= val:
                                continue
                            seen[key] = val
                            e.wait_ge(self.sems[key], val)
                        ins = o.fn(e)
                        if self.annotate and o.stage:
                            ins.annotate(o.stage)
                        key, val = o.done
                        ins.then_inc(self.sems[key], 16 if o.is_dma else 1)
                    if ename == "sp":
                        fin = {}
                        for en in self.ENGS:
                            for o in all_ops[en][-1:]:
                                key, val = o.done
                                fin[key] = max(fin.get(key, 0), val)
                        for o in self.last_dma.values():
                            key, val = o.done
                            fin[key] = max(fin.get(key, 0), val)
                        for key, val in fin.items():
                            if seen.get(key, 0) < val:
                                e.wait_ge(self.sems[key], val)
                return body

            for ename in self.ENGS:
                if all_ops[ename] or ename == "sp":
                    engmap[ename](make(ename))


class TB:
    def __init__(self, h, excl=False):
        self.h = h
        self.bufs = {}
        self.excl = excl

    def __getitem__(self, idx):
        return self.h[idx]

    def b(self, key=None):
        if key not in self.bufs:
            self.bufs[key] = Buf(self.excl)
        return self.bufs[key]


class TBV:
    def __init__(self, ap):
        self.ap = ap
        self.buf = Buf(True)

    def __getitem__(self, idx):
        return self.ap[idx]

    def b(self, key=None):
        return self.buf


class DT:
    def __init__(self, ap):
        self.ap = ap
        self.bufs = {}

    def __getitem__(self, idx):
        return self.ap[idx]

    def b(self, key=None):
        if key not in self.bufs:
            self.bufs[key] = Buf()
        return self.bufs[key]


D = 2048
KT = 16
NH = 8
NKV = 2
PAST = 512
IN_W = 3840
RW_IN = 1792
GRID_W = 64
NORM_EPS = 1e-6
GN_EPS = 64e-5
LWC = -math.exp(-0.5)


def host_consts(TS, TP):
    c = {}
    bf = ml_dtypes.bfloat16
    c["ident_f"] = np.eye(128, dtype=np.float32)
    c["ident_b"] = np.eye(128, dtype=np.float32).astype(bf)
    c["ones_b"] = np.ones((128, 128), np.float32).astype(bf)
    bo = np.zeros((128, 128), np.float32)
    bo[:64, :64] = 1.0
    bo[64:, 64:] = 1.0
    c["bones_b"] = bo.astype(bf)
    c["bones_f"] = bo
    prot = np.zeros((128, 128), np.float32)
    for m in range(128):
        j = m % 64
        if j < 32:
            prot[m + 32, m] = -1.0
        else:
            prot[m - 32, m] = 1.0
    c["prot_f"] = prot
    t = np.arange(TS)
    rows = (t // GRID_W).astype(np.float64)
    cols = (t % GRID_W).astype(np.float64)
    inv = 1.0 / (10000.0 ** (np.arange(0, 64, 2, dtype=np.float64) / 64.0))
    cosT = np.zeros((128, TS), np.float64)
    sinT = np.zeros((128, TS), np.float64)
    for d in range(128):
        pos = rows if d < 64 else cols
        f = inv[(d % 64) % 32]
        ang = np.float32(pos).astype(np.float32) * np.float32(f)
        cosT[d] = np.cos(ang.astype(np.float64))
        sinT[d] = np.sin(ang.astype(np.float64))
    c["cosT"] = cosT.astype(np.float32)
    c["sinT"] = sinT.astype(np.float32)
    i = np.arange(128)
    ang = 2 * np.pi * np.outer(i, i) / 128.0
    c["csC"] = (np.concatenate([np.cos(ang), np.sin(ang)], 1) / np.sqrt(128.0)).astype(np.float32).astype(bf)
    for nm, T in (("S", TS), ("P", TP)):
        i = np.arange(T)
        ang = 2 * np.pi * ((np.outer(i, i)) % T) / float(T)
        c["ct" + nm] = (np.cos(ang) / np.sqrt(T)).astype(np.float32).astype(bf)
        c["nst" + nm] = (-np.sin(ang) / np.sqrt(T)).astype(np.float32).astype(bf)
    idx = np.arange(128)
    su = (idx[:, None] < idx[None, :]).astype(np.float32)
    iu = (idx[:, None] <= idx[None, :]).astype(np.float32)
    sl = su.T.copy()
    il = iu.T.copy()
    c["masks"] = np.stack([su, sl, iu, il, -su, -sl], 0).astype(np.float32)
    c["masks4"] = np.repeat(c["masks"][:, None, :, :], 4, axis=1).transpose(2, 0, 1, 3).copy().astype(np.float32)
    c["ident4"] = np.repeat(np.eye(128, dtype=np.float32)[:, None, :], 4, axis=1).copy()
    c["tri2"] = np.stack([np.concatenate([iu, su], 1), np.concatenate([il, sl], 1)], 0).astype(np.float32) * np.float32(LWC)
    return c


class KB:
    def __init__(self, cfg):
        self.cfg = cfg
        self.TS = cfg["TS"]
        self.TP = cfg["TP"]
        self.NPS = cfg["NPS"]
        self.DFF = cfg["DFF"]
        self.FT = self.DFF // 128
        self.DEPTH = cfg["DEPTH"]
        self.dbg = set(cfg.get("dbg", ()))
        self.nc = bass.Bass("TRN2", target_bir_lowering=False)
        self.P = Prog(self.nc)
        self.P.limit = cfg.get("limit", 1 << 60)
        self.P.annotate = bool(cfg.get("annotate"))
        self.P.stage = "init"
        self.st = contextlib.ExitStack()
        self.dram = {}
        self.uid = 0
        self.bar_uid = 0
        self.groups = {"s": (self.TS, 1, self.TS), "p": (self.NPS * self.TP, self.NPS, self.TP)}

    def din(self, name, shape, dt=F32):
        t = DT(self.nc.dram_tensor(name, list(shape), dt, kind="ExternalInput").ap())
        self.dram[name] = t
        return t

    def dout(self, name, shape, dt=F32):
        t = DT(self.nc.dram_tensor(name, list(shape), dt, kind="ExternalOutput").ap())
        self.dram[name] = t
        return t

    def dscr(self, name, shape, dt):
        kind = "ExternalOutput" if name in self.dbg else "Internal"
        t = DT(self.nc.dram_tensor(name, list(shape), dt, kind=kind).ap())
        self.dram[name] = t
        return t

    def sb(self, name, shape, dt, stack=None):
        self.uid += 1
        h = (stack or self.st).enter_context(self.nc.sbuf_tensor(f"{name}_{self.uid}", list(shape), dt))
        return TB(h)

    def ps(self, name, shape, dt=F32, stack=None):
        self.uid += 1
        h = (stack or self.st).enter_context(self.nc.psum_tensor(f"{name}_{self.uid}", list(shape), dt))
        return TB(h, excl=True)

    def dma(self, out, in_, reads, writes, eng="sp", **kw):
        return self.P.op(eng, lambda e: e.dma_start(out=out, in_=in_, **kw), reads, writes, dma=True)

    def mm(self, out, lhsT, rhs, start, stop, reads, writes):
        return self.P.op("pe", lambda e: e.matmul(out, lhsT=lhsT, rhs=rhs, start=start, stop=stop), reads, writes)

    def tr(self, out, in_, ident, reads, writes):
        return self.P.op("pe", lambda e: e.transpose(out=out, in_=in_, identity=ident), reads, writes)

    def act(self, out, in_, func, reads, writes, **kw):
        return self.P.op("act", lambda e: e.activation(out=out, in_=in_, func=func, **kw), reads, writes)

    def tt(self, eng, out, in0, in1, op, reads, writes):
        return self.P.op(eng, lambda e: e.tensor_tensor(out=out, in0=in0, in1=in1, op=op), reads, writes)

    def ts(self, eng, out, in0, s1, s2, op0, op1, reads, writes):
        if s2 is None:
            return self.P.op(eng, lambda e: e.tensor_scalar(out=out, in0=in0, scalar1=s1, scalar2=None, op0=op0), reads, writes)
        return self.P.op(eng, lambda e: e.tensor_scalar(out=out, in0=in0, scalar1=s1, scalar2=s2, op0=op0, op1=op1), reads, writes)

    def stt(self, eng, out, in0, scalar, in1, op0, op1, reads, writes):
        return self.P.op(eng, lambda e: e.scalar_tensor_tensor(out=out, in0=in0, scalar=scalar, in1=in1, op0=op0, op1=op1), reads, writes)

    def cp(self, eng, out, in_, reads, writes):
        if eng == "act":
            return self.P.op("act", lambda e: e.copy(out=out, in_=in_), reads, writes)
        return self.P.op(eng, lambda e: e.tensor_copy(out=out, in_=in_), reads, writes)

    def memset(self, eng, ap, val, writes):
        return self.P.op(eng, lambda e: e.memset(ap, val), [], writes)

    def recip(self, out, in_, reads, writes):
        return self.P.op("dve", lambda e: e.reciprocal(out=out, in_=in_), reads, writes)

    def barrier(self):
        P = self.P
        fin = {}
        for en in P.ENGS:
            for o in P.q[en][-1:]:
                key, val = o.done
                fin[key] = max(fin.get(key, 0), val)
        for o in P.last_dma.values():
            key, val = o.done
            fin[key] = max(fin.get(key, 0), val)
        for en in P.ENGS:
            d = P.pending.setdefault(en, {})
            for key, val in fin.items():
                d[key] = max(d.get(key, 0), val)

    def declare(self):
        TS, TP, NPS, DEPTH, DFF = self.TS, self.TP, self.NPS, self.DEPTH, self.DFF
        TPG = NPS * TP
        di = self.din
        self.xs = di("xs", [TS, D])
        self.xp = di("xp", [TPG, D])
        self.ck = di("ck", [DEPTH, PAST, 256])
        self.cv = di("cv", [DEPTH, PAST, 256])
        self.st0 = di("st0", [DEPTH, 2, 8, 64, 64])
        self.cc = di("cc", [2, D])
        self.w_ada = di("w_ada", [DEPTH, D, 6 * D])
        self.b_ada = di("b_ada", [DEPTH, 6 * D])
        self.norm1_g = di("norm1_g", [DEPTH, D])
        self.norm2_g = di("norm2_g", [DEPTH, D])
        self.w_in = di("w_in", [DEPTH, D, IN_W])
        self.w_out = di("w_out", [DEPTH, D, D])
        self.q_norm_g = di("q_norm_g", [DEPTH, 128])
        self.k_norm_g = di("k_norm_g", [DEPTH, 128])
        self.rw_conv = di("rw_conv", [DEPTH, 3, RW_IN])
        self.rw_w0 = di("rw_w0", [DEPTH, 2, 512])
        self.rw_w2 = di("rw_w2", [DEPTH, 2, 64, 512])
        self.rw_a0 = di("rw_a0", [DEPTH, 2, 512])
        self.rw_a2 = di("rw_a2", [DEPTH, 2, 64, 512])
        self.rw_g2 = di("rw_g2", [DEPTH, 128, 512])
        self.rw_kk = di("rw_kk", [DEPTH, 512])
        self.rw_ka = di("rw_ka", [DEPTH, 512])
        self.rw_rk = di("rw_rk", [DEPTH, 512])
        self.rw_lnx_g = di("rw_lnx_g", [DEPTH, 512])
        self.rw_lnx_b = di("rw_lnx_b", [DEPTH, 512])
        self.ffn_up = di("ffn_up", [DEPTH, D, 2 * DFF])
        self.ffn_conv_w = di("ffn_conv_w", [DEPTH, 3, 2 * DFF])
        self.ffn_conv_b = di("ffn_conv_b", [DEPTH, 2 * DFF])
        self.ffn_down = di("ffn_down", [DEPTH, DFF, D])
        self.final_norm_g = di("final_norm_g", [D])
        self.c_ident_f = di("ident_f", [128, 128])
        self.c_ident_b = di("ident_b", [128, 128], BF16)
        self.c_ones_b = di("ones_b", [128, 128], BF16)
        self.c_bones_b = di("bones_b", [128, 128], BF16)
        self.c_bones_f = di("bones_f", [128, 128])
        self.c_prot_f = di("prot_f", [128, 128])
        self.c_cosT = di("cosT", [128, TS])
        self.c_sinT = di("sinT", [128, TS])
        self.c_csC = di("csC", [128, 256], BF16)
        self.c_ct = {"s": di("ctS", [TS, TS], BF16), "p": di("ctP", [TP, TP], BF16)}
        self.c_nst = {"s": di("nstS", [TS, TS], BF16), "p": di("nstP", [TP, TP], BF16)}
        self.c_masks = di("masks", [6, 128, 128])
        self.c_tri2 = di("tri2", [2, 128, 256])
        self.c_masks4 = di("masks4", [128, 6, 4, 128])
        self.c_ident4 = di("ident4", [128, 4, 128])
        self.ys = self.dout("ys", [TS, D])
        self.yp = self.dout("yp", [TPG, D])
        self.nk = self.dout("nk", [NPS, DEPTH, TP, 256])
        self.nv = self.dout("nv", [NPS, DEPTH, TP, 256])
        self.ns = self.dout("ns", [NPS, DEPTH, 2, 8, 64, 64])
        ds = self.dscr
        FT = self.FT
        self.Win_t = ds("Win_t", [DEPTH, 30, 128, KT, 128], BF16)
        self.Wv_t = ds("Wv_t", [DEPTH, 128, KT, 256], BF16)
        self.Wout_t = ds("Wout_t", [DEPTH, 16, 128, KT, 128], BF16)
        self.Wup_t = ds("Wup_t", [DEPTH, 2 * FT, 128, KT, 128], BF16)
        self.Wdn_t = ds("Wdn_t", [DEPTH, 16, 128, FT, 128], BF16)
        self.mod_rows = [ds(f"modrows{l}", [2, 6 * D], F32) for l in range(DEPTH)]
        self.XT = {}
        self.XTB = {}
        self.QT = {}
        self.KTs = {}
        self.Vs = {}
        self.RW = {}
        self.AB = {}
        self.MIXT = {}
        for g, (Tg, nseq, Tseq) in self.groups.items():
            self.XT[g] = ds("XT_" + g, [KT, 128, Tg], F32)
            self.XTB[g] = ds("XTB_" + g, [KT, 128, Tg], F32)
            self.QT[g] = ds("QT_" + g, [NH, 128, Tg], BF16)
            self.KTs[g] = ds("KT_" + g, [NKV, 128, Tg], BF16)
            self.Vs[g] = ds("V_" + g, [Tg, 256], BF16)
            self.RW[g] = ds("RW_" + g, [RW_IN, Tg], F32)
            self.AB[g] = ds("AB_" + g, [Tg, 1024], BF16)
            self.MIXT[g] = ds("MIXT_" + g, [KT, 128, Tg], BF16)

    def load_const(self, name, src, shape, dt):
        t = self.sb(name, shape, dt)
        self.dma(t[:], src.ap, [src.b()], [t.b()])
        return t

    def setup_consts(self):
        self.P.stage = "consts"
        TS = self.TS
        self.ident_f = self.load_const("ident_f", self.c_ident_f, [128, 128], F32)
        self.ident_b = self.load_const("ident_b", self.c_ident_b, [128, 128], BF16)
        self.ones_b = self.load_const("ones_b", self.c_ones_b, [128, 128], BF16)
        self.bones_b = self.load_const("bones_b", self.c_bones_b, [128, 128], BF16)
        self.bones_f = self.load_const("bones_f", self.c_bones_f, [128, 128], F32)
        self.prot_f = self.load_const("prot_f", self.c_prot_f, [128, 128], F32)
        self.csC = self.load_const("csC", self.c_csC, [128, 256], BF16)
        self.masks = self.sb("masks", [128, 6, 128], F32)
        self.dma(self.masks[:], self.c_masks.ap.rearrange("m p c -> p m c"), [self.c_masks.b()], [self.masks.b()])
        self.tri2 = self.sb("tri2", [128, 2, 256], F32)
        self.dma(self.tri2[:], self.c_tri2.ap.rearrange("m p c -> p m c"), [self.c_tri2.b()], [self.tri2.b()])
        self.masks4 = self.load_const("masks4", self.c_masks4, [128, 6, 4, 128], F32)
        self.ident4 = self.load_const("ident4", self.c_ident4, [128, 4, 128], F32)
        self.eps6 = self.sb("eps6", [128, 1], F32)
        self.memset("pool", self.eps6[:], NORM_EPS, [self.eps6.b()])
        self.eps12 = self.sb("eps12", [128, 1], F32)
        self.memset("pool", self.eps12[:], 1e-12, [self.eps12.b()])
        self.epsgn = self.sb("epsgn", [128, 1], F32)
        self.memset("pool", self.epsgn[:], GN_EPS, [self.epsgn.b()])
        self.pw = [self.ps(f"pw{i}", [128, 1024], F32) for i in range(4)]
        self.pb = [TBV(self.pw[i // 2].h[:, (i % 2) * 512:(i % 2) * 512 + 512]) for i in range(8)]

    def cast_jobs(self, l, which):
        FT = self.FT
        jobs = []
        if which == "in":
            for c0 in range(0, IN_W, 512):
                wd = min(512, IN_W - c0)
                dsts = []
                for j in range(wd // 128):
                    ot = c0 // 128 + j
                    if ot == 14:
                        dsts.append((self.Wv_t[l], self.Wv_t.b(l), j * 128, 256))
                    elif ot != 15:
                        dsts.append((self.Win_t[l, ot], self.Win_t.b((l, ot)), j * 128, 128))
                jobs.append((self.w_in, l, c0, wd, KT, dsts))
        else:
            for c0 in range(0, D, 512):
                dsts = [(self.Wout_t[l, c0 // 128 + j], self.Wout_t.b((l, c0 // 128 + j)), j * 128, 128) for j in range(4)]
                jobs.append((self.w_out, l, c0, 512, KT, dsts))
            for c0 in range(0, 2 * self.DFF, 512):
                wd = min(512, 2 * self.DFF - c0)
                dsts = [(self.Wup_t[l, c0 // 128 + j], self.Wup_t.b((l, c0 // 128 + j)), j * 128, 128) for j in range(wd // 128)]
                jobs.append((self.ffn_up, l, c0, wd, KT, dsts))
            cw = 128 if FT > 16 else 512
            for c0 in range(0, D, cw):
                dsts = [(self.Wdn_t[l, c0 // 128 + j], self.Wdn_t.b((l, c0 // 128 + j)), j * 128, 128) for j in range(cw // 128)]
                jobs.append((self.ffn_down, l, c0, cw, FT, dsts))
        return jobs

    def bg_begin(self, stk, jobs, engs=("pool",)):
        nel = max(KT * 512, self.FT * (128 if self.FT > 16 else 512))
        self.bg = dict(jobs=list(jobs), i=0, pend=None, engs=engs, n=0,
                       wf=[self.sb("wcf", [128, nel], F32, stk) for _ in range(2)],
                       wb=[self.sb("wcb", [128, nel], BF16, stk) for _ in range(2)])

    def bg_step(self):
        bg = getattr(self, "bg", None)
        if bg is None:
            return
        if bg["pend"] is not None:
            (src, l, c0, wd, ktn, dsts), f, b = bg["pend"]
            fv = f[:, 0:ktn * wd].rearrange("p (kt c) -> p kt c", c=wd)
            off = 0
            for (dap, dbuf, co, w) in dsts:
                view = b[:, off:off + ktn * w].rearrange("p (kt c) -> p kt c", c=w)
                self.cp(bg["engs"][bg["n"] % len(bg["engs"])], view, fv[:, :, co:co + w], [f.b()], [b.b()])
                bg["n"] += 1
                self.dma(dap, view, [b.b()], [dbuf])
                off += ktn * w
            bg["pend"] = None
        if bg["i"] < len(bg["jobs"]):
            job = bg["jobs"][bg["i"]]
            f = bg["wf"][bg["i"] % 2]
            b = bg["wb"][bg["i"] % 2]
            (src, l, c0, wd, ktn, dsts) = job
            fv = f[:, 0:ktn * wd].rearrange("p (kt c) -> p kt c", c=wd)
            self.dma(fv, src[l, :, c0:c0 + wd].rearrange("(kt p) c -> p kt c", p=128), [src.b()], [f.b()])
            bg["pend"] = (job, f, b)
            bg["i"] += 1

    def bg_end(self):
        bg = getattr(self, "bg", None)
        if bg is None:
            return
        while bg["pend"] is not None or bg["i"] < len(bg["jobs"]):
            self.bg_step()
        self.bg = None

    def cast_weights(self):
        self.P.stage = "cast"
        sched = self.cfg.get("bg_cast", True)
        jobs = self.cast_jobs(0, "in")
        self.bg_sched = {}
        if sched:
            r0 = self.cast_jobs(0, "rest")
            h = len(r0) // 2
            self.bg_sched[("attn", 0, "s")] = r0[:h]
            self.bg_sched[("rwkv", 0, "p")] = r0[h:]
            for l in range(1, self.DEPTH):
                rl = self.cast_jobs(l, "rest")
                k3 = min(7, len(rl))
                self.bg_sched[("s3", l - 1)] = self.cast_jobs(l, "in") + rl[:k3]
                rl = rl[k3:]
                h = len(rl) // 2
                self.bg_sched[("attn", l, "s")] = rl[:h]
                self.bg_sched[("rwkv", l, "p")] = rl[h:]
        else:
            jobs += self.cast_jobs(0, "rest")
            for l in range(1, self.DEPTH):
                jobs += self.cast_jobs(l, "in") + self.cast_jobs(l, "rest")
        with contextlib.ExitStack() as stk:
            self.bg_begin(stk, jobs, engs=("act", "dve", "pool", "dve"))
            self.bg_end()
            self.barrier()

    def load_pp(self, dst, dst_b, src_rows, n, src_b):
        k = self._pp_i = getattr(self, "_pp_i", 0) + 1
        stg = self.pp_stage[k % 2]
        bank = self.pb[6 + (k % 2)]
        self.dma(stg[0:n, :], src_rows, [src_b], [stg.b()])
        self.tr(bank[:, 0:n], stg[0:n, :], self.ident_f[0:n, 0:n], [stg.b(), self.ident_f.b()], [bank.b()])
        self.cp("dve", dst, bank[:, 0:n], [bank.b()], [dst_b])

    def to_feature_major(self):
        self.P.stage = "tofm"
        with contextlib.ExitStack() as stk:
            xin = [self.sb("xin", [128, D], F32, stk) for _ in range(2)]
            xo = [self.sb("xo", [128, KT, 128], F32, stk) for _ in range(2)]
            i = 0
            for g, src in (("s", self.xs), ("p", self.xp)):
                Tg = self.groups[g][0]
                for blk in range(Tg // 128):
                    a = xin[i % 2]
                    o = xo[i % 2]
                    self.dma(a[:], src[blk * 128:(blk + 1) * 128, :], [src.b()], [a.b()])
                    for q in range(4):
                        bank = self.pb[(i * 4 + q) % 4]
                        for j in range(4):
                            kt = q * 4 + j
                            self.tr(bank[:, j * 128:(j + 1) * 128], a[:, kt * 128:(kt + 1) * 128], self.ident_f[:],
                                    [a.b(), self.ident_f.b()], [bank.b()])
                        eng = "act" if q % 2 else "dve"
                        self.cp(eng, o[:, q * 4:(q + 1) * 4, :], bank[:].rearrange("p (j t) -> p j t", j=4), [bank.b()], [o.b()])
                    self.dma(self.XT[g][:, :, blk * 128:(blk + 1) * 128].rearrange("kt p t -> p kt t"), o[:],
                             [o.b()], [self.XT[g].b(blk // 4)])
                    i += 1
            self.barrier()

    def setup_params(self):
        self.P.stage = "params"
        DEPTH, FT = self.DEPTH, self.FT
        self.pp_stage = [self.sb("ppstg", [128, 128], F32) for _ in range(2)]
        self.prm = []
        ccT = self.sb("ccT", [128, 32], F32)
        self.load_pp(ccT[:], ccT.b(), self.cc.ap.rearrange("v (kt p) -> (v kt) p", p=128), 32, self.cc.b())
        sc = self.sb("sc", [128, 32], F32)
        self.act(sc[:], ccT[:], AF.Silu, [ccT.b()], [sc.b()])
        fng = self.sb("fng", [128, KT], F32)
        self.load_pp(fng[:], fng.b(), self.final_norm_g.ap.rearrange("(kt p) -> kt p", p=128), KT, self.final_norm_g.b())
        self.fng = fng
        for l in range(DEPTH):
            pr = {}
            def vec(name, src, n, rows):
                t = self.sb(name, [128, n], F32)
                self.load_pp(t[:], t.b(), rows, n, src.b())
                return t
            pr["n1g"] = vec("n1g", self.norm1_g, KT, self.norm1_g[l].rearrange("(kt p) -> kt p", p=128))
            pr["n2g"] = vec("n2g", self.norm2_g, KT, self.norm2_g[l].rearrange("(kt p) -> kt p", p=128))
            pr["qng"] = vec("qng", self.q_norm_g, 1, self.q_norm_g[l:l + 1, :])
            pr["kng"] = vec("kng", self.k_norm_g, 1, self.k_norm_g[l:l + 1, :])
            pr["rwconv"] = vec("rwconv", self.rw_conv, 42, self.rw_conv[l].rearrange("j (t p) -> (j t) p", p=128))
            for nm, src in (("kk", self.rw_kk), ("ka", self.rw_ka), ("rk", self.rw_rk), ("lng", self.rw_lnx_g), ("lnb", self.rw_lnx_b)):
                pr[nm] = vec(nm, src, 4, src[l].rearrange("(t p) -> t p", p=128))
            pr["a0"] = vec("a0", self.rw_a0, 8, self.rw_a0[l].rearrange("d (t p) -> (d t) p", p=128))
            c1 = self.sb("c1", [128, 4], F32)
            self.ts("dve", c1[:], pr["ka"][:], -1.0, 1.0, ALU.mult, ALU.add, [pr["ka"].b()], [c1.b()])
            pr["c1"] = c1
            nft = 2 * FT
            fcw = self.sb("fcw", [128, 3, nft], F32)
            for j in range(3):
                self.load_pp(fcw[:, j, :], fcw.b(), self.ffn_conv_w[l, j].rearrange("(t p) -> t p", p=128), nft, self.ffn_conv_w.b())
            pr["fcw"] = fcw
            nfcw = self.sb("nfcw", [128, 3, nft], F32)
            self.ts("dve", nfcw[:], fcw[:], -1.0, None, ALU.mult, None, [fcw.b()], [nfcw.b()])
            pr["nfcw"] = nfcw
            pr["fcb"] = vec("fcb", self.ffn_conv_b, nft, self.ffn_conv_b[l].rearrange("(t p) -> t p", p=128))
            bada = vec("bada", self.b_ada, 96, self.b_ada[l].rearrange("(t p) -> t p", p=128))
            mods = [self.sb(f"mod{v}", [128, 96], F32) for v in range(2)]
            modr = self.mod_rows[l]
            with contextlib.ExitStack() as stk:
                wa = [self.sb("wada", [128, KT, 1024], F32, stk) for _ in range(2)]
                rowt = [self.sb("modrow", [2, 512], F32, stk) for _ in range(2)]
                for blk in range(12):
                    w = wa[blk % 2]
                    self.dma(w[:], self.w_ada[l, :, blk * 1024:(blk + 1) * 1024].rearrange("(kt p) c -> p kt c", p=128),
                             [self.w_ada.b()], [w.b()], eng=("sp", "act")[blk % 2])
                    for hf in range(2):
                        cb = blk * 2 + hf
                        acc = self.pb[4 + cb % 2]
                        for kt in range(KT):
                            self.mm(acc[0:2, :], sc[:, kt:32:16], w[:, kt, hf * 512:(hf + 1) * 512], kt == 0, kt == KT - 1,
                                    [w.b(), sc.b()], [acc.b()])
                        rt_ = rowt[cb % 2]
                        self.cp("dve" if cb % 2 else "act", rt_[:], acc[0:2, :], [acc.b()], [rt_.b()])
                        self.dma(modr[:, cb * 512:(cb + 1) * 512], rt_[:], [rt_.b()], [modr.b()])
                for v in range(2):
                    m = mods[v]
                    self.load_pp(m[:], m.b(), modr[v].rearrange("(t p) -> t p", p=128), 96, modr.b())
                    self.tt("dve", m[:], m[:], bada[:], ALU.add, [m.b(), bada.b()], [m.b()])
                self.barrier()
            pr["mod"] = mods
            for v in range(2):
                for nm, gname, j in (("gs1", "n1g", 1), ("gs2", "n2g", 4)):
                    t = self.sb(f"{nm}_{v}", [128, KT], F32)
                    self.stt("dve", t[:], mods[v][:, j * 16:(j + 1) * 16], 1.0, pr[gname][:], ALU.add, ALU.mult,
                             [mods[v].b(), pr[gname].b()], [t.b()])
                    pr[f"{nm}_{v}"] = t
            self.prm.append(pr)

    def norm_mod(self, xt, chunks, gs, sh_ap_fn, hT, stk, ss_banks):
        Wtot = chunks[-1][1]
        sq = [self.sb("nsq", [128, Wtot], BF16, stk) for _ in range(2)]
        rstd = self.sb("nrstd", [128, Wtot], F32, stk)
        tmp = [self.sb("ntmp", [128, Wtot], F32, stk) for _ in range(2)]
        for kt in range(KT):
            s = sq[kt % 2]
            self.act(s[:], xt[:, kt, :], AF.Square, [xt.b()], [s.b()])
            for ci, (c0, c1) in enumerate(chunks):
                bk = ss_banks[ci]
                self.mm(bk[:, 0:c1 - c0], self.ones_b[:], s[:, c0:c1], kt == 0, kt == KT - 1,
                        [s.b(), self.ones_b.b()], [bk.b()])
        for ci, (c0, c1) in enumerate(chunks):
            bk = ss_banks[ci]
            self.act(rstd[:, c0:c1], bk[:, 0:c1 - c0], AF.Sqrt, [bk.b(), self.eps6.b()], [rstd.b()],
                     scale=1.0 / D, bias=self.eps6[:, 0:1])
        self.recip(rstd[:], rstd[:], [rstd.b()], [rstd.b()])
        for kt in range(KT):
            t = tmp[kt % 2]
            self.tt("pool" if kt % 2 else "dve", t[:], xt[:, kt, :], rstd[:], ALU.mult, [xt.b(), rstd.b()], [t.b()])
            self.act(hT[:, kt, :], t[:], AF.Identity, [t.b(), gs.b()], [hT.b()],
                     scale=gs[:, kt:kt + 1], bias=sh_ap_fn(kt))

    def zero_rwkv(self, g):
        Tg = self.groups[g][0]
        with contextlib.ExitStack() as stk:
            z = self.sb("zz", [128, Tg], BF16, stk)
            self.memset("pool", z[:], 0.0, [z.b()])
            for r in range(12, 16):
                self.dma(self.MIXT[g][r], z[:], [z.b()], [self.MIXT[g].b(i) for i in range((Tg + 511) // 512)])
            self.barrier()

    def stage1_x(self, l, g, c0, X):
        self._X1 = X
        return self.stage1(l, g, c0)

    def stage1(self, l, g, c0):
        self.P.stage = f"s1_{l}_{g}"
        W = 512
        v = 0 if g == "s" else 1
        pr = self.prm[l]
        mod = pr["mod"][v]
        ti = c0 // 512
        pb = self.pb
        is_s = (g == "s")
        with contextlib.ExitStack() as stk:
            xt = self.sb("xt", [128, KT, W], F32, stk)
            hT = self.sb("hT", [128, KT, W], BF16, stk)
            self.dma(xt[:], self._X1[g][:, :, c0:c0 + W].rearrange("kt p t -> p kt t"), [self._X1[g].b(ti)], [xt.b()])
            self.norm_mod(xt, [(0, W)], pr[f"gs1_{v}"], lambda kt: mod[:, kt:kt + 1], hT, stk, [pb[7]])
            wts = [self.sb("wt", [128, KT, 128], BF16, stk) for _ in range(3)]
            wv = self.sb("wv", [128, KT, 256], BF16, stk)
            uT = self.sb("uT", [128, W], BF16, stk)
            abt = self.sb("abt", [128, 4, 256], BF16, stk)
            sqh = self.sb("sqh", [128, W], BF16, stk)
            rq = self.sb("rq", [128, W], F32, stk)
            qn = [self.sb("qn", [128, W], F32, stk) for _ in range(2)]
            t1 = self.sb("t1", [128, W], F32, stk)
            t2 = self.sb("t2", [128, W], F32, stk)
            qr = [self.sb("qr", [128, W], BF16, stk) for _ in range(2)]
            ktok = self.sb("ktok", [128, 4, 128], F32, stk)
            vt = self.sb("vt", [128, 4, 256], BF16, stk)
            vtf = self.sb("vtf", [128, 4, 256], F32, stk)
            rwt = [self.sb("rwt", [128, W], F32, stk) for _ in range(2)]
            if is_s:
                cosb = self.sb("cosb", [128, W], F32, stk)
                sinb = self.sb("sinb", [128, W], F32, stk)
                self.dma(cosb[:], self.c_cosT[:, c0:c0 + W], [self.c_cosT.b()], [cosb.b()])
                self.dma(sinb[:], self.c_sinT[:, c0:c0 + W], [self.c_sinT.b()], [sinb.b()])

            order = list(range(14)) + list(range(16, 30))

            def load_w(oi):
                ot = order[oi]
                w = wts[oi % 3]
                self.dma(w[:], self.Win_t[l, ot], [self.Win_t.b((l, ot))], [w.b()])
            load_w(0)
            load_w(1)
            self.dma(wv[:], self.Wv_t[l], [self.Wv_t.b(l)], [wv.b()])
            for oi, ot in enumerate(order):
                if oi + 2 < len(order):
                    load_w(oi + 2)
                w = wts[oi % 3]
                acc = pb[oi % 3]
                for kt in range(KT):
                    self.mm(acc[:], w[:, kt, :], hT[:, kt, :], kt == 0, kt == KT - 1, [w.b(), hT.b()], [acc.b()])
                skip = self.cfg.get("s1_skip", "")
                if ("f" in skip and ot < 4) or ("q" in skip and 4 <= ot < 14) or ("r" in skip and ot >= 16):
                    continue
                if ot < 4:
                    self.cp("act", uT[:], acc[:], [acc.b()], [uT.b()])
                    for sub in range(4):
                        reg = pb[5][:, (sub % 2) * 256:(sub % 2) * 256 + 256]
                        self.mm(reg, uT[:, sub * 128:(sub + 1) * 128], self.csC[:], True, True,
                                [uT.b(), self.csC.b()], [pb[5].b()])
                        self.cp("dve", abt[:, sub, :], reg, [pb[5].b()], [abt.b()])
                    self.dma(self.AB[g][c0:c0 + W, ot * 256:(ot + 1) * 256].rearrange("(s p) c -> p s c", p=128), abt[:],
                             [abt.b()], [self.AB[g].b(ti)])
                elif ot < 14:
                    isk = ot >= 12
                    hd = ot - 12 if isk else ot - 4
                    gq = pr["kng"] if isk else pr["qng"]
                    q_n = qn[oi % 2]
                    q_r = qr[oi % 2]
                    self.act(sqh[:], acc[:], AF.Square, [acc.b()], [sqh.b()])
                    self.mm(pb[3][:], self.ones_b[:], sqh[:], True, True, [sqh.b(), self.ones_b.b()], [pb[3].b()])
                    self.act(rq[:], pb[3][:], AF.Sqrt, [pb[3].b(), self.eps6.b()], [rq.b()], scale=1.0 / 128, bias=self.eps6[:, 0:1])
                    self.recip(rq[:], rq[:], [rq.b()], [rq.b()])
                    self.stt("dve", q_n[:], acc[:], gq[:, 0:1], rq[:], ALU.mult, ALU.mult, [acc.b(), gq.b(), rq.b()], [q_n.b()])
                    if isk and not is_s:
                        for sub in range(4):
                            self.tr(pb[5][:, sub * 128:(sub + 1) * 128], q_n[:, sub * 128:(sub + 1) * 128], self.ident_f[:],
                                    [q_n.b(), self.ident_f.b()], [pb[5].b()])
                        self.cp("dve", ktok[:], pb[5][:].rearrange("p (s c) -> p s c", s=4), [pb[5].b()], [ktok.b()])
                        for sub in range(4):
                            tok = sub * 128
                            sq_, off = tok // self.TP, tok % self.TP
                            self.dma(self.nk[sq_, l, off:off + 128, hd * 128:(hd + 1) * 128], ktok[:, sub, :],
                                     [ktok.b()], [self.nk.b()])
                    if is_s:
                        self.mm(pb[4][:], self.prot_f[:], q_n[:], True, True, [self.prot_f.b(), q_n.b()], [pb[4].b()])
                        self.tt("pool", t1[:], q_n[:], cosb[:], ALU.mult, [q_n.b(), cosb.b()], [t1.b()])
                        self.tt("dve", t2[:], pb[4][:], sinb[:], ALU.mult, [pb[4].b(), sinb.b()], [t2.b()])
                        self.tt("pool", q_r[:], t1[:], t2[:], ALU.add, [t1.b(), t2.b()], [q_r.b()])
                    else:
                        self.cp("pool", q_r[:], q_n[:], [q_n.b()], [q_r.b()])
                    dst = self.KTs[g] if isk else self.QT[g]
                    self.dma(dst[hd, :, c0:c0 + W], q_r[:], [q_r.b()], [dst.b(ti)])
                else:
                    r = rwt[oi % 2]
                    self.cp("act" if oi % 2 else "dve", r[:], acc[:], [acc.b()], [r.b()])
                    self.dma(self.RW[g][(ot - 16) * 128:(ot - 15) * 128, c0:c0 + W], r[:], [r.b()], [self.RW[g].b(ti)])
                if oi == self.cfg.get("v_at", 13) and "v" not in skip:
                    for sub in range(4):
                        reg = pb[6][:, (sub % 2) * 256:(sub % 2) * 256 + 256]
                        for kt in range(KT):
                            self.mm(reg, hT[:, kt, sub * 128:(sub + 1) * 128], wv[:, kt, :], kt == 0, kt == KT - 1,
                                    [hT.b(), wv.b()], [pb[6].b()])
                        self.cp("act", vt[:, sub, :], reg, [pb[6].b()], [vt.b()])
                        if not is_s:
                            self.cp(self.cfg.get("vtf_eng", "dve"), vtf[:, sub, :], reg, [pb[6].b()], [vtf.b()])
                    self.dma(self.Vs[g][c0:c0 + W, :].rearrange("(s p) c -> p s c", p=128), vt[:], [vt.b()], [self.Vs[g].b(ti)])
                    if not is_s:
                        for sub in range(4):
                            tok = sub * 128
                            sq_, off = tok // self.TP, tok % self.TP
                            self.dma(self.nv[sq_, l, off:off + 128, :], vtf[:, sub, :], [vtf.b()], [self.nv.b()])
            self.barrier()

    def stage_fourier(self, l, g):
        self.P.stage = f"four_{l}_{g}"
        Tg, nseq, Tseq = self.groups[g]
        nch = Tseq // 128
        TW = min(512, Tseq)
        pb = self.pb
        allab = [self.AB[g].b(i) for i in range((Tg + 511) // 512)]
        with contextlib.ExitStack() as stk:
            ab = self.sb("fab", [128, nch, 1024], BF16, stk)
            ctb = [self.sb("fct", [128, nch, TW], BF16, stk) for _ in range(2)]
            nsb = [self.sb("fns", [128, nch, TW], BF16, stk) for _ in range(2)]
            fo = [self.sb("ffo", [128, TW], BF16, stk) for _ in range(2)]
            n = 0
            for s in range(nseq):
                self.dma(ab[:], self.AB[g][s * Tseq:(s + 1) * Tseq, :].rearrange("(c p) x -> p c x", p=128), allab, [ab.b()])
                for ti, t0 in enumerate(range(0, Tseq, TW)):
                    cb = ctb[ti % 2]
                    sbb = nsb[ti % 2]
                    self.dma(cb[:], self.c_ct[g][:, t0:t0 + TW].rearrange("(c p) t -> p c t", p=128), [self.c_ct[g].b()], [cb.b()])
                    self.dma(sbb[:], self.c_nst[g][:, t0:t0 + TW].rearrange("(c p) t -> p c t", p=128), [self.c_nst[g].b()], [sbb.b()])
                    for grp in range(4):
                        acc = pb[n % 4]
                        for c in range(nch):
                            self.mm(acc[:, 0:TW], ab[:, c, grp * 256:grp * 256 + 128], cb[:, c, :], c == 0, False,
                                    [ab.b(), cb.b()], [acc.b()])
                            self.mm(acc[:, 0:TW], ab[:, c, grp * 256 + 128:grp * 256 + 256], sbb[:, c, :], False, c == nch - 1,
                                    [ab.b(), sbb.b()], [acc.b()])
                        f = fo[n % 2]
                        self.cp("act" if n % 2 else "dve", f[:], acc[:, 0:TW], [acc.b()], [f.b()])
                        c0 = s * Tseq + t0
                        self.dma(self.MIXT[g][grp, :, c0:c0 + TW], f[:], [f.b()], [self.MIXT[g].b(c0 // 512)])
                        n += 1
            self.barrier()

    def stage_attn(self, l, g):
        self.P.stage = f"attn_{l}_{g}"
        Tg, nseq, Tseq = self.groups[g]
        is_s = (g == "s")
        Stot = Tseq + (PAST if is_s else 0)
        nck = Stot // 128
        QW = min(512, Tseq)
        pb = self.pb
        ntile = (Tg + 511) // 512
        scale = 128.0 ** -0.5
        with contextlib.ExitStack() as stk:
            kT = self.sb("akT", [128, NKV, Stot], BF16, stk)
            vv = self.sb("avv", [128, nck, 256], BF16, stk)
            qT = [self.sb("aqT", [128, Tseq], BF16, stk) for _ in range(2)]
            pT = [self.sb("apT", [128, QW], BF16, stk) for _ in range(3)]
            rden = self.sb("arden", [128, QW], F32, stk)
            ob = [self.sb("aob", [128, QW], BF16, stk) for _ in range(2)]
            if is_s:
                ckf = self.sb("ackf", [128, PAST // 128, 256], F32, stk)
                cvf = self.sb("acvf", [128, PAST // 128, 256], F32, stk)
            nq = 0
            bgj = self.bg_sched.pop(("attn", l, g), None)
            if bgj:
                self.bg_begin(stk, bgj, engs=("dve", "pool", "dve"))
            for s in range(nseq):
                c0s = s * Tseq
                for kv in range(NKV):
                    self.dma(kT[:, kv, 0:Tseq], self.KTs[g][kv, :, c0s:c0s + Tseq], [self.KTs[g].b(i) for i in range(ntile)], [kT.b()])
                self.dma(vv[:, 0:Tseq // 128, :], self.Vs[g][c0s:c0s + Tseq, :].rearrange("(c p) x -> p c x", p=128),
                         [self.Vs[g].b(i) for i in range(ntile)], [vv.b()])
                if is_s:
                    self.dma(ckf[:], self.ck[l].rearrange("(c p) x -> p c x", p=128), [self.ck.b()], [ckf.b()])
                    self.dma(cvf[:], self.cv[l].rearrange("(c p) x -> p c x", p=128), [self.cv.b()], [cvf.b()])
                    self.cp("pool", vv[:, Tseq // 128:nck, :], cvf[:], [cvf.b()], [vv.b()])
                    for kv in range(NKV):
                        bank = pb[kv]
                        for c in range(PAST // 128):
                            self.tr(bank[:, c * 128:(c + 1) * 128], ckf[:, c, kv * 128:(kv + 1) * 128], self.ident_f[:],
                                    [ckf.b(), self.ident_f.b()], [bank.b()])
                        self.cp("act", kT[:, kv, Tseq:Stot], bank[:, 0:PAST], [bank.b()], [kT.b()])
                for h in range(NH):
                    kv = h // (NH // NKV)
                    q = qT[h % 2]
                    self.dma(q[:], self.QT[g][h, :, c0s:c0s + Tseq], [self.QT[g].b(i) for i in range(ntile)], [q.b()])
                    for q0 in range(0, Tseq, QW):
                        self.bg_step()
                        oacc = pb[4 + nq % 2]
                        dacc = pb[6 + nq % 2]
                        def score(c):
                            sT = pb[c % 3]
                            p = pT[c % 3]
                            self.mm(sT[:, 0:QW], kT[:, kv, c * 128:(c + 1) * 128], q[:, q0:q0 + QW], True, True,
                                    [kT.b(), q.b()], [sT.b()])
                            self.act(p[:], sT[:, 0:QW], AF.Exp, [sT.b()], [p.b()], scale=scale)
                        pipe = self.cfg.get("attn_pipe", True)
                        if pipe:
                            score(0)
                        for c in range(nck):
                            if not pipe:
                                score(c)
                            elif c + 1 < nck:
                                score(c + 1)
                            p = pT[c % 3]
                            self.mm(oacc[:, 0:QW], vv[:, c, kv * 128:(kv + 1) * 128], p[:], c == 0, c == nck - 1,
                                    [vv.b(), p.b()], [oacc.b()])
                            self.mm(dacc[:, 0:QW], self.ones_b[:], p[:], c == 0, c == nck - 1,
                                    [self.ones_b.b(), p.b()], [dacc.b()])
                        self.recip(rden[:], dacc[:, 0:QW], [dacc.b()], [rden.b()])
                        o = ob[nq % 2]
                        self.tt("dve", o[:], oacc[:, 0:QW], rden[:], ALU.mult, [oacc.b(), rden.b()], [o.b()])
                        cc0 = c0s + q0
                        self.dma(self.MIXT[g][4 + h, :, cc0:cc0 + QW], o[:], [o.b()], [self.MIXT[g].b(cc0 // 512)])
                        nq += 1
            self.bg_end()
            self.barrier()

    def stage3(self, l, g, c0, X):
        self.P.stage = f"s3_{l}_{g}"
        W = 512
        v = 0 if g == "s" else 1
        pr = self.prm[l]
        mod = pr["mod"][v]
        ti = c0 // 512
        pb = self.pb
        with contextlib.ExitStack() as stk:
            xt = self.sb("xt3", [128, KT, W], F32, stk)
            mx = self.sb("mx3", [128, KT, W], BF16, stk)
            wts = [self.sb("wt3", [128, KT, 128], BF16, stk) for _ in range(3)]
            self.dma(xt[:], X[g][:, :, c0:c0 + W].rearrange("kt p t -> p kt t"), [X[g].b(ti)], [xt.b()])
            self.dma(mx[:], self.MIXT[g][:, :, c0:c0 + W].rearrange("kt p t -> p kt t"), [self.MIXT[g].b(ti)], [mx.b()])
            rem = self.bg_sched.get(("s3", l), [])
            if rem:
                take, self.bg_sched[("s3", l)] = rem[:3], rem[3:]
                self.bg_begin(stk, take, engs=("act", "pool", "act", "pool"))

            def load_w(ot):
                w = wts[ot % 3]
                self.dma(w[:], self.Wout_t[l, ot], [self.Wout_t.b((l, ot))], [w.b()])
            load_w(0)
            load_w(1)
            for ot in range(KT):
                if ot + 2 < KT:
                    load_w(ot + 2)
                w = wts[ot % 3]
                acc = pb[ot % 4]
                for kt in range(KT):
                    self.mm(acc[:], w[:, kt, :], mx[:, kt, :], kt == 0, kt == KT - 1, [w.b(), mx.b()], [acc.b()])
                self.stt("dve", xt[:, ot, :], acc[:], mod[:, 32 + ot:33 + ot], xt[:, ot, :], ALU.mult, ALU.add,
                         [acc.b(), xt.b(), xt.b(ot)], [xt.b(ot)])
                if ot % 4 == 1:
                    self.bg_step()
            self.bg_end()
            self.dma(X[g][:, :, c0:c0 + W].rearrange("kt p t -> p kt t"), xt[:], [xt.b(ot) for ot in range(KT)] + [xt.b()], [X[g].b(ti)])
            self.barrier()

    def stage4(self, l, g, c0, X, Y):
        self.P.stage = f"s4_{l}_{g}"
        W = 512
        Tg, nseq, Tseq = self.groups[g]
        v = 0 if g == "s" else 1
        pr = self.prm[l]
        mod = pr["mod"][v]
        ti = c0 // 512
        nti = (Tg + 511) // 512
        pb, pw = self.pb, self.pw
        FT = self.FT
        fcw, fcb, nfcw = pr["fcw"], pr["fcb"], pr["nfcw"]
        with contextlib.ExitStack() as stk:
            xt = self.sb("xt4", [128, KT, W + 2], F32, stk)
            hT = self.sb("hT4", [128, KT, W + 2], BF16, stk)
            actT = self.sb("act4", [128, FT, W], BF16, stk)
            wts = [self.sb("wt4", [128, KT, 128], BF16, stk) for _ in range(3)]
            wdn = [self.sb("wd4", [128, FT, 128], BF16, stk) for _ in range(2)]
            ua = self.sb("ua4", [128, W], F32, stk)
            ug = self.sb("ug4", [128, W], F32, stk)
            sg = self.sb("sg4", [128, W], F32, stk)
            self.dma(xt[:, :, 0:W], X[g][:, :, c0:c0 + W].rearrange("kt p t -> p kt t"), [X[g].b(ti)], [xt.b()])
            lzero = (c0 % Tseq == 0)
            rzero = ((c0 + W) % Tseq == 0)
            if lzero:
                self.memset("pool", xt[:, :, W:W + 1], 0.0, [xt.b()])
            else:
                self.dma(xt[:, :, W:W + 1], X[g][:, :, c0 - 1:c0].rearrange("kt p t -> p kt t"), [X[g].b(ti - 1)], [xt.b()],
                         allow_slow_non_contiguous=True)
            if rzero:
                self.memset("pool", xt[:, :, W + 1:W + 2], 0.0, [xt.b()])
            else:
                self.dma(xt[:, :, W + 1:W + 2], X[g][:, :, c0 + W:c0 + W + 1].rearrange("kt p t -> p kt t"), [X[g].b(ti + 1)], [xt.b()],
                         allow_slow_non_contiguous=True)
            self.norm_mod(xt, [(0, W), (W, W + 2)], pr[f"gs2_{v}"], lambda kt: mod[:, 48 + kt:49 + kt], hT, stk, [pb[7], pb[6]])
            if lzero:
                self.memset("pool", hT[:, :, W:W + 1], 0.0, [hT.b()])
            if rzero:
                self.memset("pool", hT[:, :, W + 1:W + 2], 0.0, [hT.b()])
            inner = [b for b in range(Tseq, W, Tseq)] if Tseq < W else []

            def load_w(j):
                i, half = j // 2, j % 2
                w = wts[j % 3]
                ot = i + half * FT
                self.dma(w[:], self.Wup_t[l, ot], [self.Wup_t.b((l, ot))], [w.b()])
            load_w(0)
            load_w(1)
            for j in range(2 * FT):
                if j + 2 < 2 * FT:
                    load_w(j + 2)
                i, half = j // 2, j % 2
                ot = i + half * FT
                w = wts[j % 3]
                wide = j % 3
                A, Bk = pb[2 * wide], pb[2 * wide + 1]
                for kt in range(KT):
                    self.mm(A[:], w[:, kt, :], hT[:, kt, 0:W], kt == 0, kt == KT - 1, [w.b(), hT.b()], [A.b()])
                for kt in range(KT):
                    self.mm(Bk[:, 0:2], w[:, kt, :], hT[:, kt, W:W + 2], kt == 0, kt == KT - 1, [w.b(), hT.b()], [Bk.b()])
                u = ug if half else ua
                w0 = fcw[:, 0, ot:ot + 1]
                w1 = fcw[:, 1, ot:ot + 1]
                w2 = fcw[:, 2, ot:ot + 1]
                self.act(u[:], A[:], AF.Identity, [A.b(), fcw.b(), fcb.b()], [u.b()], scale=w1, bias=fcb[:, ot:ot + 1])
                self.stt("dve", u[:, 1:W], A[:, 0:W - 1], w0, u[:, 1:W], ALU.mult, ALU.add, [A.b(), u.b()], [u.b()])
                self.stt("dve", u[:, 0:W - 1], A[:, 1:W], w2, u[:, 0:W - 1], ALU.mult, ALU.add, [A.b(), u.b()], [u.b()])
                self.stt("dve", u[:, 0:1], Bk[:, 0:1], w0, u[:, 0:1], ALU.mult, ALU.add, [Bk.b(), u.b()], [u.b()])
                self.stt("dve", u[:, W - 1:W], Bk[:, 1:2], w2, u[:, W - 1:W], ALU.mult, ALU.add, [Bk.b(), u.b()], [u.b()])
                for bnd in inner:
                    self.stt("dve", u[:, bnd - 1:bnd], A[:, bnd:bnd + 1], nfcw[:, 2, ot:ot + 1], u[:, bnd - 1:bnd], ALU.mult, ALU.add,
                             [A.b(), u.b(), nfcw.b()], [u.b()])
                    self.stt("dve", u[:, bnd:bnd + 1], A[:, bnd - 1:bnd], nfcw[:, 0, ot:ot + 1], u[:, bnd:bnd + 1], ALU.mult, ALU.add,
                             [A.b(), u.b(), nfcw.b()], [u.b()])
                if half:
                    self.act(sg[:], ug[:], AF.Silu, [ug.b()], [sg.b()])
                    self.tt("pool", actT[:, i, :], sg[:], ua[:], ALU.mult, [sg.b(), ua.b()], [actT.b()])
            self.dma(wdn[0][:], self.Wdn_t[l, 0], [self.Wdn_t.b((l, 0))], [wdn[0].b()])
            for ot in range(KT):
                if ot + 1 < KT:
                    self.dma(wdn[(ot + 1) % 2][:], self.Wdn_t[l, ot + 1], [self.Wdn_t.b((l, ot + 1))], [wdn[(ot + 1) % 2].b()])
                w = wdn[ot % 2]
                acc = pb[6 + ot % 2]
                for kt in range(FT):
                    self.mm(acc[:], w[:, kt, :], actT[:, kt, :], kt == 0, kt == FT - 1, [w.b(), actT.b()], [acc.b()])
                self.stt("dve", xt[:, ot, 0:W], acc[:], mod[:, 80 + ot:81 + ot], xt[:, ot, 0:W], ALU.mult, ALU.add,
                         [acc.b(), xt.b(), xt.b(ot)], [xt.b(ot)])
            self.dma(Y[g][:, :, c0:c0 + W].rearrange("kt p t -> p kt t"), xt[:, :, 0:W], [xt.b(ot) for ot in range(KT)] + [xt.b()], [Y[g].b(ti)])
            self.barrier()

    def stage5(self, g, c0, X):
        self.P.stage = f"s5_{g}"
        W = 512
        ti = c0 // 512
        pb = self.pb
        out = self.ys if g == "s" else self.yp
        with contextlib.ExitStack() as stk:
            xt = self.sb("xt5", [128, KT, W], F32, stk)
            xn = self.sb("xn5", [128, KT, W], F32, stk)
            sq = [self.sb("sq5", [128, W], BF16, stk) for _ in range(2)]
            rstd = self.sb("rstd5", [128, W], F32, stk)
            yt = [self.sb("yt5", [128, D], F32, stk) for _ in range(2)]
            self.dma(xt[:], X[g][:, :, c0:c0 + W].rearrange("kt p t -> p kt t"), [X[g].b(ti)], [xt.b()])
            for kt in range(KT):
                s = sq[kt % 2]
                self.act(s[:], xt[:, kt, :], AF.Square, [xt.b()], [s.b()])
                self.mm(pb[7][:], self.ones_b[:], s[:], kt == 0, kt == KT - 1, [s.b(), self.ones_b.b()], [pb[7].b()])
            self.act(rstd[:], pb[7][:], AF.Sqrt, [pb[7].b(), self.eps6.b()], [rstd.b()], scale=1.0 / D, bias=self.eps6[:, 0:1])
            self.recip(rstd[:], rstd[:], [rstd.b()], [rstd.b()])
            for kt in range(KT):
                self.stt("dve", xn[:, kt, :], xt[:, kt, :], self.fng[:, kt:kt + 1], rstd[:], ALU.mult, ALU.mult,
                         [xt.b(), rstd.b(), self.fng.b()], [xn.b()])
            for sub in range(4):
                y = yt[sub % 2]
                for q in range(4):
                    bank = pb[q]
                    for j in range(4):
                        kt = q * 4 + j
                        self.tr(bank[:, j * 128:(j + 1) * 128], xn[:, kt, sub * 128:(sub + 1) * 128], self.ident_f[:],
                                [xn.b(), self.ident_f.b()], [bank.b()])
                    self.cp("act" if q % 2 else "dve", y[:, q * 512:(q + 1) * 512], bank[:], [bank.b()], [y.b()])
                self.dma(out[c0 + sub * 128:c0 + (sub + 1) * 128, :], y[:], [y.b()], [out.b()])
            self.barrier()

    def stage_rwkv(self, l, g):
        self.P.stage = f"rwkv_{l}_{g}"
        Tg, nseq, Tseq = self.groups[g]
        is_s = (g == "s")
        T = Tseq
        nch = T // 128
        NB = 4
        pr = self.prm[l]
        pb = self.pb
        ntile = (Tg + 511) // 512
        rwall = [self.RW[g].b(i) for i in range(ntile)]
        conv = pr["rwconv"]
        CW = min(512, T)
        with contextlib.ExitStack() as stk:
            sbt = lambda name, shape, dt: self.sb(name, shape, dt, stk)
            w2e = sbt("w2e", [65, 2, 512], F32)
            a2f = sbt("a2f", [128, 2, 512], F32)
            g2f = sbt("g2f", [128, 512], F32)
            self.dma(w2e[0:64, :, :], self.rw_w2[l].rearrange("d k c -> k d c"), [self.rw_w2.b()], [w2e.b()])
            self.dma(w2e[64:65, :, :], self.rw_w0[l:l + 1], [self.rw_w0.b()], [w2e.b()])
            self.dma(a2f[64:128, :, :], self.rw_a2[l].rearrange("d k c -> k d c"), [self.rw_a2.b()], [a2f.b()])
            self.dma(g2f[:], self.rw_g2[l], [self.rw_g2.b()], [g2f.b()])
            t12 = sbt("t12", [128, T], F32)
            tw = sbt("tw", [65, T], F32)
            sg = sbt("sg", [128, T], F32)
            rc = sbt("rc", [128, T], F32)
            kkn = sbt("kkn", [128, T], F32)
            bA = [sbt("bA", [128, T], BF16) for _ in range(2)]
            KM = [sbt("KM", [128, T], BF16) for _ in range(2)]
            bon = sbt("bon", [128, T], F32)
            yacc = sbt("yacc", [128, T], F32)
            s1 = sbt("scr1", [128, T + 2], F32)
            s2 = sbt("scr2", [128, T], F32)
            s3 = sbt("scr3", [128, T], F32)
            al = sbt("al", [128, T], BF16)
            be = sbt("be", [128, T], BF16)
            ka_ = sbt("kap", [128, T], BF16)
            rt = sbt("rt", [128, T], BF16)
            VtmP = sbt("VtmP", [128, nch, 2, 128], BF16)
            BTt = sbt("BTt", [128, nch, 128], BF16)
            KTt = sbt("KTt", [128, nch, 128], BF16)
            PL = sbt("PL", [128, nch], F32)
            sigT = [sbt("sigT", [128, 128], F32) for _ in range(2)]
            Ptmp = [sbt("Ptmp", [128, 3, 256], F32) for _ in range(2)]
            NI = NB * 2
            Xa = sbt("Xa", [128, NI, 128], BF16)
            XTa = sbt("XTa", [128, NI, 128], BF16)
            Xb = sbt("Xb", [128, NI, 128], BF16)
            XTb = sbt("XTb", [128, NI, 128], BF16)
            Pf = sbt("Pf", [128, NI, 128], F32)
            Pfin = sbt("Pfin", [128, NI, 128], BF16)
            MakT = sbt("MakT", [128, NI, 128], BF16)
            Wbr = sbt("Wbr", [128, NI, 128], BF16)
            Wkr = sbt("Wkr", [128, NI, 128], BF16)
            Sf = sbt("Sf", [128, 128], F32)
            Sb = sbt("Sb", [128, 128], BF16)
            Sld = sbt("Sld", [128, 128], F32)
            RHSs = sbt("RHSs", [128, 128], BF16)
            Upad = sbt("Upad", [128, 2, 128], BF16)
            Ufull = sbt("Ufull", [128, 128], BF16)
            tS = sbt("tS", [128, 128], F32)
            sq = sbt("sqr", [128, CW], BF16)
            rstd = sbt("rstdr", [128, CW], F32)
            ob = [sbt("obr", [128, CW], BF16) for _ in range(2)]

            bgj = self.bg_sched.pop(("rwkv", l, g), None)
            if bgj:
                self.bg_begin(stk, bgj, engs=("act", "dve", "pool", "act", "dve"))
            self.memset("pool", VtmP[:], 0.0, [VtmP.b()])
            self.memset("pool", Upad[:], 0.0, [Upad.b()])
            self.memset("pool", tw[64:65, :], 1.0, [tw.b()])
            self.memset("pool", s1[:, 0:1], 0.0, [s1.b()])
            self.memset("pool", s1[:, T + 1:T + 2], 0.0, [s1.b()])

            def conv_tile(tile_idx, c0s, dst):
                self.dma(s1[:, 1:T + 1], self.RW[g][tile_idx * 128:(tile_idx + 1) * 128, c0s:c0s + T], rwall, [s1.b()])
                w = lambda j: conv[:, j * 14 + tile_idx:j * 14 + tile_idx + 1]
                self.act(dst[:], s1[:, 1:T + 1], AF.Copy, [s1.b(), conv.b()], [dst.b()], scale=w(1))
                self.stt("dve", dst[:], s1[:, 0:T], w(0), dst[:], ALU.mult, ALU.add, [s1.b(), dst.b()], [dst.b()])
                self.stt("dve", dst[:], s1[:, 2:T + 2], w(2), dst[:], ALU.mult, ALU.add, [s1.b(), dst.b()], [dst.b()])

            for s in range(nseq):
                c0s = s * T
                conv_tile(12, c0s, t12)
                self.act(tw[0:64, :], t12[0:64, :], AF.Tanh, [t12.b()], [tw.b()])
                conv_tile(13, c0s, sg)
                self.act(sg[:], sg[:], AF.Sigmoid, [sg.b()], [sg.b()])
                for ct in range(4):
                    self.bg_step()
                    conv_tile(ct, c0s, rc)
                    conv_tile(4 + ct, c0s, s2)
                    self.ts("pool", s3[:], s2[:], pr["kk"][:, ct:ct + 1], None, ALU.mult, None, [s2.b(), pr["kk"].b()], [s3.b()])
                    for x0 in range(0, T, CW):
                        bank = pb[(x0 // CW) % 2]
                        self.act(sq[:], s3[:, x0:x0 + CW], AF.Square, [s3.b()], [sq.b()])
                        self.mm(bank[:, 0:CW], self.bones_b[:], sq[:], True, True, [self.bones_b.b(), sq.b()], [bank.b()])
                        self.act(rstd[:], bank[:, 0:CW], AF.Sqrt, [bank.b(), self.eps12.b()], [rstd.b()], bias=self.eps12[:, 0:1])
                        self.recip(rstd[:], rstd[:], [rstd.b()], [rstd.b()])
                        self.tt("dve", kkn[:, x0:x0 + CW], s3[:, x0:x0 + CW], rstd[:], ALU.mult, [s3.b(), rstd.b()], [kkn.b()])
                    for d in range(2):
                        for x0 in range(0, T, CW):
                            bank = pb[2 + (x0 // CW) % 2]
                            self.mm(bank[:, 0:CW], a2f[64:128, d, ct * 128:(ct + 1) * 128], t12[64:128, x0:x0 + CW], True, True,
                                    [a2f.b(), t12.b()], [bank.b()])
                            self.act(s1[:, 1 + x0:1 + x0 + CW], bank[:, 0:CW], AF.Sigmoid, [bank.b(), pr["a0"].b()], [s1.b()],
                                     bias=pr["a0"][:, d * 4 + ct:d * 4 + ct + 1])
                        self.tt("pool", bA[d][:], kkn[:], s1[:, 1:T + 1], ALU.mult, [kkn.b(), s1.b()], [bA[d].b()])
                        self.ts("dve", s1[:, 1:T + 1], s1[:, 1:T + 1], pr["ka"][:, ct:ct + 1], pr["c1"][:, ct:ct + 1], ALU.mult, ALU.add,
                                [s1.b(), pr["ka"].b(), pr["c1"].b()], [s1.b()])
                        self.tt("dve", KM[d][:], s1[:, 1:T + 1], s2[:], ALU.mult, [s1.b(), s2.b()], [KM[d].b()])
                    conv_tile(8 + ct, c0s, s3)
                    self.tt("pool", s2[:], KM[0][:], KM[1][:], ALU.add, [KM[0].b(), KM[1].b()], [s2.b()])
                    self.stt("dve", s2[:], rc[:], pr["rk"][:, ct:ct + 1], s2[:], ALU.mult, ALU.mult, [rc.b(), s2.b(), pr["rk"].b()], [s2.b()])
                    for x0 in range(0, T, CW):
                        bank = pb[(x0 // CW) % 2]
                        self.mm(bank[:, 0:CW], self.bones_f[:], s2[:, x0:x0 + CW], True, True, [self.bones_f.b(), s2.b()], [bank.b()])
                        self.tt("dve", bon[:, x0:x0 + CW], bank[:, 0:CW], s3[:, x0:x0 + CW], ALU.mult, [bank.b(), s3.b()], [bon.b()])
                    for c in range(nch):
                        bank = pb[2 + c % 2]
                        self.tr(bank[:, 0:128], s3[:, c * 128:(c + 1) * 128], self.ident_f[:], [s3.b(), self.ident_f.b()], [bank.b()])
                        for hh in range(2):
                            self.cp("act" if hh else "dve", VtmP[:, c, hh, hh * 64:(hh + 1) * 64], bank[:, hh * 64:(hh + 1) * 64],
                                    [bank.b()], [VtmP.b()])
                    for d in range(2):
                        self.rwkv_dir(l, g, s, ct, d, T, nch, NB, stk, locals())
                    for x0 in range(0, T, CW):
                        bank = pb[(x0 // CW) % 2]
                        bank2 = pb[2 + (x0 // CW) % 2]
                        bank3 = pb[4 + (x0 // CW) % 2]
                        ys_ = yacc[:, x0:x0 + CW]
                        self.mm(bank[:, 0:CW], self.bones_f[:], ys_, True, True, [self.bones_f.b(), yacc.b()], [bank.b()])
                        self.stt("dve", s2[:, x0:x0 + CW], bank[:, 0:CW], -1.0 / 64, ys_, ALU.mult, ALU.add, [bank.b(), yacc.b()], [s2.b()])
                        self.act(sq[:], s2[:, x0:x0 + CW], AF.Square, [s2.b()], [sq.b()])
                        self.mm(bank2[:, 0:CW], self.bones_b[:], sq[:], True, True, [self.bones_b.b(), sq.b()], [bank2.b()])
                        self.act(rstd[:], bank2[:, 0:CW], AF.Sqrt, [bank2.b(), self.epsgn.b()], [rstd.b()], scale=1.0 / 64, bias=self.epsgn[:, 0:1])
                        self.recip(rstd[:], rstd[:], [rstd.b()], [rstd.b()])
                        self.tt("dve", s2[:, x0:x0 + CW], s2[:, x0:x0 + CW], rstd[:], ALU.mult, [s2.b(), rstd.b()], [s2.b()])
                        self.ts("dve", s2[:, x0:x0 + CW], s2[:, x0:x0 + CW], pr["lng"][:, ct:ct + 1], pr["lnb"][:, ct:ct + 1], ALU.mult, ALU.add,
                                [s2.b(), pr["lng"].b(), pr["lnb"].b()], [s2.b()])
                        self.tt("pool", s2[:, x0:x0 + CW], s2[:, x0:x0 + CW], bon[:, x0:x0 + CW], ALU.add, [s2.b(), bon.b()], [s2.b()])
                        self.mm(bank3[:, 0:CW], g2f[:, ct * 128:(ct + 1) * 128], sg[:, x0:x0 + CW], True, True, [g2f.b(), sg.b()], [bank3.b()])
                        o = ob[(x0 // CW) % 2]
                        self.tt("dve", o[:], bank3[:, 0:CW], s2[:, x0:x0 + CW], ALU.mult, [bank3.b(), s2.b()], [o.b()])
                        cc0 = c0s + x0
                        self.dma(self.MIXT[g][12 + ct, :, cc0:cc0 + CW], o[:], [o.b()], [self.MIXT[g].b(cc0 // 512)])
            self.bg_end()
            self.barrier()

    def rwkv_dir(self, l, g, s, ct, d, T, nch, NB, stk, L):
        pr = self.prm[l]
        pb = self.pb
        is_s = (g == "s")
        tw, w2e, sigT, Ptmp, kkn, bA, KM, rc = L["tw"], L["w2e"], L["sigT"], L["Ptmp"], L["kkn"], L["bA"], L["KM"], L["rc"]
        al, be, ka_, rt, PL = L["al"], L["be"], L["ka_"], L["rt"], L["PL"]
        BTt, KTt, VtmP = L["BTt"], L["KTt"], L["VtmP"]
        Xa, XTa, Xb, XTb, Pf, Pfin, MakT, Wbr, Wkr = (L[k] for k in ("Xa", "XTa", "Xb", "XTb", "Pf", "Pfin", "MakT", "Wbr", "Wkr"))
        Sf, Sb, Sld, RHSs, Upad, Ufull, tS, yacc = (L[k] for k in ("Sf", "Sb", "Sld", "RHSs", "Upad", "Ufull", "tS", "yacc"))
        masks = self.masks
        m_n, m_nt, m_s, m_i = ((4, 5, 0, 2) if d == 0 else (5, 4, 1, 3))
        for cp_ in range(0, nch, 2):
            cs = [c for c in (cp_, cp_ + 1) if c < nch]
            bank = pb[(cp_ // 2) % 2]
            for j, c in enumerate(cs):
                sgt = sigT[c % 2]
                bk2 = pb[2 + c % 2]
                self.mm(bk2[:, 0:128], tw[0:65, c * 128:(c + 1) * 128], w2e[0:65, d, ct * 128:(ct + 1) * 128], True, True,
                        [tw.b(), w2e.b()], [bk2.b()])
                self.act(sgt[:], bk2[:, 0:128], AF.Sigmoid, [bk2.b()], [sgt.b()])
                self.mm(bank[:, j * 256:(j + 1) * 256], sgt[:], self.tri2[:, d, :], True, True, [sgt.b(), self.tri2.b()], [bank.b()])
            n = len(cs)
            pt = Ptmp[(cp_ // 2) % 2]
            bv = bank[:, 0:n * 256].rearrange("p (j x) -> p j x", j=n)
            cols = slice(cp_ * 128, (cp_ + n) * 128)
            v3 = lambda tb_: tb_[:, cols].rearrange("p (j x) -> p j x", j=n)
            self.act(pt[:, 0, 0:n * 128].rearrange("p (j x) -> p j x", j=n), bv[:, :, 0:128], AF.Exp, [bank.b()], [pt.b()])
            self.act(pt[:, 1, 0:n * 128].rearrange("p (j x) -> p j x", j=n), bv[:, :, 0:128], AF.Exp, [bank.b()], [pt.b()], scale=-1.0)
            self.act(pt[:, 2, 0:n * 128].rearrange("p (j x) -> p j x", j=n), bv[:, :, 128:256], AF.Exp, [bank.b()], [pt.b()])
            for j, c in enumerate(cs):
                col = (127 if d == 0 else 0)
                self.cp("pool", PL[:, c:c + 1], pt[:, 0, j * 128 + col:j * 128 + col + 1], [pt.b()], [PL.b()])
            w_ = n * 128
            self.tt("dve", al[:, cols], pt[:, 2, 0:w_], kkn[:, cols], ALU.mult, [pt.b(), kkn.b()], [al.b()])
            self.tt("pool", be[:, cols], pt[:, 1, 0:w_], bA[d][:, cols], ALU.mult, [pt.b(), bA[d].b()], [be.b()])
            self.tt("dve", ka_[:, cols], pt[:, 1, 0:w_], KM[d][:, cols], ALU.mult, [pt.b(), KM[d].b()], [ka_.b()])
            self.tt("pool", rt[:, cols], pt[:, 0, 0:w_], rc[:, cols], ALU.mult, [pt.b(), rc.b()], [rt.b()])
        self.bg_step()
        for c in range(nch):
            bank = pb[c % 2]
            self.mm(bank[:, 0:128], be[:, c * 128:(c + 1) * 128], self.ident_b[:], True, True, [be.b(), self.ident_b.b()], [bank.b()])
            self.mm(bank[:, 128:256], ka_[:, c * 128:(c + 1) * 128], self.ident_b[:], True, True, [ka_.b(), self.ident_b.b()], [bank.b()])
            self.cp("act", BTt[:, c, :], bank[:, 0:128], [bank.b()], [BTt.b()])
            self.cp("dve", KTt[:, c, :], bank[:, 128:256], [bank.b()], [KTt.b()])
        self.bg_step()
        if is_s:
            self.memset("pool", Sld[:], 0.0, [Sld.b()])
            for hh in range(2):
                self.dma(Sld[hh * 64:(hh + 1) * 64, hh * 64:(hh + 1) * 64], self.st0[l, d, ct * 2 + hh], [self.st0.b()], [Sld.b()])
            self.tr(pb[7][:, 0:128], Sld[:], self.ident_f[:], [Sld.b(), self.ident_f.b()], [pb[7].b()])
            self.cp("dve", Sf[:], pb[7][:, 0:128], [pb[7].b()], [Sf.b()])
        else:
            self.memset("pool", Sf[:], 0.0, [Sf.b()])
        self.cp("act", Sb[:], Sf[:], [Sf.b()], [Sb.b()])
        order = list(range(nch)) if d == 0 else list(range(nch - 1, -1, -1))
        for b0 in range(0, nch, NB):
            batch = order[b0:b0 + NB]
            nb_ = len(batch)
            grps = [[(c, hh) for c in batch] for hh in range(2)]
            ngrp = 2
            bk = [0]

            def nbank():
                bk[0] += 1
                return pb[bk[0] % 4]
            m4 = self.masks4
            for gi in range(ngrp):
                g4 = slice(gi * 4, gi * 4 + nb_)
                gin = grps[gi]

                def ops(c, hh):
                    rows = slice(hh * 64, (hh + 1) * 64)
                    cc = slice(c * 128, (c + 1) * 128)
                    return al[rows, cc], be[rows, cc], ka_[rows, cc], rt[rows, cc]
                rb = [al.b(), be.b(), ka_.b(), rt.b()]
                for (dst, mi, sel) in ((Xa, m_n, (1, 0)), (XTa, m_nt, (0, 1)), (MakT, m_s, (2, 0)), (Wbr, m_i, (1, 3)), (Wkr, m_i, (2, 3))):
                    bank = nbank()
                    for j, (c, hh) in enumerate(gin):
                        o_ = ops(c, hh)
                        self.mm(bank[:, j * 128:(j + 1) * 128], o_[sel[0]], o_[sel[1]], True, True, rb, [bank.b()])
                    self.tt("dve", dst[:, g4, :], bank[:, 0:nb_ * 128].rearrange("p (j x) -> p j x", j=nb_), m4[:, mi, 0:nb_, :], ALU.mult,
                            [bank.b(), m4.b()], [dst.b(gi)])
                self.tt("pool", Pf[:, g4, :], Xa[:, g4, :], self.ident4[:, 0:nb_, :], ALU.add, [Xa.b(gi), self.ident4.b()], [Pf.b(gi)])
                self.cp("act", Pfin[:, g4, :], Pf[:, g4, :], [Pf.b(gi)], [Pfin.b(gi)])
            X, XT, Xn, XTn = Xa, XTa, Xb, XTb
            for lev in range(6):
                for gi in range(ngrp):
                    g4 = slice(gi * 4, gi * 4 + nb_)
                    if lev < 5:
                        bA_ = nbank()
                        for j in range(nb_):
                            ii = gi * 4 + j
                            self.mm(bA_[:, j * 128:(j + 1) * 128], XT[:, ii, :], X[:, ii, :], True, True, [X.b(gi), XT.b(gi)], [bA_.b()])
                    bB_ = nbank()
                    for j in range(nb_):
                        ii = gi * 4 + j
                        self.mm(bB_[:, j * 128:(j + 1) * 128], X[:, ii, :], XT[:, ii, :], True, True, [X.b(gi), XT.b(gi)], [bB_.b()])
                    if lev < 5:
                        self.cp("act", Xn[:, g4, :], bA_[:, 0:nb_ * 128].rearrange("p (j x) -> p j x", j=nb_), [bA_.b()], [Xn.b(gi)])
                    self.cp("dve", XTn[:, g4, :], bB_[:, 0:nb_ * 128].rearrange("p (j x) -> p j x", j=nb_), [bB_.b()], [XTn.b(gi)])
                    bC_ = nbank()
                    for j in range(nb_):
                        ii = gi * 4 + j
                        self.mm(bC_[:, j * 128:(j + 1) * 128], XTn[:, ii, :], Pfin[:, ii, :], True, True, [XTn.b(gi), Pfin.b(gi)], [bC_.b()])
                    self.tt("dve", Pf[:, g4, :], Pf[:, g4, :], bC_[:, 0:nb_ * 128].rearrange("p (j x) -> p j x", j=nb_), ALU.add, [Pf.b(gi), bC_.b()], [Pf.b(gi)])
                    self.cp("act", Pfin[:, g4, :], Pf[:, g4, :], [Pf.b(gi)], [Pfin.b(gi)])
                X, XT, Xn, XTn = Xn, XTn, X, XT
            self.bg_step()
            for bi, c in enumerate(batch):
                cc = slice(c * 128, (c + 1) * 128)
                p_rhs, p_u, p_y, p_s = pb[4], pb[5], pb[6], pb[7]
                for hh in range(2):
                    ii = hh * 4 + bi
                    rows = slice(hh * 64, (hh + 1) * 64)
                    vcols = slice(hh * 64, (hh + 1) * 64)
                    self.mm(p_rhs[:, vcols], al[rows, cc], Sb[rows, vcols], True, False, [al.b(), Sb.b()], [p_rhs.b()])
                    self.mm(p_rhs[:, vcols], MakT[:, ii, :], VtmP[:, c, hh, vcols], False, True, [MakT.b(ii // 4), VtmP.b()], [p_rhs.b()])
                self.cp("act", RHSs[:], p_rhs[:, 0:128], [p_rhs.b()], [RHSs.b()])
                for hh in range(2):
                    ii = hh * 4 + bi
                    vcols = slice(hh * 64, (hh + 1) * 64)
                    self.mm(p_u[:, vcols], Pfin[:, ii, :], RHSs[:, vcols], True, True, [Pfin.b(ii // 4), RHSs.b()], [p_u.b()])
                self.ts("dve", Ufull[:], p_u[:, 0:128], -1.0, None, ALU.mult, None, [p_u.b()], [Ufull.b()])
                for hh in range(2):
                    vcols = slice(hh * 64, (hh + 1) * 64)
                    self.cp("act" if hh else "dve", Upad[:, hh, vcols], Ufull[:, vcols], [Ufull.b()], [Upad.b()])
                self.mm(p_y[:, 0:128], Sb[:], rt[:, cc], True, False, [Sb.b(), rt.b()], [p_y.b()])
                for hh in range(2):
                    ii = hh * 4 + bi
                    self.mm(p_y[:, 0:128], Upad[:, hh, :], Wbr[:, ii, :], False, False, [Upad.b(), Wbr.b(ii // 4)], [p_y.b()])
                    self.mm(p_y[:, 0:128], VtmP[:, c, hh, :], Wkr[:, ii, :], False, hh == 1, [VtmP.b(), Wkr.b(ii // 4)], [p_y.b()])
                if d == 0:
                    self.cp("act", yacc[:, cc], p_y[:, 0:128], [p_y.b()], [yacc.b()])
                else:
                    self.tt("dve", yacc[:, cc], yacc[:, cc], p_y[:, 0:128], ALU.add, [yacc.b(), p_y.b()], [yacc.b()])
                self.mm(p_s[:, 0:128], BTt[:, c, :], Ufull[:], True, False, [BTt.b(), Ufull.b()], [p_s.b()])
                for hh in range(2):
                    self.mm(p_s[:, 0:128], KTt[:, c, :], VtmP[:, c, hh, :], False, hh == 1, [KTt.b(), VtmP.b()], [p_s.b()])
                self.stt("dve", tS[:], p_s[:, 0:128], PL[:, c:c + 1], self.bones_f[:], ALU.mult, ALU.mult,
                         [p_s.b(), PL.b(), self.bones_f.b()], [tS.b()])
                self.stt("dve", Sf[:], Sf[:], PL[:, c:c + 1], tS[:], ALU.mult, ALU.add, [Sf.b(), PL.b(), tS.b()], [Sf.b()])
                self.cp("act", Sb[:], Sf[:], [Sf.b()], [Sb.b()])
        self.bg_step()
        if not is_s:
            self.tr(pb[7][:, 0:128], Sf[:], self.ident_f[:], [Sf.b(), self.ident_f.b()], [pb[7].b()])
            self.cp("dve", Sld[:], pb[7][:, 0:128], [pb[7].b()], [Sld.b()])
            for hh in range(2):
                self.dma(self.ns[s, l, d, ct * 2 + hh], Sld[hh * 64:(hh + 1) * 64, hh * 64:(hh + 1) * 64], [Sld.b()], [self.ns.b()])

    def tiles(self):
        out = []
        for g, (Tg, nseq, Tseq) in self.groups.items():
            for c0 in range(0, Tg, 512):
                out.append((g, c0))
        return out

    def build(self):
        stages = self.cfg.get("stages", "all")
        self.declare()
        self.setup_consts()
        if stages != "nocast":
            self.cast_weights()
        if stages == "s0a":
            self.P.emit()
            return self.nc
        self.setup_params()
        if stages == "s0b":
            self.P.emit()
            return self.nc
        self.to_feature_major()
        if stages in ("s0c", "nocast"):
            self.P.emit()
            return self.nc
        X, Y = self.XT, self.XTB
        for l in range(self.DEPTH):
            for (g, c0) in self.tiles():
                if g in self.cfg.get("s1_groups", "sp"):
                    self.stage1_x(l, g, c0, X)
            if stages == "s1":
                break
            for g in self.groups:
                self.stage_fourier(l, g)
                self.stage_attn(l, g)
                if stages == "s2fa":
                    self.zero_rwkv(g)
                else:
                    self.stage_rwkv(l, g)
            if stages == "s2":
                break
            for (g, c0) in self.tiles():
                self.stage3(l, g, c0, X)
            rem = self.bg_sched.pop(("s3", l), [])
            if rem:
                with contextlib.ExitStack() as stk:
                    self.bg_begin(stk, rem, engs=("act", "dve", "pool", "dve"))
                    self.bg_end()
                    self.barrier()
            for (g, c0) in self.tiles():
                self.stage4(l, g, c0, X, Y)
            X, Y = Y, X
        if stages in ("all", "s2fa"):
            for (g, c0) in self.tiles():
                self.stage5(g, c0, X)
        self.P.emit()
        return self.nc


def make_in_maps(inputs, cfg, ncores):
    TS, TP, NPS, DEPTH = cfg["TS"], cfg["TP"], cfg["NPS"], cfg["DEPTH"]
    consts = host_consts(TS, TP)
    f = lambda a: np.ascontiguousarray(np.asarray(a, dtype=np.float32))
    shared = {}
    for k in ("w_ada", "b_ada", "norm1_g", "norm2_g", "w_in", "w_out", "q_norm_g", "k_norm_g", "rw_conv",
              "rw_w0", "rw_w2", "rw_a0", "rw_a2", "rw_g2", "rw_kk", "rw_ka", "rw_lnx_g", "rw_lnx_b",
              "ffn_up", "ffn_conv_w", "ffn_conv_b", "ffn_down", "final_norm_g"):
        shared[k] = f(inputs[k])
    shared["rw_rk"] = f(inputs["rw_rk"]).reshape(DEPTH, 512)
    shared.update(consts)
    maps = []
    for b in range(ncores):
        m = dict(shared)
        m["xs"] = f(inputs["x_sample"][b])
        m["xp"] = f(inputs["x_prompt"][NPS * b:NPS * (b + 1)]).reshape(NPS * TP, D)
        m["ck"] = f(inputs["cache_attn_k"][b]).reshape(DEPTH, PAST, 256)
        m["cv"] = f(inputs["cache_attn_v"][b]).reshape(DEPTH, PAST, 256)
        m["st0"] = f(inputs["state_rwkv"][b])
        m["cc"] = np.stack([f(inputs["c"][b]), f(inputs["c_ctx"])], 0)
        maps.append(m)
    return maps


_CACHE = {}


def kernel(**inputs):
    xs = np.asarray(inputs["x_sample"])
    xp = np.asarray(inputs["x_prompt"])
    ncores = xs.shape[0]
    TS, TP = xs.shape[1], xp.shape[1]
    NPS = xp.shape[0] // ncores
    DEPTH = np.asarray(inputs["w_in"]).shape[0]
    DFF = np.asarray(inputs["ffn_down"]).shape[1]
    cfg = dict(TS=TS, TP=TP, NPS=NPS, DEPTH=DEPTH, DFF=DFF, stages="all")
    key = (TS, TP, NPS, DEPTH, DFF)
    if key not in _CACHE:
        kb = KB(cfg)
        _CACHE[key] = kb.build()
    nc = _CACHE[key]
    maps = make_in_maps(inputs, cfg, ncores)
    decl = set()
    for alloc in nc.allocations:
        if isinstance(alloc, mybir.MemoryLocationSet) and alloc.kind == "ExternalInput":
            decl.add(alloc.memorylocations[0].name)
    maps = [{k: v for k, v in m.items() if k in decl} for m in maps]
    res = run_bass_kernel_spmd(nc, maps, core_ids=list(range(ncores)))
    r = res.results
    y_sample = np.stack([np.asarray(r[b]["ys"], np.float32) for b in range(ncores)], 0)
    y_prompt = np.concatenate([np.asarray(r[b]["yp"], np.float32).reshape(NPS, TP, D) for b in range(ncores)], 0)
    nk = np.concatenate([np.asarray(r[b]["nk"], np.float32).reshape(NPS, DEPTH, TP, NKV, 128) for b in range(ncores)], 0)
    nv = np.concatenate([np.asarray(r[b]["nv"], np.float32).reshape(NPS, DEPTH, TP, NKV, 128) for b in range(ncores)], 0)
    ns = np.concatenate([np.asarray(r[b]["ns"], np.float32) for b in range(ncores)], 0)
    return (y_prompt, y_sample, nk, nv, ns)
```

```python
import contextlib
import math
import numpy as np
import ml_dtypes
import concourse.bass as bass
import concourse.mybir as mybir
from concourse.bass_utils import run_bass_kernel_spmd

F32 = mybir.dt.float32
BF16 = mybir.dt.bfloat16
ALU = mybir.AluOpType
AF = mybir.ActivationFunctionType
AX = mybir.AxisListType

EPOCH = 8000
RING = 8


class Buf:
    __slots__ = ("wc", "wd", "rc", "rd", "excl")

    def __init__(self, excl=False):
        self.excl = excl
        self.wc = {}
        self.wd = {}
        self.rc = {}
        self.rd = {}


class Op:
    __slots__ = ("eng", "fn", "waits", "done", "is_dma", "stage")

    def __init__(self, eng, fn, is_dma):
        self.stage = None
        self.eng = eng
        self.fn = fn
        self.waits = {}
        self.done = None
        self.is_dma = is_dma


class Prog:
    ENGS = ("pe", "act", "dve", "pool", "sp")

    def __init__(self, nc):
        self.nc = nc
        self.q = {e: [] for e in self.ENGS}
        self.cnt = {e: 0 for e in self.ENGS}
        self.dcnt = {e: 0 for e in self.ENGS}
        self.sems = {}
        self.semkeys = []
        self.last_dma = {}
        self.pending = {}

    def _semkey(self, key):
        if key not in self.sems:
            self.sems[key] = None
            self.semkeys.append(key)
        return key

    def _add_dep(self, op, dep, raw):
        if dep is None or dep is op:
            return
        if dep.eng == op.eng and not dep.is_dma and not op.is_dma:
            if op.eng == "pe":
                return
        key, val = dep.done
        if op.waits.get(key, 0) < val:
            op.waits[key] = val

    def op(self, eng, fn, reads=(), writes=(), dma=False):
        o = Op(eng, fn, dma)
        o.stage = getattr(self, "stage", None)
        if any(b.excl for b in reads):
            writes = list(writes) + [b for b in reads if b.excl and b not in writes]
            reads = [b for b in reads if not b.excl]
        self.nops = getattr(self, "nops", 0) + 1
        if self.nops > getattr(self, "limit", 1 << 60):
            return o
        pend = self.pending.pop(eng, None)
        if pend:
            for key, val in pend.items():
                if o.waits.get(key, 0) < val:
                    o.waits[key] = val
        for b in reads:
            for d in b.wc.values():
                self._add_dep(o, d, True)
            for lst in b.wd.values():
                for d in lst:
                    self._add_dep(o, d, True)
        for b in writes:
            for d in b.wc.values():
                self._add_dep(o, d, False)
            for lst in b.wd.values():
                for d in lst:
                    self._add_dep(o, d, False)
            for d in b.rc.values():
                self._add_dep(o, d, False)
            for lst in b.rd.values():
                for d in lst:
                    self._add_dep(o, d, False)
        if dma:
            k = self.dcnt[eng]
            self.dcnt[eng] += 1
            slot = k % RING
            key = self._semkey(("d", eng, slot))
            o.done = (key, 16 * (k // RING + 1))
            prev = self.last_dma.get((eng, slot))
            if prev is not None:
                self._add_dep(o, prev, True)
            self.last_dma[(eng, slot)] = o
        else:
            k = self.cnt[eng]
            self.cnt[eng] += 1
            key = self._semkey(("c", eng, k // EPOCH))
            o.done = (key, k % EPOCH + 1)
        for b in reads:
            if dma:
                lst = b.rd.setdefault(eng, [])
                lst.append(o)
                if len(lst) > RING:
                    del lst[0]
            else:
                b.rc[eng] = o
        for b in writes:
            b.rc = {}
            b.rd = {}
            if dma:
                lst = b.wd.setdefault(eng, [])
                lst.append(o)
                if len(lst) > RING:
                    del lst[0]
            else:
                b.wc[eng] = o
        self.q[eng].append(o)
        return o

    def emit(self):
        nc = self.nc
        with contextlib.ExitStack() as st:
            for key in self.semkeys:
                self.sems[key] = st.enter_context(nc.semaphore("s_" + "_".join(str(x) for x in key)))
            block = st.enter_context(nc.Block())
            engmap = {"pe": block.tensor, "act": block.scalar, "dve": block.vector,
                      "pool": block.gpsimd, "sp": block.sync}
            all_ops = self.q

            def make(ename):
                ops = all_ops[ename]

                def body(e):
                    seen = {}
                    for o in ops:
                        for key, val in o.waits.items():
                            if seen.get(key, 0) >= val:
                                continue
                            seen[key] = val
                            e.wait_ge(self.sems[key], val)
                        ins = o.fn(e)
                        if self.annotate and o.stage:
                            ins.annotate(o.stage)
                        key, val = o.done
                        ins.then_inc(self.sems[key], 16 if o.is_dma else 1)
                    if ename == "sp":
                        fin = {}
                        for en in self.ENGS:
                            for o in all_ops[en][-1:]:
                                key, val = o.done
                                fin[key] = max(fin.get(key, 0), val)
                        for o in self.last_dma.values():
                            key, val = o.done
                            fin[key] = max(fin.get(key, 0), val)
                        for key, val in fin.items():
                            if seen.get(key, 0) < val:
                                e.wait_ge(self.sems[key], val)
                return body

            for ename in self.ENGS:
                if all_ops[ename] or ename == "sp":
                    engmap[ename](make(ename))


class TB:
    def __init__(self, h, excl=False):
        self.h = h
        self.bufs = {}
        self.excl = excl

    def __getitem__(self, idx):
        return self.h[idx]

    def b(self, key=None):
        if key not in self.bufs:
            self.bufs[key] = Buf(self.excl)
        return self.bufs[key]


class TBV:
    def __init__(self, ap):
        self.ap = ap
        self.buf = Buf(True)

    def __getitem__(self, idx):
        return self.ap[idx]

    def b(self, key=None):
        return self.buf


class DT:
    def __init__(self, ap):
        self.ap = ap
        self.bufs = {}

    def __getitem__(self, idx):
        return self.ap[idx]

    def b(self, key=None):
        if key not in self.bufs:
            self.bufs[key] = Buf()
        return self.bufs[key]


D = 2048
KT = 16
NH = 8
NKV = 2
PAST = 512
IN_W = 3840
RW_IN = 1792
GRID_W = 64
NORM_EPS = 1e-6
GN_EPS = 64e-5
LWC = -math.exp(-0.5)


def host_consts(TS, TP):
    c = {}
    bf = ml_dtypes.bfloat16
    c["ident_f"] = np.eye(128, dtype=np.float32)
    c["ident_b"] = np.eye(128, dtype=np.float32).astype(bf)
    c["ones_b"] = np.ones((128, 128), np.float32).astype(bf)
    bo = np.zeros((128, 128), np.float32)
    bo[:64, :64] = 1.0
    bo[64:, 64:] = 1.0
    c["bones_b"] = bo.astype(bf)
    c["bones_f"] = bo
    prot = np.zeros((128, 128), np.float32)
    for m in range(128):
        j = m % 64
        if j < 32:
            prot[m + 32, m] = -1.0
        else:
            prot[m - 32, m] = 1.0
    c["prot_f"] = prot
    t = np.arange(TS)
    rows = (t // GRID_W).astype(np.float64)
    cols = (t % GRID_W).astype(np.float64)
    inv = 1.0 / (10000.0 ** (np.arange(0, 64, 2, dtype=np.float64) / 64.0))
    cosT = np.zeros((128, TS), np.float64)
    sinT = np.zeros((128, TS), np.float64)
    for d in range(128):
        pos = rows if d < 64 else cols
        f = inv[(d % 64) % 32]
        ang = np.float32(pos).astype(np.float32) * np.float32(f)
        cosT[d] = np.cos(ang.astype(np.float64))
        sinT[d] = np.sin(ang.astype(np.float64))
    c["cosT"] = cosT.astype(np.float32)
    c["sinT"] = sinT.astype(np.float32)
    i = np.arange(128)
    ang = 2 * np.pi * np.outer(i, i) / 128.0
    c["csC"] = (np.concatenate([np.cos(ang), np.sin(ang)], 1) / np.sqrt(128.0)).astype(np.float32).astype(bf)
    for nm, T in (("S", TS), ("P", TP)):
        i = np.arange(T)
        ang = 2 * np.pi * ((np.outer(i, i)) % T) / float(T)
        c["ct" + nm] = (np.cos(ang) / np.sqrt(T)).astype(np.float32).astype(bf)
        c["nst" + nm] = (-np.sin(ang) / np.sqrt(T)).astype(np.float32).astype(bf)
    idx = np.arange(128)
    su = (idx[:, None] < idx[None, :]).astype(np.float32)
    iu = (idx[:, None] <= idx[None, :]).astype(np.float32)
    sl = su.T.copy()
    il = iu.T.copy()
    c["masks"] = np.stack([su, sl, iu, il, -su, -sl], 0).astype(np.float32)
    c["masks4"] = np.repeat(c["masks"][:, None, :, :], 4, axis=1).transpose(2, 0, 1, 3).copy().astype(np.float32)
    c["ident4"] = np.repeat(np.eye(128, dtype=np.float32)[:, None, :], 4, axis=1).copy()
    c["tri2"] = np.stack([np.concatenate([iu, su], 1), np.concatenate([il, sl], 1)], 0).astype(np.float32) * np.float32(LWC)
    return c


class KB:
    def __init__(self, cfg):
        self.cfg = cfg
        self.TS = cfg["TS"]
        self.TP = cfg["TP"]
        self.NPS = cfg["NPS"]
        self.DFF = cfg["DFF"]
        self.FT = self.DFF // 128
        self.DEPTH = cfg["DEPTH"]
        self.dbg = set(cfg.get("dbg", ()))
        self.nc = bass.Bass("TRN2", target_bir_lowering=False)
        self.P = Prog(self.nc)
        self.P.limit = cfg.get("limit", 1 << 60)
        self.P.annotate = bool(cfg.get("annotate"))
        self.P.stage = "init"
        self.st = contextlib.ExitStack()
        self.dram = {}
        self.uid = 0
        self.bar_uid = 0
        self.groups = {"s": (self.TS, 1, self.TS), "p": (self.NPS * self.TP, self.NPS, self.TP)}

    def din(self, name, shape, dt=F32):
        t = DT(self.nc.dram_tensor(name, list(shape), dt, kind="ExternalInput").ap())
        self.dram[name] = t
        return t

    def dout(self, name, shape, dt=F32):
        t = DT(self.nc.dram_tensor(name, list(shape), dt, kind="ExternalOutput").ap())
        self.dram[name] = t
        return t

    def dscr(self, name, shape, dt):
        kind = "ExternalOutput" if name in self.dbg else "Internal"
        t = DT(self.nc.dram_tensor(name, list(shape), dt, kind=kind).ap())
        self.dram[name] = t
        return t

    def sb(self, name, shape, dt, stack=None):
        self.uid += 1
        h = (stack or self.st).enter_context(self.nc.sbuf_tensor(f"{name}_{self.uid}", list(shape), dt))
        return TB(h)

    def ps(self, name, shape, dt=F32, stack=None):
        self.uid += 1
        h = (stack or self.st).enter_context(self.nc.psum_tensor(f"{name}_{self.uid}", list(shape), dt))
        return TB(h, excl=True)

    def dma(self, out, in_, reads, writes, eng="sp", **kw):
        return self.P.op(eng, lambda e: e.dma_start(out=out, in_=in_, **kw), reads, writes, dma=True)

    def mm(self, out, lhsT, rhs, start, stop, reads, writes):
        return self.P.op("pe", lambda e: e.matmul(out, lhsT=lhsT, rhs=rhs, start=start, stop=stop), reads, writes)

    def tr(self, out, in_, ident, reads, writes):
        return self.P.op("pe", lambda e: e.transpose(out=out, in_=in_, identity=ident), reads, writes)

    def act(self, out, in_, func, reads, writes, **kw):
        return self.P.op("act", lambda e: e.activation(out=out, in_=in_, func=func, **kw), reads, writes)

    def tt(self, eng, out, in0, in1, op, reads, writes):
        return self.P.op(eng, lambda e: e.tensor_tensor(out=out, in0=in0, in1=in1, op=op), reads, writes)

    def ts(self, eng, out, in0, s1, s2, op0, op1, reads, writes):
        if s2 is None:
            return self.P.op(eng, lambda e: e.tensor_scalar(out=out, in0=in0, scalar1=s1, scalar2=None, op0=op0), reads, writes)
        return self.P.op(eng, lambda e: e.tensor_scalar(out=out, in0=in0, scalar1=s1, scalar2=s2, op0=op0, op1=op1), reads, writes)

    def stt(self, eng, out, in0, scalar, in1, op0, op1, reads, writes):
        return self.P.op(eng, lambda e: e.scalar_tensor_tensor(out=out, in0=in0, scalar=scalar, in1=in1, op0=op0, op1=op1), reads, writes)

    def cp(self, eng, out, in_, reads, writes):
        if eng == "act":
            return self.P.op("act", lambda e: e.copy(out=out, in_=in_), reads, writes)
        return self.P.op(eng, lambda e: e.tensor_copy(out=out, in_=in_), reads, writes)

    def memset(self, eng, ap, val, writes):
        return self.P.op(eng, lambda e: e.memset(ap, val), [], writes)

    def recip(self, out, in_, reads, writes):
        return self.P.op("dve", lambda e: e.reciprocal(out=out, in_=in_), reads, writes)

    def barrier(self):
        P = self.P
        fin = {}
        for en in P.ENGS:
            for o in P.q[en][-1:]:
                key, val = o.done
                fin[key] = max(fin.get(key, 0), val)
        for o in P.last_dma.values():
            key, val = o.done
            fin[key] = max(fin.get(key, 0), val)
        for en in P.ENGS:
            d = P.pending.setdefault(en, {})
            for key, val in fin.items():
                d[key] = max(d.get(key, 0), val)

    def declare(self):
        TS, TP, NPS, DEPTH, DFF = self.TS, self.TP, self.NPS, self.DEPTH, self.DFF
        TPG = NPS * TP
        di = self.din
        self.xs = di("xs", [TS, D])
        self.xp = di("xp", [TPG, D])
        self.ck = di("ck", [DEPTH, PAST, 256])
        self.cv = di("cv", [DEPTH, PAST, 256])
        self.st0 = di("st0", [DEPTH, 2, 8, 64, 64])
        self.cc = di("cc", [2, D])
        self.w_ada = di("w_ada", [DEPTH, D, 6 * D])
        self.b_ada = di("b_ada", [DEPTH, 6 * D])
        self.norm1_g = di("norm1_g", [DEPTH, D])
        self.norm2_g = di("norm2_g", [DEPTH, D])
        self.w_in = di("w_in", [DEPTH, D, IN_W])
        self.w_out = di("w_out", [DEPTH, D, D])
        self.q_norm_g = di("q_norm_g", [DEPTH, 128])
        self.k_norm_g = di("k_norm_g", [DEPTH, 128])
        self.rw_conv = di("rw_conv", [DEPTH, 3, RW_IN])
        self.rw_w0 = di("rw_w0", [DEPTH, 2, 512])
        self.rw_w2 = di("rw_w2", [DEPTH, 2, 64, 512])
        self.rw_a0 = di("rw_a0", [DEPTH, 2, 512])
        self.rw_a2 = di("rw_a2", [DEPTH, 2, 64, 512])
        self.rw_g2 = di("rw_g2", [DEPTH, 128, 512])
        self.rw_kk = di("rw_kk", [DEPTH, 512])
        self.rw_ka = di("rw_ka", [DEPTH, 512])
        self.rw_rk = di("rw_rk", [DEPTH, 512])
        self.rw_lnx_g = di("rw_lnx_g", [DEPTH, 512])
        self.rw_lnx_b = di("rw_lnx_b", [DEPTH, 512])
        self.ffn_up = di("ffn_up", [DEPTH, D, 2 * DFF])
        self.ffn_conv_w = di("ffn_conv_w", [DEPTH, 3, 2 * DFF])
        self.ffn_conv_b = di("ffn_conv_b", [DEPTH, 2 * DFF])
        self.ffn_down = di("ffn_down", [DEPTH, DFF, D])
        self.final_norm_g = di("final_norm_g", [D])
        self.c_ident_f = di("ident_f", [128, 128])
        self.c_ident_b = di("ident_b", [128, 128], BF16)
        self.c_ones_b = di("ones_b", [128, 128], BF16)
        self.c_bones_b = di("bones_b", [128, 128], BF16)
        self.c_bones_f = di("bones_f", [128, 128])
        self.c_prot_f = di("prot_f", [128, 128])
        self.c_cosT = di("cosT", [128, TS])
        self.c_sinT = di("sinT", [128, TS])
        self.c_csC = di("csC", [128, 256], BF16)
        self.c_ct = {"s": di("ctS", [TS, TS], BF16), "p": di("ctP", [TP, TP], BF16)}
        self.c_nst = {"s": di("nstS", [TS, TS], BF16), "p": di("nstP", [TP, TP], BF16)}
        self.c_masks = di("masks", [6, 128, 128])
        self.c_tri2 = di("tri2", [2, 128, 256])
        self.c_masks4 = di("masks4", [128, 6, 4, 128])
        self.c_ident4 = di("ident4", [128, 4, 128])
        self.ys = self.dout("ys", [TS, D])
        self.yp = self.dout("yp", [TPG, D])
        self.nk = self.dout("nk", [NPS, DEPTH, TP, 256])
        self.nv = self.dout("nv", [NPS, DEPTH, TP, 256])
        self.ns = self.dout("ns", [NPS, DEPTH, 2, 8, 64, 64])
        ds = self.dscr
        FT = self.FT
        self.Win_t = ds("Win_t", [DEPTH, 30, 128, KT, 128], BF16)
        self.Wv_t = ds("Wv_t", [DEPTH, 128, KT, 256], BF16)
        self.Wout_t = ds("Wout_t", [DEPTH, 16, 128, KT, 128], BF16)
        self.Wup_t = ds("Wup_t", [DEPTH, 2 * FT, 128, KT, 128], BF16)
        self.Wdn_t = ds("Wdn_t", [DEPTH, 16, 128, FT, 128], BF16)
        self.mod_rows = [ds(f"modrows{l}", [2, 6 * D], F32) for l in range(DEPTH)]
        self.XT = {}
        self.XTB = {}
        self.QT = {}
        self.KTs = {}
        self.Vs = {}
        self.RW = {}
        self.AB = {}
        self.MIXT = {}
        for g, (Tg, nseq, Tseq) in self.groups.items():
            self.XT[g] = ds("XT_" + g, [KT, 128, Tg], F32)
            self.XTB[g] = ds("XTB_" + g, [KT, 128, Tg], F32)
            self.QT[g] = ds("QT_" + g, [NH, 128, Tg], BF16)
            self.KTs[g] = ds("KT_" + g, [NKV, 128, Tg], BF16)
            self.Vs[g] = ds("V_" + g, [Tg, 256], BF16)
            self.RW[g] = ds("RW_" + g, [RW_IN, Tg], F32)
            self.AB[g] = ds("AB_" + g, [Tg, 1024], BF16)
            self.MIXT[g] = ds("MIXT_" + g, [KT, 128, Tg], BF16)

    def load_const(self, name, src, shape, dt):
        t = self.sb(name, shape, dt)
        self.dma(t[:], src.ap, [src.b()], [t.b()])
        return t

    def setup_consts(self):
        self.P.stage = "consts"
        TS = self.TS
        self.ident_f = self.load_const("ident_f", self.c_ident_f, [128, 128], F32)
        self.ident_b = self.load_const("ident_b", self.c_ident_b, [128, 128], BF16)
        self.ones_b = self.load_const("ones_b", self.c_ones_b, [128, 128], BF16)
        self.bones_b = self.load_const("bones_b", self.c_bones_b, [128, 128], BF16)
        self.bones_f = self.load_const("bones_f", self.c_bones_f, [128, 128], F32)
        self.prot_f = self.load_const("prot_f", self.c_prot_f, [128, 128], F32)
        self.csC = self.load_const("csC", self.c_csC, [128, 256], BF16)
        self.masks = self.sb("masks", [128, 6, 128], F32)
        self.dma(self.masks[:], self.c_masks.ap.rearrange("m p c -> p m c"), [self.c_masks.b()], [self.masks.b()])
        self.tri2 = self.sb("tri2", [128, 2, 256], F32)
        self.dma(self.tri2[:], self.c_tri2.ap.rearrange("m p c -> p m c"), [self.c_tri2.b()], [self.tri2.b()])
        self.masks4 = self.load_const("masks4", self.c_masks4, [128, 6, 4, 128], F32)
        self.ident4 = self.load_const("ident4", self.c_ident4, [128, 4, 128], F32)
        self.eps6 = self.sb("eps6", [128, 1], F32)
        self.memset("pool", self.eps6[:], NORM_EPS, [self.eps6.b()])
        self.eps12 = self.sb("eps12", [128, 1], F32)
        self.memset("pool", self.eps12[:], 1e-12, [self.eps12.b()])
        self.epsgn = self.sb("epsgn", [128, 1], F32)
        self.memset("pool", self.epsgn[:], GN_EPS, [self.epsgn.b()])
        self.pw = [self.ps(f"pw{i}", [128, 1024], F32) for i in range(4)]
        self.pb = [TBV(self.pw[i // 2].h[:, (i % 2) * 512:(i % 2) * 512 + 512]) for i in range(8)]

    def cast_jobs(self, l, which):
        FT = self.FT
        jobs = []
        if which == "in":
            for c0 in range(0, IN_W, 512):
                wd = min(512, IN_W - c0)
                dsts = []
                for j in range(wd // 128):
                    ot = c0 // 128 + j
                    if ot == 14:
                        dsts.append((self.Wv_t[l], self.Wv_t.b(l), j * 128, 256))
                    elif ot != 15:
                        dsts.append((self.Win_t[l, ot], self.Win_t.b((l, ot)), j * 128, 128))
                jobs.append((self.w_in, l, c0, wd, KT, dsts))
        else:
            for c0 in range(0, D, 512):
                dsts = [(self.Wout_t[l, c0 // 128 + j], self.Wout_t.b((l, c0 // 128 + j)), j * 128, 128) for j in range(4)]
                jobs.append((self.w_out, l, c0, 512, KT, dsts))
            for c0 in range(0, 2 * self.DFF, 512):
                wd = min(512, 2 * self.DFF - c0)
                dsts = [(self.Wup_t[l, c0 // 128 + j], self.Wup_t.b((l, c0 // 128 + j)), j * 128, 128) for j in range(wd // 128)]
                jobs.append((self.ffn_up, l, c0, wd, KT, dsts))
            cw = 128 if FT > 16 else 512
            for c0 in range(0, D, cw):
                dsts = [(self.Wdn_t[l, c0 // 128 + j], self.Wdn_t.b((l, c0 // 128 + j)), j * 128, 128) for j in range(cw // 128)]
                jobs.append((self.ffn_down, l, c0, cw, FT, dsts))
        return jobs

    def bg_begin(self, stk, jobs, engs=("pool",)):
        nel = max(KT * 512, self.FT * (128 if self.FT > 16 else 512))
        self.bg = dict(jobs=list(jobs), i=0, pend=None, engs=engs, n=0,
                       wf=[self.sb("wcf", [128, nel], F32, stk) for _ in range(2)],
                       wb=[self.sb("wcb", [128, nel], BF16, stk) for _ in range(2)])

    def bg_step(self):
        bg = getattr(self, "bg", None)
        if bg is None:
            return
        if bg["pend"] is not None:
            (src, l, c0, wd, ktn, dsts), f, b = bg["pend"]
            fv = f[:, 0:ktn * wd].rearrange("p (kt c) -> p kt c", c=wd)
            off = 0
            for (dap, dbuf, co, w) in dsts:
                view = b[:, off:off + ktn * w].rearrange("p (kt c) -> p kt c", c=w)
                self.cp(bg["engs"][bg["n"] % len(bg["engs"])], view, fv[:, :, co:co + w], [f.b()], [b.b()])
                bg["n"] += 1
                self.dma(dap, view, [b.b()], [dbuf])
                off += ktn * w
            bg["pend"] = None
        if bg["i"] < len(bg["jobs"]):
            job = bg["jobs"][bg["i"]]
            f = bg["wf"][bg["i"] % 2]
            b = bg["wb"][bg["i"] % 2]
            (src, l, c0, wd, ktn, dsts) = job
            fv = f[:, 0:ktn * wd].rearrange("p (kt c) -> p kt c", c=wd)
            self.dma(fv, src[l, :, c0:c0 + wd].rearrange("(kt p) c -> p kt c", p=128), [src.b()], [f.b()])
            bg["pend"] = (job, f, b)
            bg["i"] += 1

    def bg_end(self):
        bg = getattr(self, "bg", None)
        if bg is None:
            return
        while bg["pend"] is not None or bg["i"] < len(bg["jobs"]):
            self.bg_step()
        self.bg = None

    def cast_weights(self):
        self.P.stage = "cast"
        sched = self.cfg.get("bg_cast", True)
        jobs = self.cast_jobs(0, "in")
        self.bg_sched = {}
        if sched:
            r0 = self.cast_jobs(0, "rest")
            h = len(r0) // 2
            self.bg_sched[("attn", 0, "s")] = r0[:h]
            self.bg_sched[("rwkv", 0, "p")] = r0[h:]
            for l in range(1, self.DEPTH):
                self.bg_sched[("rwkv", l - 1, "p")] = self.bg_sched.get(("rwkv", l - 1, "p"), []) + self.cast_jobs(l, "in")
                rl = self.cast_jobs(l, "rest")
                h = len(rl) // 2
                self.bg_sched[("attn", l, "s")] = rl[:h]
                self.bg_sched[("rwkv", l, "p")] = rl[h:]
        else:
            jobs += self.cast_jobs(0, "rest")
            for l in range(1, self.DEPTH):
                jobs += self.cast_jobs(l, "in") + self.cast_jobs(l, "rest")
        with contextlib.ExitStack() as stk:
            self.bg_begin(stk, jobs, engs=("act", "dve", "pool", "dve"))
            self.bg_end()
            self.barrier()

    def load_pp(self, dst, dst_b, src_rows, n, src_b):
        k = self._pp_i = getattr(self, "_pp_i", 0) + 1
        stg = self.pp_stage[k % 2]
        bank = self.pb[6 + (k % 2)]
        self.dma(stg[0:n, :], src_rows, [src_b], [stg.b()])
        self.tr(bank[:, 0:n], stg[0:n, :], self.ident_f[0:n, 0:n], [stg.b(), self.ident_f.b()], [bank.b()])
        self.cp("dve", dst, bank[:, 0:n], [bank.b()], [dst_b])

    def to_feature_major(self):
        self.P.stage = "tofm"
        with contextlib.ExitStack() as stk:
            xin = [self.sb("xin", [128, D], F32, stk) for _ in range(2)]
            xo = [self.sb("xo", [128, KT, 128], F32, stk) for _ in range(2)]
            i = 0
            for g, src in (("s", self.xs), ("p", self.xp)):
                Tg = self.groups[g][0]
                for blk in range(Tg // 128):
                    a = xin[i % 2]
                    o = xo[i % 2]
                    self.dma(a[:], src[blk * 128:(blk + 1) * 128, :], [src.b()], [a.b()])
                    for q in range(4):
                        bank = self.pb[(i * 4 + q) % 4]
                        for j in range(4):
                            kt = q * 4 + j
                            self.tr(bank[:, j * 128:(j + 1) * 128], a[:, kt * 128:(kt + 1) * 128], self.ident_f[:],
                                    [a.b(), self.ident_f.b()], [bank.b()])
                        eng = "act" if q % 2 else "dve"
                        self.cp(eng, o[:, q * 4:(q + 1) * 4, :], bank[:].rearrange("p (j t) -> p j t", j=4), [bank.b()], [o.b()])
                    self.dma(self.XT[g][:, :, blk * 128:(blk + 1) * 128].rearrange("kt p t -> p kt t"), o[:],
                             [o.b()], [self.XT[g].b(blk // 4)])
                    i += 1
            self.barrier()

    def setup_params(self):
        self.P.stage = "params"
        DEPTH, FT = self.DEPTH, self.FT
        self.pp_stage = [self.sb("ppstg", [128, 128], F32) for _ in range(2)]
        self.prm = []
        ccT = self.sb("ccT", [128, 32], F32)
        self.load_pp(ccT[:], ccT.b(), self.cc.ap.rearrange("v (kt p) -> (v kt) p", p=128), 32, self.cc.b())
        sc = self.sb("sc", [128, 32], F32)
        self.act(sc[:], ccT[:], AF.Silu, [ccT.b()], [sc.b()])
        fng = self.sb("fng", [128, KT], F32)
        self.load_pp(fng[:], fng.b(), self.final_norm_g.ap.rearrange("(kt p) -> kt p", p=128), KT, self.final_norm_g.b())
        self.fng = fng
        for l in range(DEPTH):
            pr = {}
            def vec(name, src, n, rows):
                t = self.sb(name, [128, n], F32)
                self.load_pp(t[:], t.b(), rows, n, src.b())
                return t
            pr["n1g"] = vec("n1g", self.norm1_g, KT, self.norm1_g[l].rearrange("(kt p) -> kt p", p=128))
            pr["n2g"] = vec("n2g", self.norm2_g, KT, self.norm2_g[l].rearrange("(kt p) -> kt p", p=128))
            pr["qng"] = vec("qng", self.q_norm_g, 1, self.q_norm_g[l:l + 1, :])
            pr["kng"] = vec("kng", self.k_norm_g, 1, self.k_norm_g[l:l + 1, :])
            pr["rwconv"] = vec("rwconv", self.rw_conv, 42, self.rw_conv[l].rearrange("j (t p) -> (j t) p", p=128))
            for nm, src in (("kk", self.rw_kk), ("ka", self.rw_ka), ("rk", self.rw_rk), ("lng", self.rw_lnx_g), ("lnb", self.rw_lnx_b)):
                pr[nm] = vec(nm, src, 4, src[l].rearrange("(t p) -> t p", p=128))
            pr["a0"] = vec("a0", self.rw_a0, 8, self.rw_a0[l].rearrange("d (t p) -> (d t) p", p=128))
            c1 = self.sb("c1", [128, 4], F32)
            self.ts("dve", c1[:], pr["ka"][:], -1.0, 1.0, ALU.mult, ALU.add, [pr["ka"].b()], [c1.b()])
            pr["c1"] = c1
            nft = 2 * FT
            fcw = self.sb("fcw", [128, 3, nft], F32)
            for j in range(3):
                self.load_pp(fcw[:, j, :], fcw.b(), self.ffn_conv_w[l, j].rearrange("(t p) -> t p", p=128), nft, self.ffn_conv_w.b())
            pr["fcw"] = fcw
            nfcw = self.sb("nfcw", [128, 3, nft], F32)
            self.ts("dve", nfcw[:], fcw[:], -1.0, None, ALU.mult, None, [fcw.b()], [nfcw.b()])
            pr["nfcw"] = nfcw
            pr["fcb"] = vec("fcb", self.ffn_conv_b, nft, self.ffn_conv_b[l].rearrange("(t p) -> t p", p=128))
            bada = vec("bada", self.b_ada, 96, self.b_ada[l].rearrange("(t p) -> t p", p=128))
            mods = [self.sb(f"mod{v}", [128, 96], F32) for v in range(2)]
            modr = self.mod_rows[l]
            with contextlib.ExitStack() as stk:
                wa = [self.sb("wada", [128, KT, 1024], F32, stk) for _ in range(2)]
                rowt = [self.sb("modrow", [2, 512], F32, stk) for _ in range(2)]
                for blk in range(12):
                    w = wa[blk % 2]
                    self.dma(w[:], self.w_ada[l, :, blk * 1024:(blk + 1) * 1024].rearrange("(kt p) c -> p kt c", p=128),
                             [self.w_ada.b()], [w.b()], eng=("sp", "act")[blk % 2])
                    for hf in range(2):
                        cb = blk * 2 + hf
                        acc = self.pb[4 + cb % 2]
                        for kt in range(KT):
                            self.mm(acc[0:2, :], sc[:, kt:32:16], w[:, kt, hf * 512:(hf + 1) * 512], kt == 0, kt == KT - 1,
                                    [w.b(), sc.b()], [acc.b()])
                        rt_ = rowt[cb % 2]
                        self.cp("dve" if cb % 2 else "act", rt_[:], acc[0:2, :], [acc.b()], [rt_.b()])
                        self.dma(modr[:, cb * 512:(cb + 1) * 512], rt_[:], [rt_.b()], [modr.b()])
                for v in range(2):
                    m = mods[v]
                    self.load_pp(m[:], m.b(), modr[v].rearrange("(t p) -> t p", p=128), 96, modr.b())
                    self.tt("dve", m[:], m[:], bada[:], ALU.add, [m.b(), bada.b()], [m.b()])
                self.barrier()
            pr["mod"] = mods
            for v in range(2):
                for nm, gname, j in (("gs1", "n1g", 1), ("gs2", "n2g", 4)):
                    t = self.sb(f"{nm}_{v}", [128, KT], F32)
                    self.stt("dve", t[:], mods[v][:, j * 16:(j + 1) * 16], 1.0, pr[gname][:], ALU.add, ALU.mult,
                             [mods[v].b(), pr[gname].b()], [t.b()])
                    pr[f"{nm}_{v}"] = t
            self.prm.append(pr)

    def norm_mod(self, xt, chunks, gs, sh_ap_fn, hT, stk, ss_banks):
        Wtot = chunks[-1][1]
        sq = [self.sb("nsq", [128, Wtot], BF16, stk) for _ in range(2)]
        rstd = self.sb("nrstd", [128, Wtot], F32, stk)
        tmp = [self.sb("ntmp", [128, Wtot], F32, stk) for _ in range(2)]
        for kt in range(KT):
            s = sq[kt % 2]
            self.act(s[:], xt[:, kt, :], AF.Square, [xt.b()], [s.b()])
            for ci, (c0, c1) in enumerate(chunks):
                bk = ss_banks[ci]
                self.mm(bk[:, 0:c1 - c0], self.ones_b[:], s[:, c0:c1], kt == 0, kt == KT - 1,
                        [s.b(), self.ones_b.b()], [bk.b()])
        for ci, (c0, c1) in enumerate(chunks):
            bk = ss_banks[ci]
            self.act(rstd[:, c0:c1], bk[:, 0:c1 - c0], AF.Sqrt, [bk.b(), self.eps6.b()], [rstd.b()],
                     scale=1.0 / D, bias=self.eps6[:, 0:1])
        self.recip(rstd[:], rstd[:], [rstd.b()], [rstd.b()])
        for kt in range(KT):
            t = tmp[kt % 2]
            self.tt("pool" if kt % 2 else "dve", t[:], xt[:, kt, :], rstd[:], ALU.mult, [xt.b(), rstd.b()], [t.b()])
            self.act(hT[:, kt, :], t[:], AF.Identity, [t.b(), gs.b()], [hT.b()],
                     scale=gs[:, kt:kt + 1], bias=sh_ap_fn(kt))

    def zero_rwkv(self, g):
        Tg = self.groups[g][0]
        with contextlib.ExitStack() as stk:
            z = self.sb("zz", [128, Tg], BF16, stk)
            self.memset("pool", z[:], 0.0, [z.b()])
            for r in range(12, 16):
                self.dma(self.MIXT[g][r], z[:], [z.b()], [self.MIXT[g].b(i) for i in range((Tg + 511) // 512)])
            self.barrier()

    def stage1_x(self, l, g, c0, X):
        self._X1 = X
        return self.stage1(l, g, c0)

    def stage1(self, l, g, c0):
        self.P.stage = f"s1_{l}_{g}"
        W = 512
        v = 0 if g == "s" else 1
        pr = self.prm[l]
        mod = pr["mod"][v]
        ti = c0 // 512
        pb = self.pb
        is_s = (g == "s")
        with contextlib.ExitStack() as stk:
            xt = self.sb("xt", [128, KT, W], F32, stk)
            hT = self.sb("hT", [128, KT, W], BF16, stk)
            self.dma(xt[:], self._X1[g][:, :, c0:c0 + W].rearrange("kt p t -> p kt t"), [self._X1[g].b(ti)], [xt.b()])
            self.norm_mod(xt, [(0, W)], pr[f"gs1_{v}"], lambda kt: mod[:, kt:kt + 1], hT, stk, [pb[7]])
            wts = [self.sb("wt", [128, KT, 128], BF16, stk) for _ in range(3)]
            wv = self.sb("wv", [128, KT, 256], BF16, stk)
            uT = self.sb("uT", [128, W], BF16, stk)
            abt = self.sb("abt", [128, 4, 256], BF16, stk)
            sqh = self.sb("sqh", [128, W], BF16, stk)
            rq = self.sb("rq", [128, W], F32, stk)
            qn = [self.sb("qn", [128, W], F32, stk) for _ in range(2)]
            t1 = self.sb("t1", [128, W], F32, stk)
            t2 = self.sb("t2", [128, W], F32, stk)
            qr = [self.sb("qr", [128, W], BF16, stk) for _ in range(2)]
            ktok = self.sb("ktok", [128, 4, 128], F32, stk)
            vt = self.sb("vt", [128, 4, 256], BF16, stk)
            vtf = self.sb("vtf", [128, 4, 256], F32, stk)
            rwt = [self.sb("rwt", [128, W], F32, stk) for _ in range(2)]
            if is_s:
                cosb = self.sb("cosb", [128, W], F32, stk)
                sinb = self.sb("sinb", [128, W], F32, stk)
                self.dma(cosb[:], self.c_cosT[:, c0:c0 + W], [self.c_cosT.b()], [cosb.b()])
                self.dma(sinb[:], self.c_sinT[:, c0:c0 + W], [self.c_sinT.b()], [sinb.b()])

            order = list(range(14)) + list(range(16, 30))

            def load_w(oi):
                ot = order[oi]
                w = wts[oi % 3]
                self.dma(w[:], self.Win_t[l, ot], [self.Win_t.b((l, ot))], [w.b()])
            load_w(0)
            load_w(1)
            self.dma(wv[:], self.Wv_t[l], [self.Wv_t.b(l)], [wv.b()])
            for oi, ot in enumerate(order):
                if oi + 2 < len(order):
                    load_w(oi + 2)
                w = wts[oi % 3]
                acc = pb[oi % 3]
                for kt in range(KT):
                    self.mm(acc[:], w[:, kt, :], hT[:, kt, :], kt == 0, kt == KT - 1, [w.b(), hT.b()], [acc.b()])
                skip = self.cfg.get("s1_skip", "")
                if ("f" in skip and ot < 4) or ("q" in skip and 4 <= ot < 14) or ("r" in skip and ot >= 16):
                    continue
                if ot < 4:
                    self.cp("act", uT[:], acc[:], [acc.b()], [uT.b()])
                    for sub in range(4):
                        reg = pb[5][:, (sub % 2) * 256:(sub % 2) * 256 + 256]
                        self.mm(reg, uT[:, sub * 128:(sub + 1) * 128], self.csC[:], True, True,
                                [uT.b(), self.csC.b()], [pb[5].b()])
                        self.cp("dve", abt[:, sub, :], reg, [pb[5].b()], [abt.b()])
                    self.dma(self.AB[g][c0:c0 + W, ot * 256:(ot + 1) * 256].rearrange("(s p) c -> p s c", p=128), abt[:],
                             [abt.b()], [self.AB[g].b(ti)])
                elif ot < 14:
                    isk = ot >= 12
                    hd = ot - 12 if isk else ot - 4
                    gq = pr["kng"] if isk else pr["qng"]
                    q_n = qn[oi % 2]
                    q_r = qr[oi % 2]
                    self.act(sqh[:], acc[:], AF.Square, [acc.b()], [sqh.b()])
                    self.mm(pb[3][:], self.ones_b[:], sqh[:], True, True, [sqh.b(), self.ones_b.b()], [pb[3].b()])
                    self.act(rq[:], pb[3][:], AF.Sqrt, [pb[3].b(), self.eps6.b()], [rq.b()], scale=1.0 / 128, bias=self.eps6[:, 0:1])
                    self.recip(rq[:], rq[:], [rq.b()], [rq.b()])
                    self.stt("dve", q_n[:], acc[:], gq[:, 0:1], rq[:], ALU.mult, ALU.mult, [acc.b(), gq.b(), rq.b()], [q_n.b()])
                    if isk and not is_s:
                        for sub in range(4):
                            self.tr(pb[5][:, sub * 128:(sub + 1) * 128], q_n[:, sub * 128:(sub + 1) * 128], self.ident_f[:],
                                    [q_n.b(), self.ident_f.b()], [pb[5].b()])
                        self.cp("dve", ktok[:], pb[5][:].rearrange("p (s c) -> p s c", s=4), [pb[5].b()], [ktok.b()])
                        for sub in range(4):
                            tok = sub * 128
                            sq_, off = tok // self.TP, tok % self.TP
                            self.dma(self.nk[sq_, l, off:off + 128, hd * 128:(hd + 1) * 128], ktok[:, sub, :],
                                     [ktok.b()], [self.nk.b()])
                    if is_s:
                        self.mm(pb[4][:], self.prot_f[:], q_n[:], True, True, [self.prot_f.b(), q_n.b()], [pb[4].b()])
                        self.tt("pool", t1[:], q_n[:], cosb[:], ALU.mult, [q_n.b(), cosb.b()], [t1.b()])
                        self.tt("dve", t2[:], pb[4][:], sinb[:], ALU.mult, [pb[4].b(), sinb.b()], [t2.b()])
                        self.tt("pool", q_r[:], t1[:], t2[:], ALU.add, [t1.b(), t2.b()], [q_r.b()])
                    else:
                        self.cp("pool", q_r[:], q_n[:], [q_n.b()], [q_r.b()])
                    dst = self.KTs[g] if isk else self.QT[g]
                    self.dma(dst[hd, :, c0:c0 + W], q_r[:], [q_r.b()], [dst.b(ti)])
                else:
                    r = rwt[oi % 2]
                    self.cp("act" if oi % 2 else "dve", r[:], acc[:], [acc.b()], [r.b()])
                    self.dma(self.RW[g][(ot - 16) * 128:(ot - 15) * 128, c0:c0 + W], r[:], [r.b()], [self.RW[g].b(ti)])
                if oi == self.cfg.get("v_at", 13) and "v" not in skip:
                    for sub in range(4):
                        reg = pb[6][:, (sub % 2) * 256:(sub % 2) * 256 + 256]
                        for kt in range(KT):
                            self.mm(reg, hT[:, kt, sub * 128:(sub + 1) * 128], wv[:, kt, :], kt == 0, kt == KT - 1,
                                    [hT.b(), wv.b()], [pb[6].b()])
                        self.cp("act", vt[:, sub, :], reg, [pb[6].b()], [vt.b()])
                        if not is_s:
                            self.cp(self.cfg.get("vtf_eng", "dve"), vtf[:, sub, :], reg, [pb[6].b()], [vtf.b()])
                    self.dma(self.Vs[g][c0:c0 + W, :].rearrange("(s p) c -> p s c", p=128), vt[:], [vt.b()], [self.Vs[g].b(ti)])
                    if not is_s:
                        for sub in range(4):
                            tok = sub * 128
                            sq_, off = tok // self.TP, tok % self.TP
                            self.dma(self.nv[sq_, l, off:off + 128, :], vtf[:, sub, :], [vtf.b()], [self.nv.b()])
            self.barrier()

    def stage_fourier(self, l, g):
        self.P.stage = f"four_{l}_{g}"
        Tg, nseq, Tseq = self.groups[g]
        nch = Tseq // 128
        TW = min(512, Tseq)
        pb = self.pb
        allab = [self.AB[g].b(i) for i in range((Tg + 511) // 512)]
        with contextlib.ExitStack() as stk:
            ab = self.sb("fab", [128, nch, 1024], BF16, stk)
            ctb = [self.sb("fct", [128, nch, TW], BF16, stk) for _ in range(2)]
            nsb = [self.sb("fns", [128, nch, TW], BF16, stk) for _ in range(2)]
            fo = [self.sb("ffo", [128, TW], BF16, stk) for _ in range(2)]
            n = 0
            for s in range(nseq):
                self.dma(ab[:], self.AB[g][s * Tseq:(s + 1) * Tseq, :].rearrange("(c p) x -> p c x", p=128), allab, [ab.b()])
                for ti, t0 in enumerate(range(0, Tseq, TW)):
                    cb = ctb[ti % 2]
                    sbb = nsb[ti % 2]
                    self.dma(cb[:], self.c_ct[g][:, t0:t0 + TW].rearrange("(c p) t -> p c t", p=128), [self.c_ct[g].b()], [cb.b()])
                    self.dma(sbb[:], self.c_nst[g][:, t0:t0 + TW].rearrange("(c p) t -> p c t", p=128), [self.c_nst[g].b()], [sbb.b()])
                    for grp in range(4):
                        acc = pb[n % 4]
                        for c in range(nch):
                            self.mm(acc[:, 0:TW], ab[:, c, grp * 256:grp * 256 + 128], cb[:, c, :], c == 0, False,
                                    [ab.b(), cb.b()], [acc.b()])
                            self.mm(acc[:, 0:TW], ab[:, c, grp * 256 + 128:grp * 256 + 256], sbb[:, c, :], False, c == nch - 1,
                                    [ab.b(), sbb.b()], [acc.b()])
                        f = fo[n % 2]
                        self.cp("act" if n % 2 else "dve", f[:], acc[:, 0:TW], [acc.b()], [f.b()])
                        c0 = s * Tseq + t0
                        self.dma(self.MIXT[g][grp, :, c0:c0 + TW], f[:], [f.b()], [self.MIXT[g].b(c0 // 512)])
                        n += 1
            self.barrier()

    def stage_attn(self, l, g):
        self.P.stage = f"attn_{l}_{g}"
        Tg, nseq, Tseq = self.groups[g]
        is_s = (g == "s")
        Stot = Tseq + (PAST if is_s else 0)
        nck = Stot // 128
        QW = min(512, Tseq)
        pb = self.pb
        ntile = (Tg + 511) // 512
        scale = 128.0 ** -0.5
        with contextlib.ExitStack() as stk:
            kT = self.sb("akT", [128, NKV, Stot], BF16, stk)
            vv = self.sb("avv", [128, nck, 256], BF16, stk)
            qT = [self.sb("aqT", [128, Tseq], BF16, stk) for _ in range(2)]
            pT = [self.sb("apT", [128, QW], BF16, stk) for _ in range(3)]
            rden = self.sb("arden", [128, QW], F32, stk)
            ob = [self.sb("aob", [128, QW], BF16, stk) for _ in range(2)]
            if is_s:
                ckf = self.sb("ackf", [128, PAST // 128, 256], F32, stk)
                cvf = self.sb("acvf", [128, PAST // 128, 256], F32, stk)
            nq = 0
            bgj = self.bg_sched.pop(("attn", l, g), None)
            if bgj:
                self.bg_begin(stk, bgj, engs=("dve",))
            for s in range(nseq):
                c0s = s * Tseq
                for kv in range(NKV):
                    self.dma(kT[:, kv, 0:Tseq], self.KTs[g][kv, :, c0s:c0s + Tseq], [self.KTs[g].b(i) for i in range(ntile)], [kT.b()])
                self.dma(vv[:, 0:Tseq // 128, :], self.Vs[g][c0s:c0s + Tseq, :].rearrange("(c p) x -> p c x", p=128),
                         [self.Vs[g].b(i) for i in range(ntile)], [vv.b()])
                if is_s:
                    self.dma(ckf[:], self.ck[l].rearrange("(c p) x -> p c x", p=128), [self.ck.b()], [ckf.b()])
                    self.dma(cvf[:], self.cv[l].rearrange("(c p) x -> p c x", p=128), [self.cv.b()], [cvf.b()])
                    self.cp("pool", vv[:, Tseq // 128:nck, :], cvf[:], [cvf.b()], [vv.b()])
                    for kv in range(NKV):
                        bank = pb[kv]
                        for c in range(PAST // 128):
                            self.tr(bank[:, c * 128:(c + 1) * 128], ckf[:, c, kv * 128:(kv + 1) * 128], self.ident_f[:],
                                    [ckf.b(), self.ident_f.b()], [bank.b()])
                        self.cp("act", kT[:, kv, Tseq:Stot], bank[:, 0:PAST], [bank.b()], [kT.b()])
                for h in range(NH):
                    kv = h // (NH // NKV)
                    q = qT[h % 2]
                    self.dma(q[:], self.QT[g][h, :, c0s:c0s + Tseq], [self.QT[g].b(i) for i in range(ntile)], [q.b()])
                    for q0 in range(0, Tseq, QW):
                        self.bg_step()
                        oacc = pb[4 + nq % 2]
                        dacc = pb[6 + nq % 2]
                        def score(c):
                            sT = pb[c % 3]
                            p = pT[c % 3]
                            self.mm(sT[:, 0:QW], kT[:, kv, c * 128:(c + 1) * 128], q[:, q0:q0 + QW], True, True,
                                    [kT.b(), q.b()], [sT.b()])
                            self.act(p[:], sT[:, 0:QW], AF.Exp, [sT.b()], [p.b()], scale=scale)
                        pipe = self.cfg.get("attn_pipe", True)
                        if pipe:
                            score(0)
                        for c in range(nck):
                            if not pipe:
                                score(c)
                            elif c + 1 < nck:
                                score(c + 1)
                            p = pT[c % 3]
                            self.mm(oacc[:, 0:QW], vv[:, c, kv * 128:(kv + 1) * 128], p[:], c == 0, c == nck - 1,
                                    [vv.b(), p.b()], [oacc.b()])
                            self.mm(dacc[:, 0:QW], self.ones_b[:], p[:], c == 0, c == nck - 1,
                                    [self.ones_b.b(), p.b()], [dacc.b()])
                        self.recip(rden[:], dacc[:, 0:QW], [dacc.b()], [rden.b()])
                        o = ob[nq % 2]
                        self.tt("dve", o[:], oacc[:, 0:QW], rden[:], ALU.mult, [oacc.b(), rden.b()], [o.b()])
                        cc0 = c0s + q0
                        self.dma(self.MIXT[g][4 + h, :, cc0:cc0 + QW], o[:], [o.b()], [self.MIXT[g].b(cc0 // 512)])
                        nq += 1
            self.bg_end()
            self.barrier()

    def stage3(self, l, g, c0, X):
        self.P.stage = f"s3_{l}_{g}"
        W = 512
        v = 0 if g == "s" else 1
        pr = self.prm[l]
        mod = pr["mod"][v]
        ti = c0 // 512
        pb = self.pb
        with contextlib.ExitStack() as stk:
            xt = self.sb("xt3", [128, KT, W], F32, stk)
            mx = self.sb("mx3", [128, KT, W], BF16, stk)
            wts = [self.sb("wt3", [128, KT, 128], BF16, stk) for _ in range(3)]
            self.dma(xt[:], X[g][:, :, c0:c0 + W].rearrange("kt p t -> p kt t"), [X[g].b(ti)], [xt.b()])
            self.dma(mx[:], self.MIXT[g][:, :, c0:c0 + W].rearrange("kt p t -> p kt t"), [self.MIXT[g].b(ti)], [mx.b()])

            def load_w(ot):
                w = wts[ot % 3]
                self.dma(w[:], self.Wout_t[l, ot], [self.Wout_t.b((l, ot))], [w.b()])
            load_w(0)
            load_w(1)
            for ot in range(KT):
                if ot + 2 < KT:
                    load_w(ot + 2)
                w = wts[ot % 3]
                acc = pb[ot % 4]
                for kt in range(KT):
                    self.mm(acc[:], w[:, kt, :], mx[:, kt, :], kt == 0, kt == KT - 1, [w.b(), mx.b()], [acc.b()])
                self.stt("dve", xt[:, ot, :], acc[:], mod[:, 32 + ot:33 + ot], xt[:, ot, :], ALU.mult, ALU.add,
                         [acc.b(), xt.b(), xt.b(ot)], [xt.b(ot)])
            self.dma(X[g][:, :, c0:c0 + W].rearrange("kt p t -> p kt t"), xt[:], [xt.b(ot) for ot in range(KT)] + [xt.b()], [X[g].b(ti)])
            self.barrier()

    def stage4(self, l, g, c0, X, Y):
        self.P.stage = f"s4_{l}_{g}"
        W = 512
        Tg, nseq, Tseq = self.groups[g]
        v = 0 if g == "s" else 1
        pr = self.prm[l]
        mod = pr["mod"][v]
        ti = c0 // 512
        nti = (Tg + 511) // 512
        pb, pw = self.pb, self.pw
        FT = self.FT
        fcw, fcb, nfcw = pr["fcw"], pr["fcb"], pr["nfcw"]
        with contextlib.ExitStack() as stk:
            xt = self.sb("xt4", [128, KT, W + 2], F32, stk)
            hT = self.sb("hT4", [128, KT, W + 2], BF16, stk)
            actT = self.sb("act4", [128, FT, W], BF16, stk)
            wts = [self.sb("wt4", [128, KT, 128], BF16, stk) for _ in range(3)]
            wdn = [self.sb("wd4", [128, FT, 128], BF16, stk) for _ in range(2)]
            ua = self.sb("ua4", [128, W], F32, stk)
            ug = self.sb("ug4", [128, W], F32, stk)
            sg = self.sb("sg4", [128, W], F32, stk)
            self.dma(xt[:, :, 0:W], X[g][:, :, c0:c0 + W].rearrange("kt p t -> p kt t"), [X[g].b(ti)], [xt.b()])
            lzero = (c0 % Tseq == 0)
            rzero = ((c0 + W) % Tseq == 0)
            if lzero:
                self.memset("pool", xt[:, :, W:W + 1], 0.0, [xt.b()])
            else:
                self.dma(xt[:, :, W:W + 1], X[g][:, :, c0 - 1:c0].rearrange("kt p t -> p kt t"), [X[g].b(ti - 1)], [xt.b()],
                         allow_slow_non_contiguous=True)
            if rzero:
                self.memset("pool", xt[:, :, W + 1:W + 2], 0.0, [xt.b()])
            else:
                self.dma(xt[:, :, W + 1:W + 2], X[g][:, :, c0 + W:c0 + W + 1].rearrange("kt p t -> p kt t"), [X[g].b(ti + 1)], [xt.b()],
                         allow_slow_non_contiguous=True)
            self.norm_mod(xt, [(0, W), (W, W + 2)], pr[f"gs2_{v}"], lambda kt: mod[:, 48 + kt:49 + kt], hT, stk, [pb[7], pb[6]])
            if lzero:
                self.memset("pool", hT[:, :, W:W + 1], 0.0, [hT.b()])
            if rzero:
                self.memset("pool", hT[:, :, W + 1:W + 2], 0.0, [hT.b()])
            inner = [b for b in range(Tseq, W, Tseq)] if Tseq < W else []

            def load_w(j):
                i, half = j // 2, j % 2
                w = wts[j % 3]
                ot = i + half * FT
                self.dma(w[:], self.Wup_t[l, ot], [self.Wup_t.b((l, ot))], [w.b()])
            load_w(0)
            load_w(1)
            for j in range(2 * FT):
                if j + 2 < 2 * FT:
                    load_w(j + 2)
                i, half = j // 2, j % 2
                ot = i + half * FT
                w = wts[j % 3]
                wide = j % 3
                A, Bk = pb[2 * wide], pb[2 * wide + 1]
                for kt in range(KT):
                    self.mm(A[:], w[:, kt, :], hT[:, kt, 0:W], kt == 0, kt == KT - 1, [w.b(), hT.b()], [A.b()])
                for kt in range(KT):
                    self.mm(Bk[:, 0:2], w[:, kt, :], hT[:, kt, W:W + 2], kt == 0, kt == KT - 1, [w.b(), hT.b()], [Bk.b()])
                u = ug if half else ua
                w0 = fcw[:, 0, ot:ot + 1]
                w1 = fcw[:, 1, ot:ot + 1]
                w2 = fcw[:, 2, ot:ot + 1]
                self.act(u[:], A[:], AF.Identity, [A.b(), fcw.b(), fcb.b()], [u.b()], scale=w1, bias=fcb[:, ot:ot + 1])
                self.stt("dve", u[:, 1:W], A[:, 0:W - 1], w0, u[:, 1:W], ALU.mult, ALU.add, [A.b(), u.b()], [u.b()])
                self.stt("dve", u[:, 0:W - 1], A[:, 1:W], w2, u[:, 0:W - 1], ALU.mult, ALU.add, [A.b(), u.b()], [u.b()])
                self.stt("dve", u[:, 0:1], Bk[:, 0:1], w0, u[:, 0:1], ALU.mult, ALU.add, [Bk.b(), u.b()], [u.b()])
                self.stt("dve", u[:, W - 1:W], Bk[:, 1:2], w2, u[:, W - 1:W], ALU.mult, ALU.add, [Bk.b(), u.b()], [u.b()])
                for bnd in inner:
                    self.stt("dve", u[:, bnd - 1:bnd], A[:, bnd:bnd + 1], nfcw[:, 2, ot:ot + 1], u[:, bnd - 1:bnd], ALU.mult, ALU.add,
                             [A.b(), u.b(), nfcw.b()], [u.b()])
                    self.stt("dve", u[:, bnd:bnd + 1], A[:, bnd - 1:bnd], nfcw[:, 0, ot:ot + 1], u[:, bnd:bnd + 1], ALU.mult, ALU.add,
                             [A.b(), u.b(), nfcw.b()], [u.b()])
                if half:
                    self.act(sg[:], ug[:], AF.Silu, [ug.b()], [sg.b()])
                    self.tt("pool", actT[:, i, :], sg[:], ua[:], ALU.mult, [sg.b(), ua.b()], [actT.b()])
            self.dma(wdn[0][:], self.Wdn_t[l, 0], [self.Wdn_t.b((l, 0))], [wdn[0].b()])
            for ot in range(KT):
                if ot + 1 < KT:
                    self.dma(wdn[(ot + 1) % 2][:], self.Wdn_t[l, ot + 1], [self.Wdn_t.b((l, ot + 1))], [wdn[(ot + 1) % 2].b()])
                w = wdn[ot % 2]
                acc = pb[6 + ot % 2]
                for kt in range(FT):
                    self.mm(acc[:], w[:, kt, :], actT[:, kt, :], kt == 0, kt == FT - 1, [w.b(), actT.b()], [acc.b()])
                self.stt("dve", xt[:, ot, 0:W], acc[:], mod[:, 80 + ot:81 + ot], xt[:, ot, 0:W], ALU.mult, ALU.add,
                         [acc.b(), xt.b(), xt.b(ot)], [xt.b(ot)])
            self.dma(Y[g][:, :, c0:c0 + W].rearrange("kt p t -> p kt t"), xt[:, :, 0:W], [xt.b(ot) for ot in range(KT)] + [xt.b()], [Y[g].b(ti)])
            self.barrier()

    def stage5(self, g, c0, X):
        self.P.stage = f"s5_{g}"
        W = 512
        ti = c0 // 512
        pb = self.pb
        out = self.ys if g == "s" else self.yp
        with contextlib.ExitStack() as stk:
            xt = self.sb("xt5", [128, KT, W], F32, stk)
            xn = self.sb("xn5", [128, KT, W], F32, stk)
            sq = [self.sb("sq5", [128, W], BF16, stk) for _ in range(2)]
            rstd = self.sb("rstd5", [128, W], F32, stk)
            yt = [self.sb("yt5", [128, D], F32, stk) for _ in range(2)]
            self.dma(xt[:], X[g][:, :, c0:c0 + W].rearrange("kt p t -> p kt t"), [X[g].b(ti)], [xt.b()])
            for kt in range(KT):
                s = sq[kt % 2]
                self.act(s[:], xt[:, kt, :], AF.Square, [xt.b()], [s.b()])
                self.mm(pb[7][:], self.ones_b[:], s[:], kt == 0, kt == KT - 1, [s.b(), self.ones_b.b()], [pb[7].b()])
            self.act(rstd[:], pb[7][:], AF.Sqrt, [pb[7].b(), self.eps6.b()], [rstd.b()], scale=1.0 / D, bias=self.eps6[:, 0:1])
            self.recip(rstd[:], rstd[:], [rstd.b()], [rstd.b()])
            for kt in range(KT):
                self.stt("dve", xn[:, kt, :], xt[:, kt, :], self.fng[:, kt:kt + 1], rstd[:], ALU.mult, ALU.mult,
                         [xt.b(), rstd.b(), self.fng.b()], [xn.b()])
            for sub in range(4):
                y = yt[sub % 2]
                for q in range(4):
                    bank = pb[q]
                    for j in range(4):
                        kt = q * 4 + j
                        self.tr(bank[:, j * 128:(j + 1) * 128], xn[:, kt, sub * 128:(sub + 1) * 128], self.ident_f[:],
                                [xn.b(), self.ident_f.b()], [bank.b()])
                    self.cp("act" if q % 2 else "dve", y[:, q * 512:(q + 1) * 512], bank[:], [bank.b()], [y.b()])
                self.dma(out[c0 + sub * 128:c0 + (sub + 1) * 128, :], y[:], [y.b()], [out.b()])
            self.barrier()

    def stage_rwkv(self, l, g):
        self.P.stage = f"rwkv_{l}_{g}"
        Tg, nseq, Tseq = self.groups[g]
        is_s = (g == "s")
        T = Tseq
        nch = T // 128
        NB = 4
        pr = self.prm[l]
        pb = self.pb
        ntile = (Tg + 511) // 512
        rwall = [self.RW[g].b(i) for i in range(ntile)]
        conv = pr["rwconv"]
        CW = min(512, T)
        with contextlib.ExitStack() as stk:
            sbt = lambda name, shape, dt: self.sb(name, shape, dt, stk)
            w2e = sbt("w2e", [65, 2, 512], F32)
            a2f = sbt("a2f", [128, 2, 512], F32)
            g2f = sbt("g2f", [128, 512], F32)
            self.dma(w2e[0:64, :, :], self.rw_w2[l].rearrange("d k c -> k d c"), [self.rw_w2.b()], [w2e.b()])
            self.dma(w2e[64:65, :, :], self.rw_w0[l:l + 1], [self.rw_w0.b()], [w2e.b()])
            self.dma(a2f[64:128, :, :], self.rw_a2[l].rearrange("d k c -> k d c"), [self.rw_a2.b()], [a2f.b()])
            self.dma(g2f[:], self.rw_g2[l], [self.rw_g2.b()], [g2f.b()])
            t12 = sbt("t12", [128, T], F32)
            tw = sbt("tw", [65, T], F32)
            sg = sbt("sg", [128, T], F32)
            rc = sbt("rc", [128, T], F32)
            kkn = sbt("kkn", [128, T], F32)
            bA = [sbt("bA", [128, T], BF16) for _ in range(2)]
            KM = [sbt("KM", [128, T], BF16) for _ in range(2)]
            bon = sbt("bon", [128, T], F32)
            yacc = sbt("yacc", [128, T], F32)
            s1 = sbt("scr1", [128, T + 2], F32)
            s2 = sbt("scr2", [128, T], F32)
            s3 = sbt("scr3", [128, T], F32)
            al = sbt("al", [128, T], BF16)
            be = sbt("be", [128, T], BF16)
            ka_ = sbt("kap", [128, T], BF16)
            rt = sbt("rt", [128, T], BF16)
            VtmP = sbt("VtmP", [128, nch, 2, 128], BF16)
            BTt = sbt("BTt", [128, nch, 128], BF16)
            KTt = sbt("KTt", [128, nch, 128], BF16)
            PL = sbt("PL", [128, nch], F32)
            sigT = [sbt("sigT", [128, 128], F32) for _ in range(2)]
            Ptmp = [sbt("Ptmp", [128, 3, 256], F32) for _ in range(2)]
            NI = NB * 2
            Xa = sbt("Xa", [128, NI, 128], BF16)
            XTa = sbt("XTa", [128, NI, 128], BF16)
            Xb = sbt("Xb", [128, NI, 128], BF16)
            XTb = sbt("XTb", [128, NI, 128], BF16)
            Pf = sbt("Pf", [128, NI, 128], F32)
            Pfin = sbt("Pfin", [128, NI, 128], BF16)
            MakT = sbt("MakT", [128, NI, 128], BF16)
            Wbr = sbt("Wbr", [128, NI, 128], BF16)
            Wkr = sbt("Wkr", [128, NI, 128], BF16)
            Sf = sbt("Sf", [128, 128], F32)
            Sb = sbt("Sb", [128, 128], BF16)
            Sld = sbt("Sld", [128, 128], F32)
            RHSs = sbt("RHSs", [128, 128], BF16)
            Upad = sbt("Upad", [128, 2, 128], BF16)
            Ufull = sbt("Ufull", [128, 128], BF16)
            tS = sbt("tS", [128, 128], F32)
            sq = sbt("sqr", [128, CW], BF16)
            rstd = sbt("rstdr", [128, CW], F32)
            ob = [sbt("obr", [128, CW], BF16) for _ in range(2)]

            bgj = self.bg_sched.pop(("rwkv", l, g), None)
            if bgj:
                self.bg_begin(stk, bgj, engs=("act", "dve"))
            self.memset("pool", VtmP[:], 0.0, [VtmP.b()])
            self.memset("pool", Upad[:], 0.0, [Upad.b()])
            self.memset("pool", tw[64:65, :], 1.0, [tw.b()])
            self.memset("pool", s1[:, 0:1], 0.0, [s1.b()])
            self.memset("pool", s1[:, T + 1:T + 2], 0.0, [s1.b()])

            def conv_tile(tile_idx, c0s, dst):
                self.dma(s1[:, 1:T + 1], self.RW[g][tile_idx * 128:(tile_idx + 1) * 128, c0s:c0s + T], rwall, [s1.b()])
                w = lambda j: conv[:, j * 14 + tile_idx:j * 14 + tile_idx + 1]
                self.act(dst[:], s1[:, 1:T + 1], AF.Copy, [s1.b(), conv.b()], [dst.b()], scale=w(1))
                self.stt("dve", dst[:], s1[:, 0:T], w(0), dst[:], ALU.mult, ALU.add, [s1.b(), dst.b()], [dst.b()])
                self.stt("dve", dst[:], s1[:, 2:T + 2], w(2), dst[:], ALU.mult, ALU.add, [s1.b(), dst.b()], [dst.b()])

            for s in range(nseq):
                c0s = s * T
                conv_tile(12, c0s, t12)
                self.act(tw[0:64, :], t12[0:64, :], AF.Tanh, [t12.b()], [tw.b()])
                conv_tile(13, c0s, sg)
                self.act(sg[:], sg[:], AF.Sigmoid, [sg.b()], [sg.b()])
                for ct in range(4):
                    self.bg_step()
                    conv_tile(ct, c0s, rc)
                    conv_tile(4 + ct, c0s, s2)
                    self.ts("pool", s3[:], s2[:], pr["kk"][:, ct:ct + 1], None, ALU.mult, None, [s2.b(), pr["kk"].b()], [s3.b()])
                    for x0 in range(0, T, CW):
                        bank = pb[(x0 // CW) % 2]
                        self.act(sq[:], s3[:, x0:x0 + CW], AF.Square, [s3.b()], [sq.b()])
                        self.mm(bank[:, 0:CW], self.bones_b[:], sq[:], True, True, [self.bones_b.b(), sq.b()], [bank.b()])
                        self.act(rstd[:], bank[:, 0:CW], AF.Sqrt, [bank.b(), self.eps12.b()], [rstd.b()], bias=self.eps12[:, 0:1])
                        self.recip(rstd[:], rstd[:], [rstd.b()], [rstd.b()])
                        self.tt("dve", kkn[:, x0:x0 + CW], s3[:, x0:x0 + CW], rstd[:], ALU.mult, [s3.b(), rstd.b()], [kkn.b()])
                    for d in range(2):
                        for x0 in range(0, T, CW):
                            bank = pb[2 + (x0 // CW) % 2]
                            self.mm(bank[:, 0:CW], a2f[64:128, d, ct * 128:(ct + 1) * 128], t12[64:128, x0:x0 + CW], True, True,
                                    [a2f.b(), t12.b()], [bank.b()])
                            self.act(s1[:, 1 + x0:1 + x0 + CW], bank[:, 0:CW], AF.Sigmoid, [bank.b(), pr["a0"].b()], [s1.b()],
                                     bias=pr["a0"][:, d * 4 + ct:d * 4 + ct + 1])
                        self.tt("pool", bA[d][:], kkn[:], s1[:, 1:T + 1], ALU.mult, [kkn.b(), s1.b()], [bA[d].b()])
                        self.ts("dve", s1[:, 1:T + 1], s1[:, 1:T + 1], pr["ka"][:, ct:ct + 1], pr["c1"][:, ct:ct + 1], ALU.mult, ALU.add,
                                [s1.b(), pr["ka"].b(), pr["c1"].b()], [s1.b()])
                        self.tt("dve", KM[d][:], s1[:, 1:T + 1], s2[:], ALU.mult, [s1.b(), s2.b()], [KM[d].b()])
                    conv_tile(8 + ct, c0s, s3)
                    self.tt("pool", s2[:], KM[0][:], KM[1][:], ALU.add, [KM[0].b(), KM[1].b()], [s2.b()])
                    self.stt("dve", s2[:], rc[:], pr["rk"][:, ct:ct + 1], s2[:], ALU.mult, ALU.mult, [rc.b(), s2.b(), pr["rk"].b()], [s2.b()])
                    for x0 in range(0, T, CW):
                        bank = pb[(x0 // CW) % 2]
                        self.mm(bank[:, 0:CW], self.bones_f[:], s2[:, x0:x0 + CW], True, True, [self.bones_f.b(), s2.b()], [bank.b()])
                        self.tt("dve", bon[:, x0:x0 + CW], bank[:, 0:CW], s3[:, x0:x0 + CW], ALU.mult, [bank.b(), s3.b()], [bon.b()])
                    for c in range(nch):
                        bank = pb[2 + c % 2]
                        self.tr(bank[:, 0:128], s3[:, c * 128:(c + 1) * 128], self.ident_f[:], [s3.b(), self.ident_f.b()], [bank.b()])
                        for hh in range(2):
                            self.cp("act" if hh else "dve", VtmP[:, c, hh, hh * 64:(hh + 1) * 64], bank[:, hh * 64:(hh + 1) * 64],
                                    [bank.b()], [VtmP.b()])
                    for d in range(2):
                        self.rwkv_dir(l, g, s, ct, d, T, nch, NB, stk, locals())
                    for x0 in range(0, T, CW):
                        bank = pb[(x0 // CW) % 2]
                        bank2 = pb[2 + (x0 // CW) % 2]
                        bank3 = pb[4 + (x0 // CW) % 2]
                        ys_ = yacc[:, x0:x0 + CW]
                        self.mm(bank[:, 0:CW], self.bones_f[:], ys_, True, True, [self.bones_f.b(), yacc.b()], [bank.b()])
                        self.stt("dve", s2[:, x0:x0 + CW], bank[:, 0:CW], -1.0 / 64, ys_, ALU.mult, ALU.add, [bank.b(), yacc.b()], [s2.b()])
                        self.act(sq[:], s2[:, x0:x0 + CW], AF.Square, [s2.b()], [sq.b()])
                        self.mm(bank2[:, 0:CW], self.bones_b[:], sq[:], True, True, [self.bones_b.b(), sq.b()], [bank2.b()])
                        self.act(rstd[:], bank2[:, 0:CW], AF.Sqrt, [bank2.b(), self.epsgn.b()], [rstd.b()], scale=1.0 / 64, bias=self.epsgn[:, 0:1])
                        self.recip(rstd[:], rstd[:], [rstd.b()], [rstd.b()])
                        self.tt("dve", s2[:, x0:x0 + CW], s2[:, x0:x0 + CW], rstd[:], ALU.mult, [s2.b(), rstd.b()], [s2.b()])
                        self.ts("dve", s2[:, x0:x0 + CW], s2[:, x0:x0 + CW], pr["lng"][:, ct:ct + 1], pr["lnb"][:, ct:ct + 1], ALU.mult, ALU.add,
                                [s2.b(), pr["lng"].b(), pr["lnb"].b()], [s2.b()])
                        self.tt("pool", s2[:, x0:x0 + CW], s2[:, x0:x0 + CW], bon[:, x0:x0 + CW], ALU.add, [s2.b(), bon.b()], [s2.b()])
                        self.mm(bank3[:, 0:CW], g2f[:, ct * 128:(ct + 1) * 128], sg[:, x0:x0 + CW], True, True, [g2f.b(), sg.b()], [bank3.b()])
                        o = ob[(x0 // CW) % 2]
                        self.tt("dve", o[:], bank3[:, 0:CW], s2[:, x0:x0 + CW], ALU.mult, [bank3.b(), s2.b()], [o.b()])
                        cc0 = c0s + x0
                        self.dma(self.MIXT[g][12 + ct, :, cc0:cc0 + CW], o[:], [o.b()], [self.MIXT[g].b(cc0 // 512)])
            self.bg_end()
            self.barrier()

    def rwkv_dir(self, l, g, s, ct, d, T, nch, NB, stk, L):
        pr = self.prm[l]
        pb = self.pb
        is_s = (g == "s")
        tw, w2e, sigT, Ptmp, kkn, bA, KM, rc = L["tw"], L["w2e"], L["sigT"], L["Ptmp"], L["kkn"], L["bA"], L["KM"], L["rc"]
        al, be, ka_, rt, PL = L["al"], L["be"], L["ka_"], L["rt"], L["PL"]
        BTt, KTt, VtmP = L["BTt"], L["KTt"], L["VtmP"]
        Xa, XTa, Xb, XTb, Pf, Pfin, MakT, Wbr, Wkr = (L[k] for k in ("Xa", "XTa", "Xb", "XTb", "Pf", "Pfin", "MakT", "Wbr", "Wkr"))
        Sf, Sb, Sld, RHSs, Upad, Ufull, tS, yacc = (L[k] for k in ("Sf", "Sb", "Sld", "RHSs", "Upad", "Ufull", "tS", "yacc"))
        masks = self.masks
        m_n, m_nt, m_s, m_i = ((4, 5, 0, 2) if d == 0 else (5, 4, 1, 3))
        for cp_ in range(0, nch, 2):
            cs = [c for c in (cp_, cp_ + 1) if c < nch]
            bank = pb[(cp_ // 2) % 2]
            for j, c in enumerate(cs):
                sgt = sigT[c % 2]
                bk2 = pb[2 + c % 2]
                self.mm(bk2[:, 0:128], tw[0:65, c * 128:(c + 1) * 128], w2e[0:65, d, ct * 128:(ct + 1) * 128], True, True,
                        [tw.b(), w2e.b()], [bk2.b()])
                self.act(sgt[:], bk2[:, 0:128], AF.Sigmoid, [bk2.b()], [sgt.b()])
                self.mm(bank[:, j * 256:(j + 1) * 256], sgt[:], self.tri2[:, d, :], True, True, [sgt.b(), self.tri2.b()], [bank.b()])
            n = len(cs)
            pt = Ptmp[(cp_ // 2) % 2]
            bv = bank[:, 0:n * 256].rearrange("p (j x) -> p j x", j=n)
            cols = slice(cp_ * 128, (cp_ + n) * 128)
            v3 = lambda tb_: tb_[:, cols].rearrange("p (j x) -> p j x", j=n)
            self.act(pt[:, 0, 0:n * 128].rearrange("p (j x) -> p j x", j=n), bv[:, :, 0:128], AF.Exp, [bank.b()], [pt.b()])
            self.act(pt[:, 1, 0:n * 128].rearrange("p (j x) -> p j x", j=n), bv[:, :, 0:128], AF.Exp, [bank.b()], [pt.b()], scale=-1.0)
            self.act(pt[:, 2, 0:n * 128].rearrange("p (j x) -> p j x", j=n), bv[:, :, 128:256], AF.Exp, [bank.b()], [pt.b()])
            for j, c in enumerate(cs):
                col = (127 if d == 0 else 0)
                self.cp("pool", PL[:, c:c + 1], pt[:, 0, j * 128 + col:j * 128 + col + 1], [pt.b()], [PL.b()])
            w_ = n * 128
            self.tt("dve", al[:, cols], pt[:, 2, 0:w_], kkn[:, cols], ALU.mult, [pt.b(), kkn.b()], [al.b()])
            self.tt("pool", be[:, cols], pt[:, 1, 0:w_], bA[d][:, cols], ALU.mult, [pt.b(), bA[d].b()], [be.b()])
            self.tt("dve", ka_[:, cols], pt[:, 1, 0:w_], KM[d][:, cols], ALU.mult, [pt.b(), KM[d].b()], [ka_.b()])
            self.tt("pool", rt[:, cols], pt[:, 0, 0:w_], rc[:, cols], ALU.mult, [pt.b(), rc.b()], [rt.b()])
        self.bg_step()
        for c in range(nch):
            bank = pb[c % 2]
            self.mm(bank[:, 0:128], be[:, c * 128:(c + 1) * 128], self.ident_b[:], True, True, [be.b(), self.ident_b.b()], [bank.b()])
            self.mm(bank[:, 128:256], ka_[:, c * 128:(c + 1) * 128], self.ident_b[:], True, True, [ka_.b(), self.ident_b.b()], [bank.b()])
            self.cp("act", BTt[:, c, :], bank[:, 0:128], [bank.b()], [BTt.b()])
            self.cp("dve", KTt[:, c, :], bank[:, 128:256], [bank.b()], [KTt.b()])
        self.bg_step()
        if is_s:
            self.memset("pool", Sld[:], 0.0, [Sld.b()])
            for hh in range(2):
                self.dma(Sld[hh * 64:(hh + 1) * 64, hh * 64:(hh + 1) * 64], self.st0[l, d, ct * 2 + hh], [self.st0.b()], [Sld.b()])
            self.tr(pb[7][:, 0:128], Sld[:], self.ident_f[:], [Sld.b(), self.ident_f.b()], [pb[7].b()])
            self.cp("dve", Sf[:], pb[7][:, 0:128], [pb[7].b()], [Sf.b()])
        else:
            self.memset("pool", Sf[:], 0.0, [Sf.b()])
        self.cp("act", Sb[:], Sf[:], [Sf.b()], [Sb.b()])
        order = list(range(nch)) if d == 0 else list(range(nch - 1, -1, -1))
        for b0 in range(0, nch, NB):
            batch = order[b0:b0 + NB]
            nb_ = len(batch)
            grps = [[(c, hh) for c in batch] for hh in range(2)]
            ngrp = 2
            bk = [0]

            def nbank():
                bk[0] += 1
                return pb[bk[0] % 4]
            m4 = self.masks4
            for gi in range(ngrp):
                g4 = slice(gi * 4, gi * 4 + nb_)
                gin = grps[gi]

                def ops(c, hh):
                    rows = slice(hh * 64, (hh + 1) * 64)
                    cc = slice(c * 128, (c + 1) * 128)
                    return al[rows, cc], be[rows, cc], ka_[rows, cc], rt[rows, cc]
                rb = [al.b(), be.b(), ka_.b(), rt.b()]
                for (dst, mi, sel) in ((Xa, m_n, (1, 0)), (XTa, m_nt, (0, 1)), (MakT, m_s, (2, 0)), (Wbr, m_i, (1, 3)), (Wkr, m_i, (2, 3))):
                    bank = nbank()
                    for j, (c, hh) in enumerate(gin):
                        o_ = ops(c, hh)
                        self.mm(bank[:, j * 128:(j + 1) * 128], o_[sel[0]], o_[sel[1]], True, True, rb, [bank.b()])
                    self.tt("dve", dst[:, g4, :], bank[:, 0:nb_ * 128].rearrange("p (j x) -> p j x", j=nb_), m4[:, mi, 0:nb_, :], ALU.mult,
                            [bank.b(), m4.b()], [dst.b(gi)])
                self.tt("pool", Pf[:, g4, :], Xa[:, g4, :], self.ident4[:, 0:nb_, :], ALU.add, [Xa.b(gi), self.ident4.b()], [Pf.b(gi)])
                self.cp("act", Pfin[:, g4, :], Pf[:, g4, :], [Pf.b(gi)], [Pfin.b(gi)])
            X, XT, Xn, XTn = Xa, XTa, Xb, XTb
            for lev in range(6):
                for gi in range(ngrp):
                    g4 = slice(gi * 4, gi * 4 + nb_)
                    if lev < 5:
                        bA_ = nbank()
                        for j in range(nb_):
                            ii = gi * 4 + j
                            self.mm(bA_[:, j * 128:(j + 1) * 128], XT[:, ii, :], X[:, ii, :], True, True, [X.b(gi), XT.b(gi)], [bA_.b()])
                    bB_ = nbank()
                    for j in range(nb_):
                        ii = gi * 4 + j
                        self.mm(bB_[:, j * 128:(j + 1) * 128], X[:, ii, :], XT[:, ii, :], True, True, [X.b(gi), XT.b(gi)], [bB_.b()])
                    if lev < 5:
                        self.cp("act", Xn[:, g4, :], bA_[:, 0:nb_ * 128].rearrange("p (j x) -> p j x", j=nb_), [bA_.b()], [Xn.b(gi)])
                    self.cp("dve", XTn[:, g4, :], bB_[:, 0:nb_ * 128].rearrange("p (j x) -> p j x", j=nb_), [bB_.b()], [XTn.b(gi)])
                    bC_ = nbank()
                    for j in range(nb_):
                        ii = gi * 4 + j
                        self.mm(bC_[:, j * 128:(j + 1) * 128], XTn[:, ii, :], Pfin[:, ii, :], True, True, [XTn.b(gi), Pfin.b(gi)], [bC_.b()])
                    self.tt("dve", Pf[:, g4, :], Pf[:, g4, :], bC_[:, 0:nb_ * 128].rearrange("p (j x) -> p j x", j=nb_), ALU.add, [Pf.b(gi), bC_.b()], [Pf.b(gi)])
                    self.cp("act", Pfin[:, g4, :], Pf[:, g4, :], [Pf.b(gi)], [Pfin.b(gi)])
                X, XT, Xn, XTn = Xn, XTn, X, XT
            self.bg_step()
            for bi, c in enumerate(batch):
                cc = slice(c * 128, (c + 1) * 128)
                p_rhs, p_u, p_y, p_s = pb[4], pb[5], pb[6], pb[7]
                for hh in range(2):
                    ii = hh * 4 + bi
                    rows = slice(hh * 64, (hh + 1) * 64)
                    vcols = slice(hh * 64, (hh + 1) * 64)
                    self.mm(p_rhs[:, vcols], al[rows, cc], Sb[rows, vcols], True, False, [al.b(), Sb.b()], [p_rhs.b()])
                    self.mm(p_rhs[:, vcols], MakT[:, ii, :], VtmP[:, c, hh, vcols], False, True, [MakT.b(ii // 4), VtmP.b()], [p_rhs.b()])
                self.cp("act", RHSs[:], p_rhs[:, 0:128], [p_rhs.b()], [RHSs.b()])
                for hh in range(2):
                    ii = hh * 4 + bi
                    vcols = slice(hh * 64, (hh + 1) * 64)
                    self.mm(p_u[:, vcols], Pfin[:, ii, :], RHSs[:, vcols], True, True, [Pfin.b(ii // 4), RHSs.b()], [p_u.b()])
                self.ts("dve", Ufull[:], p_u[:, 0:128], -1.0, None, ALU.mult, None, [p_u.b()], [Ufull.b()])
                for hh in range(2):
                    vcols = slice(hh * 64, (hh + 1) * 64)
                    self.cp("act" if hh else "dve", Upad[:, hh, vcols], Ufull[:, vcols], [Ufull.b()], [Upad.b()])
                self.mm(p_y[:, 0:128], Sb[:], rt[:, cc], True, False, [Sb.b(), rt.b()], [p_y.b()])
                for hh in range(2):
                    ii = hh * 4 + bi
                    self.mm(p_y[:, 0:128], Upad[:, hh, :], Wbr[:, ii, :], False, False, [Upad.b(), Wbr.b(ii // 4)], [p_y.b()])
                    self.mm(p_y[:, 0:128], VtmP[:, c, hh, :], Wkr[:, ii, :], False, hh == 1, [VtmP.b(), Wkr.b(ii // 4)], [p_y.b()])
                if d == 0:
                    self.cp("act", yacc[:, cc], p_y[:, 0:128], [p_y.b()], [yacc.b()])
                else:
                    self.tt("dve", yacc[:, cc], yacc[:, cc], p_y[:, 0:128], ALU.add, [yacc.b(), p_y.b()], [yacc.b()])
                self.mm(p_s[:, 0:128], BTt[:, c, :], Ufull[:], True, False, [BTt.b(), Ufull.b()], [p_s.b()])
                for hh in range(2):
                    self.mm(p_s[:, 0:128], KTt[:, c, :], VtmP[:, c, hh, :], False, hh == 1, [KTt.b(), VtmP.b()], [p_s.b()])
                self.stt("dve", tS[:], p_s[:, 0:128], PL[:, c:c + 1], self.bones_f[:], ALU.mult, ALU.mult,
                         [p_s.b(), PL.b(), self.bones_f.b()], [tS.b()])
                self.stt("dve", Sf[:], Sf[:], PL[:, c:c + 1], tS[:], ALU.mult, ALU.add, [Sf.b(), PL.b(), tS.b()], [Sf.b()])
                self.cp("act", Sb[:], Sf[:], [Sf.b()], [Sb.b()])
        self.bg_step()
        if not is_s:
            self.tr(pb[7][:, 0:128], Sf[:], self.ident_f[:], [Sf.b(), self.ident_f.b()], [pb[7].b()])
            self.cp("dve", Sld[:], pb[7][:, 0:128], [pb[7].b()], [Sld.b()])
            for hh in range(2):
                self.dma(self.ns[s, l, d, ct * 2 + hh], Sld[hh * 64:(hh + 1) * 64, hh * 64:(hh + 1) * 64], [Sld.b()], [self.ns.b()])

    def tiles(self):
        out = []
        for g, (Tg, nseq, Tseq) in self.groups.items():
            for c0 in range(0, Tg, 512):
                out.append((g, c0))
        return out

    def build(self):
        stages = self.cfg.get("stages", "all")
        self.declare()
        self.setup_consts()
        if stages != "nocast":
            self.cast_weights()
        if stages == "s0a":
            self.P.emit()
            return self.nc
        self.setup_params()
        if stages == "s0b":
            self.P.emit()
            return self.nc
        self.to_feature_major()
        if stages in ("s0c", "nocast"):
            self.P.emit()
            return self.nc
        X, Y = self.XT, self.XTB
        for l in range(self.DEPTH):
            for (g, c0) in self.tiles():
                if g in self.cfg.get("s1_groups", "sp"):
                    self.stage1_x(l, g, c0, X)
            if stages == "s1":
                break
            for g in self.groups:
                self.stage_fourier(l, g)
                self.stage_attn(l, g)
                if stages == "s2fa":
                    self.zero_rwkv(g)
                else:
                    self.stage_rwkv(l, g)
            if stages == "s2":
                break
            for (g, c0) in self.tiles():
                self.stage3(l, g, c0, X)
            for (g, c0) in self.tiles():
                self.stage4(l, g, c0, X, Y)
            X, Y = Y, X
        if stages in ("all", "s2fa"):
            for (g, c0) in self.tiles():
                self.stage5(g, c0, X)
        self.P.emit()
        return self.nc


def make_in_maps(inputs, cfg, ncores):
    TS, TP, NPS, DEPTH = cfg["TS"], cfg["TP"], cfg["NPS"], cfg["DEPTH"]
    consts = host_consts(TS, TP)
    f = lambda a: np.ascontiguousarray(np.asarray(a, dtype=np.float32))
    shared = {}
    for k in ("w_ada", "b_ada", "norm1_g", "norm2_g", "w_in", "w_out", "q_norm_g", "k_norm_g", "rw_conv",
              "rw_w0", "rw_w2", "rw_a0", "rw_a2", "rw_g2", "rw_kk", "rw_ka", "rw_lnx_g", "rw_lnx_b",
              "ffn_up", "ffn_conv_w", "ffn_conv_b", "ffn_down", "final_norm_g"):
        shared[k] = f(inputs[k])
    shared["rw_rk"] = f(inputs["rw_rk"]).reshape(DEPTH, 512)
    shared.update(consts)
    maps = []
    for b in range(ncores):
        m = dict(shared)
        m["xs"] = f(inputs["x_sample"][b])
        m["xp"] = f(inputs["x_prompt"][NPS * b:NPS * (b + 1)]).reshape(NPS * TP, D)
        m["ck"] = f(inputs["cache_attn_k"][b]).reshape(DEPTH, PAST, 256)
        m["cv"] = f(inputs["cache_attn_v"][b]).reshape(DEPTH, PAST, 256)
        m["st0"] = f(inputs["state_rwkv"][b])
        m["cc"] = np.stack([f(inputs["c"][b]), f(inputs["c_ctx"])], 0)
        maps.append(m)
    return maps


_CACHE = {}


def kernel(**inputs):
    xs = np.asarray(inputs["x_sample"])
    xp = np.asarray(inputs["x_prompt"])
    ncores = xs.shape[0]
    TS, TP = xs.shape[1], xp.shape[1]
    NPS = xp.shape[0] // ncores
    DEPTH = np.asarray(inputs["w_in"]).shape[0]
    DFF = np.asarray(inputs["ffn_down"]).shape[1]
    cfg = dict(TS=TS, TP=TP, NPS=NPS, DEPTH=DEPTH, DFF=DFF, stages="all")
    key = (TS, TP, NPS, DEPTH, DFF)
    if key not in _CACHE:
        kb = KB(cfg)
        _CACHE[key] = kb.build()
    nc = _CACHE[key]
    maps = make_in_maps(inputs, cfg, ncores)
    decl = set()
    for alloc in nc.allocations:
        if isinstance(alloc, mybir.MemoryLocationSet) and alloc.kind == "ExternalInput":
            decl.add(alloc.memorylocations[0].name)
    maps = [{k: v for k, v in m.items() if k in decl} for m in maps]
    res = run_bass_kernel_spmd(nc, maps, core_ids=list(range(ncores)))
    r = res.results
    y_sample = np.stack([np.asarray(r[b]["ys"], np.float32) for b in range(ncores)], 0)
    y_prompt = np.concatenate([np.asarray(r[b]["yp"], np.float32).reshape(NPS, TP, D) for b in range(ncores)], 0)
    nk = np.concatenate([np.asarray(r[b]["nk"], np.float32).reshape(NPS, DEPTH, TP, NKV, 128) for b in range(ncores)], 0)
    nv = np.concatenate([np.asarray(r[b]["nv"], np.float32).reshape(NPS, DEPTH, TP, NKV, 128) for b in range(ncores)], 0)
    ns = np.concatenate([np.asarray(r[b]["ns"], np.float32) for b in range(ncores)], 0)
    return (y_prompt, y_sample, nk, nv, ns)
```

```python
import contextlib
import math
import numpy as np
import ml_dtypes
import concourse.bass as bass
import concourse.mybir as mybir
from concourse.bass_utils import run_bass_kernel_spmd

F32 = mybir.dt.float32
BF16 = mybir.dt.bfloat16
ALU = mybir.AluOpType
AF = mybir.ActivationFunctionType
AX = mybir.AxisListType

EPOCH = 8000
RING = 8


class Buf:
    __slots__ = ("wc", "wd", "rc", "rd", "excl")

    def __init__(self, excl=False):
        self.excl = excl
        self.wc = {}
        self.wd = {}
        self.rc = {}
        self.rd = {}


class Op:
    __slots__ = ("eng", "fn", "waits", "done", "is_dma", "stage")

    def __init__(self, eng, fn, is_dma):
        self.stage = None
        self.eng = eng
        self.fn = fn
        self.waits = {}
        self.done = None
        self.is_dma = is_dma


class Prog:
    ENGS = ("pe", "act", "dve", "pool", "sp")

    def __init__(self, nc):
        self.nc = nc
        self.q = {e: [] for e in self.ENGS}
        self.cnt = {e: 0 for e in self.ENGS}
        self.dcnt = {e: 0 for e in self.ENGS}
        self.sems = {}
        self.semkeys = []
        self.last_dma = {}
        self.pending = {}

    def _semkey(self, key):
        if key not in self.sems:
            self.sems[key] = None
            self.semkeys.append(key)
        return key

    def _add_dep(self, op, dep, raw):
        if dep is None or dep is op:
            return
        if dep.eng == op.eng and not dep.is_dma and not op.is_dma:
            if op.eng == "pe":
                return
        key, val = dep.done
        if op.waits.get(key, 0) < val:
            op.waits[key] = val

    def op(self, eng, fn, reads=(), writes=(), dma=False):
        o = Op(eng, fn, dma)
        o.stage = getattr(self, "stage", None)
        if any(b.excl for b in reads):
            writes = list(writes) + [b for b in reads if b.excl and b not in writes]
            reads = [b for b in reads if not b.excl]
        self.nops = getattr(self, "nops", 0) + 1
        if self.nops > getattr(self, "limit", 1 << 60):
            return o
        pend = self.pending.pop(eng, None)
        if pend:
            for key, val in pend.items():
                if o.waits.get(key, 0) < val:
                    o.waits[key] = val
        for b in reads:
            for d in b.wc.values():
                self._add_dep(o, d, True)
            for lst in b.wd.values():
                for d in lst:
                    self._add_dep(o, d, True)
        for b in writes:
            for d in b.wc.values():
                self._add_dep(o, d, False)
            for lst in b.wd.values():
                for d in lst:
                    self._add_dep(o, d, False)
            for d in b.rc.values():
                self._add_dep(o, d, False)
            for lst in b.rd.values():
                for d in lst:
                    self._add_dep(o, d, False)
        if dma:
            k = self.dcnt[eng]
            self.dcnt[eng] += 1
            slot = k % RING
            key = self._semkey(("d", eng, slot))
            o.done = (key, 16 * (k // RING + 1))
            prev = self.last_dma.get((eng, slot))
            if prev is not None:
                self._add_dep(o, prev, True)
            self.last_dma[(eng, slot)] = o
        else:
            k = self.cnt[eng]
            self.cnt[eng] += 1
            key = self._semkey(("c", eng, k // EPOCH))
            o.done = (key, k % EPOCH + 1)
        for b in reads:
            if dma:
                lst = b.rd.setdefault(eng, [])
                lst.append(o)
                if len(lst) > RING:
                    del lst[0]
            else:
                b.rc[eng] = o
        for b in writes:
            b.rc = {}
            b.rd = {}
            if dma:
                lst = b.wd.setdefault(eng, [])
                lst.append(o)
                if len(lst) > RING:
                    del lst[0]
            else:
                b.wc[eng] = o
        self.q[eng].append(o)
        return o

    def emit(self):
        nc = self.nc
        with contextlib.ExitStack() as st:
            for key in self.semkeys:
                self.sems[key] = st.enter_context(nc.semaphore("s_" + "_".join(str(x) for x in key)))
            block = st.enter_context(nc.Block())
            engmap = {"pe": block.tensor, "act": block.scalar, "dve": block.vector,
                      "pool": block.gpsimd, "sp": block.sync}
            all_ops = self.q

            def make(ename):
                ops = all_ops[ename]

                def body(e):
                    seen = {}
                    for o in ops:
                        for key, val in o.waits.items():
                            if seen.get(key, 0) >= val:
                                continue
                            seen[key] = val
                            e.wait_ge(self.sems[key], val)
                        ins = o.fn(e)
                        if self.annotate and o.stage:
                            ins.annotate(o.stage)
                        key, val = o.done
                        ins.then_inc(self.sems[key], 16 if o.is_dma else 1)
                    if ename == "sp":
                        fin = {}
                        for en in self.ENGS:
                            for o in all_ops[en][-1:]:
                                key, val = o.done
                                fin[key] = max(fin.get(key, 0), val)
                        for o in self.last_dma.values():
                            key, val = o.done
                            fin[key] = max(fin.get(key, 0), val)
                        for key, val in fin.items():
                            if seen.get(key, 0) < val:
                                e.wait_ge(self.sems[key], val)
                return body

            for ename in self.ENGS:
                if all_ops[ename] or ename == "sp":
                    engmap[ename](make(ename))


class TB:
    def __init__(self, h, excl=False):
        self.h = h
        self.bufs = {}
        self.excl = excl

    def __getitem__(self, idx):
        return self.h[idx]

    def b(self, key=None):
        if key not in self.bufs:
            self.bufs[key] = Buf(self.excl)
        return self.bufs[key]


class TBV:
    def __init__(self, ap):
        self.ap = ap
        self.buf = Buf(True)

    def __getitem__(self, idx):
        return self.ap[idx]

    def b(self, key=None):
        return self.buf


class DT:
    def __init__(self, ap):
        self.ap = ap
        self.bufs = {}

    def __getitem__(self, idx):
        return self.ap[idx]

    def b(self, key=None):
        if key not in self.bufs:
            self.bufs[key] = Buf()
        return self.bufs[key]


D = 2048
KT = 16
NH = 8
NKV = 2
PAST = 512
IN_W = 3840
RW_IN = 1792
GRID_W = 64
NORM_EPS = 1e-6
GN_EPS = 64e-5
LWC = -math.exp(-0.5)


def host_consts(TS, TP):
    c = {}
    bf = ml_dtypes.bfloat16
    c["ident_f"] = np.eye(128, dtype=np.float32)
    c["ident_b"] = np.eye(128, dtype=np.float32).astype(bf)
    c["ones_b"] = np.ones((128, 128), np.float32).astype(bf)
    bo = np.zeros((128, 128), np.float32)
    bo[:64, :64] = 1.0
    bo[64:, 64:] = 1.0
    c["bones_b"] = bo.astype(bf)
    c["bones_f"] = bo
    prot = np.zeros((128, 128), np.float32)
    for m in range(128):
        j = m % 64
        if j < 32:
            prot[m + 32, m] = -1.0
        else:
            prot[m - 32, m] = 1.0
    c["prot_f"] = prot
    t = np.arange(TS)
    rows = (t // GRID_W).astype(np.float64)
    cols = (t % GRID_W).astype(np.float64)
    inv = 1.0 / (10000.0 ** (np.arange(0, 64, 2, dtype=np.float64) / 64.0))
    cosT = np.zeros((128, TS), np.float64)
    sinT = np.zeros((128, TS), np.float64)
    for d in range(128):
        pos = rows if d < 64 else cols
        f = inv[(d % 64) % 32]
        ang = np.float32(pos).astype(np.float32) * np.float32(f)
        cosT[d] = np.cos(ang.astype(np.float64))
        sinT[d] = np.sin(ang.astype(np.float64))
    c["cosT"] = cosT.astype(np.float32)
    c["sinT"] = sinT.astype(np.float32)
    i = np.arange(128)
    ang = 2 * np.pi * np.outer(i, i) / 128.0
    c["csC"] = (np.concatenate([np.cos(ang), np.sin(ang)], 1) / np.sqrt(128.0)).astype(np.float32).astype(bf)
    for nm, T in (("S", TS), ("P", TP)):
        i = np.arange(T)
        ang = 2 * np.pi * ((np.outer(i, i)) % T) / float(T)
        c["ct" + nm] = (np.cos(ang) / np.sqrt(T)).astype(np.float32).astype(bf)
        c["nst" + nm] = (-np.sin(ang) / np.sqrt(T)).astype(np.float32).astype(bf)
    idx = np.arange(128)
    su = (idx[:, None] < idx[None, :]).astype(np.float32)
    iu = (idx[:, None] <= idx[None, :]).astype(np.float32)
    sl = su.T.copy()
    il = iu.T.copy()
    c["masks"] = np.stack([su, sl, iu, il, -su, -sl], 0).astype(np.float32)
    c["masks4"] = np.repeat(c["masks"][:, None, :, :], 4, axis=1).transpose(2, 0, 1, 3).copy().astype(np.float32)
    c["ident4"] = np.repeat(np.eye(128, dtype=np.float32)[:, None, :], 4, axis=1).copy()
    c["tri2"] = np.stack([np.concatenate([iu, su], 1), np.concatenate([il, sl], 1)], 0).astype(np.float32) * np.float32(LWC)
    return c


class KB:
    def __init__(self, cfg):
        self.cfg = cfg
        self.TS = cfg["TS"]
        self.TP = cfg["TP"]
        self.NPS = cfg["NPS"]
        self.DFF = cfg["DFF"]
        self.FT = self.DFF // 128
        self.DEPTH = cfg["DEPTH"]
        self.dbg = set(cfg.get("dbg", ()))
        self.nc = bass.Bass("TRN2", target_bir_lowering=False)
        self.P = Prog(self.nc)
        self.P.limit = cfg.get("limit", 1 << 60)
        self.P.annotate = bool(cfg.get("annotate"))
        self.P.stage = "init"
        self.st = contextlib.ExitStack()
        self.dram = {}
        self.uid = 0
        self.bar_uid = 0
        self.groups = {"s": (self.TS, 1, self.TS), "p": (self.NPS * self.TP, self.NPS, self.TP)}

    def din(self, name, shape, dt=F32):
        t = DT(self.nc.dram_tensor(name, list(shape), dt, kind="ExternalInput").ap())
        self.dram[name] = t
        return t

    def dout(self, name, shape, dt=F32):
        t = DT(self.nc.dram_tensor(name, list(shape), dt, kind="ExternalOutput").ap())
        self.dram[name] = t
        return t

    def dscr(self, name, shape, dt):
        kind = "ExternalOutput" if name in self.dbg else "Internal"
        t = DT(self.nc.dram_tensor(name, list(shape), dt, kind=kind).ap())
        self.dram[name] = t
        return t

    def sb(self, name, shape, dt, stack=None):
        self.uid += 1
        h = (stack or self.st).enter_context(self.nc.sbuf_tensor(f"{name}_{self.uid}", list(shape), dt))
        return TB(h)

    def ps(self, name, shape, dt=F32, stack=None):
        self.uid += 1
        h = (stack or self.st).enter_context(self.nc.psum_tensor(f"{name}_{self.uid}", list(shape), dt))
        return TB(h, excl=True)

    def dma(self, out, in_, reads, writes, eng="sp", **kw):
        return self.P.op(eng, lambda e: e.dma_start(out=out, in_=in_, **kw), reads, writes, dma=True)

    def mm(self, out, lhsT, rhs, start, stop, reads, writes):
        return self.P.op("pe", lambda e: e.matmul(out, lhsT=lhsT, rhs=rhs, start=start, stop=stop), reads, writes)

    def tr(self, out, in_, ident, reads, writes):
        return self.P.op("pe", lambda e: e.transpose(out=out, in_=in_, identity=ident), reads, writes)

    def act(self, out, in_, func, reads, writes, **kw):
        return self.P.op("act", lambda e: e.activation(out=out, in_=in_, func=func, **kw), reads, writes)

    def tt(self, eng, out, in0, in1, op, reads, writes):
        return self.P.op(eng, lambda e: e.tensor_tensor(out=out, in0=in0, in1=in1, op=op), reads, writes)

    def ts(self, eng, out, in0, s1, s2, op0, op1, reads, writes):
        if s2 is None:
            return self.P.op(eng, lambda e: e.tensor_scalar(out=out, in0=in0, scalar1=s1, scalar2=None, op0=op0), reads, writes)
        return self.P.op(eng, lambda e: e.tensor_scalar(out=out, in0=in0, scalar1=s1, scalar2=s2, op0=op0, op1=op1), reads, writes)

    def stt(self, eng, out, in0, scalar, in1, op0, op1, reads, writes):
        return self.P.op(eng, lambda e: e.scalar_tensor_tensor(out=out, in0=in0, scalar=scalar, in1=in1, op0=op0, op1=op1), reads, writes)

    def cp(self, eng, out, in_, reads, writes):
        if eng == "act":
            return self.P.op("act", lambda e: e.copy(out=out, in_=in_), reads, writes)
        return self.P.op(eng, lambda e: e.tensor_copy(out=out, in_=in_), reads, writes)

    def memset(self, eng, ap, val, writes):
        return self.P.op(eng, lambda e: e.memset(ap, val), [], writes)

    def recip(self, out, in_, reads, writes):
        return self.P.op("dve", lambda e: e.reciprocal(out=out, in_=in_), reads, writes)

    def barrier(self):
        P = self.P
        fin = {}
        for en in P.ENGS:
            for o in P.q[en][-1:]:
                key, val = o.done
                fin[key] = max(fin.get(key, 0), val)
        for o in P.last_dma.values():
            key, val = o.done
            fin[key] = max(fin.get(key, 0), val)
        for en in P.ENGS:
            d = P.pending.setdefault(en, {})
            for key, val in fin.items():
                d[key] = max(d.get(key, 0), val)

    def declare(self):
        TS, TP, NPS, DEPTH, DFF = self.TS, self.TP, self.NPS, self.DEPTH, self.DFF
        TPG = NPS * TP
        di = self.din
        self.xs = di("xs", [TS, D])
        self.xp = di("xp", [TPG, D])
        self.ck = di("ck", [DEPTH, PAST, 256])
        self.cv = di("cv", [DEPTH, PAST, 256])
        self.st0 = di("st0", [DEPTH, 2, 8, 64, 64])
        self.cc = di("cc", [2, D])
        self.w_ada = di("w_ada", [DEPTH, D, 6 * D])
        self.b_ada = di("b_ada", [DEPTH, 6 * D])
        self.norm1_g = di("norm1_g", [DEPTH, D])
        self.norm2_g = di("norm2_g", [DEPTH, D])
        self.w_in = di("w_in", [DEPTH, D, IN_W])
        self.w_out = di("w_out", [DEPTH, D, D])
        self.q_norm_g = di("q_norm_g", [DEPTH, 128])
        self.k_norm_g = di("k_norm_g", [DEPTH, 128])
        self.rw_conv = di("rw_conv", [DEPTH, 3, RW_IN])
        self.rw_w0 = di("rw_w0", [DEPTH, 2, 512])
        self.rw_w2 = di("rw_w2", [DEPTH, 2, 64, 512])
        self.rw_a0 = di("rw_a0", [DEPTH, 2, 512])
        self.rw_a2 = di("rw_a2", [DEPTH, 2, 64, 512])
        self.rw_g2 = di("rw_g2", [DEPTH, 128, 512])
        self.rw_kk = di("rw_kk", [DEPTH, 512])
        self.rw_ka = di("rw_ka", [DEPTH, 512])
        self.rw_rk = di("rw_rk", [DEPTH, 512])
        self.rw_lnx_g = di("rw_lnx_g", [DEPTH, 512])
        self.rw_lnx_b = di("rw_lnx_b", [DEPTH, 512])
        self.ffn_up = di("ffn_up", [DEPTH, D, 2 * DFF])
        self.ffn_conv_w = di("ffn_conv_w", [DEPTH, 3, 2 * DFF])
        self.ffn_conv_b = di("ffn_conv_b", [DEPTH, 2 * DFF])
        self.ffn_down = di("ffn_down", [DEPTH, DFF, D])
        self.final_norm_g = di("final_norm_g", [D])
        self.c_ident_f = di("ident_f", [128, 128])
        self.c_ident_b = di("ident_b", [128, 128], BF16)
        self.c_ones_b = di("ones_b", [128, 128], BF16)
        self.c_bones_b = di("bones_b", [128, 128], BF16)
        self.c_bones_f = di("bones_f", [128, 128])
        self.c_prot_f = di("prot_f", [128, 128])
        self.c_cosT = di("cosT", [128, TS])
        self.c_sinT = di("sinT", [128, TS])
        self.c_csC = di("csC", [128, 256], BF16)
        self.c_ct = {"s": di("ctS", [TS, TS], BF16), "p": di("ctP", [TP, TP], BF16)}
        self.c_nst = {"s": di("nstS", [TS, TS], BF16), "p": di("nstP", [TP, TP], BF16)}
        self.c_masks = di("masks", [6, 128, 128])
        self.c_tri2 = di("tri2", [2, 128, 256])
        self.c_masks4 = di("masks4", [128, 6, 4, 128])
        self.c_ident4 = di("ident4", [128, 4, 128])
        self.ys = self.dout("ys", [TS, D])
        self.yp = self.dout("yp", [TPG, D])
        self.nk = self.dout("nk", [NPS, DEPTH, TP, 256])
        self.nv = self.dout("nv", [NPS, DEPTH, TP, 256])
        self.ns = self.dout("ns", [NPS, DEPTH, 2, 8, 64, 64])
        ds = self.dscr
        FT = self.FT
        self.Win_t = ds("Win_t", [DEPTH, 30, 128, KT, 128], BF16)
        self.Wv_t = ds("Wv_t", [DEPTH, 128, KT, 256], BF16)
        self.Wout_t = ds("Wout_t", [DEPTH, 16, 128, KT, 128], BF16)
        self.Wup_t = ds("Wup_t", [DEPTH, 2 * FT, 128, KT, 128], BF16)
        self.Wdn_t = ds("Wdn_t", [DEPTH, 16, 128, FT, 128], BF16)
        self.mod_rows = [ds(f"modrows{l}", [2, 6 * D], F32) for l in range(DEPTH)]
        self.XT = {}
        self.XTB = {}
        self.QT = {}
        self.KTs = {}
        self.Vs = {}
        self.RW = {}
        self.AB = {}
        self.MIXT = {}
        for g, (Tg, nseq, Tseq) in self.groups.items():
            self.XT[g] = ds("XT_" + g, [KT, 128, Tg], F32)
            self.XTB[g] = ds("XTB_" + g, [KT, 128, Tg], F32)
            self.QT[g] = ds("QT_" + g, [NH, 128, Tg], BF16)
            self.KTs[g] = ds("KT_" + g, [NKV, 128, Tg], BF16)
            self.Vs[g] = ds("V_" + g, [Tg, 256], BF16)
            self.RW[g] = ds("RW_" + g, [RW_IN, Tg], F32)
            self.AB[g] = ds("AB_" + g, [Tg, 1024], BF16)
            self.MIXT[g] = ds("MIXT_" + g, [KT, 128, Tg], BF16)

    def load_const(self, name, src, shape, dt):
        t = self.sb(name, shape, dt)
        self.dma(t[:], src.ap, [src.b()], [t.b()])
        return t

    def setup_consts(self):
        self.P.stage = "consts"
        TS = self.TS
        self.ident_f = self.load_const("ident_f", self.c_ident_f, [128, 128], F32)
        self.ident_b = self.load_const("ident_b", self.c_ident_b, [128, 128], BF16)
        self.ones_b = self.load_const("ones_b", self.c_ones_b, [128, 128], BF16)
        self.bones_b = self.load_const("bones_b", self.c_bones_b, [128, 128], BF16)
        self.bones_f = self.load_const("bones_f", self.c_bones_f, [128, 128], F32)
        self.prot_f = self.load_const("prot_f", self.c_prot_f, [128, 128], F32)
        self.csC = self.load_const("csC", self.c_csC, [128, 256], BF16)
        self.masks = self.sb("masks", [128, 6, 128], F32)
        self.dma(self.masks[:], self.c_masks.ap.rearrange("m p c -> p m c"), [self.c_masks.b()], [self.masks.b()])
        self.tri2 = self.sb("tri2", [128, 2, 256], F32)
        self.dma(self.tri2[:], self.c_tri2.ap.rearrange("m p c -> p m c"), [self.c_tri2.b()], [self.tri2.b()])
        self.masks4 = self.load_const("masks4", self.c_masks4, [128, 6, 4, 128], F32)
        self.ident4 = self.load_const("ident4", self.c_ident4, [128, 4, 128], F32)
        self.eps6 = self.sb("eps6", [128, 1], F32)
        self.memset("pool", self.eps6[:], NORM_EPS, [self.eps6.b()])
        self.eps12 = self.sb("eps12", [128, 1], F32)
        self.memset("pool", self.eps12[:], 1e-12, [self.eps12.b()])
        self.epsgn = self.sb("epsgn", [128, 1], F32)
        self.memset("pool", self.epsgn[:], GN_EPS, [self.epsgn.b()])
        self.pw = [self.ps(f"pw{i}", [128, 1024], F32) for i in range(4)]
        self.pb = [TBV(self.pw[i // 2].h[:, (i % 2) * 512:(i % 2) * 512 + 512]) for i in range(8)]

    def cast_jobs(self, l, which):
        FT = self.FT
        jobs = []
        if which == "in":
            for c0 in range(0, IN_W, 512):
                wd = min(512, IN_W - c0)
                dsts = []
                for j in range(wd // 128):
                    ot = c0 // 128 + j
                    if ot == 14:
                        dsts.append((self.Wv_t[l], self.Wv_t.b(l), j * 128, 256))
                    elif ot != 15:
                        dsts.append((self.Win_t[l, ot], self.Win_t.b((l, ot)), j * 128, 128))
                jobs.append((self.w_in, l, c0, wd, KT, dsts))
        else:
            for c0 in range(0, D, 512):
                dsts = [(self.Wout_t[l, c0 // 128 + j], self.Wout_t.b((l, c0 // 128 + j)), j * 128, 128) for j in range(4)]
                jobs.append((self.w_out, l, c0, 512, KT, dsts))
            for c0 in range(0, 2 * self.DFF, 512):
                wd = min(512, 2 * self.DFF - c0)
                dsts = [(self.Wup_t[l, c0 // 128 + j], self.Wup_t.b((l, c0 // 128 + j)), j * 128, 128) for j in range(wd // 128)]
                jobs.append((self.ffn_up, l, c0, wd, KT, dsts))
            cw = 128 if FT > 16 else 512
            for c0 in range(0, D, cw):
                dsts = [(self.Wdn_t[l, c0 // 128 + j], self.Wdn_t.b((l, c0 // 128 + j)), j * 128, 128) for j in range(cw // 128)]
                jobs.append((self.ffn_down, l, c0, cw, FT, dsts))
        return jobs

    def bg_begin(self, stk, jobs, engs=("pool",)):
        nel = max(KT * 512, self.FT * (128 if self.FT > 16 else 512))
        self.bg = dict(jobs=list(jobs), i=0, pend=None, engs=engs, n=0,
                       wf=[self.sb("wcf", [128, nel], F32, stk) for _ in range(2)],
                       wb=[self.sb("wcb", [128, nel], BF16, stk) for _ in range(2)])

    def bg_step(self):
        bg = getattr(self, "bg", None)
        if bg is None:
            return
        if bg["pend"] is not None:
            (src, l, c0, wd, ktn, dsts), f, b = bg["pend"]
            fv = f[:, 0:ktn * wd].rearrange("p (kt c) -> p kt c", c=wd)
            off = 0
            for (dap, dbuf, co, w) in dsts:
                view = b[:, off:off + ktn * w].rearrange("p (kt c) -> p kt c", c=w)
                self.cp(bg["engs"][bg["n"] % len(bg["engs"])], view, fv[:, :, co:co + w], [f.b()], [b.b()])
                bg["n"] += 1
                self.dma(dap, view, [b.b()], [dbuf])
                off += ktn * w
            bg["pend"] = None
        if bg["i"] < len(bg["jobs"]):
            job = bg["jobs"][bg["i"]]
            f = bg["wf"][bg["i"] % 2]
            b = bg["wb"][bg["i"] % 2]
            (src, l, c0, wd, ktn, dsts) = job
            fv = f[:, 0:ktn * wd].rearrange("p (kt c) -> p kt c", c=wd)
            self.dma(fv, src[l, :, c0:c0 + wd].rearrange("(kt p) c -> p kt c", p=128), [src.b()], [f.b()])
            bg["pend"] = (job, f, b)
            bg["i"] += 1

    def bg_end(self):
        bg = getattr(self, "bg", None)
        if bg is None:
            return
        while bg["pend"] is not None or bg["i"] < len(bg["jobs"]):
            self.bg_step()
        self.bg = None

    def cast_weights(self):
        self.P.stage = "cast"
        sched = self.cfg.get("bg_cast", True)
        jobs = self.cast_jobs(0, "in")
        self.bg_sched = {}
        if sched:
            r0 = self.cast_jobs(0, "rest")
            h = len(r0) // 2
            self.bg_sched[("attn", 0, "s")] = r0[:h]
            self.bg_sched[("rwkv", 0, "p")] = r0[h:]
            for l in range(1, self.DEPTH):
                self.bg_sched[("rwkv", l - 1, "p")] = self.bg_sched.get(("rwkv", l - 1, "p"), []) + self.cast_jobs(l, "in")
                rl = self.cast_jobs(l, "rest")
                h = len(rl) // 2
                self.bg_sched[("attn", l, "s")] = rl[:h]
                self.bg_sched[("rwkv", l, "p")] = rl[h:]
        else:
            jobs += self.cast_jobs(0, "rest")
            for l in range(1, self.DEPTH):
                jobs += self.cast_jobs(l, "in") + self.cast_jobs(l, "rest")
        with contextlib.ExitStack() as stk:
            self.bg_begin(stk, jobs, engs=("act", "dve", "pool", "dve"))
            self.bg_end()
            self.barrier()

    def load_pp(self, dst, dst_b, src_rows, n, src_b):
        k = self._pp_i = getattr(self, "_pp_i", 0) + 1
        stg = self.pp_stage[k % 2]
        bank = self.pb[6 + (k % 2)]
        self.dma(stg[0:n, :], src_rows, [src_b], [stg.b()])
        self.tr(bank[:, 0:n], stg[0:n, :], self.ident_f[0:n, 0:n], [stg.b(), self.ident_f.b()], [bank.b()])
        self.cp("dve", dst, bank[:, 0:n], [bank.b()], [dst_b])

    def to_feature_major(self):
        self.P.stage = "tofm"
        with contextlib.ExitStack() as stk:
            xin = [self.sb("xin", [128, D], F32, stk) for _ in range(2)]
            xo = [self.sb("xo", [128, KT, 128], F32, stk) for _ in range(2)]
            i = 0
            for g, src in (("s", self.xs), ("p", self.xp)):
                Tg = self.groups[g][0]
                for blk in range(Tg // 128):
                    a = xin[i % 2]
                    o = xo[i % 2]
                    self.dma(a[:], src[blk * 128:(blk + 1) * 128, :], [src.b()], [a.b()])
                    for q in range(4):
                        bank = self.pb[(i * 4 + q) % 4]
                        for j in range(4):
                            kt = q * 4 + j
                            self.tr(bank[:, j * 128:(j + 1) * 128], a[:, kt * 128:(kt + 1) * 128], self.ident_f[:],
                                    [a.b(), self.ident_f.b()], [bank.b()])
                        eng = "act" if q % 2 else "dve"
                        self.cp(eng, o[:, q * 4:(q + 1) * 4, :], bank[:].rearrange("p (j t) -> p j t", j=4), [bank.b()], [o.b()])
                    self.dma(self.XT[g][:, :, blk * 128:(blk + 1) * 128].rearrange("kt p t -> p kt t"), o[:],
                             [o.b()], [self.XT[g].b(blk // 4)])
                    i += 1
            self.barrier()

    def setup_params(self):
        self.P.stage = "params"
        DEPTH, FT = self.DEPTH, self.FT
        self.pp_stage = [self.sb("ppstg", [128, 128], F32) for _ in range(2)]
        self.prm = []
        ccT = self.sb("ccT", [128, 32], F32)
        self.load_pp(ccT[:], ccT.b(), self.cc.ap.rearrange("v (kt p) -> (v kt) p", p=128), 32, self.cc.b())
        sc = self.sb("sc", [128, 32], F32)
        self.act(sc[:], ccT[:], AF.Silu, [ccT.b()], [sc.b()])
        fng = self.sb("fng", [128, KT], F32)
        self.load_pp(fng[:], fng.b(), self.final_norm_g.ap.rearrange("(kt p) -> kt p", p=128), KT, self.final_norm_g.b())
        self.fng = fng
        for l in range(DEPTH):
            pr = {}
            def vec(name, src, n, rows):
                t = self.sb(name, [128, n], F32)
                self.load_pp(t[:], t.b(), rows, n, src.b())
                return t
            pr["n1g"] = vec("n1g", self.norm1_g, KT, self.norm1_g[l].rearrange("(kt p) -> kt p", p=128))
            pr["n2g"] = vec("n2g", self.norm2_g, KT, self.norm2_g[l].rearrange("(kt p) -> kt p", p=128))
            pr["qng"] = vec("qng", self.q_norm_g, 1, self.q_norm_g[l:l + 1, :])
            pr["kng"] = vec("kng", self.k_norm_g, 1, self.k_norm_g[l:l + 1, :])
            pr["rwconv"] = vec("rwconv", self.rw_conv, 42, self.rw_conv[l].rearrange("j (t p) -> (j t) p", p=128))
            for nm, src in (("kk", self.rw_kk), ("ka", self.rw_ka), ("rk", self.rw_rk), ("lng", self.rw_lnx_g), ("lnb", self.rw_lnx_b)):
                pr[nm] = vec(nm, src, 4, src[l].rearrange("(t p) -> t p", p=128))
            pr["a0"] = vec("a0", self.rw_a0, 8, self.rw_a0[l].rearrange("d (t p) -> (d t) p", p=128))
            c1 = self.sb("c1", [128, 4], F32)
            self.ts("dve", c1[:], pr["ka"][:], -1.0, 1.0, ALU.mult, ALU.add, [pr["ka"].b()], [c1.b()])
            pr["c1"] = c1
            nft = 2 * FT
            fcw = self.sb("fcw", [128, 3, nft], F32)
            for j in range(3):
                self.load_pp(fcw[:, j, :], fcw.b(), self.ffn_conv_w[l, j].rearrange("(t p) -> t p", p=128), nft, self.ffn_conv_w.b())
            pr["fcw"] = fcw
            nfcw = self.sb("nfcw", [128, 3, nft], F32)
            self.ts("dve", nfcw[:], fcw[:], -1.0, None, ALU.mult, None, [fcw.b()], [nfcw.b()])
            pr["nfcw"] = nfcw
            pr["fcb"] = vec("fcb", self.ffn_conv_b, nft, self.ffn_conv_b[l].rearrange("(t p) -> t p", p=128))
            bada = vec("bada", self.b_ada, 96, self.b_ada[l].rearrange("(t p) -> t p", p=128))
            mods = [self.sb(f"mod{v}", [128, 96], F32) for v in range(2)]
            modr = self.mod_rows[l]
            with contextlib.ExitStack() as stk:
                wa = [self.sb("wada", [128, KT, 1024], F32, stk) for _ in range(2)]
                rowt = [self.sb("modrow", [2, 512], F32, stk) for _ in range(2)]
                for blk in range(12):
                    w = wa[blk % 2]
                    self.dma(w[:], self.w_ada[l, :, blk * 1024:(blk + 1) * 1024].rearrange("(kt p) c -> p kt c", p=128),
                             [self.w_ada.b()], [w.b()], eng=("sp", "act")[blk % 2])
                    for hf in range(2):
                        cb = blk * 2 + hf
                        acc = self.pb[4 + cb % 2]
                        for kt in range(KT):
                            self.mm(acc[0:2, :], sc[:, kt:32:16], w[:, kt, hf * 512:(hf + 1) * 512], kt == 0, kt == KT - 1,
                                    [w.b(), sc.b()], [acc.b()])
                        rt_ = rowt[cb % 2]
                        self.cp("dve" if cb % 2 else "act", rt_[:], acc[0:2, :], [acc.b()], [rt_.b()])
                        self.dma(modr[:, cb * 512:(cb + 1) * 512], rt_[:], [rt_.b()], [modr.b()])
                for v in range(2):
                    m = mods[v]
                    self.load_pp(m[:], m.b(), modr[v].rearrange("(t p) -> t p", p=128), 96, modr.b())
                    self.tt("dve", m[:], m[:], bada[:], ALU.add, [m.b(), bada.b()], [m.b()])
                self.barrier()
            pr["mod"] = mods
            for v in range(2):
                for nm, gname, j in (("gs1", "n1g", 1), ("gs2", "n2g", 4)):
                    t = self.sb(f"{nm}_{v}", [128, KT], F32)
                    self.stt("dve", t[:], mods[v][:, j * 16:(j + 1) * 16], 1.0, pr[gname][:], ALU.add, ALU.mult,
                             [mods[v].b(), pr[gname].b()], [t.b()])
                    pr[f"{nm}_{v}"] = t
            self.prm.append(pr)

    def norm_mod(self, xt, chunks, gs, sh_ap_fn, hT, stk, ss_banks):
        Wtot = chunks[-1][1]
        sq = [self.sb("nsq", [128, Wtot], BF16, stk) for _ in range(2)]
        rstd = self.sb("nrstd", [128, Wtot], F32, stk)
        tmp = [self.sb("ntmp", [128, Wtot], F32, stk) for _ in range(2)]
        for kt in range(KT):
            s = sq[kt % 2]
            self.act(s[:], xt[:, kt, :], AF.Square, [xt.b()], [s.b()])
            for ci, (c0, c1) in enumerate(chunks):
                bk = ss_banks[ci]
                self.mm(bk[:, 0:c1 - c0], self.ones_b[:], s[:, c0:c1], kt == 0, kt == KT - 1,
                        [s.b(), self.ones_b.b()], [bk.b()])
        for ci, (c0, c1) in enumerate(chunks):
            bk = ss_banks[ci]
            self.act(rstd[:, c0:c1], bk[:, 0:c1 - c0], AF.Sqrt, [bk.b(), self.eps6.b()], [rstd.b()],
                     scale=1.0 / D, bias=self.eps6[:, 0:1])
        self.recip(rstd[:], rstd[:], [rstd.b()], [rstd.b()])
        for kt in range(KT):
            t = tmp[kt % 2]
            self.tt("pool" if kt % 2 else "dve", t[:], xt[:, kt, :], rstd[:], ALU.mult, [xt.b(), rstd.b()], [t.b()])
            self.act(hT[:, kt, :], t[:], AF.Identity, [t.b(), gs.b()], [hT.b()],
                     scale=gs[:, kt:kt + 1], bias=sh_ap_fn(kt))

    def zero_rwkv(self, g):
        Tg = self.groups[g][0]
        with contextlib.ExitStack() as stk:
            z = self.sb("zz", [128, Tg], BF16, stk)
            self.memset("pool", z[:], 0.0, [z.b()])
            for r in range(12, 16):
                self.dma(self.MIXT[g][r], z[:], [z.b()], [self.MIXT[g].b(i) for i in range((Tg + 511) // 512)])
            self.barrier()

    def stage1_x(self, l, g, c0, X):
        self._X1 = X
        return self.stage1(l, g, c0)

    def stage1(self, l, g, c0):
        self.P.stage = f"s1_{l}_{g}"
        W = 512
        v = 0 if g == "s" else 1
        pr = self.prm[l]
        mod = pr["mod"][v]
        ti = c0 // 512
        pb = self.pb
        is_s = (g == "s")
        with contextlib.ExitStack() as stk:
            xt = self.sb("xt", [128, KT, W], F32, stk)
            hT = self.sb("hT", [128, KT, W], BF16, stk)
            self.dma(xt[:], self._X1[g][:, :, c0:c0 + W].rearrange("kt p t -> p kt t"), [self._X1[g].b(ti)], [xt.b()])
            self.norm_mod(xt, [(0, W)], pr[f"gs1_{v}"], lambda kt: mod[:, kt:kt + 1], hT, stk, [pb[7]])
            wts = [self.sb("wt", [128, KT, 128], BF16, stk) for _ in range(3)]
            wv = self.sb("wv", [128, KT, 256], BF16, stk)
            uT = self.sb("uT", [128, W], BF16, stk)
            abt = self.sb("abt", [128, 4, 256], BF16, stk)
            sqh = self.sb("sqh", [128, W], BF16, stk)
            rq = self.sb("rq", [128, W], F32, stk)
            qn = [self.sb("qn", [128, W], F32, stk) for _ in range(2)]
            t1 = self.sb("t1", [128, W], F32, stk)
            t2 = self.sb("t2", [128, W], F32, stk)
            qr = [self.sb("qr", [128, W], BF16, stk) for _ in range(2)]
            ktok = self.sb("ktok", [128, 4, 128], F32, stk)
            vt = self.sb("vt", [128, 4, 256], BF16, stk)
            vtf = self.sb("vtf", [128, 4, 256], F32, stk)
            rwt = [self.sb("rwt", [128, W], F32, stk) for _ in range(2)]
            if is_s:
                cosb = self.sb("cosb", [128, W], F32, stk)
                sinb = self.sb("sinb", [128, W], F32, stk)
                self.dma(cosb[:], self.c_cosT[:, c0:c0 + W], [self.c_cosT.b()], [cosb.b()])
                self.dma(sinb[:], self.c_sinT[:, c0:c0 + W], [self.c_sinT.b()], [sinb.b()])

            order = list(range(14)) + list(range(16, 30))

            def load_w(oi):
                ot = order[oi]
                w = wts[oi % 3]
                self.dma(w[:], self.Win_t[l, ot], [self.Win_t.b((l, ot))], [w.b()])
            load_w(0)
            load_w(1)
            self.dma(wv[:], self.Wv_t[l], [self.Wv_t.b(l)], [wv.b()])
            for oi, ot in enumerate(order):
                if oi + 2 < len(order):
                    load_w(oi + 2)
                w = wts[oi % 3]
                acc = pb[oi % 3]
                for kt in range(KT):
                    self.mm(acc[:], w[:, kt, :], hT[:, kt, :], kt == 0, kt == KT - 1, [w.b(), hT.b()], [acc.b()])
                skip = self.cfg.get("s1_skip", "")
                if ("f" in skip and ot < 4) or ("q" in skip and 4 <= ot < 14) or ("r" in skip and ot >= 16):
                    continue
                if ot < 4:
                    self.cp("act", uT[:], acc[:], [acc.b()], [uT.b()])
                    for sub in range(4):
                        reg = pb[5][:, (sub % 2) * 256:(sub % 2) * 256 + 256]
                        self.mm(reg, uT[:, sub * 128:(sub + 1) * 128], self.csC[:], True, True,
                                [uT.b(), self.csC.b()], [pb[5].b()])
                        self.cp("dve", abt[:, sub, :], reg, [pb[5].b()], [abt.b()])
                    self.dma(self.AB[g][c0:c0 + W, ot * 256:(ot + 1) * 256].rearrange("(s p) c -> p s c", p=128), abt[:],
                             [abt.b()], [self.AB[g].b(ti)])
                elif ot < 14:
                    isk = ot >= 12
                    hd = ot - 12 if isk else ot - 4
                    gq = pr["kng"] if isk else pr["qng"]
                    q_n = qn[oi % 2]
                    q_r = qr[oi % 2]
                    self.act(sqh[:], acc[:], AF.Square, [acc.b()], [sqh.b()])
                    self.mm(pb[3][:], self.ones_b[:], sqh[:], True, True, [sqh.b(), self.ones_b.b()], [pb[3].b()])
                    self.act(rq[:], pb[3][:], AF.Sqrt, [pb[3].b(), self.eps6.b()], [rq.b()], scale=1.0 / 128, bias=self.eps6[:, 0:1])
                    self.recip(rq[:], rq[:], [rq.b()], [rq.b()])
                    self.stt("dve", q_n[:], acc[:], gq[:, 0:1], rq[:], ALU.mult, ALU.mult, [acc.b(), gq.b(), rq.b()], [q_n.b()])
                    if isk and not is_s:
                        for sub in range(4):
                            self.tr(pb[5][:, sub * 128:(sub + 1) * 128], q_n[:, sub * 128:(sub + 1) * 128], self.ident_f[:],
                                    [q_n.b(), self.ident_f.b()], [pb[5].b()])
                        self.cp("dve", ktok[:], pb[5][:].rearrange("p (s c) -> p s c", s=4), [pb[5].b()], [ktok.b()])
                        for sub in range(4):
                            tok = sub * 128
                            sq_, off = tok // self.TP, tok % self.TP
                            self.dma(self.nk[sq_, l, off:off + 128, hd * 128:(hd + 1) * 128], ktok[:, sub, :],
                                     [ktok.b()], [self.nk.b()])
                    if is_s:
                        self.mm(pb[4][:], self.prot_f[:], q_n[:], True, True, [self.prot_f.b(), q_n.b()], [pb[4].b()])
                        self.tt("pool", t1[:], q_n[:], cosb[:], ALU.mult, [q_n.b(), cosb.b()], [t1.b()])
                        self.tt("dve", t2[:], pb[4][:], sinb[:], ALU.mult, [pb[4].b(), sinb.b()], [t2.b()])
                        self.tt("pool", q_r[:], t1[:], t2[:], ALU.add, [t1.b(), t2.b()], [q_r.b()])
                    else:
                        self.cp("pool", q_r[:], q_n[:], [q_n.b()], [q_r.b()])
                    dst = self.KTs[g] if isk else self.QT[g]
                    self.dma(dst[hd, :, c0:c0 + W], q_r[:], [q_r.b()], [dst.b(ti)])
                else:
                    r = rwt[oi % 2]
                    self.cp("act" if oi % 2 else "dve", r[:], acc[:], [acc.b()], [r.b()])
                    self.dma(self.RW[g][(ot - 16) * 128:(ot - 15) * 128, c0:c0 + W], r[:], [r.b()], [self.RW[g].b(ti)])
                if oi == self.cfg.get("v_at", 13) and "v" not in skip:
                    for sub in range(4):
                        reg = pb[6][:, (sub % 2) * 256:(sub % 2) * 256 + 256]
                        for kt in range(KT):
                            self.mm(reg, hT[:, kt, sub * 128:(sub + 1) * 128], wv[:, kt, :], kt == 0, kt == KT - 1,
                                    [hT.b(), wv.b()], [pb[6].b()])
                        self.cp("act", vt[:, sub, :], reg, [pb[6].b()], [vt.b()])
                        if not is_s:
                            self.cp(self.cfg.get("vtf_eng", "dve"), vtf[:, sub, :], reg, [pb[6].b()], [vtf.b()])
                    self.dma(self.Vs[g][c0:c0 + W, :].rearrange("(s p) c -> p s c", p=128), vt[:], [vt.b()], [self.Vs[g].b(ti)])
                    if not is_s:
                        for sub in range(4):
                            tok = sub * 128
                            sq_, off = tok // self.TP, tok % self.TP
                            self.dma(self.nv[sq_, l, off:off + 128, :], vtf[:, sub, :], [vtf.b()], [self.nv.b()])
            self.barrier()

    def stage_fourier(self, l, g):
        self.P.stage = f"four_{l}_{g}"
        Tg, nseq, Tseq = self.groups[g]
        nch = Tseq // 128
        TW = min(512, Tseq)
        pb = self.pb
        allab = [self.AB[g].b(i) for i in range((Tg + 511) // 512)]
        with contextlib.ExitStack() as stk:
            ab = self.sb("fab", [128, nch, 1024], BF16, stk)
            ctb = [self.sb("fct", [128, nch, TW], BF16, stk) for _ in range(2)]
            nsb = [self.sb("fns", [128, nch, TW], BF16, stk) for _ in range(2)]
            fo = [self.sb("ffo", [128, TW], BF16, stk) for _ in range(2)]
            n = 0
            for s in range(nseq):
                self.dma(ab[:], self.AB[g][s * Tseq:(s + 1) * Tseq, :].rearrange("(c p) x -> p c x", p=128), allab, [ab.b()])
                for ti, t0 in enumerate(range(0, Tseq, TW)):
                    cb = ctb[ti % 2]
                    sbb = nsb[ti % 2]
                    self.dma(cb[:], self.c_ct[g][:, t0:t0 + TW].rearrange("(c p) t -> p c t", p=128), [self.c_ct[g].b()], [cb.b()])
                    self.dma(sbb[:], self.c_nst[g][:, t0:t0 + TW].rearrange("(c p) t -> p c t", p=128), [self.c_nst[g].b()], [sbb.b()])
                    for grp in range(4):
                        acc = pb[n % 4]
                        for c in range(nch):
                            self.mm(acc[:, 0:TW], ab[:, c, grp * 256:grp * 256 + 128], cb[:, c, :], c == 0, False,
                                    [ab.b(), cb.b()], [acc.b()])
                            self.mm(acc[:, 0:TW], ab[:, c, grp * 256 + 128:grp * 256 + 256], sbb[:, c, :], False, c == nch - 1,
                                    [ab.b(), sbb.b()], [acc.b()])
                        f = fo[n % 2]
                        self.cp("act" if n % 2 else "dve", f[:], acc[:, 0:TW], [acc.b()], [f.b()])
                        c0 = s * Tseq + t0
                        self.dma(self.MIXT[g][grp, :, c0:c0 + TW], f[:], [f.b()], [self.MIXT[g].b(c0 // 512)])
                        n += 1
            self.barrier()

    def stage_attn(self, l, g):
        self.P.stage = f"attn_{l}_{g}"
        Tg, nseq, Tseq = self.groups[g]
        is_s = (g == "s")
        Stot = Tseq + (PAST if is_s else 0)
        nck = Stot // 128
        QW = min(512, Tseq)
        pb = self.pb
        ntile = (Tg + 511) // 512
        scale = 128.0 ** -0.5
        with contextlib.ExitStack() as stk:
            kT = self.sb("akT", [128, NKV, Stot], BF16, stk)
            vv = self.sb("avv", [128, nck, 256], BF16, stk)
            qT = [self.sb("aqT", [128, Tseq], BF16, stk) for _ in range(2)]
            pT = [self.sb("apT", [128, QW], BF16, stk) for _ in range(3)]
            rden = self.sb("arden", [128, QW], F32, stk)
            ob = [self.sb("aob", [128, QW], BF16, stk) for _ in range(2)]
            if is_s:
                ckf = self.sb("ackf", [128, PAST // 128, 256], F32, stk)
                cvf = self.sb("acvf", [128, PAST // 128, 256], F32, stk)
            nq = 0
            bgj = self.bg_sched.pop(("attn", l, g), None)
            if bgj:
                self.bg_begin(stk, bgj, engs=("dve",))
            for s in range(nseq):
                c0s = s * Tseq
                for kv in range(NKV):
                    self.dma(kT[:, kv, 0:Tseq], self.KTs[g][kv, :, c0s:c0s + Tseq], [self.KTs[g].b(i) for i in range(ntile)], [kT.b()])
                self.dma(vv[:, 0:Tseq // 128, :], self.Vs[g][c0s:c0s + Tseq, :].rearrange("(c p) x -> p c x", p=128),
                         [self.Vs[g].b(i) for i in range(ntile)], [vv.b()])
                if is_s:
                    self.dma(ckf[:], self.ck[l].rearrange("(c p) x -> p c x", p=128), [self.ck.b()], [ckf.b()])
                    self.dma(cvf[:], self.cv[l].rearrange("(c p) x -> p c x", p=128), [self.cv.b()], [cvf.b()])
                    self.cp("pool", vv[:, Tseq // 128:nck, :], cvf[:], [cvf.b()], [vv.b()])
                    for kv in range(NKV):
                        bank = pb[kv]
                        for c in range(PAST // 128):
                            self.tr(bank[:, c * 128:(c + 1) * 128], ckf[:, c, kv * 128:(kv + 1) * 128], self.ident_f[:],
                                    [ckf.b(), self.ident_f.b()], [bank.b()])
                        self.cp("act", kT[:, kv, Tseq:Stot], bank[:, 0:PAST], [bank.b()], [kT.b()])
                for h in range(NH):
                    kv = h // (NH // NKV)
                    q = qT[h % 2]
                    self.dma(q[:], self.QT[g][h, :, c0s:c0s + Tseq], [self.QT[g].b(i) for i in range(ntile)], [q.b()])
                    for q0 in range(0, Tseq, QW):
                        self.bg_step()
                        oacc = pb[4 + nq % 2]
                        dacc = pb[6 + nq % 2]
                        def score(c):
                            sT = pb[c % 3]
                            p = pT[c % 3]
                            self.mm(sT[:, 0:QW], kT[:, kv, c * 128:(c + 1) * 128], q[:, q0:q0 + QW], True, True,
                                    [kT.b(), q.b()], [sT.b()])
                            self.act(p[:], sT[:, 0:QW], AF.Exp, [sT.b()], [p.b()], scale=scale)
                        pipe = self.cfg.get("attn_pipe", True)
                        if pipe:
                            score(0)
                        for c in range(nck):
                            if not pipe:
                                score(c)
                            elif c + 1 < nck:
                                score(c + 1)
                            p = pT[c % 3]
                            self.mm(oacc[:, 0:QW], vv[:, c, kv * 128:(kv + 1) * 128], p[:], c == 0, c == nck - 1,
                                    [vv.b(), p.b()], [oacc.b()])
                            self.mm(dacc[:, 0:QW], self.ones_b[:], p[:], c == 0, c == nck - 1,
                                    [self.ones_b.b(), p.b()], [dacc.b()])
                        self.recip(rden[:], dacc[:, 0:QW], [dacc.b()], [rden.b()])
                        o = ob[nq % 2]
                        self.tt("dve", o[:], oacc[:, 0:QW], rden[:], ALU.mult, [oacc.b(), rden.b()], [o.b()])
                        cc0 = c0s + q0
                        self.dma(self.MIXT[g][4 + h, :, cc0:cc0 + QW], o[:], [o.b()], [self.MIXT[g].b(cc0 // 512)])
                        nq += 1
            self.bg_end()
            self.barrier()

    def stage3(self, l, g, c0, X):
        self.P.stage = f"s3_{l}_{g}"
        W = 512
        v = 0 if g == "s" else 1
        pr = self.prm[l]
        mod = pr["mod"][v]
        ti = c0 // 512
        pb = self.pb
        with contextlib.ExitStack() as stk:
            xt = self.sb("xt3", [128, KT, W], F32, stk)
            mx = self.sb("mx3", [128, KT, W], BF16, stk)
            wts = [self.sb("wt3", [128, KT, 128], BF16, stk) for _ in range(3)]
            self.dma(xt[:], X[g][:, :, c0:c0 + W].rearrange("kt p t -> p kt t"), [X[g].b(ti)], [xt.b()])
            self.dma(mx[:], self.MIXT[g][:, :, c0:c0 + W].rearrange("kt p t -> p kt t"), [self.MIXT[g].b(ti)], [mx.b()])

            def load_w(ot):
                w = wts[ot % 3]
                self.dma(w[:], self.Wout_t[l, ot], [self.Wout_t.b((l, ot))], [w.b()])
            load_w(0)
            load_w(1)
            for ot in range(KT):
                if ot + 2 < KT:
                    load_w(ot + 2)
                w = wts[ot % 3]
                acc = pb[ot % 4]
                for kt in range(KT):
                    self.mm(acc[:], w[:, kt, :], mx[:, kt, :], kt == 0, kt == KT - 1, [w.b(), mx.b()], [acc.b()])
                self.stt("dve", xt[:, ot, :], acc[:], mod[:, 32 + ot:33 + ot], xt[:, ot, :], ALU.mult, ALU.add,
                         [acc.b(), xt.b(), xt.b(ot)], [xt.b(ot)])
            self.dma(X[g][:, :, c0:c0 + W].rearrange("kt p t -> p kt t"), xt[:], [xt.b(ot) for ot in range(KT)] + [xt.b()], [X[g].b(ti)])
            self.barrier()

    def stage4(self, l, g, c0, X, Y):
        self.P.stage = f"s4_{l}_{g}"
        W = 512
        Tg, nseq, Tseq = self.groups[g]
        v = 0 if g == "s" else 1
        pr = self.prm[l]
        mod = pr["mod"][v]
        ti = c0 // 512
        nti = (Tg + 511) // 512
        pb, pw = self.pb, self.pw
        FT = self.FT
        fcw, fcb, nfcw = pr["fcw"], pr["fcb"], pr["nfcw"]
        with contextlib.ExitStack() as stk:
            xt = self.sb("xt4", [128, KT, W + 2], F32, stk)
            hT = self.sb("hT4", [128, KT, W + 2], BF16, stk)
            actT = self.sb("act4", [128, FT, W], BF16, stk)
            wts = [self.sb("wt4", [128, KT, 128], BF16, stk) for _ in range(3)]
            wdn = [self.sb("wd4", [128, FT, 128], BF16, stk) for _ in range(2)]
            ua = self.sb("ua4", [128, W], F32, stk)
            ug = self.sb("ug4", [128, W], F32, stk)
            sg = self.sb("sg4", [128, W], F32, stk)
            self.dma(xt[:, :, 0:W], X[g][:, :, c0:c0 + W].rearrange("kt p t -> p kt t"), [X[g].b(ti)], [xt.b()])
            lzero = (c0 % Tseq == 0)
            rzero = ((c0 + W) % Tseq == 0)
            if lzero:
                self.memset("pool", xt[:, :, W:W + 1], 0.0, [xt.b()])
            else:
                self.dma(xt[:, :, W:W + 1], X[g][:, :, c0 - 1:c0].rearrange("kt p t -> p kt t"), [X[g].b(ti - 1)], [xt.b()],
                         allow_slow_non_contiguous=True)
            if rzero:
                self.memset("pool", xt[:, :, W + 1:W + 2], 0.0, [xt.b()])
            else:
                self.dma(xt[:, :, W + 1:W + 2], X[g][:, :, c0 + W:c0 + W + 1].rearrange("kt p t -> p kt t"), [X[g].b(ti + 1)], [xt.b()],
                         allow_slow_non_contiguous=True)
            self.norm_mod(xt, [(0, W), (W, W + 2)], pr[f"gs2_{v}"], lambda kt: mod[:, 48 + kt:49 + kt], hT, stk, [pb[7], pb[6]])
            if lzero:
                self.memset("pool", hT[:, :, W:W + 1], 0.0, [hT.b()])
            if rzero:
                self.memset("pool", hT[:, :, W + 1:W + 2], 0.0, [hT.b()])
            inner = [b for b in range(Tseq, W, Tseq)] if Tseq < W else []

            def load_w(j):
                i, half = j // 2, j % 2
                w = wts[j % 3]
                ot = i + half * FT
                self.dma(w[:], self.Wup_t[l, ot], [self.Wup_t.b((l, ot))], [w.b()])
            load_w(0)
            load_w(1)
            for j in range(2 * FT):
                if j + 2 < 2 * FT:
                    load_w(j + 2)
                i, half = j // 2, j % 2
                ot = i + half * FT
                w = wts[j % 3]
                wide = j % 3
                A, Bk = pb[2 * wide], pb[2 * wide + 1]
                for kt in range(KT):
                    self.mm(A[:], w[:, kt, :], hT[:, kt, 0:W], kt == 0, kt == KT - 1, [w.b(), hT.b()], [A.b()])
                for kt in range(KT):
                    self.mm(Bk[:, 0:2], w[:, kt, :], hT[:, kt, W:W + 2], kt == 0, kt == KT - 1, [w.b(), hT.b()], [Bk.b()])
                u = ug if half else ua
                w0 = fcw[:, 0, ot:ot + 1]
                w1 = fcw[:, 1, ot:ot + 1]
                w2 = fcw[:, 2, ot:ot + 1]
                self.act(u[:], A[:], AF.Identity, [A.b(), fcw.b(), fcb.b()], [u.b()], scale=w1, bias=fcb[:, ot:ot + 1])
                self.stt("dve", u[:, 1:W], A[:, 0:W - 1], w0, u[:, 1:W], ALU.mult, ALU.add, [A.b(), u.b()], [u.b()])
                self.stt("dve", u[:, 0:W - 1], A[:, 1:W], w2, u[:, 0:W - 1], ALU.mult, ALU.add, [A.b(), u.b()], [u.b()])
                self.stt("dve", u[:, 0:1], Bk[:, 0:1], w0, u[:, 0:1], ALU.mult, ALU.add, [Bk.b(), u.b()], [u.b()])
                self.stt("dve", u[:, W - 1:W], Bk[:, 1:2], w2, u[:, W - 1:W], ALU.mult, ALU.add, [Bk.b(), u.b()], [u.b()])
                for bnd in inner:
                    self.stt("dve", u[:, bnd - 1:bnd], A[:, bnd:bnd + 1], nfcw[:, 2, ot:ot + 1], u[:, bnd - 1:bnd], ALU.mult, ALU.add,
                             [A.b(), u.b(), nfcw.b()], [u.b()])
                    self.stt("dve", u[:, bnd:bnd + 1], A[:, bnd - 1:bnd], nfcw[:, 0, ot:ot + 1], u[:, bnd:bnd + 1], ALU.mult, ALU.add,
                             [A.b(), u.b(), nfcw.b()], [u.b()])
                if half:
                    self.act(sg[:], ug[:], AF.Silu, [ug.b()], [sg.b()])
                    self.tt("pool", actT[:, i, :], sg[:], ua[:], ALU.mult, [sg.b(), ua.b()], [actT.b()])
            self.dma(wdn[0][:], self.Wdn_t[l, 0], [self.Wdn_t.b((l, 0))], [wdn[0].b()])
            for ot in range(KT):
                if ot + 1 < KT:
                    self.dma(wdn[(ot + 1) % 2][:], self.Wdn_t[l, ot + 1], [self.Wdn_t.b((l, ot + 1))], [wdn[(ot + 1) % 2].b()])
                w = wdn[ot % 2]
                acc = pb[6 + ot % 2]
                for kt in range(FT):
                    self.mm(acc[:], w[:, kt, :], actT[:, kt, :], kt == 0, kt == FT - 1, [w.b(), actT.b()], [acc.b()])
                self.stt("dve", xt[:, ot, 0:W], acc[:], mod[:, 80 + ot:81 + ot], xt[:, ot, 0:W], ALU.mult, ALU.add,
                         [acc.b(), xt.b(), xt.b(ot)], [xt.b(ot)])
            self.dma(Y[g][:, :, c0:c0 + W].rearrange("kt p t -> p kt t"), xt[:, :, 0:W], [xt.b(ot) for ot in range(KT)] + [xt.b()], [Y[g].b(ti)])
            self.barrier()

    def stage5(self, g, c0, X):
        self.P.stage = f"s5_{g}"
        W = 512
        ti = c0 // 512
        pb = self.pb
        out = self.ys if g == "s" else self.yp
        with contextlib.ExitStack() as stk:
            xt = self.sb("xt5", [128, KT, W], F32, stk)
            xn = self.sb("xn5", [128, KT, W], F32, stk)
            sq = [self.sb("sq5", [128, W], BF16, stk) for _ in range(2)]
            rstd = self.sb("rstd5", [128, W], F32, stk)
            yt = [self.sb("yt5", [128, D], F32, stk) for _ in range(2)]
            self.dma(xt[:], X[g][:, :, c0:c0 + W].rearrange("kt p t -> p kt t"), [X[g].b(ti)], [xt.b()])
            for kt in range(KT):
                s = sq[kt % 2]
                self.act(s[:], xt[:, kt, :], AF.Square, [xt.b()], [s.b()])
                self.mm(pb[7][:], self.ones_b[:], s[:], kt == 0, kt == KT - 1, [s.b(), self.ones_b.b()], [pb[7].b()])
            self.act(rstd[:], pb[7][:], AF.Sqrt, [pb[7].b(), self.eps6.b()], [rstd.b()], scale=1.0 / D, bias=self.eps6[:, 0:1])
            self.recip(rstd[:], rstd[:], [rstd.b()], [rstd.b()])
            for kt in range(KT):
                self.stt("dve", xn[:, kt, :], xt[:, kt, :], self.fng[:, kt:kt + 1], rstd[:], ALU.mult, ALU.mult,
                         [xt.b(), rstd.b(), self.fng.b()], [xn.b()])
            for sub in range(4):
                y = yt[sub % 2]
                for q in range(4):
                    bank = pb[q]
                    for j in range(4):
                        kt = q * 4 + j
                        self.tr(bank[:, j * 128:(j + 1) * 128], xn[:, kt, sub * 128:(sub + 1) * 128], self.ident_f[:],
                                [xn.b(), self.ident_f.b()], [bank.b()])
                    self.cp("act" if q % 2 else "dve", y[:, q * 512:(q + 1) * 512], bank[:], [bank.b()], [y.b()])
                self.dma(out[c0 + sub * 128:c0 + (sub + 1) * 128, :], y[:], [y.b()], [out.b()])
            self.barrier()

    def stage_rwkv(self, l, g):
        self.P.stage = f"rwkv_{l}_{g}"
        Tg, nseq, Tseq = self.groups[g]
        is_s = (g == "s")
        T = Tseq
        nch = T // 128
        NB = 4
        pr = self.prm[l]
        pb = self.pb
        ntile = (Tg + 511) // 512
        rwall = [self.RW[g].b(i) for i in range(ntile)]
        conv = pr["rwconv"]
        CW = min(512, T)
        with contextlib.ExitStack() as stk:
            sbt = lambda name, shape, dt: self.sb(name, shape, dt, stk)
            w2e = sbt("w2e", [65, 2, 512], F32)
            a2f = sbt("a2f", [128, 2, 512], F32)
            g2f = sbt("g2f", [128, 512], F32)
            self.dma(w2e[0:64, :, :], self.rw_w2[l].rearrange("d k c -> k d c"), [self.rw_w2.b()], [w2e.b()])
            self.dma(w2e[64:65, :, :], self.rw_w0[l:l + 1], [self.rw_w0.b()], [w2e.b()])
            self.dma(a2f[64:128, :, :], self.rw_a2[l].rearrange("d k c -> k d c"), [self.rw_a2.b()], [a2f.b()])
            self.dma(g2f[:], self.rw_g2[l], [self.rw_g2.b()], [g2f.b()])
            t12 = sbt("t12", [128, T], F32)
            tw = sbt("tw", [65, T], F32)
            sg = sbt("sg", [128, T], F32)
            rc = sbt("rc", [128, T], F32)
            kkn = sbt("kkn", [128, T], F32)
            bA = [sbt("bA", [128, T], BF16) for _ in range(2)]
            KM = [sbt("KM", [128, T], BF16) for _ in range(2)]
            bon = sbt("bon", [128, T], F32)
            yacc = sbt("yacc", [128, T], F32)
            s1 = sbt("scr1", [128, T + 2], F32)
            s2 = sbt("scr2", [128, T], F32)
            s3 = sbt("scr3", [128, T], F32)
            al = sbt("al", [128, T], BF16)
            be = sbt("be", [128, T], BF16)
            ka_ = sbt("kap", [128, T], BF16)
            rt = sbt("rt", [128, T], BF16)
            VtmP = sbt("VtmP", [128, nch, 2, 128], BF16)
            BTt = sbt("BTt", [128, nch, 128], BF16)
            KTt = sbt("KTt", [128, nch, 128], BF16)
            PL = sbt("PL", [128, nch], F32)
            sigT = [sbt("sigT", [128, 128], F32) for _ in range(2)]
            Ptmp = [sbt("Ptmp", [128, 3, 256], F32) for _ in range(2)]
            NI = NB * 2
            Xa = sbt("Xa", [128, NI, 128], BF16)
            XTa = sbt("XTa", [128, NI, 128], BF16)
            Xb = sbt("Xb", [128, NI, 128], BF16)
            XTb = sbt("XTb", [128, NI, 128], BF16)
            Pf = sbt("Pf", [128, NI, 128], F32)
            Pfin = sbt("Pfin", [128, NI, 128], BF16)
            MakT = sbt("MakT", [128, NI, 128], BF16)
            Wbr = sbt("Wbr", [128, NI, 128], BF16)
            Wkr = sbt("Wkr", [128, NI, 128], BF16)
            Sf = sbt("Sf", [128, 128], F32)
            Sb = sbt("Sb", [128, 128], BF16)
            Sld = sbt("Sld", [128, 128], F32)
            RHSs = sbt("RHSs", [128, 128], BF16)
            Upad = sbt("Upad", [128, 2, 128], BF16)
            Ufull = sbt("Ufull", [128, 128], BF16)
            tS = sbt("tS", [128, 128], F32)
            sq = sbt("sqr", [128, CW], BF16)
            rstd = sbt("rstdr", [128, CW], F32)
            ob = [sbt("obr", [128, CW], BF16) for _ in range(2)]

            bgj = self.bg_sched.pop(("rwkv", l, g), None)
            if bgj:
                self.bg_begin(stk, bgj, engs=("act", "dve"))
            self.memset("pool", VtmP[:], 0.0, [VtmP.b()])
            self.memset("pool", Upad[:], 0.0, [Upad.b()])
            self.memset("pool", tw[64:65, :], 1.0, [tw.b()])
            self.memset("pool", s1[:, 0:1], 0.0, [s1.b()])
            self.memset("pool", s1[:, T + 1:T + 2], 0.0, [s1.b()])

            def conv_tile(tile_idx, c0s, dst):
                self.dma(s1[:, 1:T + 1], self.RW[g][tile_idx * 128:(tile_idx + 1) * 128, c0s:c0s + T], rwall, [s1.b()])
                w = lambda j: conv[:, j * 14 + tile_idx:j * 14 + tile_idx + 1]
                self.act(dst[:], s1[:, 1:T + 1], AF.Copy, [s1.b(), conv.b()], [dst.b()], scale=w(1))
                self.stt("dve", dst[:], s1[:, 0:T], w(0), dst[:], ALU.mult, ALU.add, [s1.b(), dst.b()], [dst.b()])
                self.stt("dve", dst[:], s1[:, 2:T + 2], w(2), dst[:], ALU.mult, ALU.add, [s1.b(), dst.b()], [dst.b()])

            for s in range(nseq):
                c0s = s * T
                conv_tile(12, c0s, t12)
                self.act(tw[0:64, :], t12[0:64, :], AF.Tanh, [t12.b()], [tw.b()])
                conv_tile(13, c0s, sg)
                self.act(sg[:], sg[:], AF.Sigmoid, [sg.b()], [sg.b()])
                for ct in range(4):
                    self.bg_step()
                    conv_tile(ct, c0s, rc)
                    conv_tile(4 + ct, c0s, s2)
                    self.ts("pool", s3[:], s2[:], pr["kk"][:, ct:ct + 1], None, ALU.mult, None, [s2.b(), pr["kk"].b()], [s3.b()])
                    for x0 in range(0, T, CW):
                        bank = pb[(x0 // CW) % 2]
                        self.act(sq[:], s3[:, x0:x0 + CW], AF.Square, [s3.b()], [sq.b()])
                        self.mm(bank[:, 0:CW], self.bones_b[:], sq[:], True, True, [self.bones_b.b(), sq.b()], [bank.b()])
                        self.act(rstd[:], bank[:, 0:CW], AF.Sqrt, [bank.b(), self.eps12.b()], [rstd.b()], bias=self.eps12[:, 0:1])
                        self.recip(rstd[:], rstd[:], [rstd.b()], [rstd.b()])
                        self.tt("dve", kkn[:, x0:x0 + CW], s3[:, x0:x0 + CW], rstd[:], ALU.mult, [s3.b(), rstd.b()], [kkn.b()])
                    for d in range(2):
                        for x0 in range(0, T, CW):
                            bank = pb[2 + (x0 // CW) % 2]
                            self.mm(bank[:, 0:CW], a2f[64:128, d, ct * 128:(ct + 1) * 128], t12[64:128, x0:x0 + CW], True, True,
                                    [a2f.b(), t12.b()], [bank.b()])
                            self.act(s1[:, 1 + x0:1 + x0 + CW], bank[:, 0:CW], AF.Sigmoid, [bank.b(), pr["a0"].b()], [s1.b()],
                                     bias=pr["a0"][:, d * 4 + ct:d * 4 + ct + 1])
                        self.tt("pool", bA[d][:], kkn[:], s1[:, 1:T + 1], ALU.mult, [kkn.b(), s1.b()], [bA[d].b()])
                        self.ts("dve", s1[:, 1:T + 1], s1[:, 1:T + 1], pr["ka"][:, ct:ct + 1], pr["c1"][:, ct:ct + 1], ALU.mult, ALU.add,
                                [s1.b(), pr["ka"].b(), pr["c1"].b()], [s1.b()])
                        self.tt("dve", KM[d][:], s1[:, 1:T + 1], s2[:], ALU.mult, [s1.b(), s2.b()], [KM[d].b()])
                    conv_tile(8 + ct, c0s, s3)
                    self.tt("pool", s2[:], KM[0][:], KM[1][:], ALU.add, [KM[0].b(), KM[1].b()], [s2.b()])
                    self.stt("dve", s2[:], rc[:], pr["rk"][:, ct:ct + 1], s2[:], ALU.mult, ALU.mult, [rc.b(), s2.b(), pr["rk"].b()], [s2.b()])
                    for x0 in range(0, T, CW):
                        bank = pb[(x0 // CW) % 2]
                        self.mm(bank[:, 0:CW], self.bones_f[:], s2[:, x0:x0 + CW], True, True, [self.bones_f.b(), s2.b()], [bank.b()])
                        self.tt("dve", bon[:, x0:x0 + CW], bank[:, 0:CW], s3[:, x0:x0 + CW], ALU.mult, [bank.b(), s3.b()], [bon.b()])
                    for c in range(nch):
                        bank = pb[2 + c % 2]
                        self.tr(bank[:, 0:128], s3[:, c * 128:(c + 1) * 128], self.ident_f[:], [s3.b(), self.ident_f.b()], [bank.b()])
                        for hh in range(2):
                            self.cp("act" if hh else "dve", VtmP[:, c, hh, hh * 64:(hh + 1) * 64], bank[:, hh * 64:(hh + 1) * 64],
                                    [bank.b()], [VtmP.b()])
                    for d in range(2):
                        self.rwkv_dir(l, g, s, ct, d, T, nch, NB, stk, locals())
                    for x0 in range(0, T, CW):
                        bank = pb[(x0 // CW) % 2]
                        bank2 = pb[2 + (x0 // CW) % 2]
                        bank3 = pb[4 + (x0 // CW) % 2]
                        ys_ = yacc[:, x0:x0 + CW]
                        self.mm(bank[:, 0:CW], self.bones_f[:], ys_, True, True, [self.bones_f.b(), yacc.b()], [bank.b()])
                        self.stt("dve", s2[:, x0:x0 + CW], bank[:, 0:CW], -1.0 / 64, ys_, ALU.mult, ALU.add, [bank.b(), yacc.b()], [s2.b()])
                        self.act(sq[:], s2[:, x0:x0 + CW], AF.Square, [s2.b()], [sq.b()])
                        self.mm(bank2[:, 0:CW], self.bones_b[:], sq[:], True, True, [self.bones_b.b(), sq.b()], [bank2.b()])
                        self.act(rstd[:], bank2[:, 0:CW], AF.Sqrt, [bank2.b(), self.epsgn.b()], [rstd.b()], scale=1.0 / 64, bias=self.epsgn[:, 0:1])
                        self.recip(rstd[:], rstd[:], [rstd.b()], [rstd.b()])
                        self.tt("dve", s2[:, x0:x0 + CW], s2[:, x0:x0 + CW], rstd[:], ALU.mult, [s2.b(), rstd.b()], [s2.b()])
                        self.ts("dve", s2[:, x0:x0 + CW], s2[:, x0:x0 + CW], pr["lng"][:, ct:ct + 1], pr["lnb"][:, ct:ct + 1], ALU.mult, ALU.add,
                                [s2.b(), pr["lng"].b(), pr["lnb"].b()], [s2.b()])
                        self.tt("pool", s2[:, x0:x0 + CW], s2[:, x0:x0 + CW], bon[:, x0:x0 + CW], ALU.add, [s2.b(), bon.b()], [s2.b()])
                        self.mm(bank3[:, 0:CW], g2f[:, ct * 128:(ct + 1) * 128], sg[:, x0:x0 + CW], True, True, [g2f.b(), sg.b()], [bank3.b()])
                        o = ob[(x0 // CW) % 2]
                        self.tt("dve", o[:], bank3[:, 0:CW], s2[:, x0:x0 + CW], ALU.mult, [bank3.b(), s2.b()], [o.b()])
                        cc0 = c0s + x0
                        self.dma(self.MIXT[g][12 + ct, :, cc0:cc0 + CW], o[:], [o.b()], [self.MIXT[g].b(cc0 // 512)])
            self.bg_end()
            self.barrier()

    def rwkv_dir(self, l, g, s, ct, d, T, nch, NB, stk, L):
        pr = self.prm[l]
        pb = self.pb
        is_s = (g == "s")
        tw, w2e, sigT, Ptmp, kkn, bA, KM, rc = L["tw"], L["w2e"], L["sigT"], L["Ptmp"], L["kkn"], L["bA"], L["KM"], L["rc"]
        al, be, ka_, rt, PL = L["al"], L["be"], L["ka_"], L["rt"], L["PL"]
        BTt, KTt, VtmP = L["BTt"], L["KTt"], L["VtmP"]
        Xa, XTa, Xb, XTb, Pf, Pfin, MakT, Wbr, Wkr = (L[k] for k in ("Xa", "XTa", "Xb", "XTb", "Pf", "Pfin", "MakT", "Wbr", "Wkr"))
        Sf, Sb, Sld, RHSs, Upad, Ufull, tS, yacc = (L[k] for k in ("Sf", "Sb", "Sld", "RHSs", "Upad", "Ufull", "tS", "yacc"))
        masks = self.masks
        m_n, m_nt, m_s, m_i = ((4, 5, 0, 2) if d == 0 else (5, 4, 1, 3))
        for cp_ in range(0, nch, 2):
            cs = [c for c in (cp_, cp_ + 1) if c < nch]
            bank = pb[(cp_ // 2) % 2]
            for j, c in enumerate(cs):
                sgt = sigT[c % 2]
                bk2 = pb[2 + c % 2]
                self.mm(bk2[:, 0:128], tw[0:65, c * 128:(c + 1) * 128], w2e[0:65, d, ct * 128:(ct + 1) * 128], True, True,
                        [tw.b(), w2e.b()], [bk2.b()])
                self.act(sgt[:], bk2[:, 0:128], AF.Sigmoid, [bk2.b()], [sgt.b()])
                self.mm(bank[:, j * 256:(j + 1) * 256], sgt[:], self.tri2[:, d, :], True, True, [sgt.b(), self.tri2.b()], [bank.b()])
            n = len(cs)
            pt = Ptmp[(cp_ // 2) % 2]
            bv = bank[:, 0:n * 256].rearrange("p (j x) -> p j x", j=n)
            cols = slice(cp_ * 128, (cp_ + n) * 128)
            v3 = lambda tb_: tb_[:, cols].rearrange("p (j x) -> p j x", j=n)
            self.act(pt[:, 0, 0:n * 128].rearrange("p (j x) -> p j x", j=n), bv[:, :, 0:128], AF.Exp, [bank.b()], [pt.b()])
            self.act(pt[:, 1, 0:n * 128].rearrange("p (j x) -> p j x", j=n), bv[:, :, 0:128], AF.Exp, [bank.b()], [pt.b()], scale=-1.0)
            self.act(pt[:, 2, 0:n * 128].rearrange("p (j x) -> p j x", j=n), bv[:, :, 128:256], AF.Exp, [bank.b()], [pt.b()])
            for j, c in enumerate(cs):
                col = (127 if d == 0 else 0)
                self.cp("pool", PL[:, c:c + 1], pt[:, 0, j * 128 + col:j * 128 + col + 1], [pt.b()], [PL.b()])
            w_ = n * 128
            self.tt("dve", al[:, cols], pt[:, 2, 0:w_], kkn[:, cols], ALU.mult, [pt.b(), kkn.b()], [al.b()])
            self.tt("dve", be[:, cols], pt[:, 1, 0:w_], bA[d][:, cols], ALU.mult, [pt.b(), bA[d].b()], [be.b()])
            self.tt("dve", ka_[:, cols], pt[:, 1, 0:w_], KM[d][:, cols], ALU.mult, [pt.b(), KM[d].b()], [ka_.b()])
            self.tt("dve", rt[:, cols], pt[:, 0, 0:w_], rc[:, cols], ALU.mult, [pt.b(), rc.b()], [rt.b()])
        self.bg_step()
        for c in range(nch):
            bank = pb[c % 2]
            self.mm(bank[:, 0:128], be[:, c * 128:(c + 1) * 128], self.ident_b[:], True, True, [be.b(), self.ident_b.b()], [bank.b()])
            self.mm(bank[:, 128:256], ka_[:, c * 128:(c + 1) * 128], self.ident_b[:], True, True, [ka_.b(), self.ident_b.b()], [bank.b()])
            self.cp("act", BTt[:, c, :], bank[:, 0:128], [bank.b()], [BTt.b()])
            self.cp("dve", KTt[:, c, :], bank[:, 128:256], [bank.b()], [KTt.b()])
        self.bg_step()
        if is_s:
            self.memset("pool", Sld[:], 0.0, [Sld.b()])
            for hh in range(2):
                self.dma(Sld[hh * 64:(hh + 1) * 64, hh * 64:(hh + 1) * 64], self.st0[l, d, ct * 2 + hh], [self.st0.b()], [Sld.b()])
            self.tr(pb[7][:, 0:128], Sld[:], self.ident_f[:], [Sld.b(), self.ident_f.b()], [pb[7].b()])
            self.cp("dve", Sf[:], pb[7][:, 0:128], [pb[7].b()], [Sf.b()])
        else:
            self.memset("pool", Sf[:], 0.0, [Sf.b()])
        self.cp("act", Sb[:], Sf[:], [Sf.b()], [Sb.b()])
        order = list(range(nch)) if d == 0 else list(range(nch - 1, -1, -1))
        for b0 in range(0, nch, NB):
            batch = order[b0:b0 + NB]
            nb_ = len(batch)
            grps = [[(c, hh) for c in batch] for hh in range(2)]
            ngrp = 2
            bk = [0]

            def nbank():
                bk[0] += 1
                return pb[bk[0] % 4]
            m4 = self.masks4
            for gi in range(ngrp):
                g4 = slice(gi * 4, gi * 4 + nb_)
                gin = grps[gi]

                def ops(c, hh):
                    rows = slice(hh * 64, (hh + 1) * 64)
                    cc = slice(c * 128, (c + 1) * 128)
                    return al[rows, cc], be[rows, cc], ka_[rows, cc], rt[rows, cc]
                rb = [al.b(), be.b(), ka_.b(), rt.b()]
                for (dst, mi, sel) in ((Xa, m_n, (1, 0)), (XTa, m_nt, (0, 1)), (MakT, m_s, (2, 0)), (Wbr, m_i, (1, 3)), (Wkr, m_i, (2, 3))):
                    bank = nbank()
                    for j, (c, hh) in enumerate(gin):
                        o_ = ops(c, hh)
                        self.mm(bank[:, j * 128:(j + 1) * 128], o_[sel[0]], o_[sel[1]], True, True, rb, [bank.b()])
                    self.tt("dve", dst[:, g4, :], bank[:, 0:nb_ * 128].rearrange("p (j x) -> p j x", j=nb_), m4[:, mi, 0:nb_, :], ALU.mult,
                            [bank.b(), m4.b()], [dst.b(gi)])
                self.tt("pool", Pf[:, g4, :], Xa[:, g4, :], self.ident4[:, 0:nb_, :], ALU.add, [Xa.b(gi), self.ident4.b()], [Pf.b(gi)])
                self.cp("act", Pfin[:, g4, :], Pf[:, g4, :], [Pf.b(gi)], [Pfin.b(gi)])
            X, XT, Xn, XTn = Xa, XTa, Xb, XTb
            for lev in range(6):
                for gi in range(ngrp):
                    g4 = slice(gi * 4, gi * 4 + nb_)
                    if lev < 5:
                        bA_ = nbank()
                        for j in range(nb_):
                            ii = gi * 4 + j
                            self.mm(bA_[:, j * 128:(j + 1) * 128], XT[:, ii, :], X[:, ii, :], True, True, [X.b(gi), XT.b(gi)], [bA_.b()])
                    bB_ = nbank()
                    for j in range(nb_):
                        ii = gi * 4 + j
                        self.mm(bB_[:, j * 128:(j + 1) * 128], X[:, ii, :], XT[:, ii, :], True, True, [X.b(gi), XT.b(gi)], [bB_.b()])
                    if lev < 5:
                        self.cp("act", Xn[:, g4, :], bA_[:, 0:nb_ * 128].rearrange("p (j x) -> p j x", j=nb_), [bA_.b()], [Xn.b(gi)])
                    self.cp("dve", XTn[:, g4, :], bB_[:, 0:nb_ * 128].rearrange("p (j x) -> p j x", j=nb_), [bB_.b()], [XTn.b(gi)])
                    bC_ = nbank()
                    for j in range(nb_):
                        ii = gi * 4 + j
                        self.mm(bC_[:, j * 128:(j + 1) * 128], XTn[:, ii, :], Pfin[:, ii, :], True, True, [XTn.b(gi), Pfin.b(gi)], [bC_.b()])
                    self.tt("dve", Pf[:, g4, :], Pf[:, g4, :], bC_[:, 0:nb_ * 128].rearrange("p (j x) -> p j x", j=nb_), ALU.add, [Pf.b(gi), bC_.b()], [Pf.b(gi)])
                    self.cp("act", Pfin[:, g4, :], Pf[:, g4, :], [Pf.b(gi)], [Pfin.b(gi)])
                X, XT, Xn, XTn = Xn, XTn, X, XT
            self.bg_step()
            for bi, c in enumerate(batch):
                cc = slice(c * 128, (c + 1) * 128)
                p_rhs, p_u, p_y, p_s = pb[4], pb[5], pb[6], pb[7]
                for hh in range(2):
                    ii = hh * 4 + bi
                    rows = slice(hh * 64, (hh + 1) * 64)
                    vcols = slice(hh * 64, (hh + 1) * 64)
                    self.mm(p_rhs[:, vcols], al[rows, cc], Sb[rows, vcols], True, False, [al.b(), Sb.b()], [p_rhs.b()])
                    self.mm(p_rhs[:, vcols], MakT[:, ii, :], VtmP[:, c, hh, vcols], False, True, [MakT.b(ii // 4), VtmP.b()], [p_rhs.b()])
                self.cp("act", RHSs[:], p_rhs[:, 0:128], [p_rhs.b()], [RHSs.b()])
                for hh in range(2):
                    ii = hh * 4 + bi
                    vcols = slice(hh * 64, (hh + 1) * 64)
                    self.mm(p_u[:, vcols], Pfin[:, ii, :], RHSs[:, vcols], True, True, [Pfin.b(ii // 4), RHSs.b()], [p_u.b()])
                self.ts("dve", Ufull[:], p_u[:, 0:128], -1.0, None, ALU.mult, None, [p_u.b()], [Ufull.b()])
                for hh in range(2):
                    vcols = slice(hh * 64, (hh + 1) * 64)
                    self.cp("act" if hh else "dve", Upad[:, hh, vcols], Ufull[:, vcols], [Ufull.b()], [Upad.b()])
                self.mm(p_y[:, 0:128], Sb[:], rt[:, cc], True, False, [Sb.b(), rt.b()], [p_y.b()])
                for hh in range(2):
                    ii = hh * 4 + bi
                    self.mm(p_y[:, 0:128], Upad[:, hh, :], Wbr[:, ii, :], False, False, [Upad.b(), Wbr.b(ii // 4)], [p_y.b()])
                    self.mm(p_y[:, 0:128], VtmP[:, c, hh, :], Wkr[:, ii, :], False, hh == 1, [VtmP.b(), Wkr.b(ii // 4)], [p_y.b()])
                if d == 0:
                    self.cp("act", yacc[:, cc], p_y[:, 0:128], [p_y.b()], [yacc.b()])
                else:
                    self.tt("dve", yacc[:, cc], yacc[:, cc], p_y[:, 0:128], ALU.add, [yacc.b(), p_y.b()], [yacc.b()])
                self.mm(p_s[:, 0:128], BTt[:, c, :], Ufull[:], True, False, [BTt.b(), Ufull.b()], [p_s.b()])
                for hh in range(2):
                    self.mm(p_s[:, 0:128], KTt[:, c, :], VtmP[:, c, hh, :], False, hh == 1, [KTt.b(), VtmP.b()], [p_s.b()])
                self.stt("dve", tS[:], p_s[:, 0:128], PL[:, c:c + 1], self.bones_f[:], ALU.mult, ALU.mult,
                         [p_s.b(), PL.b(), self.bones_f.b()], [tS.b()])
                self.stt("dve", Sf[:], Sf[:], PL[:, c:c + 1], tS[:], ALU.mult, ALU.add, [Sf.b(), PL.b(), tS.b()], [Sf.b()])
                self.cp("act", Sb[:], Sf[:], [Sf.b()], [Sb.b()])
        self.bg_step()
        if not is_s:
            self.tr(pb[7][:, 0:128], Sf[:], self.ident_f[:], [Sf.b(), self.ident_f.b()], [pb[7].b()])
            self.cp("dve", Sld[:], pb[7][:, 0:128], [pb[7].b()], [Sld.b()])
            for hh in range(2):
                self.dma(self.ns[s, l, d, ct * 2 + hh], Sld[hh * 64:(hh + 1) * 64, hh * 64:(hh + 1) * 64], [Sld.b()], [self.ns.b()])

    def tiles(self):
        out = []
        for g, (Tg, nseq, Tseq) in self.groups.items():
            for c0 in range(0, Tg, 512):
                out.append((g, c0))
        return out

    def build(self):
        stages = self.cfg.get("stages", "all")
        self.declare()
        self.setup_consts()
        if stages != "nocast":
            self.cast_weights()
        if stages == "s0a":
            self.P.emit()
            return self.nc
        self.setup_params()
        if stages == "s0b":
            self.P.emit()
            return self.nc
        self.to_feature_major()
        if stages in ("s0c", "nocast"):
            self.P.emit()
            return self.nc
        X, Y = self.XT, self.XTB
        for l in range(self.DEPTH):
            for (g, c0) in self.tiles():
                if g in self.cfg.get("s1_groups", "sp"):
                    self.stage1_x(l, g, c0, X)
            if stages == "s1":
                break
            for g in self.groups:
                self.stage_fourier(l, g)
                self.stage_attn(l, g)
                if stages == "s2fa":
                    self.zero_rwkv(g)
                else:
                    self.stage_rwkv(l, g)
            if stages == "s2":
                break
            for (g, c0) in self.tiles():
                self.stage3(l, g, c0, X)
            for (g, c0) in self.tiles():
                self.stage4(l, g, c0, X, Y)
            X, Y = Y, X
        if stages in ("all", "s2fa"):
            for (g, c0) in self.tiles():
                self.stage5(g, c0, X)
        self.P.emit()
        return self.nc


def make_in_maps(inputs, cfg, ncores):
    TS, TP, NPS, DEPTH = cfg["TS"], cfg["TP"], cfg["NPS"], cfg["DEPTH"]
    consts = host_consts(TS, TP)
    f = lambda a: np.ascontiguousarray(np.asarray(a, dtype=np.float32))
    shared = {}
    for k in ("w_ada", "b_ada", "norm1_g", "norm2_g", "w_in", "w_out", "q_norm_g", "k_norm_g", "rw_conv",
              "rw_w0", "rw_w2", "rw_a0", "rw_a2", "rw_g2", "rw_kk", "rw_ka", "rw_lnx_g", "rw_lnx_b",
              "ffn_up", "ffn_conv_w", "ffn_conv_b", "ffn_down", "final_norm_g"):
        shared[k] = f(inputs[k])
    shared["rw_rk"] = f(inputs["rw_rk"]).reshape(DEPTH, 512)
    shared.update(consts)
    maps = []
    for b in range(ncores):
        m = dict(shared)
        m["xs"] = f(inputs["x_sample"][b])
        m["xp"] = f(inputs["x_prompt"][NPS * b:NPS * (b + 1)]).reshape(NPS * TP, D)
        m["ck"] = f(inputs["cache_attn_k"][b]).reshape(DEPTH, PAST, 256)
        m["cv"] = f(inputs["cache_attn_v"][b]).reshape(DEPTH, PAST, 256)
        m["st0"] = f(inputs["state_rwkv"][b])
        m["cc"] = np.stack([f(inputs["c"][b]), f(inputs["c_ctx"])], 0)
        maps.append(m)
    return maps


_CACHE = {}


def kernel(**inputs):
    xs = np.asarray(inputs["x_sample"])
    xp = np.asarray(inputs["x_prompt"])
    ncores = xs.shape[0]
    TS, TP = xs.shape[1], xp.shape[1]
    NPS = xp.shape[0] // ncores
    DEPTH = np.asarray(inputs["w_in"]).shape[0]
    DFF = np.asarray(inputs["ffn_down"]).shape[1]
    cfg = dict(TS=TS, TP=TP, NPS=NPS, DEPTH=DEPTH, DFF=DFF, stages="all")
    key = (TS, TP, NPS, DEPTH, DFF)
    if key not in _CACHE:
        kb = KB(cfg)
        _CACHE[key] = kb.build()
    nc = _CACHE[key]
    maps = make_in_maps(inputs, cfg, ncores)
    decl = set()
    for alloc in nc.allocations:
        if isinstance(alloc, mybir.MemoryLocationSet) and alloc.kind == "ExternalInput":
            decl.add(alloc.memorylocations[0].name)
    maps = [{k: v for k, v in m.items() if k in decl} for m in maps]
    res = run_bass_kernel_spmd(nc, maps, core_ids=list(range(ncores)))
    r = res.results
    y_sample = np.stack([np.asarray(r[b]["ys"], np.float32) for b in range(ncores)], 0)
    y_prompt = np.concatenate([np.asarray(r[b]["yp"], np.float32).reshape(NPS, TP, D) for b in range(ncores)], 0)
    nk = np.concatenate([np.asarray(r[b]["nk"], np.float32).reshape(NPS, DEPTH, TP, NKV, 128) for b in range(ncores)], 0)
    nv = np.concatenate([np.asarray(r[b]["nv"], np.float32).reshape(NPS, DEPTH, TP, NKV, 128) for b in range(ncores)], 0)
    ns = np.concatenate([np.asarray(r[b]["ns"], np.float32) for b in range(ncores)], 0)
    return (y_prompt, y_sample, nk, nv, ns)
```
